# Optimizing a Trainium2 kernel written in Bass

```python
import math
import jax, jax.numpy as jnp
from jax import lax
import numpy as np

D_MODEL = 1024
BATCH = 8
SEQ = 4096
DEPTH = 1

N_HEADS = 8
N_KV = 2
HEAD_DIM = 64
HPG = N_HEADS // N_KV
NSA_WIDTH = N_HEADS * HEAD_DIM
KV_WIDTH = N_KV * HEAD_DIM
ROPE_DIM = HEAD_DIM // 4
ROPE_THETA = 500000.0
CMP_BLOCK = 32
CMP_STRIDE = 16
SLC_BLOCK = 64
N_SLC = 16
WINDOW = 512
Q_BLOCK = 128
PHI_HIDDEN = 256
N_BRANCH = 3
LRU_WIDTH = D_MODEL
LRU_BLOCKS = 8
LRU_BD = LRU_WIDTH // LRU_BLOCKS
LRU_C = 8.0
CONV_W = 4
D_FF = 4 * D_MODEL
EPS = 1e-6
NEG = -1e30
FORCE_SCORE = 1e4
COL_WIDTHS = (NSA_WIDTH,) + (KV_WIDTH,) * 6 + (N_BRANCH * N_HEADS, LRU_WIDTH, LRU_WIDTH, 2 * D_MODEL)
IN_COLS = sum(COL_WIDTHS)

kernel_name = "hybrid_nsa_rglru_sqrelu_block"


def rmsnorm(x, w):
    xf = x.astype(jnp.float32)
    y = xf * lax.rsqrt(jnp.mean(xf * xf, axis=-1, keepdims=True) + EPS)
    return (y * w.astype(jnp.float32)).astype(x.dtype)


def rope_partial(x, pos):
    half = ROPE_DIM // 2
    inv = ROPE_THETA ** (-jnp.arange(half, dtype=jnp.float32) / half)
    ang = pos.astype(jnp.float32)[:, None] * inv[None, :]
    cos, sin = jnp.cos(ang), jnp.sin(ang)
    xr = x[..., :ROPE_DIM].astype(jnp.float32)
    x1, x2 = xr[..., :half], xr[..., half:]
    rot = jnp.concatenate([x1 * cos - x2 * sin, x2 * cos + x1 * sin], axis=-1).astype(x.dtype)
    return jnp.concatenate([rot, x[..., ROPE_DIM:]], axis=-1)


def kv_heads(t):
    b, s, _ = t.shape
    return t.reshape(b, s, N_KV, HEAD_DIM).transpose(0, 2, 1, 3)


def compress_mlp(blocks, pos_emb, w1, w2):
    z = (blocks + pos_emb).reshape(blocks.shape[:3] + (CMP_BLOCK * HEAD_DIM,))
    return jax.nn.gelu(z @ w1) @ w2


def nsa_mixer(q_raw, kc_raw, vc_raw, ks_raw, vs_raw, kw_raw, vw_raw, g_raw,
              q_norm_w, k_norm_w, phi_k_pos, phi_k_w1, phi_k_w2, phi_v_pos, phi_v_w1, phi_v_w2):
    B, S, _ = q_raw.shape
    dt = q_raw.dtype
    scale = HEAD_DIM ** -0.5
    t = jnp.arange(S, dtype=jnp.int32)
    nqb = S // Q_BLOCK

    q = q_raw.reshape(B, S, N_KV, HPG, HEAD_DIM).transpose(0, 2, 3, 1, 4)
    q = rope_partial(rmsnorm(q, q_norm_w), t)

    n_cmp = (S - CMP_BLOCK) // CMP_STRIDE + 1
    cmp_idx = jnp.arange(n_cmp)[:, None] * CMP_STRIDE + jnp.arange(CMP_BLOCK)[None, :]
    cmp_end = cmp_idx[:, -1]
    kc = compress_mlp(kv_heads(kc_raw)[:, :, cmp_idx], phi_k_pos, phi_k_w1, phi_k_w2)
    vc = compress_mlp(kv_heads(vc_raw)[:, :, cmp_idx], phi_v_pos, phi_v_w1, phi_v_w2)
    kc = rope_partial(rmsnorm(kc, k_norm_w[0]), cmp_end)
    s_c = jnp.einsum('bghsd,bgnd->bghsn', q, kc).astype(jnp.float32) * scale
    mask_c = cmp_end[None, :] <= t[:, None]
    p_c = jax.nn.softmax(jnp.where(mask_c, s_c, NEG), axis=-1)
    p_c = jnp.where((t >= CMP_BLOCK - 1)[:, None], p_c, 0.0)
    o_cmp = jnp.einsum('bghsn,bgnd->bghsd', p_c.astype(dt), vc)

    n_slc = S // SLC_BLOCK
    n_sel = min(N_SLC, n_slc)
    blk = jnp.arange(n_slc)
    slc_lo = blk * SLC_BLOCK
    overlap = ((cmp_idx[:, 0][:, None] <= (slc_lo + SLC_BLOCK - 1)[None, :]) &
               (cmp_end[:, None] >= slc_lo[None, :])).astype(jnp.float32)
    imp = jnp.einsum('bghsn,nj->bgsj', p_c, overlap)
    cur = t // SLC_BLOCK
    forced = (blk[None, :] == 0) | (blk[None, :] == cur[:, None]) | (blk[None, :] == cur[:, None] - 1)
    causal_ok = blk[None, :] <= cur[:, None]
    score = jnp.where(causal_ok, jnp.where(forced, FORCE_SCORE, imp), NEG)
    _, sel = lax.top_k(score, n_sel)

    k_s = rope_partial(rmsnorm(kv_heads(ks_raw), k_norm_w[1]), t)
    v_s = kv_heads(vs_raw)
    b_ix = jnp.arange(B)[:, None, None]
    g_ix = jnp.arange(N_KV)[None, :, None]
    q_blk = q.reshape(B, N_KV, HPG, nqb, Q_BLOCK, HEAD_DIM).transpose(3, 0, 1, 2, 4, 5)
    sel_blk = sel.reshape(B, N_KV, nqb, Q_BLOCK, n_sel).transpose(2, 0, 1, 3, 4)
    t_blk = t.reshape(nqb, Q_BLOCK)

    def attend_selected(args):
        qb, sb, tb = args
        tok = sb[..., None] * SLC_BLOCK + jnp.arange(SLC_BLOCK)
        tok = tok.reshape(B, N_KV, Q_BLOCK * n_sel * SLC_BLOCK)
        kg = k_s[b_ix, g_ix, tok].reshape(B, N_KV, Q_BLOCK, n_sel * SLC_BLOCK, HEAD_DIM)
        vg = v_s[b_ix, g_ix, tok].reshape(B, N_KV, Q_BLOCK, n_sel * SLC_BLOCK, HEAD_DIM)
        s = jnp.einsum('bghqd,bgqkd->bghqk', qb, kg).astype(jnp.float32) * scale
        m = tok.reshape(B, N_KV, Q_BLOCK, n_sel * SLC_BLOCK) <= tb[None, None, :, None]
        p = jax.nn.softmax(jnp.where(m[:, :, None], s, NEG), axis=-1)
        return jnp.einsum('bghqk,bgqkd->bghqd', p.astype(dt), vg)

    o_slc = lax.map(attend_selected, (q_blk, sel_blk, t_blk))
    o_slc = o_slc.transpose(1, 2, 3, 0, 4, 5).reshape(B, N_KV, HPG, S, HEAD_DIM)

    k_w = rope_partial(rmsnorm(kv_heads(kw_raw), k_norm_w[2]), t)
    v_w = kv_heads(vw_raw)
    win_pos = jnp.arange(nqb)[:, None] * Q_BLOCK + jnp.arange(WINDOW + Q_BLOCK)[None, :] - WINDOW
    safe = jnp.clip(win_pos, 0, S - 1)
    kb = k_w[:, :, safe]
    vb = v_w[:, :, safe]
    qw = q.reshape(B, N_KV, HPG, nqb, Q_BLOCK, HEAD_DIM)
    s_w = jnp.einsum('bghnqd,bgnkd->bghnqk', qw, kb).astype(jnp.float32) * scale
    kp = win_pos[:, None, :]
    tq = t_blk[:, :, None]
    mask_w = (kp <= tq) & (kp > tq - WINDOW) & (kp >= 0)
    p_w = jax.nn.softmax(jnp.where(mask_w, s_w, NEG), axis=-1)
    o_win = jnp.einsum('bghnqk,bgnkd->bghnqd', p_w.astype(dt), vb).reshape(B, N_KV, HPG, S, HEAD_DIM)

    g = jax.nn.sigmoid(g_raw).reshape(B, S, N_BRANCH, N_KV, HPG).transpose(2, 0, 3, 4, 1)[..., None]
    o = g[0] * o_cmp + g[1] * o_slc + g[2] * o_win
    return o.transpose(0, 3, 1, 2, 4).reshape(B, S, NSA_WIDTH)


def rglru_mixer(xr, yr, conv_w, conv_b, lru_wa, lru_ba, lru_wx, lru_bx, lru_lambda):
    B, S, W = xr.shape
    xp = jnp.pad(xr, ((0, 0), (CONV_W - 1, 0), (0, 0)))
    xc = conv_b + sum(xp[:, j:j + S] * conv_w[j] for j in range(CONV_W))
    xb = xc.reshape(B, S, LRU_BLOCKS, LRU_BD)
    r = jax.nn.sigmoid(jnp.einsum('bsnd,nde->bsne', xb, lru_wa) + lru_ba).reshape(B, S, W)
    i_g = jax.nn.sigmoid(jnp.einsum('bsnd,nde->bsne', xb, lru_wx) + lru_bx).reshape(B, S, W)
    log_a = -LRU_C * jax.nn.softplus(-lru_lambda.astype(jnp.float32)) * r.astype(jnp.float32)
    a = jnp.exp(log_a)
    bterm = jnp.sqrt(-jnp.expm1(2.0 * log_a)) * (i_g.astype(jnp.float32) * xc.astype(jnp.float32))

    def combine(left, right):
        a1, b1 = left
        a2, b2 = right
        return a1 * a2, a2 * b1 + b2

    _, h = lax.associative_scan(combine, (a, bterm), axis=1)
    return h.astype(xr.dtype) * jax.nn.gelu(yr)


def setup_inputs(seed: int = 0) -> dict:
    key = jax.random.key(seed)
    ks = jax.random.split(key, 24)
    L = DEPTH

    def nrm(k, shape, scale):
        return jax.random.normal(k, shape, jnp.float32) * scale

    u = jax.random.uniform(ks[17], (L, LRU_WIDTH), jnp.float32, 0.9, 0.999)
    a0 = u ** (1.0 / LRU_C)
    return {
        "x": nrm(ks[0], (BATCH, SEQ, D_MODEL), 1.0),
        "norm1_w": 1.0 + nrm(ks[1], (L, D_MODEL), 0.02),
        "w_in": nrm(ks[2], (L, D_MODEL, IN_COLS), D_MODEL ** -0.5),
        "q_norm_w": 1.0 + nrm(ks[3], (L, HEAD_DIM), 0.02),
        "k_norm_w": 1.0 + nrm(ks[4], (L, N_BRANCH, HEAD_DIM), 0.02),
        "phi_k_pos": nrm(ks[5], (L, CMP_BLOCK, HEAD_DIM), 0.02),
        "phi_k_w1": nrm(ks[6], (L, CMP_BLOCK * HEAD_DIM, PHI_HIDDEN), (CMP_BLOCK * HEAD_DIM) ** -0.5),
        "phi_k_w2": nrm(ks[7], (L, PHI_HIDDEN, HEAD_DIM), PHI_HIDDEN ** -0.5),
        "phi_v_pos": nrm(ks[8], (L, CMP_BLOCK, HEAD_DIM), 0.02),
        "phi_v_w1": nrm(ks[9], (L, CMP_BLOCK * HEAD_DIM, PHI_HIDDEN), (CMP_BLOCK * HEAD_DIM) ** -0.5),
        "phi_v_w2": nrm(ks[10], (L, PHI_HIDDEN, HEAD_DIM), PHI_HIDDEN ** -0.5),
        "conv_w": nrm(ks[11], (L, CONV_W, LRU_WIDTH), CONV_W ** -0.5),
        "conv_b": nrm(ks[12], (L, LRU_WIDTH), 0.01),
        "lru_wa": nrm(ks[13], (L, LRU_BLOCKS, LRU_BD, LRU_BD), LRU_BD ** -0.5),
        "lru_ba": nrm(ks[14], (L, LRU_BLOCKS, LRU_BD), 0.01),
        "lru_wx": nrm(ks[15], (L, LRU_BLOCKS, LRU_BD, LRU_BD), LRU_BD ** -0.5),
        "lru_bx": nrm(ks[16], (L, LRU_BLOCKS, LRU_BD), 0.01),
        "lru_lambda": jnp.log(a0) - jnp.log1p(-a0),
        "w_nsa_up": nrm(ks[18], (L, NSA_WIDTH, D_MODEL), NSA_WIDTH ** -0.5),
        "w_lru_up": nrm(ks[19], (L, LRU_WIDTH, D_MODEL), LRU_WIDTH ** -0.5),
        "w_o": nrm(ks[20], (L, D_MODEL, D_MODEL), D_MODEL ** -0.5),
        "norm2_w": 1.0 + nrm(ks[21], (L, D_MODEL), 0.02),
        "w_ff1": nrm(ks[22], (L, D_MODEL, D_FF), D_MODEL ** -0.5),
        "w_ff2": nrm(ks[23], (L, D_FF, D_MODEL), D_FF ** -0.5),
    }


def reference(x, norm1_w, w_in, q_norm_w, k_norm_w, phi_k_pos, phi_k_w1, phi_k_w2,
              phi_v_pos, phi_v_w1, phi_v_w2, conv_w, conv_b, lru_wa, lru_ba, lru_wx, lru_bx,
              lru_lambda, w_nsa_up, w_lru_up, w_o, norm2_w, w_ff1, w_ff2):
    h = x
    for l in range(DEPTH):
        xn = rmsnorm(h, norm1_w[l])
        proj = xn @ w_in[l]
        parts = []
        off = 0
        for wdt in COL_WIDTHS:
            parts.append(proj[..., off:off + wdt])
            off += wdt
        q_raw, kc_raw, vc_raw, ks_raw, vs_raw, kw_raw, vw_raw, g_nsa, xr, yr, g_merge = parts

        o_nsa = nsa_mixer(q_raw, kc_raw, vc_raw, ks_raw, vs_raw, kw_raw, vw_raw, g_nsa,
                          q_norm_w[l], k_norm_w[l], phi_k_pos[l], phi_k_w1[l], phi_k_w2[l],
                          phi_v_pos[l], phi_v_w1[l], phi_v_w2[l])
        o_lru = rglru_mixer(xr, yr, conv_w[l], conv_b[l], lru_wa[l], lru_ba[l], lru_wx[l],
                            lru_bx[l], lru_lambda[l])
        gate = jax.nn.sigmoid(g_merge)
        merged = gate[..., :D_MODEL] * (o_nsa @ w_nsa_up[l]) + gate[..., D_MODEL:] * (o_lru @ w_lru_up[l])
        h = h + merged @ w_o[l]

        hn = rmsnorm(h, norm2_w[l])
        h = h + jnp.square(jax.nn.relu(hn @ w_ff1[l])) @ w_ff2[l]
    return h
```

```python
import math
from contextlib import ExitStack
import numpy as np
import concourse.bass as bass
import concourse.mybir as mybir
from concourse.bass_utils import run_bass_kernel_spmd

F32 = mybir.dt.float32
BF16 = mybir.dt.bfloat16
AF = mybir.ActivationFunctionType
ALU = mybir.AluOpType
AX = mybir.AxisListType

D = 1024
DFF = 4096
EPS = 1e-6
NEGM = -30000.0
GC1 = 0.044715
GC2 = 2.0 * math.sqrt(2.0 / math.pi)


class Sched:
    CE = ('pe', 'dve', 'act', 'pool')

    def __init__(self, nc, stack, ndma=8, limit=16000, strict_same=True):
        self.nc = nc
        self.stack = stack
        self.limit = limit
        self.strict_same = strict_same
        self.prog = {e: [] for e in ('pe', 'dve', 'act', 'pool', 'sp')}
        self.nsem = 0
        self.cur_sem = {}
        self.cnt = {}
        for e in self.CE:
            self.cur_sem[e] = self._newsem(e)
            self.cnt[e] = 0
        self.dma_sems = {q: [self._newsem('d' + q) for _ in range(ndma)] for q in ('sp', 'pool')}
        self.dma_n = {q: 0 for q in ('sp', 'pool')}
        self.waited = {e: {} for e in self.prog}
        self.lastw = {}
        self.readers = {}
        self.last_tok = {}

    def _newsem(self, tag):
        self.nsem += 1
        s = self.stack.enter_context(self.nc.semaphore(f"s{self.nsem}_{tag}"))
        return (self.nsem, s)

    PSUM = frozenset(['tp0', 'tp1', 'cp0', 'cp1', 'hp0', 'hp1', 'kp', 'tb', 'pj0', 'pj1', 'st0', 'st1', 'pv', 'mz',
                      'pa', 'pbk', 'yp0', 'yp1'])

    def _deps(self, reads, writes, eng=None):
        deps = []
        for b in reads:
            if b in self.lastw:
                deps.append(self.lastw[b])
            if b in self.PSUM:
                deps.extend(t for t in self.readers.get(b, ()) if t[2] != eng)
        for b in writes:
            if b in self.lastw:
                deps.append(self.lastw[b])
            deps.extend(self.readers.get(b, ()))
        return deps

    def _waits(self, eng, deps):
        waits = []
        w = self.waited[eng]
        for (sem, val, src) in deps:
            if src == eng and (eng == 'pe' or not self.strict_same):
                continue
            if w.get(sem[0], 0) >= val:
                continue
            w[sem[0]] = val
            waits.append((sem[1], val))
        return waits

    def _commit(self, tok, reads, writes):
        for b in reads:
            if b not in writes:
                self.readers.setdefault(b, []).append(tok)
        for b in writes:
            self.lastw[b] = tok
            self.readers[b] = []

    def op(self, eng, fn, reads=(), writes=()):
        deps = self._deps(reads, writes, eng)
        waits = self._waits(eng, deps)
        if self.cnt[eng] >= self.limit:
            self.cur_sem[eng] = self._newsem(eng)
            self.cnt[eng] = 0
        self.cnt[eng] += 1
        sem = self.cur_sem[eng]
        tok = (sem, self.cnt[eng], eng)
        self.last_tok[eng] = tok
        self.prog[eng].append((waits, fn, (sem[1], 1)))
        self._commit(tok, reads, writes)
        return tok

    def dma(self, q, fn, reads=(), writes=()):
        deps = self._deps(reads, writes)
        j = self.dma_n[q]
        self.dma_n[q] += 1
        K = len(self.dma_sems[q])
        sem = self.dma_sems[q][j % K]
        if j >= K:
            deps.append((sem, 16 * (j // K), 'dma'))
        waits = self._waits(q, deps)
        tok = (sem, 16 * (j // K + 1), 'dma')
        self.prog[q].append((waits, fn, (sem[1], 16)))
        self._commit(tok, reads, writes)
        return tok

    def _dma_final(self):
        deps = []
        for q in self.dma_sems:
            K = len(self.dma_sems[q])
            n = self.dma_n[q]
            for i, sem in enumerate(self.dma_sems[q]):
                cnt = (n - i + K - 1) // K if n > i else 0
                if cnt > 0:
                    deps.append((sem, 16 * cnt, 'dma'))
        return deps

    def barrier(self):
        deps = self._dma_final() + list(self.last_tok.values())
        for e in self.prog:
            saved = self.strict_same
            self.strict_same = True
            waits = []
            w = self.waited[e]
            for (sem, val, src) in deps:
                if w.get(sem[0], 0) >= val:
                    continue
                w[sem[0]] = val
                waits.append((sem[1], val))
            self.strict_same = saved
            self.prog[e].append((waits, None, None))
        self.lastw = {}
        self.readers = {}

    def finish(self):
        self.barrier()

    def emit(self, block):
        def run(engobj, items):
            for waits, fn, inc in items:
                for (s, v) in waits:
                    engobj.wait_ge(s, v)
                if fn is not None:
                    ins = fn(engobj)
                    ins.then_inc(inc[0], inc[1])

        P = self.prog

        @block.sync
        def _(e):
            run(e, P['sp'])

        @block.tensor
        def _(e):
            run(e, P['pe'])

        @block.vector
        def _(e):
            run(e, P['dve'])

        @block.scalar
        def _(e):
            run(e, P['act'])

        @block.gpsimd
        def _(e):
            run(e, P['pool'])


def cst_layout(NT):
    off = {}
    o = 0

    def add(name, n):
        nonlocal o
        off[name] = (o, o + n)
        o += n
    add('ident', 128)
    add('nw1', 8)
    add('nw2', 8)
    add('eps', 1)
    add('wq', 64)
    add('wk0', 64)
    add('wk1', 64)
    add('wk2', 64)
    add('posk', 32)
    add('posv', 32)
    add('cw', 32)
    add('cb', 8)
    add('ba', 8)
    add('bx', 8)
    add('lam', 8)
    add('cos', NT * 8)
    add('sin', NT * 8)
    add('cosc', 16)
    add('sinc', 16)
    add('tkA', 127)
    add('tkB', 127)
    add('ovl', 130)
    return off, o


DBG = {'stop': 99}


def build_program(NT=32, passes=(0, 1, 2, 3), dbg=False):
    S_ = NT * 128
    NCMP = (S_ - 32) // 16 + 1
    NSLC = S_ // 64
    CO, NCST = cst_layout(NT)
    nc = bass.Bass("TRN2", target_bir_lowering=False)

    def din(name, shape, dt=F32):
        return nc.dram_tensor(name, shape, dt, kind="ExternalInput").ap()

    x_d = din("x", [S_, D])
    cst_d = din("cst", [128, NCST])
    wtm_d = din("wtm", [128, 8 * 1048])
    wcv_d = din("wcv", [128, 8 * 256])
    wxy_d = din("wxy", [128, 8 * 2048])
    wgm_d = din("wgm", [128, 8 * 2048])
    pk1_d = din("pk1", [128, 32 * 256])
    pv1_d = din("pv1", [128, 32 * 256])
    pk2_d = din("pk2", [128, 2 * 64])
    pv2_d = din("pv2", [128, 2 * 64])
    wa_d = din("wa", [128, 8 * 128])
    wx_d = din("wx", [128, 8 * 128])
    wnu_d = din("wnu", [128, 4 * 1024])
    wlu_d = din("wlu", [128, 8 * 1024])
    wo_d = din("wo", [128, 8 * 1024])
    wf1_d = din("wf1", [128, 8 * DFF])
    wf2_d = din("wf2", [128, 32 * D])
    E_d = din("emat", [64, S_])
    out_d = nc.dram_tensor("out", [S_, D], F32, kind="ExternalOutput").ap()
    son_d = nc.dram_tensor("sc_on", [NT * 128, 512], BF16, kind="Internal").ap()
    sol_d = nc.dram_tensor("sc_ol", [NT * 128, 1024], BF16, kind="Internal").ap()

    with ExitStack() as top:
        S = Sched(nc, top)

        uniq = [0]

        def sbt(st, name, shape, dt):
            uniq[0] += 1
            return st.enter_context(nc.sbuf_tensor(f"sb{uniq[0]}_{name}", shape, dt))

        def pst(st, name, shape, dt):
            uniq[0] += 1
            return st.enter_context(nc.psum_tensor(f"ps{uniq[0]}_{name}", shape, dt))

        cst = sbt(top, "cst", [128, NCST], F32)
        identb = sbt(top, "identb", [128, 128], BF16)
        kcT = sbt(top, "kcT", [64, 2, 256], BF16)
        vca = sbt(top, "vca", [128, 2, 2, 129], BF16)

        def C(name):
            a, b = CO[name]
            return cst[:, a:b]
        ident = C('ident')
        epsc = C('eps')

        S.dma('sp', lambda e: e.dma_start(out=cst[:], in_=cst_d[:, :]), writes=['cst'])
        S.op('dve', lambda e: e.tensor_copy(out=identb[:], in_=ident), reads=['cst'], writes=['identb'])

        def load_w(dst3, src_d, nk, ncol, name):
            step = max(1, 2048 // ncol)
            if ncol > 2048:
                for k in range(nk):
                    for c0 in range(0, ncol, 2048):
                        c1 = min(ncol, c0 + 2048)
                        S.dma('pool', lambda e, k=k, c0=c0, c1=c1: e.dma_start(
                            out=dst3[:, k, c0:c1], in_=src_d[:, k * ncol + c0:k * ncol + c1]), writes=[name])
            else:
                for k0 in range(0, nk, step):
                    k1 = min(nk, k0 + step)
                    S.dma('pool', lambda e, k0=k0, k1=k1: e.dma_start(
                        out=dst3[:, k0:k1, :],
                        in_=src_d[:, k0 * ncol:k1 * ncol].rearrange("p (a b) -> p a b", a=k1 - k0)), writes=[name])

        def norm_tile(xb, xbn, nwname, xs, ss, tp, xnT):
            S.op('act', lambda e: e.activation(out=xs[:], in_=xb[:], func=AF.Square, accum_out=ss[:, 0:1]),
                 reads=[xbn], writes=['xs', 'ss0'])
            S.op('act', lambda e: e.activation(out=ss[:, 1:2], in_=ss[:, 0:1], func=AF.Sqrt, scale=1.0 / D, bias=epsc),
                 reads=['ss0', 'cst'], writes=['ss1'])
            S.op('dve', lambda e: e.reciprocal(out=ss[:, 2:3], in_=ss[:, 1:2]), reads=['ss1'], writes=['ss2'])
            S.op('act', lambda e: e.activation(out=xs[:], in_=xb[:], func=AF.Copy, scale=ss[:, 2:3]),
                 reads=[xbn, 'ss2'], writes=['xs'])
            for k in range(8):
                S.op('pe', lambda e, k=k: e.transpose(out=tp[:, k * 128:(k + 1) * 128], in_=xs[:, k * 128:(k + 1) * 128],
                                                      identity=ident), reads=['xs', 'cst'], writes=[f'tp{k // 4}'])
            nw = C(nwname)
            for k in range(8):
                S.op('dve', lambda e, k=k: e.tensor_scalar(out=xnT[:, k, :], in0=tp[:, k * 128:(k + 1) * 128],
                                                           scalar1=nw[:, k:k + 1], scalar2=None, op0=ALU.mult),
                     reads=[f'tp{k // 4}', 'cst'], writes=['xnT'])

        def gelu(P_, shape_free, src, srcn, dst, dstn, t1, t1n, t2, t2n, src_psum=False):
            S.op('act', lambda e: e.activation(out=t1, in_=src, func=AF.Square), reads=[srcn], writes=[t1n])
            S.op('dve', lambda e: e.tensor_scalar(out=t1, in0=t1, scalar1=GC1, scalar2=1.0, op0=ALU.mult, op1=ALU.add),
                 reads=[t1n], writes=[t1n])
            S.op('dve', lambda e: e.tensor_tensor(out=t1, in0=t1, in1=src, op=ALU.mult), reads=[t1n, srcn], writes=[t1n])
            S.op('act', lambda e: e.activation(out=t2, in_=t1, func=AF.Sigmoid, scale=GC2), reads=[t1n], writes=[t2n])
            S.op('dve', lambda e: e.tensor_tensor(out=dst, in0=t2, in1=src, op=ALU.mult), reads=[t2n, srcn], writes=[dstn])

        def rmsrope(P_, H, src3, srcn, wrep, cosp, sinp, out3, outn, tmp, pre):
            sq = tmp['sq'][0:P_, 0:H, :]
            y = tmp['y'][0:P_, 0:H, :]
            st = tmp['st']
            r1 = tmp['r1'][0:P_, 0:H, :]
            r2 = tmp['r2'][0:P_, 0:H, :]
            S.op('pool', lambda e: e.tensor_tensor(out=sq, in0=src3, in1=src3, op=ALU.mult), reads=[srcn], writes=[pre + 'sq'])
            S.op('dve', lambda e: e.tensor_reduce(out=st[0:P_, 0:H], in_=sq, axis=AX.X, op=ALU.add),
                 reads=[pre + 'sq'], writes=[pre + 'st0'])
            S.op('act', lambda e: e.activation(out=st[0:P_, 8:8 + H], in_=st[0:P_, 0:H], func=AF.Sqrt, scale=1.0 / 64,
                                               bias=epsc[0:P_, :]), reads=[pre + 'st0', 'cst'], writes=[pre + 'st1'])
            S.op('dve', lambda e: e.reciprocal(out=st[0:P_, 16:16 + H], in_=st[0:P_, 8:8 + H]),
                 reads=[pre + 'st1'], writes=[pre + 'st2'])
            S.op('dve', lambda e: e.tensor_tensor(out=y, in0=src3,
                                                  in1=st[0:P_, 16:16 + H].unsqueeze(2).broadcast_to([P_, H, 64]), op=ALU.mult),
                 reads=[srcn, pre + 'st2'], writes=[pre + 'y'])
            S.op('dve', lambda e: e.tensor_tensor(out=y, in0=y, in1=wrep.unsqueeze(1).broadcast_to([P_, H, 64]), op=ALU.mult),
                 reads=[pre + 'y', 'cst', 'wq8'], writes=[pre + 'y'])
            cb_ = cosp.unsqueeze(1).broadcast_to([P_, H, 8])
            sb_ = sinp.unsqueeze(1).broadcast_to([P_, H, 8])
            y1 = y[:, :, 0:8]
            y2 = y[:, :, 8:16]
            S.op('dve', lambda e: e.tensor_tensor(out=r1, in0=y1, in1=cb_, op=ALU.mult), reads=[pre + 'y', 'cst'], writes=[pre + 'r1'])
            S.op('pool', lambda e: e.tensor_tensor(out=r2, in0=y2, in1=sb_, op=ALU.mult), reads=[pre + 'y', 'cst'], writes=[pre + 'r2'])
            S.op('dve', lambda e: e.tensor_tensor(out=out3[:, :, 0:8], in0=r1, in1=r2, op=ALU.subtract),
                 reads=[pre + 'r1', pre + 'r2'], writes=[outn])
            S.op('dve', lambda e: e.tensor_tensor(out=r1, in0=y2, in1=cb_, op=ALU.mult), reads=[pre + 'y', 'cst'], writes=[pre + 'r1'])
            S.op('pool', lambda e: e.tensor_tensor(out=r2, in0=y1, in1=sb_, op=ALU.mult), reads=[pre + 'y', 'cst'], writes=[pre + 'r2'])
            S.op('dve', lambda e: e.tensor_tensor(out=out3[:, :, 8:16], in0=r1, in1=r2, op=ALU.add),
                 reads=[pre + 'r1', pre + 'r2'], writes=[outn])
            S.op('act', lambda e: e.activation(out=out3[:, :, 16:64], in_=y[:, :, 16:64], func=AF.Copy),
                 reads=[pre + 'y'], writes=[outn])

        def pass0():
            with ExitStack() as st:
                wcv = sbt(st, "wcv", [128, 8, 256], BF16)
                pk1 = sbt(st, "pk1", [128, 32, 256], BF16)
                pv1 = sbt(st, "pv1", [128, 32, 256], BF16)
                pk2 = sbt(st, "pk2", [128, 2, 64], BF16)
                pv2 = sbt(st, "pv2", [128, 2, 64], BF16)
                rawk = sbt(st, "rawk", [128, S_ + 16], F32)
                rawv = sbt(st, "rawv", [128, S_ + 16], F32)
                zk = sbt(st, "zk", [128, 32, 256], BF16)
                zv = sbt(st, "zv", [128, 32, 256], BF16)
                xb = [sbt(st, f"xb{i}", [128, D], F32) for i in range(2)]
                xs = sbt(st, "xs", [128, D], F32)
                ss = sbt(st, "ss", [128, 4], F32)
                xnT = sbt(st, "xnT", [128, 8, 128], BF16)
                hT = sbt(st, "hT", [128, 2, 256], BF16)
                g1 = sbt(st, "g1", [128, 256], F32)
                g2 = sbt(st, "g2", [128, 256], F32)
                kc32 = sbt(st, "kc32", [128, 64], F32)
                kcn = sbt(st, "kcn", [128, 64], BF16)
                tmp = dict(sq=sbt(st, "t_sq", [128, 8, 64], F32), y=sbt(st, "t_y", [128, 8, 64], F32),
                           st=sbt(st, "t_st", [128, 32], F32), r1=sbt(st, "t_r1", [128, 8, 8], F32),
                           r2=sbt(st, "t_r2", [128, 8, 8], F32))
                tp = pst(st, "tp", [128, 1024], F32)
                cp = [pst(st, f"cp{i}", [128, 512], F32) for i in range(2)]
                hp = [pst(st, f"hp{i}", [128, 512], F32) for i in range(2)]
                kp = pst(st, "kp", [128, 512], F32)
                tb = pst(st, "tb", [128, 1024], BF16)

                load_w(wcv, wcv_d, 8, 256, 'wcv')
                load_w(pk1, pk1_d, 32, 256, 'pk1')
                load_w(pv1, pv1_d, 32, 256, 'pv1')
                load_w(pk2, pk2_d, 2, 64, 'pk2')
                load_w(pv2, pv2_d, 2, 64, 'pv2')
                for t in range(NT):
                    b = xb[t % 2]
                    bn = f"xb{t % 2}"
                    S.dma('sp', lambda e, b=b, t=t: e.dma_start(out=b[:], in_=x_d[t * 128:(t + 1) * 128, :]), writes=[bn])
                    norm_tile(b, bn, 'nw1', xs, ss, tp, xnT)
                    pb = cp[t % 2]
                    pn = f"cp{t % 2}"
                    for c in range(2):
                        for k in range(8):
                            S.op('pe', lambda e, c=c, k=k, pb=pb: e.matmul(
                                out=pb[:, c * 128:(c + 1) * 128], lhsT=wcv[:, k, c * 128:(c + 1) * 128], rhs=xnT[:, k, :],
                                start=(k == 0), stop=(k == 7)), reads=['wcv', 'xnT'], writes=[pn])
                    S.op('act', lambda e, pb=pb, t=t: e.activation(out=rawk[:, t * 128:(t + 1) * 128], in_=pb[:, 0:128], func=AF.Copy),
                         reads=[pn], writes=['rawk'])
                    S.op('act', lambda e, pb=pb, t=t: e.activation(out=rawv[:, t * 128:(t + 1) * 128], in_=pb[:, 128:256], func=AF.Copy),
                         reads=[pn], writes=['rawv'])
                if DBG['stop'] <= 1:
                    S.barrier()
                    return
                posk = C('posk')
                posv = C('posv')
                for l in range(32):
                    S.op('dve', lambda e, l=l: e.tensor_scalar(
                        out=zk[:, l, 0:NCMP], in0=rawk[:, l:l + 16 * (NCMP - 1) + 1:16], scalar1=posk[:, l:l + 1], scalar2=None,
                        op0=ALU.add), reads=['rawk', 'cst'], writes=['zk'])
                    S.op('pool', lambda e, l=l: e.tensor_scalar(
                        out=zv[:, l, 0:NCMP], in0=rawv[:, l:l + 16 * (NCMP - 1) + 1:16], scalar1=posv[:, l:l + 1], scalar2=None,
                        op0=ALU.add), reads=['rawv', 'cst'], writes=['zv'])
                if DBG['stop'] <= 2:
                    S.barrier()
                    return
                nch = [(0, min(128, NCMP))]
                if NCMP > 128:
                    nch.append((128, NCMP - 128))
                ovl = C('ovl')
                for kv in range(2):
                    z = (zk, zv)[kv]
                    zn = ('zk', 'zv')[kv]
                    w1 = (pk1, pv1)[kv]
                    w1n = ('pk1', 'pv1')[kv]
                    w2 = (pk2, pv2)[kv]
                    w2n = ('pk2', 'pv2')[kv]
                    for g in range(2):
                        gs_ = slice(g * 64, (g + 1) * 64)
                        for hc in range(2):
                            hb_ = hp[hc]
                            hn_ = f"hp{hc}"
                            for l in range(32):
                                S.op('pe', lambda e, l=l, hc=hc, hb_=hb_, z=z, w1=w1, gs_=gs_: e.matmul(
                                    out=hb_[:, 0:NCMP], lhsT=w1[gs_, l, hc * 128:(hc + 1) * 128], rhs=z[gs_, l, 0:NCMP],
                                    start=(l == 0), stop=(l == 31)), reads=[w1n, zn], writes=[hn_])
                            gelu(128, NCMP, hb_[:, 0:NCMP], hn_, hT[:, hc, 0:NCMP], f'hT{hc}',
                                 g1[:, 0:NCMP], 'g1', g2[:, 0:NCMP], 'g2')
                        if DBG['stop'] <= 3:
                            continue
                        for ci, (n0, sz) in enumerate(nch):
                            for hc in range(2):
                                S.op('pe', lambda e, hc=hc, n0=n0, sz=sz, w2=w2: e.matmul(
                                    out=kp[0:sz, 0:64], lhsT=hT[:, hc, n0:n0 + sz], rhs=w2[:, hc, :],
                                    start=(hc == 0), stop=(hc == 1)), reads=[f'hT{hc}', w2n], writes=['kp'])
                            if DBG['stop'] <= 4:
                                continue
                            if kv == 0:
                                S.op('act', lambda e, sz=sz: e.activation(out=kc32[0:sz, :], in_=kp[0:sz, 0:64], func=AF.Copy),
                                     reads=['kp'], writes=['kc32'])
                                cc = C('cosc')[:, ci * 8:(ci + 1) * 8]
                                sc = C('sinc')[:, ci * 8:(ci + 1) * 8]
                                rmsrope(sz, 1, kc32[0:sz, :].rearrange("p (h d) -> p h d", h=1), 'kc32', C('wk0')[0:sz, :],
                                        cc[0:sz, :], sc[0:sz, :], kcn[0:sz, :].rearrange("p (h d) -> p h d", h=1), 'kcn', tmp, 'p0')
                                S.op('pe', lambda e, sz=sz: e.transpose(out=tb[0:64, 0:sz], in_=kcn[0:sz, :], identity=identb[0:sz, 0:sz]),
                                     reads=['kcn', 'identb'], writes=['tb'])
                                S.op('dve', lambda e, sz=sz, n0=n0, g=g: e.tensor_copy(out=kcT[:, g, n0:n0 + sz], in_=tb[0:64, 0:sz]),
                                     reads=['tb'], writes=['kcT'])
                            else:
                                S.op('act', lambda e, sz=sz, g=g, ci=ci: e.activation(out=vca[0:sz, g, ci, 0:64], in_=kp[0:sz, 0:64], func=AF.Copy),
                                     reads=['kp'], writes=['vca'])
                                S.op('dve', lambda e, sz=sz, g=g, ci=ci: e.tensor_copy(out=vca[0:sz, g, ci, 64:129], in_=ovl[0:sz, ci * 65:(ci + 1) * 65]),
                                     reads=['cst'], writes=['vca'])
                S.barrier()

        def pass1():
            with ExitStack() as st:
                wtm = sbt(st, "wtm", [128, 8, 1048], BF16)
                wxy = sbt(st, "wxy", [128, 8, 2048], BF16)
                wa = sbt(st, "wa", [128, 8, 128], BF16)
                wx = sbt(st, "wx", [128, 8, 128], BF16)
                ksT = [sbt(st, f"ksT{g}", [128, S_], BF16) for g in range(2)]
                kwT = sbt(st, "kwT", [64, 2, S_], BF16)
                vsa = sbt(st, "vsa", [128, NT, 2, 65], BF16)
                vwa = sbt(st, "vwa", [128, NT, 2, 65], BF16)
                xb = [sbt(st, f"xb{i}", [128, D], F32) for i in range(2)]
                xs = sbt(st, "xs", [128, D], F32)
                ss = sbt(st, "ss", [128, 4], F32)
                xnT = sbt(st, "xnT", [128, 8, 128], BF16)
                qraw = sbt(st, "qraw", [128, 512], F32)
                kvraw = sbt(st, "kvraw", [128, 512], F32)
                gsg = sbt(st, "gsg", [128, 24], F32)
                qaug = sbt(st, "qaug", [128, 8, 128], BF16)
                kn = sbt(st, "kn", [128, 4, 64], BF16)
                qT = sbt(st, "qT", [64, 8, 128], BF16)
                qaT = [sbt(st, f"qaT{g}", [128, 512], BF16) for g in range(2)]
                pT = [sbt(st, f"pT{i}", [128, 512], BF16) for i in range(2)]
                oacc = sbt(st, "oacc", [128, 8, 64], F32)
                obf = sbt(st, "obf", [128, 512], BF16)
                onT = sbt(st, "onT", [128, 4, 128], BF16)
                imp = sbt(st, "imp", [128, 64], F32)
                sc1 = sbt(st, "sc1", [128, 64], F32)
                sc2 = sbt(st, "sc2", [128, 64], F32)
                m8 = sbt(st, "m8", [128, 16], F32)
                sm = sbt(st, "sm", [128, 16], F32)
                wq8 = sbt(st, "wq8", [128, 64], F32)
                tmp = dict(sq=sbt(st, "t_sq", [128, 8, 64], F32), y=sbt(st, "t_y", [128, 8, 64], F32),
                           st=sbt(st, "t_st", [128, 32], F32), r1=sbt(st, "t_r1", [128, 8, 8], F32),
                           r2=sbt(st, "t_r2", [128, 8, 8], F32))
                xrx = sbt(st, "xrx", [128, 8, 132], F32)
                yrb = sbt(st, "yrb", [128, 8, 128], F32)
                xc = sbt(st, "xc", [128, 8, 128], F32)
                xcb = sbt(st, "xcb", [128, 8, 128], BF16)
                rg = sbt(st, "rg", [128, 8, 128], F32)
                ig = sbt(st, "ig", [128, 8, 128], F32)
                av = sbt(st, "av", [128, 8, 128], F32)
                bt = sbt(st, "bt", [128, 8, 128], F32)
                hs = sbt(st, "hs", [128, 8, 128], F32)
                hst = sbt(st, "hst", [128, 8], F32)
                cl = sbt(st, "cl", [128, 8], F32)
                olT = sbt(st, "olT", [128, 8, 128], BF16)
                gt1 = sbt(st, "gt1", [128, 1024], F32)
                tp = pst(st, "tp", [128, 1024], F32)
                pj = [pst(st, f"pj{i}", [128, 512], F32) for i in range(2)]
                stp = [pst(st, f"st{i}", [128, 512], F32) for i in range(2)]
                pv = pst(st, "pv", [128, 512], F32)
                mz = pst(st, "mz", [128, 1024], BF16)

                load_w(wtm, wtm_d, 8, 1048, 'wtm')
                load_w(wxy, wxy_d, 8, 2048, 'wxy')
                load_w(wa, wa_d, 8, 128, 'wa')
                load_w(wx, wx_d, 8, 128, 'wx')
                for g in range(2):
                    for c0 in range(0, S_, 2048):
                        c1 = min(S_, c0 + 2048)
                        S.dma('pool', lambda e, g=g, c0=c0, c1=c1: e.dma_start(out=ksT[g][64:128, c0:c1], in_=E_d[:, c0:c1]),
                              writes=[f'ksE{g}'])
                S.op('dve', lambda e: e.tensor_scalar(out=wq8[:], in0=C('wq'), scalar1=0.125, scalar2=None, op0=ALU.mult),
                     reads=['cst'], writes=['wq8'])
                S.op('pool', lambda e: e.memset(vsa[:, :, :, 64:65], 1.0), writes=['vsa1'])
                S.op('pool', lambda e: e.memset(vwa[:, :, :, 64:65], 1.0), writes=['vwa1'])
                S.op('pool', lambda e: e.memset(xrx[:, :, 0:3], 0.0), writes=['xrxh'])
                S.op('pool', lambda e: e.memset(hst[:], 0.0), writes=['hst'])
                S.op('pool', lambda e: e.memset(qaug[:], 0.0), writes=['qaugq', 'qaugm0', 'qaugm1'])
                S.op('act', lambda e: e.activation(out=cl[:], in_=C('lam'), func=AF.Exp, scale=-1.0), reads=['cst'], writes=['cl'])
                S.op('dve', lambda e: e.tensor_scalar(out=cl[:], in0=cl[:], scalar1=1.0, scalar2=None, op0=ALU.add),
                     reads=['cl'], writes=['cl'])
                S.op('act', lambda e: e.activation(out=cl[:], in_=cl[:], func=AF.Ln), reads=['cl'], writes=['cl'])
                S.op('dve', lambda e: e.tensor_scalar(out=cl[:], in0=cl[:], scalar1=-8.0, scalar2=None, op0=ALU.mult),
                     reads=['cl'], writes=['cl'])
                tkA = C('tkA')
                tkB = C('tkB')
                cw = C('cw')
                cbv = C('cb')
                bav = C('ba')
                bxv = C('bx')

                for i in range(NT):
                    T0 = i * 128
                    b = xb[i % 2]
                    bn = f"xb{i % 2}"
                    S.dma('sp', lambda e, b=b, i=i: e.dma_start(out=b[:], in_=x_d[i * 128:(i + 1) * 128, :]), writes=[bn])
                    norm_tile(b, bn, 'nw1', xs, ss, tp, xnT)
                    for (c0, c1, pb, pn) in ((0, 512, pj[0], 'pj0'), (512, 1024, pj[1], 'pj1')):
                        for k in range(8):
                            S.op('pe', lambda e, k=k, c0=c0, c1=c1, pb=pb: e.matmul(
                                out=pb[:, 0:512], lhsT=xnT[:, k, :], rhs=wtm[:, k, c0:c1], start=(k == 0), stop=(k == 7)),
                                reads=['xnT', 'wtm'], writes=[pn])
                    S.op('act', lambda e: e.activation(out=qraw[:], in_=pj[0][:, 0:512], func=AF.Copy), reads=['pj0'], writes=['qraw'])
                    S.op('act', lambda e: e.activation(out=kvraw[:], in_=pj[1][:, 0:512], func=AF.Copy), reads=['pj1'], writes=['kvraw'])
                    for k in range(8):
                        S.op('pe', lambda e, k=k: e.matmul(out=pj[0][:, 0:24], lhsT=xnT[:, k, :], rhs=wtm[:, k, 1024:1048],
                                                           start=(k == 0), stop=(k == 7)), reads=['xnT', 'wtm'], writes=['pj0'])
                    S.op('act', lambda e: e.activation(out=gsg[:], in_=pj[0][:, 0:24], func=AF.Sigmoid), reads=['pj0'], writes=['gsg'])
                    for c4 in range(4):
                        pb = pj[(c4 + 1) % 2]
                        pn = f"pj{(c4 + 1) % 2}"
                        for cc in range(4):
                            c = c4 * 4 + cc
                            for k in range(8):
                                S.op('pe', lambda e, c=c, cc=cc, k=k, pb=pb: e.matmul(
                                    out=pb[:, cc * 128:(cc + 1) * 128], lhsT=wxy[:, k, c * 128:(c + 1) * 128], rhs=xnT[:, k, :],
                                    start=(k == 0), stop=(k == 7)), reads=['wxy', 'xnT'], writes=[pn])
                        src = pb[:, 0:512].rearrange("p (a b) -> p a b", a=4)
                        if c4 < 2:
                            S.op('act', lambda e, c4=c4, src=src: e.activation(out=xrx[:, c4 * 4:(c4 + 1) * 4, 3:131], in_=src, func=AF.Copy),
                                 reads=[pn], writes=['xrx'])
                        else:
                            S.op('act', lambda e, c4=c4, src=src: e.activation(out=yrb[:, (c4 - 2) * 4:(c4 - 1) * 4, :], in_=src, func=AF.Copy),
                                 reads=[pn], writes=['yrb'])
                    cosp = C('cos')[:, i * 8:(i + 1) * 8]
                    sinp = C('sin')[:, i * 8:(i + 1) * 8]
                    rmsrope(128, 8, qraw[:].rearrange("p (h d) -> p h d", h=8), 'qraw', wq8[:], cosp, sinp,
                            qaug[:, :, 0:64], 'qaugq', tmp, 'p1')
                    rmsrope(128, 2, kvraw[:, 0:128].rearrange("p (h d) -> p h d", h=2), 'kvraw', C('wk1'), cosp, sinp,
                            kn[:, 0:2, :], 'kn', tmp, 'p1')
                    rmsrope(128, 2, kvraw[:, 128:256].rearrange("p (h d) -> p h d", h=2), 'kvraw', C('wk2'), cosp, sinp,
                            kn[:, 2:4, :], 'kn', tmp, 'p1')
                    for j in range(4):
                        S.op('pe', lambda e, j=j: e.transpose(out=mz[0:64, j * 128:(j + 1) * 128], in_=kn[:, j, :], identity=identb[:]),
                             reads=['kn', 'identb'], writes=['mz'])
                    for g in range(2):
                        S.op('dve', lambda e, g=g, T0=T0: e.tensor_copy(out=ksT[g][0:64, T0:T0 + 128], in_=mz[0:64, g * 128:(g + 1) * 128]),
                             reads=['mz'], writes=[f'ksT{g}_{i}'])
                    S.op('act', lambda e, T0=T0: e.activation(out=kwT[:, :, T0:T0 + 128],
                                                              in_=mz[0:64, 256:512].rearrange("p (a b) -> p a b", a=2), func=AF.Copy),
                         reads=['mz'], writes=[f'kwT_{i}'])
                    S.op('pool', lambda e, i=i: e.tensor_copy(out=vsa[:, i, :, 0:64], in_=kvraw[:, 256:384].rearrange("p (g d) -> p g d", g=2)),
                         reads=['kvraw'], writes=[f'vsa_{i}'])
                    S.op('pool', lambda e, i=i: e.tensor_copy(out=vwa[:, i, :, 0:64], in_=kvraw[:, 384:512].rearrange("p (g d) -> p g d", g=2)),
                         reads=['kvraw'], writes=[f'vwa_{i}'])
                    for h in range(8):
                        S.op('pe', lambda e, h=h: e.transpose(out=mz[0:64, h * 128:(h + 1) * 128], in_=qaug[:, h, 0:64], identity=identb[:]),
                             reads=['qaugq', 'identb'], writes=['mz'])
                    S.op('dve', lambda e: e.tensor_copy(out=qT[:].rearrange("p a b -> p (a b)"), in_=mz[0:64, :]), reads=['mz'], writes=['qT'])

                    n_hi = min(NCMP, 8 * i + 7)
                    chunks = [(0, 0, min(128, n_hi))]
                    if n_hi > 128:
                        chunks.append((1, 128, n_hi - 128))
                    pvB = tp
                    for g in range(2):
                        for (ci, n0, Kc) in chunks:
                            sp_ = stp[ci]
                            sn_ = f"st{ci}"
                            S.op('pe', lambda e, g=g, n0=n0, Kc=Kc, sp_=sp_: e.matmul(
                                out=sp_[0:Kc, :], lhsT=kcT[:, g, n0:n0 + Kc], rhs=qT[:, 4 * g:4 * g + 4, :].rearrange("p a b -> p (a b)"),
                                start=True, stop=True), reads=['kcT', 'qT'], writes=[sn_])
                            S.op('act', lambda e, ci=ci, Kc=Kc, sp_=sp_: e.activation(out=pT[ci][0:Kc, :], in_=sp_[0:Kc, :], func=AF.Exp),
                                 reads=[sn_], writes=[f'pT{ci}'])
                            S.op('pool', lambda e, ci=ci, Kc=Kc, n0=n0, i=i: e.affine_select(
                                out=pT[ci][0:Kc, :].rearrange("p (a b) -> p a b", a=4), in_=pT[ci][0:Kc, :].rearrange("p (a b) -> p a b", a=4),
                                pattern=[[0, 4], [1, 128]], compare_op=ALU.is_ge, fill=0.0,
                                base=128 * i - 31 - 16 * n0, channel_multiplier=-16), reads=[f'pT{ci}'], writes=[f'pT{ci}'])
                        for h in range(4):
                            bank = pv if h < 2 else pvB
                            bkn = 'pv' if h < 2 else 'tp0'
                            off = (h % 2) * 129
                            for (ci, n0, Kc) in chunks:
                                S.op('pe', lambda e, h=h, ci=ci, Kc=Kc, g=g, bank=bank, off=off, lastc=(ci == len(chunks) - 1): e.matmul(
                                    out=bank[:, off:off + 129], lhsT=pT[ci][0:Kc, h * 128:(h + 1) * 128], rhs=vca[0:Kc, g, ci, :],
                                    start=(ci == 0), stop=lastc),
                                    reads=[f'pT{ci}', 'vca'], writes=[bkn])
                        for h in range(4):
                            H = 4 * g + h
                            bank = pv if h < 2 else pvB
                            bkn = 'pv' if h < 2 else 'tp0'
                            off = (h % 2) * 129
                            S.op('dve', lambda e, bank=bank, off=off: e.tensor_scalar(
                                out=sm[:, 0:1], in0=bank[:, off + 128:off + 129], scalar1=1e-30, scalar2=None, op0=ALU.max),
                                reads=[bkn], writes=['sm0'])
                            S.op('dve', lambda e: e.reciprocal(out=sm[:, 1:2], in_=sm[:, 0:1]), reads=['sm0'], writes=['sm1'])
                            S.op('dve', lambda e, H=H: e.tensor_tensor(out=sm[:, 2:3], in0=sm[:, 1:2], in1=gsg[:, H:H + 1], op=ALU.mult),
                                 reads=['sm1', 'gsg'], writes=['sm2'])
                            if h == 0:
                                S.op('dve', lambda e, bank=bank, off=off: e.tensor_scalar(
                                    out=imp[:], in0=bank[:, off + 64:off + 128], scalar1=sm[:, 1:2], scalar2=None, op0=ALU.mult),
                                    reads=[bkn, 'sm1'], writes=['imp'])
                            else:
                                S.op('dve', lambda e, bank=bank, off=off: e.scalar_tensor_tensor(
                                    out=imp[:], in0=bank[:, off + 64:off + 128], scalar=sm[:, 1:2], in1=imp[:], op0=ALU.mult, op1=ALU.add),
                                    reads=[bkn, 'sm1', 'imp'], writes=['imp'])
                            S.op('dve', lambda e, bank=bank, off=off, H=H: e.tensor_scalar(
                                out=oacc[:, H, :], in0=bank[:, off:off + 64], scalar1=sm[:, 2:3], scalar2=None, op0=ALU.mult),
                                reads=[bkn, 'sm2'], writes=['oacc'])
                        a0 = 63 - 2 * i
                        if NSLC + a0 <= 127 and a0 >= 0:
                            Asl = tkA[:, a0:a0 + NSLC]
                            Bsl = tkB[:, a0:a0 + NSLC]
                        S.op('dve', lambda e, Bsl=Bsl: e.tensor_tensor(out=sc1[:, 0:NSLC], in0=imp[:, 0:NSLC], in1=Bsl, op=ALU.mult),
                             reads=['imp', 'cst'], writes=['sc1'])
                        S.op('dve', lambda e, Asl=Asl: e.tensor_tensor(out=sc1[:, 0:NSLC], in0=sc1[:, 0:NSLC], in1=Asl, op=ALU.add),
                             reads=['sc1', 'cst'], writes=['sc1'])
                        S.op('dve', lambda e: e.memset(sc1[:, 0:1], 1e4), writes=['sc1'])
                        S.op('dve', lambda e: e.max(out=m8[:, 0:8], in_=sc1[:, 0:NSLC]), reads=['sc1'], writes=['m8a'])
                        S.op('dve', lambda e: e.match_replace(out=sc2[:, 0:NSLC], in_to_replace=m8[:, 0:8], in_values=sc1[:, 0:NSLC],
                                                              imm_value=-3.0e38), reads=['sc1', 'm8a'], writes=['sc2'])
                        S.op('dve', lambda e: e.max(out=m8[:, 8:16], in_=sc2[:, 0:NSLC]), reads=['sc2'], writes=['m8b'])
                        S.op('dve', lambda e, g=g: e.tensor_scalar(
                            out=qaug[:, 4 * g:4 * g + 4, 64:64 + NSLC], in0=sc1[:, 0:NSLC].unsqueeze(1).broadcast_to([128, 4, NSLC]),
                            scalar1=m8[:, 15:16], scalar2=NEGM, op0=ALU.is_lt, op1=ALU.mult), reads=['sc1', 'm8b'], writes=[f'qaugm{g}'])

                    for g in range(2):
                        for h in range(4):
                            S.op('pe', lambda e, g=g, h=h: e.transpose(out=mz[:, h * 128:(h + 1) * 128], in_=qaug[:, 4 * g + h, :], identity=identb[:]),
                                 reads=['qaugq', f'qaugm{g}', 'identb'], writes=['mz'])
                        S.op('dve', lambda e, g=g: e.tensor_copy(out=qaT[g][:], in_=mz[:, 0:512]), reads=['mz'], writes=[f'qaT{g}'])
                    for br in (1, 2):
                        for g in range(2):
                            kts = list(range(0, i + 1)) if br == 1 else list(range(max(0, i - 4), i + 1))
                            for idx, kt in enumerate(kts):
                                sp_ = stp[idx % 2]
                                sn_ = f"st{idx % 2}"
                                pt_ = pT[idx % 2]
                                ptn = f"pT{idx % 2}"
                                if br == 1:
                                    S.op('pe', lambda e, g=g, kt=kt, sp_=sp_: e.matmul(
                                        out=sp_[:, :], lhsT=ksT[g][:, kt * 128:(kt + 1) * 128], rhs=qaT[g][:, :], start=True, stop=True),
                                        reads=[f'ksT{g}_{kt}', f'ksE{g}', f'qaT{g}'], writes=[sn_])
                                else:
                                    S.op('pe', lambda e, g=g, kt=kt, sp_=sp_: e.matmul(
                                        out=sp_[:, :], lhsT=kwT[:, g, kt * 128:(kt + 1) * 128], rhs=qaT[g][0:64, :], start=True, stop=True),
                                        reads=[f'kwT_{kt}', f'qaT{g}'], writes=[sn_])
                                S.op('act', lambda e, sp_=sp_, pt_=pt_: e.activation(out=pt_[:], in_=sp_[:, :], func=AF.Exp),
                                     reads=[sn_], writes=[ptn])
                                if kt == i:
                                    S.op('pool', lambda e, pt_=pt_: e.affine_select(
                                        out=pt_[:].rearrange("p (a b) -> p a b", a=4), in_=pt_[:].rearrange("p (a b) -> p a b", a=4),
                                        pattern=[[0, 4], [1, 128]], compare_op=ALU.is_ge, fill=0.0, base=0, channel_multiplier=-1),
                                        reads=[ptn], writes=[ptn])
                                elif br == 2 and kt == i - 4:
                                    S.op('pool', lambda e, pt_=pt_: e.affine_select(
                                        out=pt_[:].rearrange("p (a b) -> p a b", a=4), in_=pt_[:].rearrange("p (a b) -> p a b", a=4),
                                        pattern=[[0, 4], [-1, 128]], compare_op=ALU.is_ge, fill=0.0, base=-1, channel_multiplier=1),
                                        reads=[ptn], writes=[ptn])
                                va = vsa if br == 1 else vwa
                                van = (f'vsa_{kt}', 'vsa1') if br == 1 else (f'vwa_{kt}', 'vwa1')
                                for h in range(4):
                                    S.op('pe', lambda e, h=h, kt=kt, g=g, va=va, pt_=pt_, idx=idx, last=(idx == len(kts) - 1): e.matmul(
                                        out=pv[:, h * 65:(h + 1) * 65], lhsT=pt_[:, h * 128:(h + 1) * 128], rhs=va[:, kt, g, :],
                                        start=(idx == 0 and h == 0), stop=last, skip_group_check=True),
                                        reads=[ptn, van[0], van[1]], writes=['pv'])
                            for h in range(4):
                                H = 4 * g + h
                                S.op('dve', lambda e, h=h: e.reciprocal(out=sm[:, 4:5], in_=pv[:, h * 65 + 64:h * 65 + 65]),
                                     reads=['pv'], writes=['sm4'])
                                S.op('dve', lambda e, H=H, br=br: e.tensor_tensor(out=sm[:, 5:6], in0=sm[:, 4:5], in1=gsg[:, br * 8 + H:br * 8 + H + 1],
                                                                                  op=ALU.mult), reads=['sm4', 'gsg'], writes=['sm5'])
                                S.op('dve', lambda e, h=h, H=H: e.scalar_tensor_tensor(
                                    out=oacc[:, H, :], in0=pv[:, h * 65:h * 65 + 64], scalar=sm[:, 5:6], in1=oacc[:, H, :],
                                    op0=ALU.mult, op1=ALU.add), reads=['pv', 'sm5', 'oacc'], writes=['oacc'])
                    S.op('act', lambda e: e.activation(out=obf[:], in_=oacc[:].rearrange("p a b -> p (a b)"), func=AF.Copy),
                         reads=['oacc'], writes=['obf'])
                    for c in range(4):
                        S.op('pe', lambda e, c=c: e.transpose(out=mz[:, c * 128:(c + 1) * 128], in_=obf[:, c * 128:(c + 1) * 128], identity=identb[:]),
                             reads=['obf', 'identb'], writes=['mz'])
                    S.op('dve', lambda e: e.tensor_copy(out=onT[:].rearrange("p a b -> p (a b)"), in_=mz[:, 0:512]), reads=['mz'], writes=['onT'])
                    S.dma('sp', lambda e, i=i: e.dma_start(out=son_d[i * 128:(i + 1) * 128, :], in_=onT[:].rearrange("p a b -> p (a b)")),
                          reads=['onT'], writes=[f'son{i}'])

                    for c in range(8):
                        eng = 'dve'
                        S.op(eng, lambda e, c=c: e.tensor_scalar(out=xc[:, c, :], in0=xrx[:, c, 3:131], scalar1=cw[:, 24 + c:25 + c],
                                                                 scalar2=cbv[:, c:c + 1], op0=ALU.mult, op1=ALU.add),
                             reads=['xrx', 'xrxh', 'cst'], writes=[f'xc{c}'])
                        for j in (2, 1, 0):
                            S.op(eng, lambda e, c=c, j=j: e.scalar_tensor_tensor(
                                out=xc[:, c, :], in0=xrx[:, c, j:j + 128], scalar=cw[:, j * 8 + c:j * 8 + c + 1], in1=xc[:, c, :],
                                op0=ALU.mult, op1=ALU.add), reads=['xrx', 'xrxh', 'cst', f'xc{c}'], writes=[f'xc{c}'])
                    xcall = [f'xc{c}' for c in range(8)]
                    S.op('pool', lambda e: e.tensor_copy(out=xcb[:], in_=xc[:]), reads=xcall, writes=['xcb'])
                    S.op('pool', lambda e: e.tensor_copy(out=xrx[:, :, 0:3], in_=xrx[:, :, 128:131]), reads=['xrx'], writes=['xrxh'])
                    for (wmat, wn, bias, dst, dn) in ((wa, 'wa', bav, rg, 'rg'), (wx, 'wx', bxv, ig, 'ig')):
                        for c4 in range(2):
                            pb = pj[c4]
                            pn = f"pj{c4}"
                            for cc in range(4):
                                c = c4 * 4 + cc
                                S.op('pe', lambda e, c=c, cc=cc, pb=pb, wmat=wmat: e.matmul(
                                    out=pb[:, cc * 128:(cc + 1) * 128], lhsT=wmat[:, c, :], rhs=xcb[:, c, :], start=True, stop=True),
                                    reads=[wn, 'xcb'], writes=[pn])
                            for cc in range(4):
                                c = c4 * 4 + cc
                                S.op('act', lambda e, c=c, cc=cc, pb=pb, bias=bias, dst=dst: e.activation(
                                    out=dst[:, c, :], in_=pb[:, cc * 128:(cc + 1) * 128], func=AF.Sigmoid, bias=bias[:, c:c + 1]),
                                    reads=[pn, 'cst'], writes=[dn])
                    for c in range(8):
                        S.op('act', lambda e, c=c: e.activation(out=av[:, c, :], in_=rg[:, c, :], func=AF.Exp, scale=cl[:, c:c + 1]),
                             reads=['rg', 'cl'], writes=['av'])
                    S.op('pool', lambda e: e.tensor_tensor(out=bt[:], in0=av[:], in1=av[:], op=ALU.mult), reads=['av'], writes=['bt'])
                    S.op('dve', lambda e: e.tensor_scalar(out=bt[:], in0=bt[:], scalar1=-1.0, scalar2=1.0, op0=ALU.mult, op1=ALU.add),
                         reads=['bt'], writes=['bt'])
                    S.op('act', lambda e: e.activation(out=bt[:], in_=bt[:], func=AF.Sqrt), reads=['bt'], writes=['bt'])
                    S.op('pool', lambda e: e.tensor_tensor(out=ig[:], in0=ig[:], in1=xc[:], op=ALU.mult), reads=['ig'] + xcall, writes=['ig'])
                    S.op('dve', lambda e: e.tensor_tensor(out=bt[:], in0=bt[:], in1=ig[:], op=ALU.mult), reads=['bt', 'ig'], writes=['bt'])
                    for c in range(8):
                        S.op('dve', lambda e, c=c: e.tensor_tensor_scan(out=hs[:, c, :], data0=av[:, c, :], data1=bt[:, c, :],
                                                                        initial=hst[:, c:c + 1], op0=ALU.mult, op1=ALU.add),
                             reads=['av', 'bt', 'hst'], writes=['hs'])
                    S.op('dve', lambda e: e.tensor_copy(out=hst[:], in_=hs[:, :, 127]), reads=['hs'], writes=['hst'])
                    yr2 = yrb[:].rearrange("p a b -> p (a b)")
                    gelu(128, 1024, yr2, 'yrb', rg[:].rearrange("p a b -> p (a b)"), 'rg',
                         gt1[:], 'gt1', ig[:].rearrange("p a b -> p (a b)"), 'ig')
                    S.op('dve', lambda e: e.tensor_tensor(out=olT[:], in0=hs[:], in1=rg[:], op=ALU.mult), reads=['hs', 'rg'], writes=['olT'])
                    S.dma('sp', lambda e, i=i: e.dma_start(out=sol_d[i * 128:(i + 1) * 128, :], in_=olT[:].rearrange("p a b -> p (a b)")),
                          reads=['olT'], writes=[f'sol{i}'])
                S.barrier()

        def pass2():
            with ExitStack() as st:
                wgm = sbt(st, "wgm", [128, 8, 2048], BF16)
                wnu = sbt(st, "wnu", [128, 4, 1024], BF16)
                wlu = sbt(st, "wlu", [128, 8, 1024], BF16)
                wo = sbt(st, "wo", [128, 8, 1024], BF16)
                xb = [sbt(st, f"xb{i}", [128, D], F32) for i in range(2)]
                xs = sbt(st, "xs", [128, D], F32)
                ss = sbt(st, "ss", [128, 4], F32)
                xnT = sbt(st, "xnT", [128, 8, 128], BF16)
                gm = sbt(st, "gm", [128, 16, 128], F32)
                onT = [sbt(st, f"onT{i}", [128, 4, 128], BF16) for i in range(2)]
                olT = [sbt(st, f"olT{i}", [128, 8, 128], BF16) for i in range(2)]
                t1 = sbt(st, "t1", [128, 4, 128], F32)
                t2 = sbt(st, "t2", [128, 4, 128], F32)
                mT = sbt(st, "mT", [128, 8, 128], BF16)
                ob = sbt(st, "ob", [128, D], F32)
                tp = pst(st, "tp", [128, 1024], F32)
                pj = [pst(st, f"pj{i}", [128, 512], F32) for i in range(2)]
                pa = pst(st, "pa", [128, 512], F32)
                pbk = pst(st, "pbk", [128, 512], F32)
                yp = pst(st, "yp", [128, 1024], F32)
                load_w(wgm, wgm_d, 8, 2048, 'wgm')
                load_w(wnu, wnu_d, 4, 1024, 'wnu')
                load_w(wlu, wlu_d, 8, 1024, 'wlu')
                load_w(wo, wo_d, 8, 1024, 'wo')
                for i in range(NT):
                    b = xb[i % 2]
                    bn = f"xb{i % 2}"
                    on_ = onT[i % 2]
                    onn = f"onT{i % 2}"
                    ol_ = olT[i % 2]
                    oln = f"olT{i % 2}"
                    S.dma('sp', lambda e, b=b, i=i: e.dma_start(out=b[:], in_=x_d[i * 128:(i + 1) * 128, :]), writes=[bn])
                    S.dma('sp', lambda e, on_=on_, i=i: e.dma_start(out=on_[:].rearrange("p a b -> p (a b)"), in_=son_d[i * 128:(i + 1) * 128, :]),
                          reads=[f'son{i}'], writes=[onn])
                    S.dma('sp', lambda e, ol_=ol_, i=i: e.dma_start(out=ol_[:].rearrange("p a b -> p (a b)"), in_=sol_d[i * 128:(i + 1) * 128, :]),
                          reads=[f'sol{i}'], writes=[oln])
                    norm_tile(b, bn, 'nw1', xs, ss, tp, xnT)
                    for c4 in range(4):
                        pb = pj[c4 % 2]
                        pn = f"pj{c4 % 2}"
                        for cc in range(4):
                            c = c4 * 4 + cc
                            for k in range(8):
                                S.op('pe', lambda e, c=c, cc=cc, k=k, pb=pb: e.matmul(
                                    out=pb[:, cc * 128:(cc + 1) * 128], lhsT=wgm[:, k, c * 128:(c + 1) * 128], rhs=xnT[:, k, :],
                                    start=(k == 0), stop=(k == 7)), reads=['wgm', 'xnT'], writes=[pn])
                        S.op('act', lambda e, c4=c4, pb=pb: e.activation(out=gm[:, c4 * 4:(c4 + 1) * 4, :],
                                                                         in_=pb[:, 0:512].rearrange("p (a b) -> p a b", a=4), func=AF.Sigmoid),
                             reads=[pn], writes=[f'gm{c4}'])
                    for c4 in range(2):
                        for cc in range(4):
                            c = c4 * 4 + cc
                            for k in range(4):
                                S.op('pe', lambda e, c=c, cc=cc, k=k, on_=on_: e.matmul(
                                    out=pa[:, cc * 128:(cc + 1) * 128], lhsT=wnu[:, k, c * 128:(c + 1) * 128], rhs=on_[:, k, :],
                                    start=(k == 0), stop=(k == 3)), reads=['wnu', onn], writes=['pa'])
                        for cc in range(4):
                            c = c4 * 4 + cc
                            for k in range(8):
                                S.op('pe', lambda e, c=c, cc=cc, k=k, ol_=ol_: e.matmul(
                                    out=pbk[:, cc * 128:(cc + 1) * 128], lhsT=wlu[:, k, c * 128:(c + 1) * 128], rhs=ol_[:, k, :],
                                    start=(k == 0), stop=(k == 7)), reads=['wlu', oln], writes=['pbk'])
                        S.op('dve', lambda e, c4=c4: e.tensor_tensor(out=t1[:], in0=pa[:, 0:512].rearrange("p (a b) -> p a b", a=4),
                                                                     in1=gm[:, c4 * 4:(c4 + 1) * 4, :], op=ALU.mult),
                             reads=['pa', f'gm{c4}'], writes=['t1'])
                        S.op('dve', lambda e, c4=c4: e.tensor_tensor(out=t2[:], in0=pbk[:, 0:512].rearrange("p (a b) -> p a b", a=4),
                                                                     in1=gm[:, 8 + c4 * 4:8 + (c4 + 1) * 4, :], op=ALU.mult),
                             reads=['pbk', f'gm{c4 + 2}'], writes=['t2'])
                        S.op('pool', lambda e, c4=c4: e.tensor_tensor(out=mT[:, c4 * 4:(c4 + 1) * 4, :], in0=t1[:], in1=t2[:], op=ALU.add),
                             reads=['t1', 't2'], writes=[f'mT{c4}'])
                    for n in range(2):
                        for k in range(8):
                            S.op('pe', lambda e, n=n, k=k: e.matmul(out=yp[:, n * 512:(n + 1) * 512], lhsT=mT[:, k, :],
                                                                    rhs=wo[:, k, n * 512:(n + 1) * 512], start=(k == 0), stop=(k == 7)),
                                 reads=[f'mT{k // 4}', 'wo'], writes=[f'yp{n}'])
                    S.op('dve', lambda e, b=b: e.tensor_tensor(out=ob[:], in0=yp[:], in1=b[:], op=ALU.add),
                         reads=['yp0', 'yp1', bn], writes=['ob'])
                    S.dma('sp', lambda e, i=i: e.dma_start(out=out_d[i * 128:(i + 1) * 128, :], in_=ob[:]), reads=['ob'], writes=[f'outd{i}'])
                S.barrier()

        def pass3():
            with ExitStack() as st:
                w1 = sbt(st, "w1s", [128, 8, DFF], BF16)
                w2 = sbt(st, "w2s", [128, 32, D], BF16)
                hb = [sbt(st, f"hb{i}", [128, D], F32) for i in range(2)]
                xs = sbt(st, "xs", [128, D], F32)
                ss = sbt(st, "ss", [128, 4], F32)
                hnT = sbt(st, "hnT", [128, 8, 128], BF16)
                rl = [sbt(st, f"rl{i}", [128, 512], F32) for i in range(2)]
                hid = sbt(st, "hid", [128, 32, 128], BF16)
                ob = sbt(st, "ob", [128, D], F32)
                tp = pst(st, "tp", [128, 1024], F32)
                hp = [pst(st, f"hp{i}", [128, 512], F32) for i in range(2)]
                yp = pst(st, "yp", [128, 1024], F32)
                load_w(w1, wf1_d, 8, DFF, 'w1')
                load_w(w2, wf2_d, 32, D, 'w2')
                src_d = out_d if 2 in passes else x_d
                for t in range(NT):
                    b = hb[t % 2]
                    bn = f"hb{t % 2}"
                    S.dma('sp', lambda e, b=b, t=t: e.dma_start(out=b[:], in_=src_d[t * 128:(t + 1) * 128, :]),
                          reads=[f'outd{t}'], writes=[bn])
                    norm_tile(b, bn, 'nw2', xs, ss, tp, hnT)
                    for c4 in range(8):
                        pb = hp[c4 % 2]
                        pn = f"hp{c4 % 2}"
                        for cc in range(4):
                            c = c4 * 4 + cc
                            for k in range(8):
                                S.op('pe', lambda e, c=c, cc=cc, k=k, pb=pb: e.matmul(
                                    out=pb[:, cc * 128:(cc + 1) * 128], lhsT=w1[:, k, c * 128:(c + 1) * 128], rhs=hnT[:, k, :],
                                    start=(k == 0), stop=(k == 7)), reads=['w1', 'xnT'], writes=[pn])
                        r = rl[c4 % 2]
                        rn = f"rl{c4 % 2}"
                        S.op('act', lambda e, r=r, pb=pb: e.activation(out=r[:], in_=pb[:], func=AF.Relu), reads=[pn], writes=[rn])
                        S.op('pool', lambda e, r=r, c4=c4: e.tensor_tensor(
                            out=hid[:, c4 * 4:(c4 + 1) * 4, :], in0=r[:].rearrange("p (a b) -> p a b", a=4),
                            in1=r[:].rearrange("p (a b) -> p a b", a=4), op=ALU.mult), reads=[rn], writes=[f'hid{c4}'])
                    for n in range(2):
                        for k in range(32):
                            S.op('pe', lambda e, n=n, k=k: e.matmul(out=yp[:, n * 512:(n + 1) * 512], lhsT=hid[:, k, :],
                                                                    rhs=w2[:, k, n * 512:(n + 1) * 512], start=(k == 0), stop=(k == 31)),
                                 reads=[f'hid{k // 4}', 'w2'], writes=[f'yp{n}'])
                    S.op('dve', lambda e, b=b: e.tensor_tensor(out=ob[:], in0=yp[:], in1=b[:], op=ALU.add),
                         reads=['yp0', 'yp1', bn], writes=['ob'])
                    S.dma('sp', lambda e, t=t: e.dma_start(out=out_d[t * 128:(t + 1) * 128, :], in_=ob[:]), reads=['ob'], writes=[f'outd{t}'])
        for n_, f_ in enumerate((pass0, pass1, pass2, pass3)):
            if n_ in passes:
                f_()
        S.finish()
        with nc.Block() as block:
            S.emit(block)
    return nc


def _kmaj(w, nk):
    n = w.shape[1]
    return np.ascontiguousarray(w.reshape(nk, 128, n).transpose(1, 0, 2).reshape(128, nk * n))


def _colmaj(v):
    return np.ascontiguousarray(v.reshape(-1, 128).T)


def build_consts(NT, inp):
    S_ = NT * 128
    NCMP = (S_ - 32) // 16 + 1
    CO, NCST = cst_layout(NT)
    cst = np.zeros((128, NCST), np.float32)

    def put(name, arr):
        a, b = CO[name]
        cst[:, a:b] = arr
    put('ident', np.eye(128, dtype=np.float32))
    put('nw1', _colmaj(inp['norm1_w'][0]))
    put('nw2', _colmaj(inp['norm2_w'][0]))
    put('eps', np.full((128, 1), EPS, np.float32))
    put('wq', np.tile(inp['q_norm_w'][0][None, :], (128, 1)))
    for j in range(3):
        put(f'wk{j}', np.tile(inp['k_norm_w'][0, j][None, :], (128, 1)))
    put('posk', np.tile(inp['phi_k_pos'][0].T, (2, 1)))
    put('posv', np.tile(inp['phi_v_pos'][0].T, (2, 1)))
    cw = inp['conv_w'][0]
    put('cw', np.concatenate([_colmaj(cw[j]) for j in range(4)], axis=1))
    put('cb', _colmaj(inp['conv_b'][0]))
    put('ba', _colmaj(inp['lru_ba'][0].reshape(-1)))
    put('bx', _colmaj(inp['lru_bx'][0].reshape(-1)))
    put('lam', _colmaj(inp['lru_lambda'][0]))
    half = 8
    inv = (np.float32(500000.0) ** (-np.arange(half, dtype=np.float32) / np.float32(half))).astype(np.float32)
    pos = np.arange(S_, dtype=np.float32)
    ang = (pos[:, None] * inv[None, :]).astype(np.float32)
    cos = np.cos(ang).astype(np.float32).reshape(NT, 128, 8).transpose(1, 0, 2).reshape(128, NT * 8)
    sin = np.sin(ang).astype(np.float32).reshape(NT, 128, 8).transpose(1, 0, 2).reshape(128, NT * 8)
    put('cos', cos)
    put('sin', sin)
    cend = (np.arange(256) * 16 + 31).astype(np.float32)
    angc = (cend[:, None] * inv[None, :]).astype(np.float32)
    put('cosc', np.cos(angc).astype(np.float32).reshape(2, 128, 8).transpose(1, 0, 2).reshape(128, 16))
    put('sinc', np.sin(angc).astype(np.float32).reshape(2, 128, 8).transpose(1, 0, 2).reshape(128, 16))
    A = np.zeros((128, 127), np.float32)
    B = np.ones((128, 127), np.float32)
    for q in range(128):
        cur = 1 if q >= 64 else 0
        for r in range(127):
            rel = r - 63
            if rel > cur:
                A[q, r] = -1e30
                B[q, r] = 0.0
            elif rel == cur or rel == cur - 1:
                A[q, r] = 1e4
                B[q, r] = 0.0
    put('tkA', A)
    put('tkB', B)
    ov = np.zeros((256, 65), np.float32)
    for n in range(NCMP):
        for j in range(min(64, S_ // 64)):
            if 16 * n <= 64 * j + 63 and 16 * n + 31 >= 64 * j:
                ov[n, j] = 1.0
        ov[n, 64] = 1.0
    put('ovl', ov.reshape(2, 128, 65).transpose(1, 0, 2).reshape(128, 130))
    E = np.zeros((64, S_), np.float32)
    for j in range(S_ // 64):
        E[j, j * 64:(j + 1) * 64] = 1.0
    return cst, E


def host_weights(inp):
    w_in = inp['w_in'][0]
    cols = lambda a, b: w_in[:, a:b]
    wtm = np.concatenate([cols(0, 512), cols(768, 896), cols(1024, 1152), cols(896, 1024), cols(1152, 1280), cols(1280, 1304)], axis=1)
    wcv = cols(512, 768)
    wxy = cols(1304, 3352)
    wgm = cols(3352, 5400)

    def phi1(w):
        a = w.reshape(32, 64, 256).transpose(1, 0, 2).reshape(64, 32 * 256)
        return np.ascontiguousarray(np.concatenate([a, a], axis=0))
    m = {
        'wtm': _kmaj(wtm, 8), 'wcv': _kmaj(wcv, 8), 'wxy': _kmaj(wxy, 8), 'wgm': _kmaj(wgm, 8),
        'pk1': phi1(inp['phi_k_w1'][0]), 'pv1': phi1(inp['phi_v_w1'][0]),
        'pk2': _kmaj(inp['phi_k_w2'][0], 2), 'pv2': _kmaj(inp['phi_v_w2'][0], 2),
        'wa': np.ascontiguousarray(inp['lru_wa'][0].transpose(1, 0, 2).reshape(128, 1024)),
        'wx': np.ascontiguousarray(inp['lru_wx'][0].transpose(1, 0, 2).reshape(128, 1024)),
        'wnu': _kmaj(inp['w_nsa_up'][0], 4), 'wlu': _kmaj(inp['w_lru_up'][0], 8), 'wo': _kmaj(inp['w_o'][0], 8),
        'wf1': _kmaj(inp['w_ff1'][0], 8), 'wf2': _kmaj(inp['w_ff2'][0], 32),
    }
    return m


def kernel(**inputs):
    inp = {k: np.asarray(v, dtype=np.float32) for k, v in inputs.items()}
    x = inp['x']
    B, S_, _ = x.shape
    NT = S_ // 128
    cst, E = build_consts(NT, inp)
    wm = host_weights(inp)
    nc = build_program(NT)
    in_maps = []
    for b in range(B):
        m = dict(wm)
        m['x'] = np.ascontiguousarray(x[b])
        m['cst'] = cst
        m['emat'] = E
        in_maps.append(m)
    res = run_bass_kernel_spmd(nc, in_maps, core_ids=list(range(B)))
    return np.stack([np.asarray(r['out'], dtype=np.float32) for r in res.results], axis=0)
```

```python
import math
from contextlib import ExitStack
import numpy as np
import concourse.bass as bass
import concourse.mybir as mybir
from concourse.bass_utils import run_bass_kernel_spmd

F32 = mybir.dt.float32
BF16 = mybir.dt.bfloat16
AF = mybir.ActivationFunctionType
ALU = mybir.AluOpType
AX = mybir.AxisListType

D = 1024
DFF = 4096
EPS = 1e-6
NEGM = -30000.0
GC1 = 0.044715
GC2 = 2.0 * math.sqrt(2.0 / math.pi)


class Sched:
    CE = ('pe', 'dve', 'act', 'pool')

    def __init__(self, nc, stack, ndma=8, limit=16000, strict_same=True):
        self.nc = nc
        self.stack = stack
        self.limit = limit
        self.strict_same = strict_same
        self.prog = {e: [] for e in ('pe', 'dve', 'act', 'pool', 'sp')}
        self.nsem = 0
        self.cur_sem = {}
        self.cnt = {}
        for e in self.CE:
            self.cur_sem[e] = self._newsem(e)
            self.cnt[e] = 0
        self.dma_sems = {q: [self._newsem('d' + q) for _ in range(ndma)] for q in ('sp', 'pool')}
        self.dma_n = {q: 0 for q in ('sp', 'pool')}
        self.waited = {e: {} for e in self.prog}
        self.lastw = {}
        self.readers = {}
        self.last_tok = {}

    def _newsem(self, tag):
        self.nsem += 1
        s = self.stack.enter_context(self.nc.semaphore(f"s{self.nsem}_{tag}"))
        return (self.nsem, s)

    PSUM = frozenset(['tp0', 'tp1', 'cp0', 'cp1', 'hp0', 'hp1', 'kp', 'tb', 'pj0', 'pj1', 'st0', 'st1', 'st2', 'st3',
                      'pv', 'pv0', 'pv1', 'mz', 'pa', 'pbk', 'yp0', 'yp1'])

    def _deps(self, reads, writes, eng=None):
        deps = []
        for b in reads:
            if b in self.lastw:
                deps.append(self.lastw[b])
            if b in self.PSUM:
                deps.extend(t for t in self.readers.get(b, ()) if t[2] != eng)
        for b in writes:
            if b in self.lastw:
                deps.append(self.lastw[b])
            deps.extend(self.readers.get(b, ()))
        return deps

    def _waits(self, eng, deps):
        waits = []
        w = self.waited[eng]
        for (sem, val, src) in deps:
            if src == eng and (eng == 'pe' or not self.strict_same):
                continue
            if w.get(sem[0], 0) >= val:
                continue
            w[sem[0]] = val
            waits.append((sem[1], val))
        return waits

    def _commit(self, tok, reads, writes):
        for b in reads:
            if b not in writes:
                self.readers.setdefault(b, []).append(tok)
        for b in writes:
            self.lastw[b] = tok
            self.readers[b] = []

    def op(self, eng, fn, reads=(), writes=()):
        deps = self._deps(reads, writes, eng)
        waits = self._waits(eng, deps)
        if self.cnt[eng] >= self.limit:
            self.cur_sem[eng] = self._newsem(eng)
            self.cnt[eng] = 0
        self.cnt[eng] += 1
        sem = self.cur_sem[eng]
        tok = (sem, self.cnt[eng], eng)
        self.last_tok[eng] = tok
        self.prog[eng].append((waits, fn, (sem[1], 1)))
        self._commit(tok, reads, writes)
        return tok

    def dma(self, q, fn, reads=(), writes=()):
        deps = self._deps(reads, writes)
        j = self.dma_n[q]
        self.dma_n[q] += 1
        K = len(self.dma_sems[q])
        sem = self.dma_sems[q][j % K]
        if j >= K:
            deps.append((sem, 16 * (j // K), 'dma'))
        waits = self._waits(q, deps)
        tok = (sem, 16 * (j // K + 1), 'dma')
        self.prog[q].append((waits, fn, (sem[1], 16)))
        self._commit(tok, reads, writes)
        return tok

    def _dma_final(self):
        deps = []
        for q in self.dma_sems:
            K = len(self.dma_sems[q])
            n = self.dma_n[q]
            for i, sem in enumerate(self.dma_sems[q]):
                cnt = (n - i + K - 1) // K if n > i else 0
                if cnt > 0:
                    deps.append((sem, 16 * cnt, 'dma'))
        return deps

    def barrier(self):
        deps = self._dma_final() + list(self.last_tok.values())
        for e in self.prog:
            saved = self.strict_same
            self.strict_same = True
            waits = []
            w = self.waited[e]
            for (sem, val, src) in deps:
                if w.get(sem[0], 0) >= val:
                    continue
                w[sem[0]] = val
                waits.append((sem[1], val))
            self.strict_same = saved
            self.prog[e].append((waits, None, None))
        self.lastw = {}
        self.readers = {}

    def finish(self):
        self.barrier()

    def emit(self, block):
        def run(engobj, items):
            for waits, fn, inc in items:
                for (s, v) in waits:
                    engobj.wait_ge(s, v)
                if fn is not None:
                    ins = fn(engobj)
                    ins.then_inc(inc[0], inc[1])

        P = self.prog

        @block.sync
        def _(e):
            run(e, P['sp'])

        @block.tensor
        def _(e):
            run(e, P['pe'])

        @block.vector
        def _(e):
            run(e, P['dve'])

        @block.scalar
        def _(e):
            run(e, P['act'])

        @block.gpsimd
        def _(e):
            run(e, P['pool'])


def cst_layout(NT):
    off = {}
    o = 0

    def add(name, n):
        nonlocal o
        off[name] = (o, o + n)
        o += n
    add('ident', 128)
    add('nw1', 8)
    add('nw2', 8)
    add('eps', 1)
    add('wq', 64)
    add('wk0', 64)
    add('wk1', 64)
    add('wk2', 64)
    add('posk', 32)
    add('posv', 32)
    add('cw', 32)
    add('cb', 8)
    add('ba', 8)
    add('bx', 8)
    add('lam', 8)
    add('cos', NT * 8)
    add('sin', NT * 8)
    add('cosc', 16)
    add('sinc', 16)
    add('tkA', 127)
    add('tkB', 127)
    add('ovl', 130)
    return off, o


DBG = {'stop': 99}


def build_program(NT=32, passes=(0, 1, 2, 3), dbg=False):
    S_ = NT * 128
    NCMP = (S_ - 32) // 16 + 1
    NSLC = S_ // 64
    CO, NCST = cst_layout(NT)
    nc = bass.Bass("TRN2", target_bir_lowering=False)

    def din(name, shape, dt=F32):
        return nc.dram_tensor(name, shape, dt, kind="ExternalInput").ap()

    x_d = din("x", [S_, D])
    cst_d = din("cst", [128, NCST])
    wtm_d = din("wtm", [128, 8 * 1048])
    wcv_d = din("wcv", [128, 8 * 256])
    wxy_d = din("wxy", [128, 8 * 2048])
    wgm_d = din("wgm", [128, 8 * 2048])
    pk1_d = din("pk1", [128, 32 * 256])
    pv1_d = din("pv1", [128, 32 * 256])
    pk2_d = din("pk2", [128, 2 * 64])
    pv2_d = din("pv2", [128, 2 * 64])
    wa_d = din("wa", [128, 8 * 128])
    wx_d = din("wx", [128, 8 * 128])
    wnu_d = din("wnu", [128, 4 * 1024])
    wlu_d = din("wlu", [128, 8 * 1024])
    wo_d = din("wo", [128, 8 * 1024])
    wf1_d = din("wf1", [128, 8 * DFF])
    wf2_d = din("wf2", [128, 32 * D])
    E_d = din("emat", [64, S_])
    mk_d = din("mk", [128, 1024])
    jw_d = din("jw", [8, 792])
    out_d = nc.dram_tensor("out", [S_, D], F32, kind="ExternalOutput").ap()
    son_d = nc.dram_tensor("sc_on", [NT * 128, 512], BF16, kind="Internal").ap()
    sol_d = nc.dram_tensor("sc_ol", [NT * 128, 1024], BF16, kind="Internal").ap()

    with ExitStack() as top:
        S = Sched(nc, top)

        uniq = [0]

        def sbt(st, name, shape, dt):
            uniq[0] += 1
            return st.enter_context(nc.sbuf_tensor(f"sb{uniq[0]}_{name}", shape, dt))

        def pst(st, name, shape, dt):
            uniq[0] += 1
            return st.enter_context(nc.psum_tensor(f"ps{uniq[0]}_{name}", shape, dt))

        cst = sbt(top, "cst", [128, NCST], F32)
        identb = sbt(top, "identb", [128, 128], BF16)
        kcT = sbt(top, "kcT", [64, 2, 256], BF16)
        vca = sbt(top, "vca", [128, 2, 2, 129], BF16)

        def C(name):
            a, b = CO[name]
            return cst[:, a:b]
        ident = C('ident')
        epsc = C('eps')

        S.dma('sp', lambda e: e.dma_start(out=cst[:], in_=cst_d[:, :]), writes=['cst'])
        S.op('dve', lambda e: e.tensor_copy(out=identb[:], in_=ident), reads=['cst'], writes=['identb'])

        def load_w(dst3, src_d, nk, ncol, name):
            step = max(1, 2048 // ncol)
            if ncol > 2048:
                for k in range(nk):
                    for c0 in range(0, ncol, 2048):
                        c1 = min(ncol, c0 + 2048)
                        S.dma('pool', lambda e, k=k, c0=c0, c1=c1: e.dma_start(
                            out=dst3[:, k, c0:c1], in_=src_d[:, k * ncol + c0:k * ncol + c1]), writes=[name])
            else:
                for k0 in range(0, nk, step):
                    k1 = min(nk, k0 + step)
                    S.dma('pool', lambda e, k0=k0, k1=k1: e.dma_start(
                        out=dst3[:, k0:k1, :],
                        in_=src_d[:, k0 * ncol:k1 * ncol].rearrange("p (a b) -> p a b", a=k1 - k0)), writes=[name])

        def norm_tile(xb, xbn, nwname, xs, ss, tp, xnT, tpn=('tp0', 'tp1')):
            if isinstance(tp, (list, tuple)):
                banks = tp
            else:
                banks = (tp[:, 0:512], tp[:, 512:1024])
            S.op('act', lambda e: e.activation(out=xs[:], in_=xb[:], func=AF.Square, accum_out=ss[:, 0:1]),
                 reads=[xbn], writes=['xs', 'ss0'])
            S.op('act', lambda e: e.activation(out=ss[:, 1:2], in_=ss[:, 0:1], func=AF.Sqrt, scale=1.0 / D, bias=epsc),
                 reads=['ss0', 'cst'], writes=['ss1'])
            S.op('dve', lambda e: e.reciprocal(out=ss[:, 2:3], in_=ss[:, 1:2]), reads=['ss1'], writes=['ss2'])
            S.op('act', lambda e: e.activation(out=xs[:], in_=xb[:], func=AF.Copy, scale=ss[:, 2:3]),
                 reads=[xbn, 'ss2'], writes=['xs'])
            for k in range(8):
                S.op('pe', lambda e, k=k: e.transpose(out=banks[k // 4][:, (k % 4) * 128:(k % 4 + 1) * 128],
                                                      in_=xs[:, k * 128:(k + 1) * 128],
                                                      identity=ident), reads=['xs', 'cst'], writes=[tpn[k // 4]])
            nw = C(nwname)
            for a in range(2):
                S.op('dve', lambda e, a=a: e.tensor_tensor(
                    out=xnT[:, 4 * a:4 * a + 4, :], in0=banks[a].rearrange("p (a b) -> p a b", a=4),
                    in1=nw[:, 4 * a:4 * a + 4].unsqueeze(2).broadcast_to([128, 4, 128]), op=ALU.mult),
                     reads=[tpn[a], 'cst'], writes=['xnT'])

        def gelu(P_, shape_free, src, srcn, dst, dstn, t1, t1n, t2, t2n, src_psum=False):
            S.op('act', lambda e: e.activation(out=t1, in_=src, func=AF.Square), reads=[srcn], writes=[t1n])
            S.op('dve', lambda e: e.tensor_scalar(out=t1, in0=t1, scalar1=GC1, scalar2=1.0, op0=ALU.mult, op1=ALU.add),
                 reads=[t1n], writes=[t1n])
            S.op('dve', lambda e: e.tensor_tensor(out=t1, in0=t1, in1=src, op=ALU.mult), reads=[t1n, srcn], writes=[t1n])
            S.op('act', lambda e: e.activation(out=t2, in_=t1, func=AF.Sigmoid, scale=GC2), reads=[t1n], writes=[t2n])
            S.op('dve', lambda e: e.tensor_tensor(out=dst, in0=t2, in1=src, op=ALU.mult), reads=[t2n, srcn], writes=[dstn])

        def rmsrope(P_, H, src3, srcn, wrep, cosp, sinp, out3, outn, tmp, pre):
            sq = tmp['sq'][0:P_, 0:H, :]
            y = tmp['y'][0:P_, 0:H, :]
            st = tmp['st']
            r1 = tmp['r1'][0:P_, 0:H, :]
            r2 = tmp['r2'][0:P_, 0:H, :]
            S.op('pool', lambda e: e.tensor_tensor(out=sq, in0=src3, in1=src3, op=ALU.mult), reads=[srcn], writes=[pre + 'sq'])
            S.op('dve', lambda e: e.tensor_reduce(out=st[0:P_, 0:H], in_=sq, axis=AX.X, op=ALU.add),
                 reads=[pre + 'sq'], writes=[pre + 'st0'])
            S.op('act', lambda e: e.activation(out=st[0:P_, 8:8 + H], in_=st[0:P_, 0:H], func=AF.Sqrt, scale=1.0 / 64,
                                               bias=epsc[0:P_, :]), reads=[pre + 'st0', 'cst'], writes=[pre + 'st1'])
            S.op('dve', lambda e: e.reciprocal(out=st[0:P_, 16:16 + H], in_=st[0:P_, 8:8 + H]),
                 reads=[pre + 'st1'], writes=[pre + 'st2'])
            S.op('dve', lambda e: e.tensor_tensor(out=y, in0=src3,
                                                  in1=st[0:P_, 16:16 + H].unsqueeze(2).broadcast_to([P_, H, 64]), op=ALU.mult),
                 reads=[srcn, pre + 'st2'], writes=[pre + 'y'])
            S.op('dve', lambda e: e.tensor_tensor(out=y, in0=y, in1=wrep.unsqueeze(1).broadcast_to([P_, H, 64]), op=ALU.mult),
                 reads=[pre + 'y', 'cst', 'wq8'], writes=[pre + 'y'])
            cb_ = cosp.unsqueeze(1).broadcast_to([P_, H, 8])
            sb_ = sinp.unsqueeze(1).broadcast_to([P_, H, 8])
            y1 = y[:, :, 0:8]
            y2 = y[:, :, 8:16]
            S.op('dve', lambda e: e.tensor_tensor(out=r1, in0=y1, in1=cb_, op=ALU.mult), reads=[pre + 'y', 'cst'], writes=[pre + 'r1'])
            S.op('pool', lambda e: e.tensor_tensor(out=r2, in0=y2, in1=sb_, op=ALU.mult), reads=[pre + 'y', 'cst'], writes=[pre + 'r2'])
            S.op('dve', lambda e: e.tensor_tensor(out=out3[:, :, 0:8], in0=r1, in1=r2, op=ALU.subtract),
                 reads=[pre + 'r1', pre + 'r2'], writes=[outn])
            S.op('dve', lambda e: e.tensor_tensor(out=r1, in0=y2, in1=cb_, op=ALU.mult), reads=[pre + 'y', 'cst'], writes=[pre + 'r1'])
            S.op('pool', lambda e: e.tensor_tensor(out=r2, in0=y1, in1=sb_, op=ALU.mult), reads=[pre + 'y', 'cst'], writes=[pre + 'r2'])
            S.op('dve', lambda e: e.tensor_tensor(out=out3[:, :, 8:16], in0=r1, in1=r2, op=ALU.add),
                 reads=[pre + 'r1', pre + 'r2'], writes=[outn])
            S.op('act', lambda e: e.activation(out=out3[:, :, 16:64], in_=y[:, :, 16:64], func=AF.Copy),
                 reads=[pre + 'y'], writes=[outn])

        def pass0():
            with ExitStack() as st:
                wcv = sbt(st, "wcv", [128, 8, 256], BF16)
                pk1 = sbt(st, "pk1", [128, 32, 256], BF16)
                pv1 = sbt(st, "pv1", [128, 32, 256], BF16)
                pk2 = sbt(st, "pk2", [128, 2, 64], BF16)
                pv2 = sbt(st, "pv2", [128, 2, 64], BF16)
                rawk = sbt(st, "rawk", [128, S_ + 16], F32)
                rawv = sbt(st, "rawv", [128, S_ + 16], F32)
                zk = sbt(st, "zk", [128, 32, 256], BF16)
                zv = sbt(st, "zv", [128, 32, 256], BF16)
                xb = [sbt(st, f"xb{i}", [128, D], F32) for i in range(2)]
                xs = sbt(st, "xs", [128, D], F32)
                ss = sbt(st, "ss", [128, 4], F32)
                xnT = sbt(st, "xnT", [128, 8, 128], BF16)
                hT = sbt(st, "hT", [128, 2, 256], BF16)
                g1 = sbt(st, "g1", [128, 256], F32)
                g2 = sbt(st, "g2", [128, 256], F32)
                kc32 = sbt(st, "kc32", [128, 64], F32)
                kcn = sbt(st, "kcn", [128, 64], BF16)
                tmp = dict(sq=sbt(st, "t_sq", [128, 8, 64], F32), y=sbt(st, "t_y", [128, 8, 64], F32),
                           st=sbt(st, "t_st", [128, 32], F32), r1=sbt(st, "t_r1", [128, 8, 8], F32),
                           r2=sbt(st, "t_r2", [128, 8, 8], F32))
                tp = pst(st, "tp", [128, 1024], F32)
                cp = [pst(st, f"cp{i}", [128, 512], F32) for i in range(2)]
                hp = [pst(st, f"hp{i}", [128, 512], F32) for i in range(2)]
                kp = pst(st, "kp", [128, 512], F32)
                tb = pst(st, "tb", [128, 1024], BF16)

                load_w(wcv, wcv_d, 8, 256, 'wcv')
                load_w(pk1, pk1_d, 32, 256, 'pk1')
                load_w(pv1, pv1_d, 32, 256, 'pv1')
                load_w(pk2, pk2_d, 2, 64, 'pk2')
                load_w(pv2, pv2_d, 2, 64, 'pv2')
                for t in range(NT):
                    b = xb[t % 2]
                    bn = f"xb{t % 2}"
                    S.dma('sp', lambda e, b=b, t=t: e.dma_start(out=b[:], in_=x_d[t * 128:(t + 1) * 128, :]), writes=[bn])
                    norm_tile(b, bn, 'nw1', xs, ss, tp, xnT)
                    pb = cp[t % 2]
                    pn = f"cp{t % 2}"
                    for c in range(2):
                        for k in range(8):
                            S.op('pe', lambda e, c=c, k=k, pb=pb: e.matmul(
                                out=pb[:, c * 128:(c + 1) * 128], lhsT=wcv[:, k, c * 128:(c + 1) * 128], rhs=xnT[:, k, :],
                                start=(k == 0), stop=(k == 7)), reads=['wcv', 'xnT'], writes=[pn])
                    S.op('act', lambda e, pb=pb, t=t: e.activation(out=rawk[:, t * 128:(t + 1) * 128], in_=pb[:, 0:128], func=AF.Copy),
                         reads=[pn], writes=['rawk'])
                    S.op('act', lambda e, pb=pb, t=t: e.activation(out=rawv[:, t * 128:(t + 1) * 128], in_=pb[:, 128:256], func=AF.Copy),
                         reads=[pn], writes=['rawv'])
                if DBG['stop'] <= 1:
                    S.barrier()
                    return
                posk = C('posk')
                posv = C('posv')
                for l in range(32):
                    S.op('dve', lambda e, l=l: e.tensor_scalar(
                        out=zk[:, l, 0:NCMP], in0=rawk[:, l:l + 16 * (NCMP - 1) + 1:16], scalar1=posk[:, l:l + 1], scalar2=None,
                        op0=ALU.add), reads=['rawk', 'cst'], writes=['zk'])
                    S.op('pool', lambda e, l=l: e.tensor_scalar(
                        out=zv[:, l, 0:NCMP], in0=rawv[:, l:l + 16 * (NCMP - 1) + 1:16], scalar1=posv[:, l:l + 1], scalar2=None,
                        op0=ALU.add), reads=['rawv', 'cst'], writes=['zv'])
                if DBG['stop'] <= 2:
                    S.barrier()
                    return
                nch = [(0, min(128, NCMP))]
                if NCMP > 128:
                    nch.append((128, NCMP - 128))
                ovl = C('ovl')
                for kv in range(2):
                    z = (zk, zv)[kv]
                    zn = ('zk', 'zv')[kv]
                    w1 = (pk1, pv1)[kv]
                    w1n = ('pk1', 'pv1')[kv]
                    w2 = (pk2, pv2)[kv]
                    w2n = ('pk2', 'pv2')[kv]
                    for g in range(2):
                        gs_ = slice(g * 64, (g + 1) * 64)
                        for hc in range(2):
                            hb_ = hp[hc]
                            hn_ = f"hp{hc}"
                            for l in range(32):
                                S.op('pe', lambda e, l=l, hc=hc, hb_=hb_, z=z, w1=w1, gs_=gs_: e.matmul(
                                    out=hb_[:, 0:NCMP], lhsT=w1[gs_, l, hc * 128:(hc + 1) * 128], rhs=z[gs_, l, 0:NCMP],
                                    start=(l == 0), stop=(l == 31)), reads=[w1n, zn], writes=[hn_])
                            gelu(128, NCMP, hb_[:, 0:NCMP], hn_, hT[:, hc, 0:NCMP], f'hT{hc}',
                                 g1[:, 0:NCMP], 'g1', g2[:, 0:NCMP], 'g2')
                        if DBG['stop'] <= 3:
                            continue
                        for ci, (n0, sz) in enumerate(nch):
                            for hc in range(2):
                                S.op('pe', lambda e, hc=hc, n0=n0, sz=sz, w2=w2: e.matmul(
                                    out=kp[0:sz, 0:64], lhsT=hT[:, hc, n0:n0 + sz], rhs=w2[:, hc, :],
                                    start=(hc == 0), stop=(hc == 1)), reads=[f'hT{hc}', w2n], writes=['kp'])
                            if DBG['stop'] <= 4:
                                continue
                            if kv == 0:
                                S.op('act', lambda e, sz=sz: e.activation(out=kc32[0:sz, :], in_=kp[0:sz, 0:64], func=AF.Copy),
                                     reads=['kp'], writes=['kc32'])
                                cc = C('cosc')[:, ci * 8:(ci + 1) * 8]
                                sc = C('sinc')[:, ci * 8:(ci + 1) * 8]
                                rmsrope(sz, 1, kc32[0:sz, :].rearrange("p (h d) -> p h d", h=1), 'kc32', C('wk0')[0:sz, :],
                                        cc[0:sz, :], sc[0:sz, :], kcn[0:sz, :].rearrange("p (h d) -> p h d", h=1), 'kcn', tmp, 'p0')
                                S.op('pe', lambda e, sz=sz: e.transpose(out=tb[0:64, 0:sz], in_=kcn[0:sz, :], identity=identb[0:sz, 0:sz]),
                                     reads=['kcn', 'identb'], writes=['tb'])
                                S.op('dve', lambda e, sz=sz, n0=n0, g=g: e.tensor_copy(out=kcT[:, g, n0:n0 + sz], in_=tb[0:64, 0:sz]),
                                     reads=['tb'], writes=['kcT'])
                            else:
                                S.op('act', lambda e, sz=sz, g=g, ci=ci: e.activation(out=vca[0:sz, g, ci, 0:64], in_=kp[0:sz, 0:64], func=AF.Copy),
                                     reads=['kp'], writes=['vca'])
                                S.op('dve', lambda e, sz=sz, g=g, ci=ci: e.tensor_copy(out=vca[0:sz, g, ci, 64:129], in_=ovl[0:sz, ci * 65:(ci + 1) * 65]),
                                     reads=['cst'], writes=['vca'])
                S.barrier()

        def pass1():
            with ExitStack() as st:
                wtm = sbt(st, "wtm", [128, 8, 1048], BF16)
                wxy = sbt(st, "wxy", [128, 8, 2048], BF16)
                wa = sbt(st, "wa", [128, 8, 128], BF16)
                wx = sbt(st, "wx", [128, 8, 128], BF16)
                ksT = [sbt(st, f"ksT{g}", [128, S_], BF16) for g in range(2)]
                kwT = sbt(st, "kwT", [64, 2, S_], BF16)
                vsa = sbt(st, "vsa", [128, NT, 2, 65], BF16)
                vwa = sbt(st, "vwa", [128, NT, 2, 65], BF16)
                mk = sbt(st, "mk", [128, 1024], BF16)
                jw = sbt(st, "jw", [8, 792], BF16)
                ones2 = sbt(st, "ones2", [128, 2], BF16)
                xb = [sbt(st, f"xb{i}", [128, D], F32) for i in range(2)]
                xs = sbt(st, "xs", [128, D], F32)
                ss = sbt(st, "ss", [128, 4], F32)
                xnT = sbt(st, "xnT", [128, 8, 128], BF16)
                qraw = sbt(st, "qraw", [128, 512], F32)
                kvraw = sbt(st, "kvraw", [128, 512], F32)
                gsg = sbt(st, "gsg", [128, 24], F32)
                qaug = sbt(st, "qaug", [128, 8, 128], BF16)
                kn = sbt(st, "kn", [128, 4, 64], BF16)
                qT = sbt(st, "qT", [64, 8, 128], BF16)
                qaT = [sbt(st, f"qaT{g}", [128, 512], BF16) for g in range(2)]
                NB = 4
                pT = [sbt(st, f"pT{i}", [128, 512], BF16) for i in range(NB)]
                oacc = sbt(st, "oacc", [128, 8, 64], F32)
                otmp = sbt(st, "otmp", [128, 4, 64], F32)
                obf = sbt(st, "obf", [128, 512], BF16)
                onT = sbt(st, "onT", [128, 4, 128], BF16)
                imp = sbt(st, "imp", [128, 64], F32)
                sc1 = sbt(st, "sc1", [128, 64], F32)
                sc2 = sbt(st, "sc2", [128, 64], F32)
                m8 = sbt(st, "m8", [128, 16], F32)
                sm = sbt(st, "sm", [128, 16], F32)
                wq8 = sbt(st, "wq8", [128, 64], F32)
                tmp = dict(sq=sbt(st, "t_sq", [128, 8, 64], F32), y=sbt(st, "t_y", [128, 8, 64], F32),
                           st=sbt(st, "t_st", [128, 32], F32), r1=sbt(st, "t_r1", [128, 8, 8], F32),
                           r2=sbt(st, "t_r2", [128, 8, 8], F32))
                xrx = sbt(st, "xrx", [128, 8, 132], F32)
                yrb = sbt(st, "yrb", [128, 8, 128], F32)
                xc = sbt(st, "xc", [128, 8, 128], F32)
                xcb = sbt(st, "xcb", [128, 8, 128], BF16)
                rg = sbt(st, "rg", [128, 8, 128], F32)
                ig = sbt(st, "ig", [128, 8, 128], F32)
                av = sbt(st, "av", [128, 8, 128], F32)
                bt = sbt(st, "bt", [128, 8, 128], F32)
                hs = sbt(st, "hs", [128, 8, 128], F32)
                hst = sbt(st, "hst", [128, 8], F32)
                cl = sbt(st, "cl", [128, 8], F32)
                olT = sbt(st, "olT", [128, 8, 128], BF16)
                gt1 = sbt(st, "gt1", [128, 1024], F32)
                stp = [pst(st, f"st{i}", [128, 512], F32) for i in range(NB)]
                pvb = [pst(st, f"pv{i}", [128, 512], F32) for i in range(2)]
                mz = pst(st, "mz", [128, 1024], BF16)
                pj = [stp[2], stp[3]]
                pjn = ['st2', 'st3']

                load_w(wtm, wtm_d, 8, 1048, 'wtm')
                load_w(wxy, wxy_d, 8, 2048, 'wxy')
                load_w(wa, wa_d, 8, 128, 'wa')
                load_w(wx, wx_d, 8, 128, 'wx')
                S.dma('pool', lambda e: e.dma_start(out=mk[:], in_=mk_d[:, :]), writes=['mk'])
                S.dma('pool', lambda e: e.dma_start(out=jw[:], in_=jw_d[:, :]), writes=['jw'])
                for g in range(2):
                    for c0 in range(0, S_, 2048):
                        c1 = min(S_, c0 + 2048)
                        S.dma('pool', lambda e, g=g, c0=c0, c1=c1: e.dma_start(out=ksT[g][64:128, c0:c1], in_=E_d[:, c0:c1]),
                              writes=[f'ksE{g}'])
                S.op('dve', lambda e: e.tensor_scalar(out=wq8[:], in0=C('wq'), scalar1=0.125, scalar2=None, op0=ALU.mult),
                     reads=['cst'], writes=['wq8'])
                S.op('pool', lambda e: e.memset(vsa[:, :, :, 64:65], 1.0), writes=['vsa1'])
                S.op('pool', lambda e: e.memset(vwa[:, :, :, 64:65], 1.0), writes=['vwa1'])
                S.op('pool', lambda e: e.memset(ones2[:], 1.0), writes=['ones2'])
                S.op('pool', lambda e: e.memset(xrx[:, :, 0:3], 0.0), writes=['xrxh'])
                S.op('pool', lambda e: e.memset(hst[:], 0.0), writes=['hst'])
                S.op('pool', lambda e: e.memset(qaug[:], 0.0), writes=['qaugq', 'qaugm0', 'qaugm1'])
                S.op('act', lambda e: e.activation(out=cl[:], in_=C('lam'), func=AF.Exp, scale=-1.0), reads=['cst'], writes=['cl'])
                S.op('dve', lambda e: e.tensor_scalar(out=cl[:], in0=cl[:], scalar1=1.0, scalar2=None, op0=ALU.add),
                     reads=['cl'], writes=['cl'])
                S.op('act', lambda e: e.activation(out=cl[:], in_=cl[:], func=AF.Ln), reads=['cl'], writes=['cl'])
                S.op('dve', lambda e: e.tensor_scalar(out=cl[:], in0=cl[:], scalar1=-8.0, scalar2=None, op0=ALU.mult),
                     reads=['cl'], writes=['cl'])
                tkA = C('tkA')
                tkB = C('tkB')
                cw = C('cw')
                cbv = C('cb')
                bav = C('ba')
                bxv = C('bx')
                Mc4 = mk[:, 0:512]
                Ml4 = mk[:, 512:1024]
                W8 = jw[:, 280:792]
                ctr = [0]

                for i in range(NT):
                    T0 = i * 128
                    b = xb[i % 2]
                    bn = f"xb{i % 2}"
                    S.dma('sp', lambda e, b=b, i=i: e.dma_start(out=b[:], in_=x_d[i * 128:(i + 1) * 128, :]), writes=[bn])
                    norm_tile(b, bn, 'nw1', xs, ss, [stp[0], stp[1]], xnT, tpn=('st0', 'st1'))
                    for (c0, c1, pb, pn) in ((0, 512, pj[0], pjn[0]), (512, 1024, pj[1], pjn[1])):
                        for k in range(8):
                            S.op('pe', lambda e, k=k, c0=c0, c1=c1, pb=pb: e.matmul(
                                out=pb[:, 0:512], lhsT=xnT[:, k, :], rhs=wtm[:, k, c0:c1], start=(k == 0), stop=(k == 7)),
                                reads=['xnT', 'wtm'], writes=[pn])
                    S.op('act', lambda e: e.activation(out=qraw[:], in_=pj[0][:, 0:512], func=AF.Copy), reads=[pjn[0]], writes=['qraw'])
                    S.op('act', lambda e: e.activation(out=kvraw[:], in_=pj[1][:, 0:512], func=AF.Copy), reads=[pjn[1]], writes=['kvraw'])
                    for k in range(8):
                        S.op('pe', lambda e, k=k: e.matmul(out=stp[0][:, 0:24], lhsT=xnT[:, k, :], rhs=wtm[:, k, 1024:1048],
                                                           start=(k == 0), stop=(k == 7)), reads=['xnT', 'wtm'], writes=['st0'])
                    S.op('act', lambda e: e.activation(out=gsg[:], in_=stp[0][:, 0:24], func=AF.Sigmoid), reads=['st0'], writes=['gsg'])
                    for c4 in range(4):
                        pb = stp[(c4 + 1) % 4]
                        pn = f"st{(c4 + 1) % 4}"
                        for cc in range(4):
                            c = c4 * 4 + cc
                            for k in range(8):
                                S.op('pe', lambda e, c=c, cc=cc, k=k, pb=pb: e.matmul(
                                    out=pb[:, cc * 128:(cc + 1) * 128], lhsT=wxy[:, k, c * 128:(c + 1) * 128], rhs=xnT[:, k, :],
                                    start=(k == 0), stop=(k == 7)), reads=['wxy', 'xnT'], writes=[pn])
                        src = pb[:, 0:512].rearrange("p (a b) -> p a b", a=4)
                        if c4 < 2:
                            S.op('act', lambda e, c4=c4, src=src: e.activation(out=xrx[:, c4 * 4:(c4 + 1) * 4, 3:131], in_=src, func=AF.Copy),
                                 reads=[pn], writes=['xrx'])
                        else:
                            S.op('act', lambda e, c4=c4, src=src: e.activation(out=yrb[:, (c4 - 2) * 4:(c4 - 1) * 4, :], in_=src, func=AF.Copy),
                                 reads=[pn], writes=['yrb'])
                    cosp = C('cos')[:, i * 8:(i + 1) * 8]
                    sinp = C('sin')[:, i * 8:(i + 1) * 8]
                    rmsrope(128, 8, qraw[:].rearrange("p (h d) -> p h d", h=8), 'qraw', wq8[:], cosp, sinp,
                            qaug[:, :, 0:64], 'qaugq', tmp, 'p1')
                    rmsrope(128, 2, kvraw[:, 0:128].rearrange("p (h d) -> p h d", h=2), 'kvraw', C('wk1'), cosp, sinp,
                            kn[:, 0:2, :], 'kn', tmp, 'p1')
                    rmsrope(128, 2, kvraw[:, 128:256].rearrange("p (h d) -> p h d", h=2), 'kvraw', C('wk2'), cosp, sinp,
                            kn[:, 2:4, :], 'kn', tmp, 'p1')
                    for j in range(4):
                        S.op('pe', lambda e, j=j: e.transpose(out=mz[0:64, j * 128:(j + 1) * 128], in_=kn[:, j, :], identity=identb[:]),
                             reads=['kn', 'identb'], writes=['mz'])
                    for g in range(2):
                        S.op('dve', lambda e, g=g, T0=T0: e.tensor_copy(out=ksT[g][0:64, T0:T0 + 128], in_=mz[0:64, g * 128:(g + 1) * 128]),
                             reads=['mz'], writes=[f'ksT{g}_{i}'])
                    S.op('dve', lambda e, T0=T0: e.tensor_copy(out=kwT[:, :, T0:T0 + 128],
                                                               in_=mz[0:64, 256:512].rearrange("p (a b) -> p a b", a=2)),
                         reads=['mz'], writes=[f'kwT_{i}'])
                    S.op('act', lambda e, i=i: e.activation(out=vsa[:, i, :, 0:64], in_=kvraw[:, 256:384].rearrange("p (g d) -> p g d", g=2),
                                                            func=AF.Copy), reads=['kvraw'], writes=[f'vsa_{i}'])
                    S.op('act', lambda e, i=i: e.activation(out=vwa[:, i, :, 0:64], in_=kvraw[:, 384:512].rearrange("p (g d) -> p g d", g=2),
                                                            func=AF.Copy), reads=['kvraw'], writes=[f'vwa_{i}'])
                    for h in range(8):
                        S.op('pe', lambda e, h=h: e.transpose(out=mz[0:64, h * 128:(h + 1) * 128], in_=qaug[:, h, 0:64], identity=identb[:]),
                             reads=['qaugq', 'identb'], writes=['mz'])
                    S.op('dve', lambda e: e.tensor_copy(out=qT[:].rearrange("p a b -> p (a b)"), in_=mz[0:64, :]), reads=['mz'], writes=['qT'])

                    n_hi = min(NCMP, 8 * i + 7)
                    chunks = [(0, 0, min(128, n_hi))]
                    if n_hi > 128:
                        chunks.append((1, 128, n_hi - 128))
                    nchk = len(chunks)
                    for g in range(2):
                        bufs = []
                        for (ci, n0, Kc) in chunks:
                            bi = ctr[0] % NB
                            ctr[0] += 1
                            bufs.append(bi)
                            sp_ = stp[bi]
                            sn_ = f"st{bi}"
                            pt_ = pT[bi]
                            ptn = f"pT{bi}"
                            s0 = 265 + n0 - 8 * i
                            S.op('pe', lambda e, g=g, n0=n0, Kc=Kc, sp_=sp_: e.matmul(
                                out=sp_[0:Kc, :], lhsT=kcT[:, g, n0:n0 + Kc], rhs=qT[:, 4 * g:4 * g + 4, :].rearrange("p a b -> p (a b)"),
                                start=True, stop=False), reads=['kcT', 'qT'], writes=[sn_])
                            S.op('pe', lambda e, s0=s0, Kc=Kc, sp_=sp_: e.matmul(
                                out=sp_[0:Kc, :], lhsT=jw[:, s0:s0 + Kc], rhs=W8, start=False, stop=True), reads=['jw'], writes=[sn_])
                            S.op('act', lambda e, Kc=Kc, sp_=sp_, pt_=pt_: e.activation(out=pt_[0:Kc, :], in_=sp_[0:Kc, :], func=AF.Exp),
                                 reads=[sn_], writes=[ptn])
                        for h in range(4):
                            for idx, (ci, n0, Kc) in enumerate(chunks):
                                pt_ = pT[bufs[idx]]
                                ptn = f"pT{bufs[idx]}"
                                S.op('pe', lambda e, h=h, ci=ci, Kc=Kc, g=g, pt_=pt_, idx=idx, lastc=(idx == nchk - 1): e.matmul(
                                    out=pvb[0][:, h * 128:(h + 1) * 128], lhsT=pt_[0:Kc, h * 128:(h + 1) * 128], rhs=vca[0:Kc, g, ci, 0:128],
                                    start=(idx == 0), stop=lastc), reads=[ptn, 'vca'], writes=['pv0'])
                        for h in range(4):
                            for idx, (ci, n0, Kc) in enumerate(chunks):
                                pt_ = pT[bufs[idx]]
                                ptn = f"pT{bufs[idx]}"
                                S.op('pe', lambda e, h=h, Kc=Kc, pt_=pt_, idx=idx, lastc=(idx == nchk - 1): e.matmul(
                                    out=pvb[1][:, 2 * h:2 * h + 2], lhsT=pt_[0:Kc, h * 128:(h + 1) * 128], rhs=ones2[0:Kc, :],
                                    start=(idx == 0), stop=lastc), reads=[ptn, 'ones2'], writes=['pv1'])
                        S.op('dve', lambda e: e.tensor_scalar(out=sm[:, 0:4], in0=pvb[1][:, 0:8:2], scalar1=1e-30, scalar2=None, op0=ALU.max),
                             reads=['pv1'], writes=['sm0'])
                        S.op('dve', lambda e: e.reciprocal(out=sm[:, 4:8], in_=sm[:, 0:4]), reads=['sm0'], writes=['sm1'])
                        S.op('dve', lambda e, g=g: e.tensor_tensor(out=sm[:, 8:12], in0=sm[:, 4:8], in1=gsg[:, 4 * g:4 * g + 4], op=ALU.mult),
                             reads=['sm1', 'gsg'], writes=['sm2'])
                        pv3 = pvb[0][:, 0:512].rearrange("p (h c) -> p h c", h=4)
                        S.op('dve', lambda e, g=g, pv3=pv3: e.tensor_tensor(
                            out=oacc[:, 4 * g:4 * g + 4, :], in0=pv3[:, :, 0:64], in1=sm[:, 8:12].unsqueeze(2).broadcast_to([128, 4, 64]),
                            op=ALU.mult), reads=['pv0', 'sm2'], writes=['oacc'])
                        S.op('dve', lambda e, pv3=pv3: e.tensor_tensor(
                            out=otmp[:], in0=pv3[:, :, 64:128], in1=sm[:, 4:8].unsqueeze(2).broadcast_to([128, 4, 64]),
                            op=ALU.mult), reads=['pv0', 'sm1'], writes=['otmp'])
                        S.op('dve', lambda e: e.tensor_reduce(out=imp[:], in_=otmp[:].rearrange("p h j -> p j h"), axis=AX.X, op=ALU.add),
                             reads=['otmp'], writes=['imp'])
                        a0 = 63 - 2 * i
                        Asl = tkA[:, a0:a0 + NSLC]
                        Bsl = tkB[:, a0:a0 + NSLC]
                        S.op('dve', lambda e, Bsl=Bsl: e.tensor_tensor(out=sc1[:, 0:NSLC], in0=imp[:, 0:NSLC], in1=Bsl, op=ALU.mult),
                             reads=['imp', 'cst'], writes=['sc1'])
                        S.op('dve', lambda e, Asl=Asl: e.tensor_tensor(out=sc1[:, 0:NSLC], in0=sc1[:, 0:NSLC], in1=Asl, op=ALU.add),
                             reads=['sc1', 'cst'], writes=['sc1'])
                        S.op('dve', lambda e: e.memset(sc1[:, 0:1], 1e4), writes=['sc1'])
                        S.op('dve', lambda e: e.max(out=m8[:, 0:8], in_=sc1[:, 0:NSLC]), reads=['sc1'], writes=['m8a'])
                        S.op('dve', lambda e: e.match_replace(out=sc2[:, 0:NSLC], in_to_replace=m8[:, 0:8], in_values=sc1[:, 0:NSLC],
                                                              imm_value=-3.0e38), reads=['sc1', 'm8a'], writes=['sc2'])
                        S.op('dve', lambda e: e.max(out=m8[:, 8:16], in_=sc2[:, 0:NSLC]), reads=['sc2'], writes=['m8b'])
                        S.op('dve', lambda e, g=g: e.tensor_scalar(
                            out=qaug[:, 4 * g:4 * g + 4, 64:64 + NSLC], in0=sc1[:, 0:NSLC].unsqueeze(1).broadcast_to([128, 4, NSLC]),
                            scalar1=m8[:, 15:16], scalar2=NEGM, op0=ALU.is_lt, op1=ALU.mult), reads=['sc1', 'm8b'], writes=[f'qaugm{g}'])

                    for g in range(2):
                        for h in range(4):
                            S.op('pe', lambda e, g=g, h=h: e.transpose(out=mz[:, h * 128:(h + 1) * 128], in_=qaug[:, 4 * g + h, :], identity=identb[:]),
                                 reads=['qaugq', f'qaugm{g}', 'identb'], writes=['mz'])
                        S.op('dve', lambda e, g=g: e.tensor_copy(out=qaT[g][:], in_=mz[:, 0:512]), reads=['mz'], writes=[f'qaT{g}'])

                    items = []
                    gi = 0
                    for br in (1, 2):
                        for g in range(2):
                            kts = list(range(0, i + 1)) if br == 1 else list(range(max(0, i - 4), i + 1))
                            for idx, kt in enumerate(kts):
                                items.append(dict(br=br, g=g, kt=kt, first=(idx == 0), last=(idx == len(kts) - 1), grp=gi))
                            gi += 1

                    def stage_S(it):
                        bi = ctr[0] % NB
                        ctr[0] += 1
                        it['bi'] = bi
                        sp_ = stp[bi]
                        sn_ = f"st{bi}"
                        pt_ = pT[bi]
                        ptn = f"pT{bi}"
                        g, kt, br = it['g'], it['kt'], it['br']
                        mask = None
                        if kt == i:
                            mask = Mc4
                        elif br == 2 and kt == i - 4:
                            mask = Ml4
                        if br == 1:
                            S.op('pe', lambda e: e.matmul(out=sp_[:, :], lhsT=ksT[g][:, kt * 128:(kt + 1) * 128], rhs=qaT[g][:, :],
                                                          start=True, stop=(mask is None)),
                                 reads=[f'ksT{g}_{kt}', f'ksE{g}', f'qaT{g}'], writes=[sn_])
                        else:
                            S.op('pe', lambda e: e.matmul(out=sp_[:, :], lhsT=kwT[:, g, kt * 128:(kt + 1) * 128], rhs=qaT[g][0:64, :],
                                                          start=True, stop=(mask is None)),
                                 reads=[f'kwT_{kt}', f'qaT{g}'], writes=[sn_])
                        if mask is not None:
                            S.op('pe', lambda e: e.matmul(out=sp_[:, :], lhsT=identb[:], rhs=mask, start=False, stop=True),
                                 reads=['identb', 'mk'], writes=[sn_])
                        S.op('act', lambda e: e.activation(out=pt_[:], in_=sp_[:, :], func=AF.Exp), reads=[sn_], writes=[ptn])

                    def stage_P(it):
                        bi = it['bi']
                        pt_ = pT[bi]
                        ptn = f"pT{bi}"
                        g, kt, br = it['g'], it['kt'], it['br']
                        pvt = pvb[it['grp'] % 2]
                        pvn = f"pv{it['grp'] % 2}"
                        va = vsa if br == 1 else vwa
                        van = (f'vsa_{kt}', 'vsa1') if br == 1 else (f'vwa_{kt}', 'vwa1')
                        for h in range(4):
                            S.op('pe', lambda e, h=h: e.matmul(
                                out=pvt[:, h * 65:(h + 1) * 65], lhsT=pt_[:, h * 128:(h + 1) * 128], rhs=va[:, kt, g, :],
                                start=(it['first'] and h == 0), stop=it['last'], skip_group_check=True),
                                reads=[ptn, van[0], van[1]], writes=[pvn])
                        if it['last']:
                            pv3 = pvt[:, 0:260].rearrange("p (h c) -> p h c", h=4)
                            S.op('dve', lambda e: e.reciprocal(out=sm[:, 12:16], in_=pvt[:, 64:260:65]), reads=[pvn], writes=['sm4'])
                            S.op('dve', lambda e: e.tensor_tensor(out=sm[:, 12:16], in0=sm[:, 12:16],
                                                                  in1=gsg[:, br * 8 + 4 * g:br * 8 + 4 * g + 4], op=ALU.mult),
                                 reads=['sm4', 'gsg'], writes=['sm4'])
                            S.op('dve', lambda e: e.tensor_tensor(out=otmp[:], in0=pv3[:, :, 0:64],
                                                                  in1=sm[:, 12:16].unsqueeze(2).broadcast_to([128, 4, 64]), op=ALU.mult),
                                 reads=[pvn, 'sm4'], writes=['otmp'])
                            S.op('dve', lambda e: e.tensor_tensor(out=oacc[:, 4 * g:4 * g + 4, :], in0=oacc[:, 4 * g:4 * g + 4, :], in1=otmp[:],
                                                                  op=ALU.add), reads=['otmp', 'oacc'], writes=['oacc'])
                    LOOK = 2
                    for n_ in range(len(items) + LOOK):
                        if n_ < len(items):
                            stage_S(items[n_])
                        if n_ - LOOK >= 0:
                            stage_P(items[n_ - LOOK])

                    S.op('act', lambda e: e.activation(out=obf[:], in_=oacc[:].rearrange("p a b -> p (a b)"), func=AF.Copy),
                         reads=['oacc'], writes=['obf'])
                    for c in range(4):
                        S.op('pe', lambda e, c=c: e.transpose(out=mz[:, c * 128:(c + 1) * 128], in_=obf[:, c * 128:(c + 1) * 128], identity=identb[:]),
                             reads=['obf', 'identb'], writes=['mz'])
                    S.op('dve', lambda e: e.tensor_copy(out=onT[:].rearrange("p a b -> p (a b)"), in_=mz[:, 0:512]), reads=['mz'], writes=['onT'])
                    S.dma('sp', lambda e, i=i: e.dma_start(out=son_d[i * 128:(i + 1) * 128, :], in_=onT[:].rearrange("p a b -> p (a b)")),
                          reads=['onT'], writes=[f'son{i}'])

                    for c in range(8):
                        S.op('dve', lambda e, c=c: e.tensor_scalar(out=xc[:, c, :], in0=xrx[:, c, 3:131], scalar1=cw[:, 24 + c:25 + c],
                                                                   scalar2=cbv[:, c:c + 1], op0=ALU.mult, op1=ALU.add),
                             reads=['xrx', 'xrxh', 'cst'], writes=[f'xc{c}'])
                        for j in (2, 1, 0):
                            S.op('dve', lambda e, c=c, j=j: e.scalar_tensor_tensor(
                                out=xc[:, c, :], in0=xrx[:, c, j:j + 128], scalar=cw[:, j * 8 + c:j * 8 + c + 1], in1=xc[:, c, :],
                                op0=ALU.mult, op1=ALU.add), reads=['xrx', 'xrxh', 'cst', f'xc{c}'], writes=[f'xc{c}'])
                    xcall = [f'xc{c}' for c in range(8)]
                    S.op('act', lambda e: e.activation(out=xcb[:], in_=xc[:], func=AF.Copy), reads=xcall, writes=['xcb'])
                    S.op('dve', lambda e: e.tensor_copy(out=xrx[:, :, 0:3], in_=xrx[:, :, 128:131]), reads=['xrx'], writes=['xrxh'])
                    for (wmat, wn, bias, dst, dn) in ((wa, 'wa', bav, rg, 'rg'), (wx, 'wx', bxv, ig, 'ig')):
                        for c4 in range(2):
                            pb = stp[c4]
                            pn = f"st{c4}"
                            for cc in range(4):
                                c = c4 * 4 + cc
                                S.op('pe', lambda e, c=c, cc=cc, pb=pb, wmat=wmat: e.matmul(
                                    out=pb[:, cc * 128:(cc + 1) * 128], lhsT=wmat[:, c, :], rhs=xcb[:, c, :], start=True, stop=True),
                                    reads=[wn, 'xcb'], writes=[pn])
                            S.op('dve', lambda e, c4=c4, pb=pb, bias=bias, dst=dst: e.tensor_tensor(
                                out=dst[:, 4 * c4:4 * c4 + 4, :], in0=pb[:, 0:512].rearrange("p (a b) -> p a b", a=4),
                                in1=bias[:, 4 * c4:4 * c4 + 4].unsqueeze(2).broadcast_to([128, 4, 128]), op=ALU.add),
                                reads=[pn, 'cst'], writes=[dn])
                        S.op('act', lambda e, dst=dst: e.activation(out=dst[:], in_=dst[:], func=AF.Sigmoid), reads=[dn], writes=[dn])
                    S.op('dve', lambda e: e.tensor_tensor(out=av[:], in0=rg[:], in1=cl[:].unsqueeze(2).broadcast_to([128, 8, 128]), op=ALU.mult),
                         reads=['rg', 'cl'], writes=['av'])
                    S.op('act', lambda e: e.activation(out=av[:], in_=av[:], func=AF.Exp), reads=['av'], writes=['av'])
                    S.op('act', lambda e: e.activation(out=bt[:], in_=av[:], func=AF.Square), reads=['av'], writes=['bt'])
                    S.op('act', lambda e: e.activation(out=bt[:], in_=bt[:], func=AF.Sqrt, scale=-1.0, bias=1.0), reads=['bt'], writes=['bt'])
                    S.op('dve', lambda e: e.tensor_tensor(out=ig[:], in0=ig[:], in1=xc[:], op=ALU.mult), reads=['ig'] + xcall, writes=['ig'])
                    S.op('dve', lambda e: e.tensor_tensor(out=bt[:], in0=bt[:], in1=ig[:], op=ALU.mult), reads=['bt', 'ig'], writes=['bt'])
                    for c in range(8):
                        S.op('dve', lambda e, c=c: e.tensor_tensor_scan(out=hs[:, c, :], data0=av[:, c, :], data1=bt[:, c, :],
                                                                        initial=hst[:, c:c + 1], op0=ALU.mult, op1=ALU.add),
                             reads=['av', 'bt', 'hst'], writes=['hs'])
                    S.op('dve', lambda e: e.tensor_copy(out=hst[:], in_=hs[:, :, 127]), reads=['hs'], writes=['hst'])
                    yr2 = yrb[:].rearrange("p a b -> p (a b)")
                    gelu(128, 1024, yr2, 'yrb', rg[:].rearrange("p a b -> p (a b)"), 'rg',
                         gt1[:], 'gt1', ig[:].rearrange("p a b -> p (a b)"), 'ig')
                    S.op('dve', lambda e: e.tensor_tensor(out=olT[:], in0=hs[:], in1=rg[:], op=ALU.mult), reads=['hs', 'rg'], writes=['olT'])
                    S.dma('sp', lambda e, i=i: e.dma_start(out=sol_d[i * 128:(i + 1) * 128, :], in_=olT[:].rearrange("p a b -> p (a b)")),
                          reads=['olT'], writes=[f'sol{i}'])
                S.barrier()

        def pass2():
            with ExitStack() as st:
                wgm = sbt(st, "wgm", [128, 8, 2048], BF16)
                wnu = sbt(st, "wnu", [128, 4, 1024], BF16)
                wlu = sbt(st, "wlu", [128, 8, 1024], BF16)
                wo = sbt(st, "wo", [128, 8, 1024], BF16)
                xb = [sbt(st, f"xb{i}", [128, D], F32) for i in range(2)]
                xs = sbt(st, "xs", [128, D], F32)
                ss = sbt(st, "ss", [128, 4], F32)
                xnT = sbt(st, "xnT", [128, 8, 128], BF16)
                gm = sbt(st, "gm", [128, 16, 128], F32)
                onT = [sbt(st, f"onT{i}", [128, 4, 128], BF16) for i in range(2)]
                olT = [sbt(st, f"olT{i}", [128, 8, 128], BF16) for i in range(2)]
                t1 = sbt(st, "t1", [128, 4, 128], F32)
                t2 = sbt(st, "t2", [128, 4, 128], F32)
                mT = sbt(st, "mT", [128, 8, 128], BF16)
                ob = sbt(st, "ob", [128, D], F32)
                tp = pst(st, "tp", [128, 1024], F32)
                pj = [pst(st, f"pj{i}", [128, 512], F32) for i in range(2)]
                pa = pst(st, "pa", [128, 512], F32)
                pbk = pst(st, "pbk", [128, 512], F32)
                yp = pst(st, "yp", [128, 1024], F32)
                load_w(wgm, wgm_d, 8, 2048, 'wgm')
                load_w(wnu, wnu_d, 4, 1024, 'wnu')
                load_w(wlu, wlu_d, 8, 1024, 'wlu')
                load_w(wo, wo_d, 8, 1024, 'wo')
                for i in range(NT):
                    b = xb[i % 2]
                    bn = f"xb{i % 2}"
                    on_ = onT[i % 2]
                    onn = f"onT{i % 2}"
                    ol_ = olT[i % 2]
                    oln = f"olT{i % 2}"
                    S.dma('sp', lambda e, b=b, i=i: e.dma_start(out=b[:], in_=x_d[i * 128:(i + 1) * 128, :]), writes=[bn])
                    S.dma('sp', lambda e, on_=on_, i=i: e.dma_start(out=on_[:].rearrange("p a b -> p (a b)"), in_=son_d[i * 128:(i + 1) * 128, :]),
                          reads=[f'son{i}'], writes=[onn])
                    S.dma('sp', lambda e, ol_=ol_, i=i: e.dma_start(out=ol_[:].rearrange("p a b -> p (a b)"), in_=sol_d[i * 128:(i + 1) * 128, :]),
                          reads=[f'sol{i}'], writes=[oln])
                    norm_tile(b, bn, 'nw1', xs, ss, tp, xnT)
                    for c4 in range(4):
                        pb = pj[c4 % 2]
                        pn = f"pj{c4 % 2}"
                        for cc in range(4):
                            c = c4 * 4 + cc
                            for k in range(8):
                                S.op('pe', lambda e, c=c, cc=cc, k=k, pb=pb: e.matmul(
                                    out=pb[:, cc * 128:(cc + 1) * 128], lhsT=wgm[:, k, c * 128:(c + 1) * 128], rhs=xnT[:, k, :],
                                    start=(k == 0), stop=(k == 7)), reads=['wgm', 'xnT'], writes=[pn])
                        S.op('act', lambda e, c4=c4, pb=pb: e.activation(out=gm[:, c4 * 4:(c4 + 1) * 4, :],
                                                                         in_=pb[:, 0:512].rearrange("p (a b) -> p a b", a=4), func=AF.Sigmoid),
                             reads=[pn], writes=[f'gm{c4}'])
                    for c4 in range(2):
                        for cc in range(4):
                            c = c4 * 4 + cc
                            for k in range(4):
                                S.op('pe', lambda e, c=c, cc=cc, k=k, on_=on_: e.matmul(
                                    out=pa[:, cc * 128:(cc + 1) * 128], lhsT=wnu[:, k, c * 128:(c + 1) * 128], rhs=on_[:, k, :],
                                    start=(k == 0), stop=(k == 3)), reads=['wnu', onn], writes=['pa'])
                        for cc in range(4):
                            c = c4 * 4 + cc
                            for k in range(8):
                                S.op('pe', lambda e, c=c, cc=cc, k=k, ol_=ol_: e.matmul(
                                    out=pbk[:, cc * 128:(cc + 1) * 128], lhsT=wlu[:, k, c * 128:(c + 1) * 128], rhs=ol_[:, k, :],
                                    start=(k == 0), stop=(k == 7)), reads=['wlu', oln], writes=['pbk'])
                        S.op('dve', lambda e, c4=c4: e.tensor_tensor(out=t1[:], in0=pa[:, 0:512].rearrange("p (a b) -> p a b", a=4),
                                                                     in1=gm[:, c4 * 4:(c4 + 1) * 4, :], op=ALU.mult),
                             reads=['pa', f'gm{c4}'], writes=['t1'])
                        S.op('dve', lambda e, c4=c4: e.tensor_tensor(out=t2[:], in0=pbk[:, 0:512].rearrange("p (a b) -> p a b", a=4),
                                                                     in1=gm[:, 8 + c4 * 4:8 + (c4 + 1) * 4, :], op=ALU.mult),
                             reads=['pbk', f'gm{c4 + 2}'], writes=['t2'])
                        S.op('pool', lambda e, c4=c4: e.tensor_tensor(out=mT[:, c4 * 4:(c4 + 1) * 4, :], in0=t1[:], in1=t2[:], op=ALU.add),
                             reads=['t1', 't2'], writes=[f'mT{c4}'])
                    for n in range(2):
                        for k in range(8):
                            S.op('pe', lambda e, n=n, k=k: e.matmul(out=yp[:, n * 512:(n + 1) * 512], lhsT=mT[:, k, :],
                                                                    rhs=wo[:, k, n * 512:(n + 1) * 512], start=(k == 0), stop=(k == 7)),
                                 reads=[f'mT{k // 4}', 'wo'], writes=[f'yp{n}'])
                    S.op('dve', lambda e, b=b: e.tensor_tensor(out=ob[:], in0=yp[:], in1=b[:], op=ALU.add),
                         reads=['yp0', 'yp1', bn], writes=['ob'])
                    S.dma('sp', lambda e, i=i: e.dma_start(out=out_d[i * 128:(i + 1) * 128, :], in_=ob[:]), reads=['ob'], writes=[f'outd{i}'])
                S.barrier()

        def pass3():
            with ExitStack() as st:
                w1 = sbt(st, "w1s", [128, 8, DFF], BF16)
                w2 = sbt(st, "w2s", [128, 32, D], BF16)
                hb = [sbt(st, f"hb{i}", [128, D], F32) for i in range(2)]
                xs = sbt(st, "xs", [128, D], F32)
                ss = sbt(st, "ss", [128, 4], F32)
                hnT = sbt(st, "hnT", [128, 8, 128], BF16)
                rl = [sbt(st, f"rl{i}", [128, 512], F32) for i in range(2)]
                hid = sbt(st, "hid", [128, 32, 128], BF16)
                ob = sbt(st, "ob", [128, D], F32)
                tp = pst(st, "tp", [128, 1024], F32)
                hp = [pst(st, f"hp{i}", [128, 512], F32) for i in range(2)]
                yp = pst(st, "yp", [128, 1024], F32)
                load_w(w1, wf1_d, 8, DFF, 'w1')
                load_w(w2, wf2_d, 32, D, 'w2')
                src_d = out_d if 2 in passes else x_d
                for t in range(NT):
                    b = hb[t % 2]
                    bn = f"hb{t % 2}"
                    S.dma('sp', lambda e, b=b, t=t: e.dma_start(out=b[:], in_=src_d[t * 128:(t + 1) * 128, :]),
                          reads=[f'outd{t}'], writes=[bn])
                    norm_tile(b, bn, 'nw2', xs, ss, tp, hnT)
                    for c4 in range(8):
                        pb = hp[c4 % 2]
                        pn = f"hp{c4 % 2}"
                        for cc in range(4):
                            c = c4 * 4 + cc
                            for k in range(8):
                                S.op('pe', lambda e, c=c, cc=cc, k=k, pb=pb: e.matmul(
                                    out=pb[:, cc * 128:(cc + 1) * 128], lhsT=w1[:, k, c * 128:(c + 1) * 128], rhs=hnT[:, k, :],
                                    start=(k == 0), stop=(k == 7)), reads=['w1', 'xnT'], writes=[pn])
                        r = rl[c4 % 2]
                        rn = f"rl{c4 % 2}"
                        S.op('act', lambda e, r=r, pb=pb: e.activation(out=r[:], in_=pb[:], func=AF.Relu), reads=[pn], writes=[rn])
                        S.op('pool', lambda e, r=r, c4=c4: e.tensor_tensor(
                            out=hid[:, c4 * 4:(c4 + 1) * 4, :], in0=r[:].rearrange("p (a b) -> p a b", a=4),
                            in1=r[:].rearrange("p (a b) -> p a b", a=4), op=ALU.mult), reads=[rn], writes=[f'hid{c4}'])
                    for n in range(2):
                        for k in range(32):
                            S.op('pe', lambda e, n=n, k=k: e.matmul(out=yp[:, n * 512:(n + 1) * 512], lhsT=hid[:, k, :],
                                                                    rhs=w2[:, k, n * 512:(n + 1) * 512], start=(k == 0), stop=(k == 31)),
                                 reads=[f'hid{k // 4}', 'w2'], writes=[f'yp{n}'])
                    S.op('dve', lambda e, b=b: e.tensor_tensor(out=ob[:], in0=yp[:], in1=b[:], op=ALU.add),
                         reads=['yp0', 'yp1', bn], writes=['ob'])
                    S.dma('sp', lambda e, t=t: e.dma_start(out=out_d[t * 128:(t + 1) * 128, :], in_=ob[:]), reads=['ob'], writes=[f'outd{t}'])
        for n_, f_ in enumerate((pass0, pass1, pass2, pass3)):
            if n_ in passes:
                f_()
        S.finish()
        with nc.Block() as block:
            S.emit(block)
    return nc


def _kmaj(w, nk):
    n = w.shape[1]
    return np.ascontiguousarray(w.reshape(nk, 128, n).transpose(1, 0, 2).reshape(128, nk * n))


def _colmaj(v):
    return np.ascontiguousarray(v.reshape(-1, 128).T)


def build_consts(NT, inp):
    S_ = NT * 128
    NCMP = (S_ - 32) // 16 + 1
    CO, NCST = cst_layout(NT)
    cst = np.zeros((128, NCST), np.float32)

    def put(name, arr):
        a, b = CO[name]
        cst[:, a:b] = arr
    put('ident', np.eye(128, dtype=np.float32))
    put('nw1', _colmaj(inp['norm1_w'][0]))
    put('nw2', _colmaj(inp['norm2_w'][0]))
    put('eps', np.full((128, 1), EPS, np.float32))
    put('wq', np.tile(inp['q_norm_w'][0][None, :], (128, 1)))
    for j in range(3):
        put(f'wk{j}', np.tile(inp['k_norm_w'][0, j][None, :], (128, 1)))
    put('posk', np.tile(inp['phi_k_pos'][0].T, (2, 1)))
    put('posv', np.tile(inp['phi_v_pos'][0].T, (2, 1)))
    cw = inp['conv_w'][0]
    put('cw', np.concatenate([_colmaj(cw[j]) for j in range(4)], axis=1))
    put('cb', _colmaj(inp['conv_b'][0]))
    put('ba', _colmaj(inp['lru_ba'][0].reshape(-1)))
    put('bx', _colmaj(inp['lru_bx'][0].reshape(-1)))
    put('lam', _colmaj(inp['lru_lambda'][0]))
    half = 8
    inv = (np.float32(500000.0) ** (-np.arange(half, dtype=np.float32) / np.float32(half))).astype(np.float32)
    pos = np.arange(S_, dtype=np.float32)
    ang = (pos[:, None] * inv[None, :]).astype(np.float32)
    cos = np.cos(ang).astype(np.float32).reshape(NT, 128, 8).transpose(1, 0, 2).reshape(128, NT * 8)
    sin = np.sin(ang).astype(np.float32).reshape(NT, 128, 8).transpose(1, 0, 2).reshape(128, NT * 8)
    put('cos', cos)
    put('sin', sin)
    cend = (np.arange(256) * 16 + 31).astype(np.float32)
    angc = (cend[:, None] * inv[None, :]).astype(np.float32)
    put('cosc', np.cos(angc).astype(np.float32).reshape(2, 128, 8).transpose(1, 0, 2).reshape(128, 16))
    put('sinc', np.sin(angc).astype(np.float32).reshape(2, 128, 8).transpose(1, 0, 2).reshape(128, 16))
    A = np.zeros((128, 127), np.float32)
    B = np.ones((128, 127), np.float32)
    for q in range(128):
        cur = 1 if q >= 64 else 0
        for r in range(127):
            rel = r - 63
            if rel > cur:
                A[q, r] = -1e30
                B[q, r] = 0.0
            elif rel == cur or rel == cur - 1:
                A[q, r] = 1e4
                B[q, r] = 0.0
    put('tkA', A)
    put('tkB', B)
    ov = np.zeros((256, 65), np.float32)
    for n in range(NCMP):
        for j in range(min(64, S_ // 64)):
            if 16 * n <= 64 * j + 63 and 16 * n + 31 >= 64 * j:
                ov[n, j] = 1.0
        ov[n, 64] = 1.0
    put('ovl', ov.reshape(2, 128, 65).transpose(1, 0, 2).reshape(128, 130))
    E = np.zeros((64, S_), np.float32)
    for j in range(S_ // 64):
        E[j, j * 64:(j + 1) * 64] = 1.0
    return cst, E


def build_masks():
    r = np.arange(128)[:, None]
    q = np.arange(128)[None, :]
    mc = np.where(r > q, NEGM, 0.0).astype(np.float32)
    ml = np.where(r <= q, NEGM, 0.0).astype(np.float32)
    mk = np.concatenate([np.tile(mc, (1, 4)), np.tile(ml, (1, 4))], axis=1)
    jw = np.zeros((8, 792), np.float32)
    for rr in range(8):
        jw[rr, rr + 264] = 1.0
        w = np.where(np.arange(128) < 16 * rr + 15, NEGM, 0.0).astype(np.float32)
        jw[rr, 280:792] = np.tile(w, 4)
    return np.ascontiguousarray(mk), jw


def host_weights(inp):
    w_in = inp['w_in'][0]
    cols = lambda a, b: w_in[:, a:b]
    wtm = np.concatenate([cols(0, 512), cols(768, 896), cols(1024, 1152), cols(896, 1024), cols(1152, 1280), cols(1280, 1304)], axis=1)
    wcv = cols(512, 768)
    wxy = cols(1304, 3352)
    wgm = cols(3352, 5400)

    def phi1(w):
        a = w.reshape(32, 64, 256).transpose(1, 0, 2).reshape(64, 32 * 256)
        return np.ascontiguousarray(np.concatenate([a, a], axis=0))
    m = {
        'wtm': _kmaj(wtm, 8), 'wcv': _kmaj(wcv, 8), 'wxy': _kmaj(wxy, 8), 'wgm': _kmaj(wgm, 8),
        'pk1': phi1(inp['phi_k_w1'][0]), 'pv1': phi1(inp['phi_v_w1'][0]),
        'pk2': _kmaj(inp['phi_k_w2'][0], 2), 'pv2': _kmaj(inp['phi_v_w2'][0], 2),
        'wa': np.ascontiguousarray(inp['lru_wa'][0].transpose(1, 0, 2).reshape(128, 1024)),
        'wx': np.ascontiguousarray(inp['lru_wx'][0].transpose(1, 0, 2).reshape(128, 1024)),
        'wnu': _kmaj(inp['w_nsa_up'][0], 4), 'wlu': _kmaj(inp['w_lru_up'][0], 8), 'wo': _kmaj(inp['w_o'][0], 8),
        'wf1': _kmaj(inp['w_ff1'][0], 8), 'wf2': _kmaj(inp['w_ff2'][0], 32),
    }
    return m


def kernel(**inputs):
    inp = {k: np.asarray(v, dtype=np.float32) for k, v in inputs.items()}
    x = inp['x']
    B, S_, _ = x.shape
    NT = S_ // 128
    cst, E = build_consts(NT, inp)
    wm = host_weights(inp)
    mkm, jwm = build_masks()
    nc = build_program(NT)
    in_maps = []
    for b in range(B):
        m = dict(wm)
        m['x'] = np.ascontiguousarray(x[b])
        m['cst'] = cst
        m['emat'] = E
        m['mk'], m['jw'] = mkm, jwm
        in_maps.append(m)
    res = run_bass_kernel_spmd(nc, in_maps, core_ids=list(range(B)))
    return np.stack([np.asarray(r['out'], dtype=np.float32) for r in res.results], axis=0)
```

```python
import math
from contextlib import ExitStack
import numpy as np
import concourse.bass as bass
import concourse.mybir as mybir
from concourse.bass_utils import run_bass_kernel_spmd

F32 = mybir.dt.float32
BF16 = mybir.dt.bfloat16
AF = mybir.ActivationFunctionType
ALU = mybir.AluOpType
AX = mybir.AxisListType

D = 1024
DFF = 4096
EPS = 1e-6
NEGM = -30000.0
GC1 = 0.044715
GC2 = 2.0 * math.sqrt(2.0 / math.pi)


class Sched:
    CE = ('pe', 'dve', 'act', 'pool')

    def __init__(self, nc, stack, ndma=8, limit=16000, strict_same=True):
        self.nc = nc
        self.stack = stack
        self.limit = limit
        self.strict_same = strict_same
        self.prog = {e: [] for e in ('pe', 'dve', 'act', 'pool', 'sp')}
        self.nsem = 0
        self.cur_sem = {}
        self.cnt = {}
        for e in self.CE:
            self.cur_sem[e] = self._newsem(e)
            self.cnt[e] = 0
        self.dma_sems = {q: [self._newsem('d' + q) for _ in range(ndma)] for q in ('sp', 'pool')}
        self.dma_n = {q: 0 for q in ('sp', 'pool')}
        self.waited = {e: {} for e in self.prog}
        self.lastw = {}
        self.readers = {}
        self.last_tok = {}

    def _newsem(self, tag):
        self.nsem += 1
        s = self.stack.enter_context(self.nc.semaphore(f"s{self.nsem}_{tag}"))
        return (self.nsem, s)

    PSUM = frozenset(['tp0', 'tp1', 'cp0', 'cp1', 'hp0', 'hp1', 'kp', 'tb', 'pj0', 'pj1', 'st0', 'st1', 'st2', 'st3',
                      'pv', 'pv0', 'pv1', 'mz', 'pa', 'pbk', 'yp0', 'yp1', 'fa', 'fb'])

    def _deps(self, reads, writes, eng=None):
        deps = []
        for b in reads:
            if b in self.lastw:
                deps.append(self.lastw[b])
            if b in self.PSUM:
                deps.extend(t for t in self.readers.get(b, ()) if t[2] != eng)
        for b in writes:
            if b in self.lastw:
                deps.append(self.lastw[b])
            deps.extend(self.readers.get(b, ()))
        return deps

    def _waits(self, eng, deps):
        waits = []
        w = self.waited[eng]
        for (sem, val, src) in deps:
            if src == eng and (eng == 'pe' or not self.strict_same):
                continue
            if w.get(sem[0], 0) >= val:
                continue
            w[sem[0]] = val
            waits.append((sem[1], val))
        return waits

    def _commit(self, tok, reads, writes):
        for b in reads:
            if b not in writes:
                self.readers.setdefault(b, []).append(tok)
        for b in writes:
            self.lastw[b] = tok
            self.readers[b] = []

    def op(self, eng, fn, reads=(), writes=()):
        deps = self._deps(reads, writes, eng)
        waits = self._waits(eng, deps)
        if self.cnt[eng] >= self.limit:
            self.cur_sem[eng] = self._newsem(eng)
            self.cnt[eng] = 0
        self.cnt[eng] += 1
        sem = self.cur_sem[eng]
        tok = (sem, self.cnt[eng], eng)
        self.last_tok[eng] = tok
        self.prog[eng].append((waits, fn, (sem[1], 1)))
        self._commit(tok, reads, writes)
        return tok

    def dma(self, q, fn, reads=(), writes=()):
        deps = self._deps(reads, writes)
        j = self.dma_n[q]
        self.dma_n[q] += 1
        K = len(self.dma_sems[q])
        sem = self.dma_sems[q][j % K]
        if j >= K:
            deps.append((sem, 16 * (j // K), 'dma'))
        waits = self._waits(q, deps)
        tok = (sem, 16 * (j // K + 1), 'dma')
        self.prog[q].append((waits, fn, (sem[1], 16)))
        self._commit(tok, reads, writes)
        return tok

    def _dma_final(self):
        deps = []
        for q in self.dma_sems:
            K = len(self.dma_sems[q])
            n = self.dma_n[q]
            for i, sem in enumerate(self.dma_sems[q]):
                cnt = (n - i + K - 1) // K if n > i else 0
                if cnt > 0:
                    deps.append((sem, 16 * cnt, 'dma'))
        return deps

    def barrier(self):
        deps = self._dma_final() + list(self.last_tok.values())
        for e in self.prog:
            saved = self.strict_same
            self.strict_same = True
            waits = []
            w = self.waited[e]
            for (sem, val, src) in deps:
                if w.get(sem[0], 0) >= val:
                    continue
                w[sem[0]] = val
                waits.append((sem[1], val))
            self.strict_same = saved
            self.prog[e].append((waits, None, None))
        self.lastw = {}
        self.readers = {}

    def finish(self):
        self.barrier()

    def emit(self, block):
        def run(engobj, items):
            for waits, fn, inc in items:
                for (s, v) in waits:
                    engobj.wait_ge(s, v)
                if fn is not None:
                    ins = fn(engobj)
                    ins.then_inc(inc[0], inc[1])

        P = self.prog

        @block.sync
        def _(e):
            run(e, P['sp'])

        @block.tensor
        def _(e):
            run(e, P['pe'])

        @block.vector
        def _(e):
            run(e, P['dve'])

        @block.scalar
        def _(e):
            run(e, P['act'])

        @block.gpsimd
        def _(e):
            run(e, P['pool'])


def cst_layout(NT):
    off = {}
    o = 0

    def add(name, n):
        nonlocal o
        off[name] = (o, o + n)
        o += n
    add('ident', 128)
    add('nw1', 8)
    add('nw2', 8)
    add('eps', 1)
    add('wq', 64)
    add('wk0', 64)
    add('wk1', 64)
    add('wk2', 64)
    add('posk', 32)
    add('posv', 32)
    add('cw', 32)
    add('cb', 8)
    add('ba', 8)
    add('bx', 8)
    add('lam', 8)
    add('cos', NT * 8)
    add('sin', NT * 8)
    add('cosc', 16)
    add('sinc', 16)
    add('tkA', 127)
    add('tkB', 127)
    add('ovl', 130)
    return off, o


DBG = {'stop': 99}


def build_program(NT=32, passes=(0, 1, 2, 3), dbg=False):
    S_ = NT * 128
    NCMP = (S_ - 32) // 16 + 1
    NSLC = S_ // 64
    CO, NCST = cst_layout(NT)
    nc = bass.Bass("TRN2", target_bir_lowering=False)

    def din(name, shape, dt=F32):
        return nc.dram_tensor(name, shape, dt, kind="ExternalInput").ap()

    x_d = din("x", [S_, D])
    cst_d = din("cst", [128, NCST])
    wtm_d = din("wtm", [128, 8 * 1048])
    wcv_d = din("wcv", [128, 8 * 256])
    wxy_d = din("wxy", [128, 8 * 2048])
    wgm_d = din("wgm", [128, 8 * 2048])
    pk1_d = din("pk1", [128, 32 * 256])
    pv1_d = din("pv1", [128, 32 * 256])
    pk2_d = din("pk2", [128, 2 * 64])
    pv2_d = din("pv2", [128, 2 * 64])
    wa_d = din("wa", [128, 8 * 128])
    wx_d = din("wx", [128, 8 * 128])
    wnu_d = din("wnu", [128, 4 * 1024])
    wlu_d = din("wlu", [128, 8 * 1024])
    wo_d = din("wo", [128, 8 * 1024])
    wf1_d = din("wf1", [128, 8 * DFF])
    wf2_d = din("wf2", [128, 32 * D])
    E_d = din("emat", [64, S_])
    mk_d = din("mk", [128, 1024])
    jw_d = din("jw", [8, 792])
    out_d = nc.dram_tensor("out", [S_, D], F32, kind="ExternalOutput").ap()
    son_d = nc.dram_tensor("sc_on", [NT * 128, 512], BF16, kind="Internal").ap()
    sol_d = nc.dram_tensor("sc_ol", [NT * 128, 1024], BF16, kind="Internal").ap()

    with ExitStack() as top:
        S = Sched(nc, top)

        uniq = [0]

        def sbt(st, name, shape, dt):
            uniq[0] += 1
            return st.enter_context(nc.sbuf_tensor(f"sb{uniq[0]}_{name}", shape, dt))

        def pst(st, name, shape, dt):
            uniq[0] += 1
            return st.enter_context(nc.psum_tensor(f"ps{uniq[0]}_{name}", shape, dt))

        cst = sbt(top, "cst", [128, NCST], F32)
        identb = sbt(top, "identb", [128, 128], BF16)
        kcT = sbt(top, "kcT", [64, 2, 256], BF16)
        vca = sbt(top, "vca", [128, 2, 2, 129], BF16)

        def C(name):
            a, b = CO[name]
            return cst[:, a:b]
        ident = C('ident')
        epsc = C('eps')

        S.dma('sp', lambda e: e.dma_start(out=cst[:], in_=cst_d[:, :]), writes=['cst'])
        S.op('dve', lambda e: e.tensor_copy(out=identb[:], in_=ident), reads=['cst'], writes=['identb'])

        def load_w(dst3, src_d, nk, ncol, name):
            step = max(1, 2048 // ncol)
            if ncol > 2048:
                for k in range(nk):
                    for c0 in range(0, ncol, 2048):
                        c1 = min(ncol, c0 + 2048)
                        S.dma('pool', lambda e, k=k, c0=c0, c1=c1: e.dma_start(
                            out=dst3[:, k, c0:c1], in_=src_d[:, k * ncol + c0:k * ncol + c1]), writes=[name])
            else:
                for k0 in range(0, nk, step):
                    k1 = min(nk, k0 + step)
                    S.dma('pool', lambda e, k0=k0, k1=k1: e.dma_start(
                        out=dst3[:, k0:k1, :],
                        in_=src_d[:, k0 * ncol:k1 * ncol].rearrange("p (a b) -> p a b", a=k1 - k0)), writes=[name])

        def norm_tile(xb, xbn, nwname, xs, ss, tp, xnT, tpn=('tp0', 'tp1')):
            if isinstance(tp, (list, tuple)):
                banks = tp
            else:
                banks = (tp[:, 0:512], tp[:, 512:1024])
            S.op('act', lambda e: e.activation(out=xs[:], in_=xb[:], func=AF.Square, accum_out=ss[:, 0:1]),
                 reads=[xbn], writes=['xs', 'ss0'])
            S.op('act', lambda e: e.activation(out=ss[:, 1:2], in_=ss[:, 0:1], func=AF.Sqrt, scale=1.0 / D, bias=epsc),
                 reads=['ss0', 'cst'], writes=['ss1'])
            S.op('dve', lambda e: e.reciprocal(out=ss[:, 2:3], in_=ss[:, 1:2]), reads=['ss1'], writes=['ss2'])
            S.op('act', lambda e: e.activation(out=xs[:], in_=xb[:], func=AF.Copy, scale=ss[:, 2:3]),
                 reads=[xbn, 'ss2'], writes=['xs'])
            for k in range(8):
                S.op('pe', lambda e, k=k: e.transpose(out=banks[k // 4][:, (k % 4) * 128:(k % 4 + 1) * 128],
                                                      in_=xs[:, k * 128:(k + 1) * 128],
                                                      identity=ident), reads=['xs', 'cst'], writes=[tpn[k // 4]])
            nw = C(nwname)
            for a in range(2):
                S.op('dve', lambda e, a=a: e.tensor_tensor(
                    out=xnT[:, 4 * a:4 * a + 4, :], in0=banks[a].rearrange("p (a b) -> p a b", a=4),
                    in1=nw[:, 4 * a:4 * a + 4].unsqueeze(2).broadcast_to([128, 4, 128]), op=ALU.mult),
                     reads=[tpn[a], 'cst'], writes=['xnT'])

        def gelu(P_, shape_free, src, srcn, dst, dstn, t1, t1n, t2, t2n, src_psum=False):
            S.op('act', lambda e: e.activation(out=t1, in_=src, func=AF.Square), reads=[srcn], writes=[t1n])
            S.op('dve', lambda e: e.tensor_scalar(out=t1, in0=t1, scalar1=GC1, scalar2=1.0, op0=ALU.mult, op1=ALU.add),
                 reads=[t1n], writes=[t1n])
            S.op('dve', lambda e: e.tensor_tensor(out=t1, in0=t1, in1=src, op=ALU.mult), reads=[t1n, srcn], writes=[t1n])
            S.op('act', lambda e: e.activation(out=t2, in_=t1, func=AF.Sigmoid, scale=GC2), reads=[t1n], writes=[t2n])
            S.op('dve', lambda e: e.tensor_tensor(out=dst, in0=t2, in1=src, op=ALU.mult), reads=[t2n, srcn], writes=[dstn])

        def rmsrope(P_, H, src3, srcn, wrep, cosp, sinp, out3, outn, tmp, pre):
            sq = tmp['sq'][0:P_, 0:H, :]
            y = tmp['y'][0:P_, 0:H, :]
            st = tmp['st']
            r1 = tmp['r1'][0:P_, 0:H, :]
            r2 = tmp['r2'][0:P_, 0:H, :]
            S.op('pool', lambda e: e.tensor_tensor(out=sq, in0=src3, in1=src3, op=ALU.mult), reads=[srcn], writes=[pre + 'sq'])
            S.op('dve', lambda e: e.tensor_reduce(out=st[0:P_, 0:H], in_=sq, axis=AX.X, op=ALU.add),
                 reads=[pre + 'sq'], writes=[pre + 'st0'])
            S.op('act', lambda e: e.activation(out=st[0:P_, 8:8 + H], in_=st[0:P_, 0:H], func=AF.Sqrt, scale=1.0 / 64,
                                               bias=epsc[0:P_, :]), reads=[pre + 'st0', 'cst'], writes=[pre + 'st1'])
            S.op('dve', lambda e: e.reciprocal(out=st[0:P_, 16:16 + H], in_=st[0:P_, 8:8 + H]),
                 reads=[pre + 'st1'], writes=[pre + 'st2'])
            S.op('dve', lambda e: e.tensor_tensor(out=y, in0=src3,
                                                  in1=st[0:P_, 16:16 + H].unsqueeze(2).broadcast_to([P_, H, 64]), op=ALU.mult),
                 reads=[srcn, pre + 'st2'], writes=[pre + 'y'])
            S.op('dve', lambda e: e.tensor_tensor(out=y, in0=y, in1=wrep.unsqueeze(1).broadcast_to([P_, H, 64]), op=ALU.mult),
                 reads=[pre + 'y', 'cst', 'wq8'], writes=[pre + 'y'])
            cb_ = cosp.unsqueeze(1).broadcast_to([P_, H, 8])
            sb_ = sinp.unsqueeze(1).broadcast_to([P_, H, 8])
            y1 = y[:, :, 0:8]
            y2 = y[:, :, 8:16]
            S.op('dve', lambda e: e.tensor_tensor(out=r1, in0=y1, in1=cb_, op=ALU.mult), reads=[pre + 'y', 'cst'], writes=[pre + 'r1'])
            S.op('pool', lambda e: e.tensor_tensor(out=r2, in0=y2, in1=sb_, op=ALU.mult), reads=[pre + 'y', 'cst'], writes=[pre + 'r2'])
            S.op('dve', lambda e: e.tensor_tensor(out=out3[:, :, 0:8], in0=r1, in1=r2, op=ALU.subtract),
                 reads=[pre + 'r1', pre + 'r2'], writes=[outn])
            S.op('dve', lambda e: e.tensor_tensor(out=r1, in0=y2, in1=cb_, op=ALU.mult), reads=[pre + 'y', 'cst'], writes=[pre + 'r1'])
            S.op('pool', lambda e: e.tensor_tensor(out=r2, in0=y1, in1=sb_, op=ALU.mult), reads=[pre + 'y', 'cst'], writes=[pre + 'r2'])
            S.op('dve', lambda e: e.tensor_tensor(out=out3[:, :, 8:16], in0=r1, in1=r2, op=ALU.add),
                 reads=[pre + 'r1', pre + 'r2'], writes=[outn])
            S.op('act', lambda e: e.activation(out=out3[:, :, 16:64], in_=y[:, :, 16:64], func=AF.Copy),
                 reads=[pre + 'y'], writes=[outn])

        def pass0():
            with ExitStack() as st:
                wcv = sbt(st, "wcv", [128, 8, 256], BF16)
                pk1 = sbt(st, "pk1", [128, 32, 256], BF16)
                pv1 = sbt(st, "pv1", [128, 32, 256], BF16)
                pk2 = sbt(st, "pk2", [128, 2, 64], BF16)
                pv2 = sbt(st, "pv2", [128, 2, 64], BF16)
                rawk = sbt(st, "rawk", [128, S_ + 16], F32)
                rawv = sbt(st, "rawv", [128, S_ + 16], F32)
                zk = sbt(st, "zk", [128, 32, 256], BF16)
                zv = sbt(st, "zv", [128, 32, 256], BF16)
                xb = [sbt(st, f"xb{i}", [128, D], F32) for i in range(2)]
                xs = sbt(st, "xs", [128, D], F32)
                ss = sbt(st, "ss", [128, 4], F32)
                xnT = sbt(st, "xnT", [128, 8, 128], BF16)
                hT = sbt(st, "hT", [128, 2, 256], BF16)
                g1 = sbt(st, "g1", [128, 256], F32)
                g2 = sbt(st, "g2", [128, 256], F32)
                kc32 = sbt(st, "kc32", [128, 64], F32)
                kcn = sbt(st, "kcn", [128, 64], BF16)
                tmp = dict(sq=sbt(st, "t_sq", [128, 8, 64], F32), y=sbt(st, "t_y", [128, 8, 64], F32),
                           st=sbt(st, "t_st", [128, 32], F32), r1=sbt(st, "t_r1", [128, 8, 8], F32),
                           r2=sbt(st, "t_r2", [128, 8, 8], F32))
                tp = pst(st, "tp", [128, 1024], F32)
                cp = [pst(st, f"cp{i}", [128, 512], F32) for i in range(2)]
                hp = [pst(st, f"hp{i}", [128, 512], F32) for i in range(2)]
                kp = pst(st, "kp", [128, 512], F32)
                tb = pst(st, "tb", [128, 1024], BF16)

                load_w(wcv, wcv_d, 8, 256, 'wcv')
                load_w(pk1, pk1_d, 32, 256, 'pk1')
                load_w(pv1, pv1_d, 32, 256, 'pv1')
                load_w(pk2, pk2_d, 2, 64, 'pk2')
                load_w(pv2, pv2_d, 2, 64, 'pv2')
                for t in range(NT):
                    b = xb[t % 2]
                    bn = f"xb{t % 2}"
                    S.dma('sp', lambda e, b=b, t=t: e.dma_start(out=b[:], in_=x_d[t * 128:(t + 1) * 128, :]), writes=[bn])
                    norm_tile(b, bn, 'nw1', xs, ss, tp, xnT)
                    pb = cp[t % 2]
                    pn = f"cp{t % 2}"
                    for c in range(2):
                        for k in range(8):
                            S.op('pe', lambda e, c=c, k=k, pb=pb: e.matmul(
                                out=pb[:, c * 128:(c + 1) * 128], lhsT=wcv[:, k, c * 128:(c + 1) * 128], rhs=xnT[:, k, :],
                                start=(k == 0), stop=(k == 7)), reads=['wcv', 'xnT'], writes=[pn])
                    S.op('act', lambda e, pb=pb, t=t: e.activation(out=rawk[:, t * 128:(t + 1) * 128], in_=pb[:, 0:128], func=AF.Copy),
                         reads=[pn], writes=['rawk'])
                    S.op('act', lambda e, pb=pb, t=t: e.activation(out=rawv[:, t * 128:(t + 1) * 128], in_=pb[:, 128:256], func=AF.Copy),
                         reads=[pn], writes=['rawv'])
                if DBG['stop'] <= 1:
                    S.barrier()
                    return
                posk = C('posk')
                posv = C('posv')
                for l in range(32):
                    S.op('dve', lambda e, l=l: e.tensor_scalar(
                        out=zk[:, l, 0:NCMP], in0=rawk[:, l:l + 16 * (NCMP - 1) + 1:16], scalar1=posk[:, l:l + 1], scalar2=None,
                        op0=ALU.add), reads=['rawk', 'cst'], writes=['zk'])
                    S.op('pool', lambda e, l=l: e.tensor_scalar(
                        out=zv[:, l, 0:NCMP], in0=rawv[:, l:l + 16 * (NCMP - 1) + 1:16], scalar1=posv[:, l:l + 1], scalar2=None,
                        op0=ALU.add), reads=['rawv', 'cst'], writes=['zv'])
                if DBG['stop'] <= 2:
                    S.barrier()
                    return
                nch = [(0, min(128, NCMP))]
                if NCMP > 128:
                    nch.append((128, NCMP - 128))
                ovl = C('ovl')
                for kv in range(2):
                    z = (zk, zv)[kv]
                    zn = ('zk', 'zv')[kv]
                    w1 = (pk1, pv1)[kv]
                    w1n = ('pk1', 'pv1')[kv]
                    w2 = (pk2, pv2)[kv]
                    w2n = ('pk2', 'pv2')[kv]
                    for g in range(2):
                        gs_ = slice(g * 64, (g + 1) * 64)
                        for hc in range(2):
                            hb_ = hp[hc]
                            hn_ = f"hp{hc}"
                            for l in range(32):
                                S.op('pe', lambda e, l=l, hc=hc, hb_=hb_, z=z, w1=w1, gs_=gs_: e.matmul(
                                    out=hb_[:, 0:NCMP], lhsT=w1[gs_, l, hc * 128:(hc + 1) * 128], rhs=z[gs_, l, 0:NCMP],
                                    start=(l == 0), stop=(l == 31)), reads=[w1n, zn], writes=[hn_])
                            gelu(128, NCMP, hb_[:, 0:NCMP], hn_, hT[:, hc, 0:NCMP], f'hT{hc}',
                                 g1[:, 0:NCMP], 'g1', g2[:, 0:NCMP], 'g2')
                        if DBG['stop'] <= 3:
                            continue
                        for ci, (n0, sz) in enumerate(nch):
                            for hc in range(2):
                                S.op('pe', lambda e, hc=hc, n0=n0, sz=sz, w2=w2: e.matmul(
                                    out=kp[0:sz, 0:64], lhsT=hT[:, hc, n0:n0 + sz], rhs=w2[:, hc, :],
                                    start=(hc == 0), stop=(hc == 1)), reads=[f'hT{hc}', w2n], writes=['kp'])
                            if DBG['stop'] <= 4:
                                continue
                            if kv == 0:
                                S.op('act', lambda e, sz=sz: e.activation(out=kc32[0:sz, :], in_=kp[0:sz, 0:64], func=AF.Copy),
                                     reads=['kp'], writes=['kc32'])
                                cc = C('cosc')[:, ci * 8:(ci + 1) * 8]
                                sc = C('sinc')[:, ci * 8:(ci + 1) * 8]
                                rmsrope(sz, 1, kc32[0:sz, :].rearrange("p (h d) -> p h d", h=1), 'kc32', C('wk0')[0:sz, :],
                                        cc[0:sz, :], sc[0:sz, :], kcn[0:sz, :].rearrange("p (h d) -> p h d", h=1), 'kcn', tmp, 'p0')
                                S.op('pe', lambda e, sz=sz: e.transpose(out=tb[0:64, 0:sz], in_=kcn[0:sz, :], identity=identb[0:sz, 0:sz]),
                                     reads=['kcn', 'identb'], writes=['tb'])
                                S.op('dve', lambda e, sz=sz, n0=n0, g=g: e.tensor_copy(out=kcT[:, g, n0:n0 + sz], in_=tb[0:64, 0:sz]),
                                     reads=['tb'], writes=['kcT'])
                            else:
                                S.op('act', lambda e, sz=sz, g=g, ci=ci: e.activation(out=vca[0:sz, g, ci, 0:64], in_=kp[0:sz, 0:64], func=AF.Copy),
                                     reads=['kp'], writes=['vca'])
                                S.op('dve', lambda e, sz=sz, g=g, ci=ci: e.tensor_copy(out=vca[0:sz, g, ci, 64:129], in_=ovl[0:sz, ci * 65:(ci + 1) * 65]),
                                     reads=['cst'], writes=['vca'])
                S.barrier()

        def pass1():
            with ExitStack() as st:
                wtm = sbt(st, "wtm", [128, 8, 1048], BF16)
                wxy = sbt(st, "wxy", [128, 8, 2048], BF16)
                wa = sbt(st, "wa", [128, 8, 128], BF16)
                wx = sbt(st, "wx", [128, 8, 128], BF16)
                ksT = [sbt(st, f"ksT{g}", [128, S_], BF16) for g in range(2)]
                kwT = sbt(st, "kwT", [64, 2, S_], BF16)
                vsa = sbt(st, "vsa", [128, NT, 2, 65], BF16)
                vwa = sbt(st, "vwa", [128, NT, 2, 65], BF16)
                mk = sbt(st, "mk", [128, 1024], BF16)
                jw = sbt(st, "jw", [8, 792], BF16)
                ones2 = sbt(st, "ones2", [128, 2], BF16)
                xb = [sbt(st, f"xb{i}", [128, D], F32) for i in range(2)]
                xs = sbt(st, "xs", [128, D], F32)
                ss = sbt(st, "ss", [128, 4], F32)
                xnT = sbt(st, "xnT", [128, 8, 128], BF16)
                qraw = sbt(st, "qraw", [128, 512], F32)
                kvraw = sbt(st, "kvraw", [128, 512], F32)
                gsgs = [sbt(st, f"gsg{i}", [128, 24], F32) for i in range(2)]
                qaug = sbt(st, "qaug", [128, 8, 128], BF16)
                kn = sbt(st, "kn", [128, 4, 64], BF16)
                qT = sbt(st, "qT", [64, 8, 128], BF16)
                qaTs = [[sbt(st, f"qaT{p}{g}", [128, 512], BF16) for g in range(2)] for p in range(2)]
                NB = 3
                pT = [sbt(st, f"pT{i}", [128, 512], BF16) for i in range(NB)]
                pTc = [sbt(st, f"pTc{i}", [128, 512], BF16) for i in range(2)]
                oaccs = [sbt(st, f"oacc{i}", [128, 8, 64], F32) for i in range(2)]
                otmpF = sbt(st, "otmpF", [128, 4, 64], F32)
                otmpA = sbt(st, "otmpA", [128, 4, 64], F32)
                obf = sbt(st, "obf", [128, 512], BF16)
                onT = sbt(st, "onT", [128, 4, 128], BF16)
                imp = sbt(st, "imp", [128, 64], F32)
                sc1 = sbt(st, "sc1", [128, 64], F32)
                sc2 = sbt(st, "sc2", [128, 64], F32)
                m8 = sbt(st, "m8", [128, 16], F32)
                sm = sbt(st, "sm", [128, 16], F32)
                wq8 = sbt(st, "wq8", [128, 64], F32)
                tmp = dict(sq=sbt(st, "t_sq", [128, 8, 64], F32), y=sbt(st, "t_y", [128, 8, 64], F32),
                           st=sbt(st, "t_st", [128, 32], F32), r1=sbt(st, "t_r1", [128, 8, 8], F32),
                           r2=sbt(st, "t_r2", [128, 8, 8], F32))
                xrxs = [sbt(st, f"xrx{i}", [128, 8, 132], F32) for i in range(2)]
                yrbs = [sbt(st, f"yrb{i}", [128, 8, 128], F32) for i in range(2)]
                xc = sbt(st, "xc", [128, 8, 128], F32)
                xcb = sbt(st, "xcb", [128, 8, 128], BF16)
                rg = sbt(st, "rg", [128, 8, 128], F32)
                ig = sbt(st, "ig", [128, 8, 128], F32)
                av = sbt(st, "av", [128, 8, 128], F32)
                bt = sbt(st, "bt", [128, 8, 128], F32)
                hs = sbt(st, "hs", [128, 8, 128], F32)
                hst = sbt(st, "hst", [128, 8], F32)
                cl = sbt(st, "cl", [128, 8], F32)
                olT = sbt(st, "olT", [128, 8, 128], BF16)
                gt1 = sbt(st, "gt1", [128, 1024], F32)
                stp = [pst(st, f"st{i}", [128, 512], F32) for i in range(NB)]
                pvb = [pst(st, f"pv{i}", [128, 512], F32) for i in range(2)]
                mz = pst(st, "mz", [128, 1024], BF16)
                fab = [pst(st, "fa", [128, 512], F32), pst(st, "fb", [128, 512], F32)]
                pj = fab
                pjn = ['fa', 'fb']

                load_w(wtm, wtm_d, 8, 1048, 'wtm')
                load_w(wxy, wxy_d, 8, 2048, 'wxy')
                load_w(wa, wa_d, 8, 128, 'wa')
                load_w(wx, wx_d, 8, 128, 'wx')
                S.dma('pool', lambda e: e.dma_start(out=mk[:], in_=mk_d[:, :]), writes=['mk'])
                S.dma('pool', lambda e: e.dma_start(out=jw[:], in_=jw_d[:, :]), writes=['jw'])
                for g in range(2):
                    for c0 in range(0, S_, 2048):
                        c1 = min(S_, c0 + 2048)
                        S.dma('pool', lambda e, g=g, c0=c0, c1=c1: e.dma_start(out=ksT[g][64:128, c0:c1], in_=E_d[:, c0:c1]),
                              writes=[f'ksE{g}'])
                S.op('dve', lambda e: e.tensor_scalar(out=wq8[:], in0=C('wq'), scalar1=0.125, scalar2=None, op0=ALU.mult),
                     reads=['cst'], writes=['wq8'])
                S.op('pool', lambda e: e.memset(vsa[:, :, :, 64:65], 1.0), writes=['vsa1'])
                S.op('pool', lambda e: e.memset(vwa[:, :, :, 64:65], 1.0), writes=['vwa1'])
                S.op('pool', lambda e: e.memset(ones2[:], 1.0), writes=['ones2'])
                S.op('pool', lambda e: e.memset(xrxs[0][:, :, 0:3], 0.0), writes=['xrxh0'])
                S.op('pool', lambda e: e.memset(hst[:], 0.0), writes=['hst'])
                S.op('pool', lambda e: e.memset(qaug[:], 0.0), writes=['qaugq', 'qaugm0', 'qaugm1'])
                S.op('act', lambda e: e.activation(out=cl[:], in_=C('lam'), func=AF.Exp, scale=-1.0), reads=['cst'], writes=['cl'])
                S.op('dve', lambda e: e.tensor_scalar(out=cl[:], in0=cl[:], scalar1=1.0, scalar2=None, op0=ALU.add),
                     reads=['cl'], writes=['cl'])
                S.op('act', lambda e: e.activation(out=cl[:], in_=cl[:], func=AF.Ln), reads=['cl'], writes=['cl'])
                S.op('dve', lambda e: e.tensor_scalar(out=cl[:], in0=cl[:], scalar1=-8.0, scalar2=None, op0=ALU.mult),
                     reads=['cl'], writes=['cl'])
                tkA = C('tkA')
                tkB = C('tkB')
                cw = C('cw')
                cbv = C('cb')
                bav = C('ba')
                bxv = C('bx')
                Mc4 = mk[:, 0:512]
                Ml4 = mk[:, 512:1024]
                W8 = jw[:, 280:792]
                ctr = [0]

                def front(i):
                    p = i % 2
                    gsg = gsgs[p]
                    qaT = qaTs[p]
                    oacc = oaccs[p]
                    xrx = xrxs[p]
                    yrb = yrbs[p]
                    xrxn = f'xrx{p}'
                    xrxhn = f'xrxh{p}'
                    yrbn = f'yrb{p}'
                    gsgn = f'gsg{p}'
                    oaccn = f'oacc{p}'
                    T0 = i * 128
                    b = xb[i % 2]
                    bn = f"xb{i % 2}"
                    S.dma('sp', lambda e, b=b, i=i: e.dma_start(out=b[:], in_=x_d[i * 128:(i + 1) * 128, :]), writes=[bn])
                    norm_tile(b, bn, 'nw1', xs, ss, fab, xnT, tpn=('fa', 'fb'))
                    yield
                    for (c0, c1, pb, pn) in ((0, 512, pj[0], pjn[0]), (512, 1024, pj[1], pjn[1])):
                        for k in range(8):
                            S.op('pe', lambda e, k=k, c0=c0, c1=c1, pb=pb: e.matmul(
                                out=pb[:, 0:512], lhsT=xnT[:, k, :], rhs=wtm[:, k, c0:c1], start=(k == 0), stop=(k == 7)),
                                reads=['xnT', 'wtm'], writes=[pn])
                    S.op('act', lambda e: e.activation(out=qraw[:], in_=pj[0][:, 0:512], func=AF.Copy), reads=[pjn[0]], writes=['qraw'])
                    S.op('act', lambda e: e.activation(out=kvraw[:], in_=pj[1][:, 0:512], func=AF.Copy), reads=[pjn[1]], writes=['kvraw'])
                    for k in range(8):
                        S.op('pe', lambda e, k=k: e.matmul(out=fab[0][:, 0:24], lhsT=xnT[:, k, :], rhs=wtm[:, k, 1024:1048],
                                                           start=(k == 0), stop=(k == 7)), reads=['xnT', 'wtm'], writes=['fa'])
                    S.op('act', lambda e: e.activation(out=gsg[:], in_=fab[0][:, 0:24], func=AF.Sigmoid), reads=['fa'], writes=[gsgn])
                    for c4 in range(4):
                        pb = fab[(c4 + 1) % 2]
                        pn = pjn[(c4 + 1) % 2]
                        yield
                        for cc in range(4):
                            c = c4 * 4 + cc
                            for k in range(8):
                                S.op('pe', lambda e, c=c, cc=cc, k=k, pb=pb: e.matmul(
                                    out=pb[:, cc * 128:(cc + 1) * 128], lhsT=wxy[:, k, c * 128:(c + 1) * 128], rhs=xnT[:, k, :],
                                    start=(k == 0), stop=(k == 7)), reads=['wxy', 'xnT'], writes=[pn])
                        src = pb[:, 0:512].rearrange("p (a b) -> p a b", a=4)
                        if c4 < 2:
                            S.op('act', lambda e, c4=c4, src=src: e.activation(out=xrx[:, c4 * 4:(c4 + 1) * 4, 3:131], in_=src, func=AF.Copy),
                                 reads=[pn], writes=[xrxn])
                        else:
                            S.op('act', lambda e, c4=c4, src=src: e.activation(out=yrb[:, (c4 - 2) * 4:(c4 - 1) * 4, :], in_=src, func=AF.Copy),
                                 reads=[pn], writes=[yrbn])
                    yield
                    cosp = C('cos')[:, i * 8:(i + 1) * 8]
                    sinp = C('sin')[:, i * 8:(i + 1) * 8]
                    rmsrope(128, 8, qraw[:].rearrange("p (h d) -> p h d", h=8), 'qraw', wq8[:], cosp, sinp,
                            qaug[:, :, 0:64], 'qaugq', tmp, 'p1')
                    yield
                    rmsrope(128, 2, kvraw[:, 0:128].rearrange("p (h d) -> p h d", h=2), 'kvraw', C('wk1'), cosp, sinp,
                            kn[:, 0:2, :], 'kn', tmp, 'p1')
                    yield
                    rmsrope(128, 2, kvraw[:, 128:256].rearrange("p (h d) -> p h d", h=2), 'kvraw', C('wk2'), cosp, sinp,
                            kn[:, 2:4, :], 'kn', tmp, 'p1')
                    yield
                    for j in range(4):
                        S.op('pe', lambda e, j=j: e.transpose(out=mz[0:64, j * 128:(j + 1) * 128], in_=kn[:, j, :], identity=identb[:]),
                             reads=['kn', 'identb'], writes=['mz'])
                    for g in range(2):
                        S.op('dve', lambda e, g=g, T0=T0: e.tensor_copy(out=ksT[g][0:64, T0:T0 + 128], in_=mz[0:64, g * 128:(g + 1) * 128]),
                             reads=['mz'], writes=[f'ksT{g}_{i}'])
                    S.op('dve', lambda e, T0=T0: e.tensor_copy(out=kwT[:, :, T0:T0 + 128],
                                                               in_=mz[0:64, 256:512].rearrange("p (a b) -> p a b", a=2)),
                         reads=['mz'], writes=[f'kwT_{i}'])
                    S.op('act', lambda e, i=i: e.activation(out=vsa[:, i, :, 0:64], in_=kvraw[:, 256:384].rearrange("p (g d) -> p g d", g=2),
                                                            func=AF.Copy), reads=['kvraw'], writes=[f'vsa_{i}'])
                    S.op('act', lambda e, i=i: e.activation(out=vwa[:, i, :, 0:64], in_=kvraw[:, 384:512].rearrange("p (g d) -> p g d", g=2),
                                                            func=AF.Copy), reads=['kvraw'], writes=[f'vwa_{i}'])
                    yield
                    for h in range(8):
                        S.op('pe', lambda e, h=h: e.transpose(out=mz[0:64, h * 128:(h + 1) * 128], in_=qaug[:, h, 0:64], identity=identb[:]),
                             reads=['qaugq', 'identb'], writes=['mz'])
                    S.op('dve', lambda e: e.tensor_copy(out=qT[:].rearrange("p a b -> p (a b)"), in_=mz[0:64, :]), reads=['mz'], writes=['qT'])

                    yield
                    n_hi = min(NCMP, 8 * i + 7)
                    chunks = [(0, 0, min(128, n_hi))]
                    if n_hi > 128:
                        chunks.append((1, 128, n_hi - 128))
                    nchk = len(chunks)
                    for g in range(2):
                        bufs = []
                        for (ci, n0, Kc) in chunks:
                            bufs.append(ci)
                            sp_ = fab[ci]
                            sn_ = pjn[ci]
                            pt_ = pTc[ci]
                            ptn = f"pTc{ci}"
                            s0 = 265 + n0 - 8 * i
                            S.op('pe', lambda e, g=g, n0=n0, Kc=Kc, sp_=sp_: e.matmul(
                                out=sp_[0:Kc, :], lhsT=kcT[:, g, n0:n0 + Kc], rhs=qT[:, 4 * g:4 * g + 4, :].rearrange("p a b -> p (a b)"),
                                start=True, stop=False), reads=['kcT', 'qT'], writes=[sn_])
                            S.op('pe', lambda e, s0=s0, Kc=Kc, sp_=sp_: e.matmul(
                                out=sp_[0:Kc, :], lhsT=jw[:, s0:s0 + Kc], rhs=W8, start=False, stop=True), reads=['jw'], writes=[sn_])
                            S.op('act', lambda e, Kc=Kc, sp_=sp_, pt_=pt_: e.activation(out=pt_[0:Kc, :], in_=sp_[0:Kc, :], func=AF.Exp),
                                 reads=[sn_], writes=[ptn])
                        for h in range(4):
                            for idx, (ci, n0, Kc) in enumerate(chunks):
                                pt_ = pTc[bufs[idx]]
                                ptn = f"pTc{bufs[idx]}"
                                S.op('pe', lambda e, h=h, ci=ci, Kc=Kc, g=g, pt_=pt_, idx=idx, lastc=(idx == nchk - 1): e.matmul(
                                    out=fab[0][:, h * 128:(h + 1) * 128], lhsT=pt_[0:Kc, h * 128:(h + 1) * 128], rhs=vca[0:Kc, g, ci, 0:128],
                                    start=(idx == 0), stop=lastc), reads=[ptn, 'vca'], writes=['fa'])
                        for h in range(4):
                            for idx, (ci, n0, Kc) in enumerate(chunks):
                                pt_ = pTc[bufs[idx]]
                                ptn = f"pTc{bufs[idx]}"
                                S.op('pe', lambda e, h=h, Kc=Kc, pt_=pt_, idx=idx, lastc=(idx == nchk - 1): e.matmul(
                                    out=fab[1][:, 2 * h:2 * h + 2], lhsT=pt_[0:Kc, h * 128:(h + 1) * 128], rhs=ones2[0:Kc, :],
                                    start=(idx == 0), stop=lastc), reads=[ptn, 'ones2'], writes=['fb'])
                        S.op('dve', lambda e: e.tensor_scalar(out=sm[:, 0:4], in0=fab[1][:, 0:8:2], scalar1=1e-30, scalar2=None, op0=ALU.max),
                             reads=['fb'], writes=['sm0'])
                        S.op('dve', lambda e: e.reciprocal(out=sm[:, 4:8], in_=sm[:, 0:4]), reads=['sm0'], writes=['sm1'])
                        S.op('dve', lambda e, g=g: e.tensor_tensor(out=sm[:, 8:12], in0=sm[:, 4:8], in1=gsg[:, 4 * g:4 * g + 4], op=ALU.mult),
                             reads=['sm1', gsgn], writes=['sm2'])
                        pv3 = fab[0][:, 0:512].rearrange("p (h c) -> p h c", h=4)
                        S.op('dve', lambda e, g=g, pv3=pv3: e.tensor_tensor(
                            out=oacc[:, 4 * g:4 * g + 4, :], in0=pv3[:, :, 0:64], in1=sm[:, 8:12].unsqueeze(2).broadcast_to([128, 4, 64]),
                            op=ALU.mult), reads=['fa', 'sm2'], writes=[oaccn])
                        S.op('dve', lambda e, pv3=pv3: e.tensor_tensor(
                            out=otmpF[:], in0=pv3[:, :, 64:128], in1=sm[:, 4:8].unsqueeze(2).broadcast_to([128, 4, 64]),
                            op=ALU.mult), reads=['fa', 'sm1'], writes=['otmpF'])
                        S.op('dve', lambda e: e.tensor_reduce(out=imp[:], in_=otmpF[:].rearrange("p h j -> p j h"), axis=AX.X, op=ALU.add),
                             reads=['otmpF'], writes=['imp'])
                        yield
                        a0 = 63 - 2 * i
                        Asl = tkA[:, a0:a0 + NSLC]
                        Bsl = tkB[:, a0:a0 + NSLC]
                        S.op('dve', lambda e, Bsl=Bsl: e.tensor_tensor(out=sc1[:, 0:NSLC], in0=imp[:, 0:NSLC], in1=Bsl, op=ALU.mult),
                             reads=['imp', 'cst'], writes=['sc1'])
                        S.op('dve', lambda e, Asl=Asl: e.tensor_tensor(out=sc1[:, 0:NSLC], in0=sc1[:, 0:NSLC], in1=Asl, op=ALU.add),
                             reads=['sc1', 'cst'], writes=['sc1'])
                        S.op('dve', lambda e: e.memset(sc1[:, 0:1], 1e4), writes=['sc1'])
                        S.op('dve', lambda e: e.max(out=m8[:, 0:8], in_=sc1[:, 0:NSLC]), reads=['sc1'], writes=['m8a'])
                        S.op('dve', lambda e: e.match_replace(out=sc2[:, 0:NSLC], in_to_replace=m8[:, 0:8], in_values=sc1[:, 0:NSLC],
                                                              imm_value=-3.0e38), reads=['sc1', 'm8a'], writes=['sc2'])
                        S.op('dve', lambda e: e.max(out=m8[:, 8:16], in_=sc2[:, 0:NSLC]), reads=['sc2'], writes=['m8b'])
                        S.op('dve', lambda e, g=g: e.tensor_scalar(
                            out=qaug[:, 4 * g:4 * g + 4, 64:64 + NSLC], in0=sc1[:, 0:NSLC].unsqueeze(1).broadcast_to([128, 4, NSLC]),
                            scalar1=m8[:, 15:16], scalar2=NEGM, op0=ALU.is_lt, op1=ALU.mult), reads=['sc1', 'm8b'], writes=[f'qaugm{g}'])

                    yield
                    for g in range(2):
                        for h in range(4):
                            S.op('pe', lambda e, g=g, h=h: e.transpose(out=mz[:, h * 128:(h + 1) * 128], in_=qaug[:, 4 * g + h, :], identity=identb[:]),
                                 reads=['qaugq', f'qaugm{g}', 'identb'], writes=['mz'])
                        S.op('dve', lambda e, g=g: e.tensor_copy(out=qaT[g][:], in_=mz[:, 0:512]), reads=['mz'], writes=[f'qaT{p}{g}'])

                def attn(i):
                    p = i % 2
                    gsg = gsgs[p]
                    qaT = qaTs[p]
                    oacc = oaccs[p]
                    xrx = xrxs[p]
                    yrb = yrbs[p]
                    xrxn = f'xrx{p}'
                    xrxhn = f'xrxh{p}'
                    yrbn = f'yrb{p}'
                    gsgn = f'gsg{p}'
                    oaccn = f'oacc{p}'
                    items = []
                    gi = 0
                    for br in (1, 2):
                        for g in range(2):
                            kts = list(range(0, i + 1)) if br == 1 else list(range(max(0, i - 4), i + 1))
                            for idx, kt in enumerate(kts):
                                items.append(dict(br=br, g=g, kt=kt, first=(idx == 0), last=(idx == len(kts) - 1), grp=gi))
                            gi += 1

                    def stage_S(it):
                        bi = ctr[0] % NB
                        ctr[0] += 1
                        it['bi'] = bi
                        sp_ = stp[bi]
                        sn_ = f"st{bi}"
                        pt_ = pT[bi]
                        ptn = f"pT{bi}"
                        g, kt, br = it['g'], it['kt'], it['br']
                        mask = None
                        if kt == i:
                            mask = Mc4
                        elif br == 2 and kt == i - 4:
                            mask = Ml4
                        if br == 1:
                            S.op('pe', lambda e: e.matmul(out=sp_[:, :], lhsT=ksT[g][:, kt * 128:(kt + 1) * 128], rhs=qaT[g][:, :],
                                                          start=True, stop=(mask is None)),
                                 reads=[f'ksT{g}_{kt}', f'ksE{g}', f'qaT{p}{g}'], writes=[sn_])
                        else:
                            S.op('pe', lambda e: e.matmul(out=sp_[:, :], lhsT=kwT[:, g, kt * 128:(kt + 1) * 128], rhs=qaT[g][0:64, :],
                                                          start=True, stop=(mask is None)),
                                 reads=[f'kwT_{kt}', f'qaT{p}{g}'], writes=[sn_])
                        if mask is not None:
                            S.op('pe', lambda e: e.matmul(out=sp_[:, :], lhsT=identb[:], rhs=mask, start=False, stop=True),
                                 reads=['identb', 'mk'], writes=[sn_])
                        S.op('act', lambda e: e.activation(out=pt_[:], in_=sp_[:, :], func=AF.Exp), reads=[sn_], writes=[ptn])

                    def stage_P(it):
                        bi = it['bi']
                        pt_ = pT[bi]
                        ptn = f"pT{bi}"
                        g, kt, br = it['g'], it['kt'], it['br']
                        pvt = pvb[it['grp'] % 2]
                        pvn = f"pv{it['grp'] % 2}"
                        va = vsa if br == 1 else vwa
                        van = (f'vsa_{kt}', 'vsa1') if br == 1 else (f'vwa_{kt}', 'vwa1')
                        for h in range(4):
                            S.op('pe', lambda e, h=h: e.matmul(
                                out=pvt[:, h * 65:(h + 1) * 65], lhsT=pt_[:, h * 128:(h + 1) * 128], rhs=va[:, kt, g, :],
                                start=(it['first'] and h == 0), stop=it['last'], skip_group_check=True),
                                reads=[ptn, van[0], van[1]], writes=[pvn])
                        if it['last']:
                            pv3 = pvt[:, 0:260].rearrange("p (h c) -> p h c", h=4)
                            S.op('dve', lambda e: e.reciprocal(out=sm[:, 12:16], in_=pvt[:, 64:260:65]), reads=[pvn], writes=['sm4'])
                            S.op('dve', lambda e: e.tensor_tensor(out=sm[:, 12:16], in0=sm[:, 12:16],
                                                                  in1=gsg[:, br * 8 + 4 * g:br * 8 + 4 * g + 4], op=ALU.mult),
                                 reads=['sm4', gsgn], writes=['sm4'])
                            S.op('dve', lambda e: e.tensor_tensor(out=otmpA[:], in0=pv3[:, :, 0:64],
                                                                  in1=sm[:, 12:16].unsqueeze(2).broadcast_to([128, 4, 64]), op=ALU.mult),
                                 reads=[pvn, 'sm4'], writes=['otmpA'])
                            S.op('dve', lambda e: e.tensor_tensor(out=oacc[:, 4 * g:4 * g + 4, :], in0=oacc[:, 4 * g:4 * g + 4, :], in1=otmpA[:],
                                                                  op=ALU.add), reads=['otmpA', oaccn], writes=[oaccn])
                    LOOK = 2
                    for n_ in range(len(items) + LOOK):
                        if n_ < len(items):
                            stage_S(items[n_])
                        if n_ - LOOK >= 0:
                            stage_P(items[n_ - LOOK])
                        yield

                    S.op('act', lambda e: e.activation(out=obf[:], in_=oacc[:].rearrange("p a b -> p (a b)"), func=AF.Copy),
                         reads=[oaccn], writes=['obf'])
                    for c in range(4):
                        S.op('pe', lambda e, c=c: e.transpose(out=mz[:, c * 128:(c + 1) * 128], in_=obf[:, c * 128:(c + 1) * 128], identity=identb[:]),
                             reads=['obf', 'identb'], writes=['mz'])
                    S.op('dve', lambda e: e.tensor_copy(out=onT[:].rearrange("p a b -> p (a b)"), in_=mz[:, 0:512]), reads=['mz'], writes=['onT'])
                    S.dma('sp', lambda e, i=i: e.dma_start(out=son_d[i * 128:(i + 1) * 128, :], in_=onT[:].rearrange("p a b -> p (a b)")),
                          reads=['onT'], writes=[f'son{i}'])

                def lru(i):
                    p = i % 2
                    gsg = gsgs[p]
                    qaT = qaTs[p]
                    oacc = oaccs[p]
                    xrx = xrxs[p]
                    yrb = yrbs[p]
                    xrxn = f'xrx{p}'
                    xrxhn = f'xrxh{p}'
                    yrbn = f'yrb{p}'
                    gsgn = f'gsg{p}'
                    oaccn = f'oacc{p}'
                    for c in range(8):
                        S.op('dve', lambda e, c=c: e.tensor_scalar(out=xc[:, c, :], in0=xrx[:, c, 3:131], scalar1=cw[:, 24 + c:25 + c],
                                                                   scalar2=cbv[:, c:c + 1], op0=ALU.mult, op1=ALU.add),
                             reads=[xrxn, xrxhn, 'cst'], writes=[f'xc{c}'])
                        yield
                        for j in (2, 1, 0):
                            S.op('dve', lambda e, c=c, j=j: e.scalar_tensor_tensor(
                                out=xc[:, c, :], in0=xrx[:, c, j:j + 128], scalar=cw[:, j * 8 + c:j * 8 + c + 1], in1=xc[:, c, :],
                                op0=ALU.mult, op1=ALU.add), reads=[xrxn, xrxhn, 'cst', f'xc{c}'], writes=[f'xc{c}'])
                    xcall = [f'xc{c}' for c in range(8)]
                    S.op('act', lambda e: e.activation(out=xcb[:], in_=xc[:], func=AF.Copy), reads=xcall, writes=['xcb'])
                    S.op('dve', lambda e: e.tensor_copy(out=xrxs[1 - p][:, :, 0:3], in_=xrx[:, :, 128:131]), reads=[xrxn], writes=[f'xrxh{1 - p}'])
                    for (wmat, wn, bias, dst, dn) in ((wa, 'wa', bav, rg, 'rg'), (wx, 'wx', bxv, ig, 'ig')):
                        for c4 in range(2):
                            pb = fab[c4]
                            pn = pjn[c4]
                            yield
                            for cc in range(4):
                                c = c4 * 4 + cc
                                S.op('pe', lambda e, c=c, cc=cc, pb=pb, wmat=wmat: e.matmul(
                                    out=pb[:, cc * 128:(cc + 1) * 128], lhsT=wmat[:, c, :], rhs=xcb[:, c, :], start=True, stop=True),
                                    reads=[wn, 'xcb'], writes=[pn])
                            S.op('dve', lambda e, c4=c4, pb=pb, bias=bias, dst=dst: e.tensor_tensor(
                                out=dst[:, 4 * c4:4 * c4 + 4, :], in0=pb[:, 0:512].rearrange("p (a b) -> p a b", a=4),
                                in1=bias[:, 4 * c4:4 * c4 + 4].unsqueeze(2).broadcast_to([128, 4, 128]), op=ALU.add),
                                reads=[pn, 'cst'], writes=[dn])
                        S.op('act', lambda e, dst=dst: e.activation(out=dst[:], in_=dst[:], func=AF.Sigmoid), reads=[dn], writes=[dn])
                    yield
                    S.op('dve', lambda e: e.tensor_tensor(out=av[:], in0=rg[:], in1=cl[:].unsqueeze(2).broadcast_to([128, 8, 128]), op=ALU.mult),
                         reads=['rg', 'cl'], writes=['av'])
                    S.op('act', lambda e: e.activation(out=av[:], in_=av[:], func=AF.Exp), reads=['av'], writes=['av'])
                    yield
                    S.op('act', lambda e: e.activation(out=bt[:], in_=av[:], func=AF.Square), reads=['av'], writes=['bt'])
                    S.op('act', lambda e: e.activation(out=bt[:], in_=bt[:], func=AF.Sqrt, scale=-1.0, bias=1.0), reads=['bt'], writes=['bt'])
                    S.op('dve', lambda e: e.tensor_tensor(out=ig[:], in0=ig[:], in1=xc[:], op=ALU.mult), reads=['ig'] + xcall, writes=['ig'])
                    S.op('dve', lambda e: e.tensor_tensor(out=bt[:], in0=bt[:], in1=ig[:], op=ALU.mult), reads=['bt', 'ig'], writes=['bt'])
                    yield
                    for c in range(8):
                        S.op('dve', lambda e, c=c: e.tensor_tensor_scan(out=hs[:, c, :], data0=av[:, c, :], data1=bt[:, c, :],
                                                                        initial=hst[:, c:c + 1], op0=ALU.mult, op1=ALU.add),
                             reads=['av', 'bt', 'hst'], writes=['hs'])
                    S.op('dve', lambda e: e.tensor_copy(out=hst[:], in_=hs[:, :, 127]), reads=['hs'], writes=['hst'])
                    yield
                    yr2 = yrb[:].rearrange("p a b -> p (a b)")
                    gelu(128, 1024, yr2, yrbn, rg[:].rearrange("p a b -> p (a b)"), 'rg',
                         gt1[:], 'gt1', ig[:].rearrange("p a b -> p (a b)"), 'ig')
                    S.op('dve', lambda e: e.tensor_tensor(out=olT[:], in0=hs[:], in1=rg[:], op=ALU.mult), reads=['hs', 'rg'], writes=['olT'])
                    S.dma('sp', lambda e, i=i: e.dma_start(out=sol_d[i * 128:(i + 1) * 128, :], in_=olT[:].rearrange("p a b -> p (a b)")),
                          reads=['olT'], writes=[f'sol{i}'])
                def drive(gens_w):
                    if DBG.get('seq'):
                        for (g_, w_) in gens_w:
                            for _ in g_:
                                pass
                        return
                    gens = [[g_, w_, 0.0] for (g_, w_) in gens_w]
                    wmax = max(w_ for (_, w_) in gens_w)
                    while gens:
                        for ent in list(gens):
                            ent[2] += ent[1] / wmax
                            while ent[2] >= 1.0:
                                ent[2] -= 1.0
                                try:
                                    next(ent[0])
                                except StopIteration:
                                    gens.remove(ent)
                                    break
                for _ in front(0):
                    pass
                for i in range(NT):
                    n_items = 2 * (i + 1) + 2 * (min(i, 4) + 1) + 3
                    gl = [(attn(i), float(n_items)), (lru(i), 24.0)]
                    if i + 1 < NT:
                        gl.append((front(i + 1), 22.0))
                    drive(gl)
                S.barrier()

        def pass2():
            with ExitStack() as st:
                wgm = sbt(st, "wgm", [128, 8, 2048], BF16)
                wnu = sbt(st, "wnu", [128, 4, 1024], BF16)
                wlu = sbt(st, "wlu", [128, 8, 1024], BF16)
                wo = sbt(st, "wo", [128, 8, 1024], BF16)
                xb = [sbt(st, f"xb{i}", [128, D], F32) for i in range(2)]
                xs = sbt(st, "xs", [128, D], F32)
                ss = sbt(st, "ss", [128, 4], F32)
                xnT = sbt(st, "xnT", [128, 8, 128], BF16)
                gm = sbt(st, "gm", [128, 16, 128], F32)
                onT = [sbt(st, f"onT{i}", [128, 4, 128], BF16) for i in range(2)]
                olT = [sbt(st, f"olT{i}", [128, 8, 128], BF16) for i in range(2)]
                t1 = sbt(st, "t1", [128, 4, 128], F32)
                t2 = sbt(st, "t2", [128, 4, 128], F32)
                mT = sbt(st, "mT", [128, 8, 128], BF16)
                ob = sbt(st, "ob", [128, D], F32)
                tp = pst(st, "tp", [128, 1024], F32)
                pj = [pst(st, f"pj{i}", [128, 512], F32) for i in range(2)]
                pa = pst(st, "pa", [128, 512], F32)
                pbk = pst(st, "pbk", [128, 512], F32)
                yp = pst(st, "yp", [128, 1024], F32)
                load_w(wgm, wgm_d, 8, 2048, 'wgm')
                load_w(wnu, wnu_d, 4, 1024, 'wnu')
                load_w(wlu, wlu_d, 8, 1024, 'wlu')
                load_w(wo, wo_d, 8, 1024, 'wo')
                for i in range(NT):
                    b = xb[i % 2]
                    bn = f"xb{i % 2}"
                    on_ = onT[i % 2]
                    onn = f"onT{i % 2}"
                    ol_ = olT[i % 2]
                    oln = f"olT{i % 2}"
                    S.dma('sp', lambda e, b=b, i=i: e.dma_start(out=b[:], in_=x_d[i * 128:(i + 1) * 128, :]), writes=[bn])
                    S.dma('sp', lambda e, on_=on_, i=i: e.dma_start(out=on_[:].rearrange("p a b -> p (a b)"), in_=son_d[i * 128:(i + 1) * 128, :]),
                          reads=[f'son{i}'], writes=[onn])
                    S.dma('sp', lambda e, ol_=ol_, i=i: e.dma_start(out=ol_[:].rearrange("p a b -> p (a b)"), in_=sol_d[i * 128:(i + 1) * 128, :]),
                          reads=[f'sol{i}'], writes=[oln])
                    norm_tile(b, bn, 'nw1', xs, ss, tp, xnT)
                    for c4 in range(4):
                        pb = pj[c4 % 2]
                        pn = f"pj{c4 % 2}"
                        for cc in range(4):
                            c = c4 * 4 + cc
                            for k in range(8):
                                S.op('pe', lambda e, c=c, cc=cc, k=k, pb=pb: e.matmul(
                                    out=pb[:, cc * 128:(cc + 1) * 128], lhsT=wgm[:, k, c * 128:(c + 1) * 128], rhs=xnT[:, k, :],
                                    start=(k == 0), stop=(k == 7)), reads=['wgm', 'xnT'], writes=[pn])
                        S.op('act', lambda e, c4=c4, pb=pb: e.activation(out=gm[:, c4 * 4:(c4 + 1) * 4, :],
                                                                         in_=pb[:, 0:512].rearrange("p (a b) -> p a b", a=4), func=AF.Sigmoid),
                             reads=[pn], writes=[f'gm{c4}'])
                    for c4 in range(2):
                        for cc in range(4):
                            c = c4 * 4 + cc
                            for k in range(4):
                                S.op('pe', lambda e, c=c, cc=cc, k=k, on_=on_: e.matmul(
                                    out=pa[:, cc * 128:(cc + 1) * 128], lhsT=wnu[:, k, c * 128:(c + 1) * 128], rhs=on_[:, k, :],
                                    start=(k == 0), stop=(k == 3)), reads=['wnu', onn], writes=['pa'])
                        for cc in range(4):
                            c = c4 * 4 + cc
                            for k in range(8):
                                S.op('pe', lambda e, c=c, cc=cc, k=k, ol_=ol_: e.matmul(
                                    out=pbk[:, cc * 128:(cc + 1) * 128], lhsT=wlu[:, k, c * 128:(c + 1) * 128], rhs=ol_[:, k, :],
                                    start=(k == 0), stop=(k == 7)), reads=['wlu', oln], writes=['pbk'])
                        S.op('dve', lambda e, c4=c4: e.tensor_tensor(out=t1[:], in0=pa[:, 0:512].rearrange("p (a b) -> p a b", a=4),
                                                                     in1=gm[:, c4 * 4:(c4 + 1) * 4, :], op=ALU.mult),
                             reads=['pa', f'gm{c4}'], writes=['t1'])
                        S.op('dve', lambda e, c4=c4: e.tensor_tensor(out=t2[:], in0=pbk[:, 0:512].rearrange("p (a b) -> p a b", a=4),
                                                                     in1=gm[:, 8 + c4 * 4:8 + (c4 + 1) * 4, :], op=ALU.mult),
                             reads=['pbk', f'gm{c4 + 2}'], writes=['t2'])
                        S.op('pool', lambda e, c4=c4: e.tensor_tensor(out=mT[:, c4 * 4:(c4 + 1) * 4, :], in0=t1[:], in1=t2[:], op=ALU.add),
                             reads=['t1', 't2'], writes=[f'mT{c4}'])
                    for n in range(2):
                        for k in range(8):
                            S.op('pe', lambda e, n=n, k=k: e.matmul(out=yp[:, n * 512:(n + 1) * 512], lhsT=mT[:, k, :],
                                                                    rhs=wo[:, k, n * 512:(n + 1) * 512], start=(k == 0), stop=(k == 7)),
                                 reads=[f'mT{k // 4}', 'wo'], writes=[f'yp{n}'])
                    S.op('dve', lambda e, b=b: e.tensor_tensor(out=ob[:], in0=yp[:], in1=b[:], op=ALU.add),
                         reads=['yp0', 'yp1', bn], writes=['ob'])
                    S.dma('sp', lambda e, i=i: e.dma_start(out=out_d[i * 128:(i + 1) * 128, :], in_=ob[:]), reads=['ob'], writes=[f'outd{i}'])
                S.barrier()

        def pass3():
            with ExitStack() as st:
                w1 = sbt(st, "w1s", [128, 8, DFF], BF16)
                w2 = sbt(st, "w2s", [128, 32, D], BF16)
                hb = [sbt(st, f"hb{i}", [128, D], F32) for i in range(2)]
                xs = sbt(st, "xs", [128, D], F32)
                ss = sbt(st, "ss", [128, 4], F32)
                hnT = sbt(st, "hnT", [128, 8, 128], BF16)
                rl = [sbt(st, f"rl{i}", [128, 512], F32) for i in range(2)]
                hid = sbt(st, "hid", [128, 32, 128], BF16)
                ob = sbt(st, "ob", [128, D], F32)
                tp = pst(st, "tp", [128, 1024], F32)
                hp = [pst(st, f"hp{i}", [128, 512], F32) for i in range(2)]
                yp = pst(st, "yp", [128, 1024], F32)
                load_w(w1, wf1_d, 8, DFF, 'w1')
                load_w(w2, wf2_d, 32, D, 'w2')
                src_d = out_d if 2 in passes else x_d
                for t in range(NT):
                    b = hb[t % 2]
                    bn = f"hb{t % 2}"
                    S.dma('sp', lambda e, b=b, t=t: e.dma_start(out=b[:], in_=src_d[t * 128:(t + 1) * 128, :]),
                          reads=[f'outd{t}'], writes=[bn])
                    norm_tile(b, bn, 'nw2', xs, ss, tp, hnT)
                    for c4 in range(8):
                        pb = hp[c4 % 2]
                        pn = f"hp{c4 % 2}"
                        for cc in range(4):
                            c = c4 * 4 + cc
                            for k in range(8):
                                S.op('pe', lambda e, c=c, cc=cc, k=k, pb=pb: e.matmul(
                                    out=pb[:, cc * 128:(cc + 1) * 128], lhsT=w1[:, k, c * 128:(c + 1) * 128], rhs=hnT[:, k, :],
                                    start=(k == 0), stop=(k == 7)), reads=['w1', 'xnT'], writes=[pn])
                        r = rl[c4 % 2]
                        rn = f"rl{c4 % 2}"
                        S.op('act', lambda e, r=r, pb=pb: e.activation(out=r[:], in_=pb[:], func=AF.Relu), reads=[pn], writes=[rn])
                        S.op('pool', lambda e, r=r, c4=c4: e.tensor_tensor(
                            out=hid[:, c4 * 4:(c4 + 1) * 4, :], in0=r[:].rearrange("p (a b) -> p a b", a=4),
                            in1=r[:].rearrange("p (a b) -> p a b", a=4), op=ALU.mult), reads=[rn], writes=[f'hid{c4}'])
                    for n in range(2):
                        for k in range(32):
                            S.op('pe', lambda e, n=n, k=k: e.matmul(out=yp[:, n * 512:(n + 1) * 512], lhsT=hid[:, k, :],
                                                                    rhs=w2[:, k, n * 512:(n + 1) * 512], start=(k == 0), stop=(k == 31)),
                                 reads=[f'hid{k // 4}', 'w2'], writes=[f'yp{n}'])
                    S.op('dve', lambda e, b=b: e.tensor_tensor(out=ob[:], in0=yp[:], in1=b[:], op=ALU.add),
                         reads=['yp0', 'yp1', bn], writes=['ob'])
                    S.dma('sp', lambda e, t=t: e.dma_start(out=out_d[t * 128:(t + 1) * 128, :], in_=ob[:]), reads=['ob'], writes=[f'outd{t}'])
        for n_, f_ in enumerate((pass0, pass1, pass2, pass3)):
            if n_ in passes:
                f_()
        S.finish()
        with nc.Block() as block:
            S.emit(block)
    return nc


def _kmaj(w, nk):
    n = w.shape[1]
    return np.ascontiguousarray(w.reshape(nk, 128, n).transpose(1, 0, 2).reshape(128, nk * n))


def _colmaj(v):
    return np.ascontiguousarray(v.reshape(-1, 128).T)


def build_consts(NT, inp):
    S_ = NT * 128
    NCMP = (S_ - 32) // 16 + 1
    CO, NCST = cst_layout(NT)
    cst = np.zeros((128, NCST), np.float32)

    def put(name, arr):
        a, b = CO[name]
        cst[:, a:b] = arr
    put('ident', np.eye(128, dtype=np.float32))
    put('nw1', _colmaj(inp['norm1_w'][0]))
    put('nw2', _colmaj(inp['norm2_w'][0]))
    put('eps', np.full((128, 1), EPS, np.float32))
    put('wq', np.tile(inp['q_norm_w'][0][None, :], (128, 1)))
    for j in range(3):
        put(f'wk{j}', np.tile(inp['k_norm_w'][0, j][None, :], (128, 1)))
    put('posk', np.tile(inp['phi_k_pos'][0].T, (2, 1)))
    put('posv', np.tile(inp['phi_v_pos'][0].T, (2, 1)))
    cw = inp['conv_w'][0]
    put('cw', np.concatenate([_colmaj(cw[j]) for j in range(4)], axis=1))
    put('cb', _colmaj(inp['conv_b'][0]))
    put('ba', _colmaj(inp['lru_ba'][0].reshape(-1)))
    put('bx', _colmaj(inp['lru_bx'][0].reshape(-1)))
    put('lam', _colmaj(inp['lru_lambda'][0]))
    half = 8
    inv = (np.float32(500000.0) ** (-np.arange(half, dtype=np.float32) / np.float32(half))).astype(np.float32)
    pos = np.arange(S_, dtype=np.float32)
    ang = (pos[:, None] * inv[None, :]).astype(np.float32)
    cos = np.cos(ang).astype(np.float32).reshape(NT, 128, 8).transpose(1, 0, 2).reshape(128, NT * 8)
    sin = np.sin(ang).astype(np.float32).reshape(NT, 128, 8).transpose(1, 0, 2).reshape(128, NT * 8)
    put('cos', cos)
    put('sin', sin)
    cend = (np.arange(256) * 16 + 31).astype(np.float32)
    angc = (cend[:, None] * inv[None, :]).astype(np.float32)
    put('cosc', np.cos(angc).astype(np.float32).reshape(2, 128, 8).transpose(1, 0, 2).reshape(128, 16))
    put('sinc', np.sin(angc).astype(np.float32).reshape(2, 128, 8).transpose(1, 0, 2).reshape(128, 16))
    A = np.zeros((128, 127), np.float32)
    B = np.ones((128, 127), np.float32)
    for q in range(128):
        cur = 1 if q >= 64 else 0
        for r in range(127):
            rel = r - 63
            if rel > cur:
                A[q, r] = -1e30
                B[q, r] = 0.0
            elif rel == cur or rel == cur - 1:
                A[q, r] = 1e4
                B[q, r] = 0.0
    put('tkA', A)
    put('tkB', B)
    ov = np.zeros((256, 65), np.float32)
    for n in range(NCMP):
        for j in range(min(64, S_ // 64)):
            if 16 * n <= 64 * j + 63 and 16 * n + 31 >= 64 * j:
                ov[n, j] = 1.0
        ov[n, 64] = 1.0
    put('ovl', ov.reshape(2, 128, 65).transpose(1, 0, 2).reshape(128, 130))
    E = np.zeros((64, S_), np.float32)
    for j in range(S_ // 64):
        E[j, j * 64:(j + 1) * 64] = 1.0
    return cst, E


def build_masks():
    r = np.arange(128)[:, None]
    q = np.arange(128)[None, :]
    mc = np.where(r > q, NEGM, 0.0).astype(np.float32)
    ml = np.where(r <= q, NEGM, 0.0).astype(np.float32)
    mk = np.concatenate([np.tile(mc, (1, 4)), np.tile(ml, (1, 4))], axis=1)
    jw = np.zeros((8, 792), np.float32)
    for rr in range(8):
        jw[rr, rr + 264] = 1.0
        w = np.where(np.arange(128) < 16 * rr + 15, NEGM, 0.0).astype(np.float32)
        jw[rr, 280:792] = np.tile(w, 4)
    return np.ascontiguousarray(mk), jw


def host_weights(inp):
    w_in = inp['w_in'][0]
    cols = lambda a, b: w_in[:, a:b]
    wtm = np.concatenate([cols(0, 512), cols(768, 896), cols(1024, 1152), cols(896, 1024), cols(1152, 1280), cols(1280, 1304)], axis=1)
    wcv = cols(512, 768)
    wxy = cols(1304, 3352)
    wgm = cols(3352, 5400)

    def phi1(w):
        a = w.reshape(32, 64, 256).transpose(1, 0, 2).reshape(64, 32 * 256)
        return np.ascontiguousarray(np.concatenate([a, a], axis=0))
    m = {
        'wtm': _kmaj(wtm, 8), 'wcv': _kmaj(wcv, 8), 'wxy': _kmaj(wxy, 8), 'wgm': _kmaj(wgm, 8),
        'pk1': phi1(inp['phi_k_w1'][0]), 'pv1': phi1(inp['phi_v_w1'][0]),
        'pk2': _kmaj(inp['phi_k_w2'][0], 2), 'pv2': _kmaj(inp['phi_v_w2'][0], 2),
        'wa': np.ascontiguousarray(inp['lru_wa'][0].transpose(1, 0, 2).reshape(128, 1024)),
        'wx': np.ascontiguousarray(inp['lru_wx'][0].transpose(1, 0, 2).reshape(128, 1024)),
        'wnu': _kmaj(inp['w_nsa_up'][0], 4), 'wlu': _kmaj(inp['w_lru_up'][0], 8), 'wo': _kmaj(inp['w_o'][0], 8),
        'wf1': _kmaj(inp['w_ff1'][0], 8), 'wf2': _kmaj(inp['w_ff2'][0], 32),
    }
    return m


def kernel(**inputs):
    inp = {k: np.asarray(v, dtype=np.float32) for k, v in inputs.items()}
    x = inp['x']
    B, S_, _ = x.shape
    NT = S_ // 128
    cst, E = build_consts(NT, inp)
    wm = host_weights(inp)
    mkm, jwm = build_masks()
    nc = build_program(NT)
    in_maps = []
    for b in range(B):
        m = dict(wm)
        m['x'] = np.ascontiguousarray(x[b])
        m['cst'] = cst
        m['emat'] = E
        m['mk'], m['jw'] = mkm, jwm
        in_maps.append(m)
    res = run_bass_kernel_spmd(nc, in_maps, core_ids=list(range(B)))
    return np.stack([np.asarray(r['out'], dtype=np.float32) for r in res.results], axis=0)
```

```python
import math
from contextlib import ExitStack
import numpy as np
import concourse.bass as bass
import concourse.mybir as mybir
from concourse.bass_utils import run_bass_kernel_spmd

F32 = mybir.dt.float32
BF16 = mybir.dt.bfloat16
AF = mybir.ActivationFunctionType
ALU = mybir.AluOpType
AX = mybir.AxisListType

D = 1024
DFF = 4096
EPS = 1e-6
NEGM = -30000.0
GC1 = 0.044715
GC2 = 2.0 * math.sqrt(2.0 / math.pi)


class Sched:
    CE = ('pe', 'dve', 'act', 'pool')

    def __init__(self, nc, stack, ndma=8, limit=16000, strict_same=True):
        self.nc = nc
        self.stack = stack
        self.limit = limit
        self.strict_same = strict_same
        self.prog = {e: [] for e in ('pe', 'dve', 'act', 'pool', 'sp')}
        self.nsem = 0
        self.cur_sem = {}
        self.cnt = {}
        for e in self.CE:
            self.cur_sem[e] = self._newsem(e)
            self.cnt[e] = 0
        self.dma_sems = {q: [self._newsem('d' + q) for _ in range(ndma)] for q in ('sp', 'pool')}
        self.dma_n = {q: 0 for q in ('sp', 'pool')}
        self.waited = {e: {} for e in self.prog}
        self.lastw = {}
        self.readers = {}
        self.last_tok = {}

    def _newsem(self, tag):
        self.nsem += 1
        s = self.stack.enter_context(self.nc.semaphore(f"s{self.nsem}_{tag}"))
        return (self.nsem, s)

    PSUM = frozenset(['tp0', 'tp1', 'cp0', 'cp1', 'hp0', 'hp1', 'kp', 'tb', 'pj0', 'pj1', 'st0', 'st1', 'st2', 'st3',
                      'pv', 'pv0', 'pv1', 'mz', 'pa', 'pbk', 'yp0', 'yp1', 'fa', 'fb'])

    def _deps(self, reads, writes, eng=None):
        deps = []
        for b in reads:
            if b in self.lastw:
                deps.append(self.lastw[b])
            if b in self.PSUM:
                deps.extend(t for t in self.readers.get(b, ()) if t[2] != eng)
        for b in writes:
            if b in self.lastw:
                deps.append(self.lastw[b])
            deps.extend(self.readers.get(b, ()))
        return deps

    def _waits(self, eng, deps):
        waits = []
        w = self.waited[eng]
        for (sem, val, src) in deps:
            if src == eng and (eng == 'pe' or not self.strict_same):
                continue
            if w.get(sem[0], 0) >= val:
                continue
            w[sem[0]] = val
            waits.append((sem[1], val))
        return waits

    def _commit(self, tok, reads, writes):
        for b in reads:
            if b not in writes:
                self.readers.setdefault(b, []).append(tok)
        for b in writes:
            self.lastw[b] = tok
            self.readers[b] = []

    def op(self, eng, fn, reads=(), writes=()):
        deps = self._deps(reads, writes, eng)
        waits = self._waits(eng, deps)
        if self.cnt[eng] >= self.limit:
            self.cur_sem[eng] = self._newsem(eng)
            self.cnt[eng] = 0
        self.cnt[eng] += 1
        sem = self.cur_sem[eng]
        tok = (sem, self.cnt[eng], eng)
        self.last_tok[eng] = tok
        self.prog[eng].append((waits, fn, (sem[1], 1)))
        self._commit(tok, reads, writes)
        return tok

    def dma(self, q, fn, reads=(), writes=()):
        deps = self._deps(reads, writes)
        j = self.dma_n[q]
        self.dma_n[q] += 1
        K = len(self.dma_sems[q])
        sem = self.dma_sems[q][j % K]
        if j >= K:
            deps.append((sem, 16 * (j // K), 'dma'))
        waits = self._waits(q, deps)
        tok = (sem, 16 * (j // K + 1), 'dma')
        self.prog[q].append((waits, fn, (sem[1], 16)))
        self._commit(tok, reads, writes)
        return tok

    def _dma_final(self):
        deps = []
        for q in self.dma_sems:
            K = len(self.dma_sems[q])
            n = self.dma_n[q]
            for i, sem in enumerate(self.dma_sems[q]):
                cnt = (n - i + K - 1) // K if n > i else 0
                if cnt > 0:
                    deps.append((sem, 16 * cnt, 'dma'))
        return deps

    def barrier(self):
        deps = self._dma_final() + list(self.last_tok.values())
        for e in self.prog:
            saved = self.strict_same
            self.strict_same = True
            waits = []
            w = self.waited[e]
            for (sem, val, src) in deps:
                if w.get(sem[0], 0) >= val:
                    continue
                w[sem[0]] = val
                waits.append((sem[1], val))
            self.strict_same = saved
            self.prog[e].append((waits, None, None))
        self.lastw = {}
        self.readers = {}

    def finish(self):
        self.barrier()

    def emit(self, block):
        def run(engobj, items):
            for waits, fn, inc in items:
                for (s, v) in waits:
                    engobj.wait_ge(s, v)
                if fn is not None:
                    ins = fn(engobj)
                    ins.then_inc(inc[0], inc[1])

        P = self.prog

        @block.sync
        def _(e):
            run(e, P['sp'])

        @block.tensor
        def _(e):
            run(e, P['pe'])

        @block.vector
        def _(e):
            run(e, P['dve'])

        @block.scalar
        def _(e):
            run(e, P['act'])

        @block.gpsimd
        def _(e):
            run(e, P['pool'])


def cst_layout(NT):
    off = {}
    o = 0

    def add(name, n):
        nonlocal o
        off[name] = (o, o + n)
        o += n
    add('ident', 128)
    add('nw1', 8)
    add('nw2', 8)
    add('eps', 1)
    add('wq', 64)
    add('wk0', 64)
    add('wk1', 64)
    add('wk2', 64)
    add('posk', 32)
    add('posv', 32)
    add('cw', 32)
    add('cb', 8)
    add('ba', 8)
    add('bx', 8)
    add('lam', 8)
    add('cos', NT * 8)
    add('sin', NT * 8)
    add('cosc', 16)
    add('sinc', 16)
    add('tkA', 127)
    add('tkB', 127)
    add('ovl', 130)
    add('mhalf', 8)
    return off, o


DBG = {'stop': 99}


def build_program(NT=32, passes=(0, 1, 2, 3), dbg=False):
    S_ = NT * 128
    NCMP = (S_ - 32) // 16 + 1
    NSLC = S_ // 64
    CO, NCST = cst_layout(NT)
    nc = bass.Bass("TRN2", target_bir_lowering=False)

    def din(name, shape, dt=F32):
        return nc.dram_tensor(name, shape, dt, kind="ExternalInput").ap()

    x_d = din("x", [S_, D])
    cst_d = din("cst", [128, NCST])
    wtm_d = din("wtm", [128, 8 * 1048])
    wcv_d = din("wcv", [128, 8 * 256])
    wxy_d = din("wxy", [128, 8 * 2048])
    wgm_d = din("wgm", [128, 8 * 2048])
    pk1_d = din("pk1", [128, 32 * 256])
    pv1_d = din("pv1", [128, 32 * 256])
    pk2_d = din("pk2", [128, 2 * 64])
    pv2_d = din("pv2", [128, 2 * 64])
    wa_d = din("wa", [128, 8 * 128])
    wx_d = din("wx", [128, 8 * 128])
    wnu_d = din("wnu", [128, 4 * 1024])
    wlu_d = din("wlu", [128, 8 * 1024])
    wo_d = din("wo", [128, 8 * 1024])
    wf1_d = din("wf1", [128, 8 * DFF])
    wf2_d = din("wf2", [128, 32 * D])
    E_d = din("emat", [64, S_])
    mk_d = din("mk", [128, 1024])
    jw_d = din("jw", [8, 792])
    out_d = nc.dram_tensor("out", [S_, D], F32, kind="ExternalOutput").ap()
    son_d = nc.dram_tensor("sc_on", [NT * 128, 512], BF16, kind="Internal").ap()
    sol_d = nc.dram_tensor("sc_ol", [NT * 128, 1024], BF16, kind="Internal").ap()

    with ExitStack() as top:
        S = Sched(nc, top)

        uniq = [0]

        def sbt(st, name, shape, dt):
            uniq[0] += 1
            return st.enter_context(nc.sbuf_tensor(f"sb{uniq[0]}_{name}", shape, dt))

        def pst(st, name, shape, dt):
            uniq[0] += 1
            return st.enter_context(nc.psum_tensor(f"ps{uniq[0]}_{name}", shape, dt))

        cst = sbt(top, "cst", [128, NCST], F32)
        identb = sbt(top, "identb", [128, 128], BF16)
        kcT = sbt(top, "kcT", [64, 2, 256], BF16)
        vca = sbt(top, "vca", [128, 2, 2, 129], BF16)

        def C(name):
            a, b = CO[name]
            return cst[:, a:b]
        ident = C('ident')
        epsc = C('eps')

        S.dma('sp', lambda e: e.dma_start(out=cst[:], in_=cst_d[:, :]), writes=['cst'])
        S.op('dve', lambda e: e.tensor_copy(out=identb[:], in_=ident), reads=['cst'], writes=['identb'])

        def load_w(dst3, src_d, nk, ncol, name):
            step = max(1, 2048 // ncol)
            if ncol > 2048:
                for k in range(nk):
                    for c0 in range(0, ncol, 2048):
                        c1 = min(ncol, c0 + 2048)
                        S.dma('pool', lambda e, k=k, c0=c0, c1=c1: e.dma_start(
                            out=dst3[:, k, c0:c1], in_=src_d[:, k * ncol + c0:k * ncol + c1]), writes=[name])
            else:
                for k0 in range(0, nk, step):
                    k1 = min(nk, k0 + step)
                    S.dma('pool', lambda e, k0=k0, k1=k1: e.dma_start(
                        out=dst3[:, k0:k1, :],
                        in_=src_d[:, k0 * ncol:k1 * ncol].rearrange("p (a b) -> p a b", a=k1 - k0)), writes=[name])

        def norm_tile(xb, xbn, nwname, xs, ss, tp, xnT, tpn=('tp0', 'tp1'), outn='xnT'):
            if isinstance(tp, (list, tuple)):
                banks = tp
            else:
                banks = (tp[:, 0:512], tp[:, 512:1024])
            S.op('act', lambda e: e.activation(out=xs[:], in_=xb[:], func=AF.Square, accum_out=ss[:, 0:1]),
                 reads=[xbn], writes=['xs', 'ss0'])
            S.op('pool', lambda e: e.tensor_scalar(out=ss[:, 1:2], in0=ss[:, 0:1], scalar1=1.0 / D, scalar2=EPS, op0=ALU.mult, op1=ALU.add),
                 reads=['ss0'], writes=['ss1'])
            S.op('pool', lambda e: e.tensor_tensor(out=ss[:, 2:3], in0=ss[:, 1:2], in1=C('mhalf')[:, 0:1], op=ALU.pow),
                 reads=['ss1', 'cst'], writes=['ss2'])
            S.op('act', lambda e: e.activation(out=xs[:], in_=xb[:], func=AF.Copy, scale=ss[:, 2:3]),
                 reads=[xbn, 'ss2'], writes=['xs'])
            for k in range(8):
                S.op('pe', lambda e, k=k: e.transpose(out=banks[k // 4][:, (k % 4) * 128:(k % 4 + 1) * 128],
                                                      in_=xs[:, k * 128:(k + 1) * 128],
                                                      identity=ident), reads=['xs', 'cst'], writes=[tpn[k // 4]])
            nw = C(nwname)
            for a in range(2):
                S.op('dve', lambda e, a=a: e.tensor_tensor(
                    out=xnT[:, 4 * a:4 * a + 4, :], in0=banks[a].rearrange("p (a b) -> p a b", a=4),
                    in1=nw[:, 4 * a:4 * a + 4].unsqueeze(2).broadcast_to([128, 4, 128]), op=ALU.mult),
                     reads=[tpn[a], 'cst'], writes=[outn])

        def gelu(P_, shape_free, src, srcn, dst, dstn, t1, t1n, t2, t2n, twice=False):
            S.op('act', lambda e: e.activation(out=t1, in_=src, func=AF.Square), reads=[srcn], writes=[t1n])
            S.op('dve', lambda e: e.tensor_scalar(out=t1, in0=t1, scalar1=GC1, scalar2=1.0, op0=ALU.mult, op1=ALU.add),
                 reads=[t1n], writes=[t1n])
            S.op('dve', lambda e: e.tensor_tensor(out=t1, in0=t1, in1=src, op=ALU.mult), reads=[t1n, srcn], writes=[t1n])
            S.op('act', lambda e: e.activation(out=t2, in_=t1, func=AF.Tanh, scale=GC2 * 0.5), reads=[t1n], writes=[t2n])
            if twice:
                S.op('dve', lambda e: e.scalar_tensor_tensor(out=dst, in0=t2, scalar=1.0, in1=src, op0=ALU.add, op1=ALU.mult),
                     reads=[t2n, srcn], writes=[dstn])
            else:
                S.op('dve', lambda e: e.scalar_tensor_tensor(out=t1, in0=t2, scalar=1.0, in1=src, op0=ALU.add, op1=ALU.mult),
                     reads=[t2n, srcn], writes=[t1n])
                S.op('act', lambda e: e.activation(out=dst, in_=t1, func=AF.Copy, scale=0.5), reads=[t1n], writes=[dstn])

        def rmsrope(P_, H, src3, srcn, wrep, cosp, sinp, out3, outn, tmp, pre):
            sq = tmp['sq'][0:P_, 0:H, :]
            y = tmp['y'][0:P_, 0:H, :]
            st = tmp['st']
            r1 = tmp['r1'][0:P_, 0:H, :]
            r2 = tmp['r2'][0:P_, 0:H, :]
            S.op('act', lambda e: e.activation(out=sq, in_=src3, func=AF.Square), reads=[srcn], writes=[pre + 'sq'])
            S.op('dve', lambda e: e.tensor_reduce(out=st[0:P_, 0:H], in_=sq, axis=AX.X, op=ALU.add),
                 reads=[pre + 'sq'], writes=[pre + 'st0'])
            S.op('pool', lambda e: e.tensor_scalar(out=st[0:P_, 8:8 + H], in0=st[0:P_, 0:H], scalar1=1.0 / 64, scalar2=EPS,
                                                   op0=ALU.mult, op1=ALU.add), reads=[pre + 'st0'], writes=[pre + 'st1'])
            S.op('pool', lambda e: e.tensor_tensor(out=st[0:P_, 16:16 + H], in0=st[0:P_, 8:8 + H], in1=C('mhalf')[0:P_, 0:H], op=ALU.pow),
                 reads=[pre + 'st1', 'cst'], writes=[pre + 'st2'])
            S.op('dve', lambda e: e.tensor_tensor(out=y, in0=src3,
                                                  in1=st[0:P_, 16:16 + H].unsqueeze(2).broadcast_to([P_, H, 64]), op=ALU.mult),
                 reads=[srcn, pre + 'st2'], writes=[pre + 'y'])
            S.op('dve', lambda e: e.tensor_tensor(out=y, in0=y, in1=wrep.unsqueeze(1).broadcast_to([P_, H, 64]), op=ALU.mult),
                 reads=[pre + 'y', 'cst', 'wq8'], writes=[pre + 'y'])
            cb_ = cosp.unsqueeze(1).broadcast_to([P_, H, 8])
            sb_ = sinp.unsqueeze(1).broadcast_to([P_, H, 8])
            y1 = y[:, :, 0:8]
            y2 = y[:, :, 8:16]
            S.op('dve', lambda e: e.tensor_tensor(out=r1, in0=y1, in1=cb_, op=ALU.mult), reads=[pre + 'y', 'cst'], writes=[pre + 'r1'])
            S.op('dve', lambda e: e.tensor_tensor(out=r2, in0=y2, in1=sb_, op=ALU.mult), reads=[pre + 'y', 'cst'], writes=[pre + 'r2'])
            S.op('dve', lambda e: e.tensor_tensor(out=out3[:, :, 0:8], in0=r1, in1=r2, op=ALU.subtract),
                 reads=[pre + 'r1', pre + 'r2'], writes=[outn])
            S.op('dve', lambda e: e.tensor_tensor(out=r1, in0=y2, in1=cb_, op=ALU.mult), reads=[pre + 'y', 'cst'], writes=[pre + 'r1'])
            S.op('dve', lambda e: e.tensor_tensor(out=r2, in0=y1, in1=sb_, op=ALU.mult), reads=[pre + 'y', 'cst'], writes=[pre + 'r2'])
            S.op('dve', lambda e: e.tensor_tensor(out=out3[:, :, 8:16], in0=r1, in1=r2, op=ALU.add),
                 reads=[pre + 'r1', pre + 'r2'], writes=[outn])
            S.op('act', lambda e: e.activation(out=out3[:, :, 16:64], in_=y[:, :, 16:64], func=AF.Copy),
                 reads=[pre + 'y'], writes=[outn])

        def pass0():
            with ExitStack() as st:
                wcv = sbt(st, "wcv", [128, 8, 256], BF16)
                pk1 = sbt(st, "pk1", [128, 32, 256], BF16)
                pv1 = sbt(st, "pv1", [128, 32, 256], BF16)
                pk2 = sbt(st, "pk2", [128, 2, 64], BF16)
                pv2 = sbt(st, "pv2", [128, 2, 64], BF16)
                rawk = sbt(st, "rawk", [128, S_ + 16], F32)
                rawv = sbt(st, "rawv", [128, S_ + 16], F32)
                zk = sbt(st, "zk", [128, 32, 256], BF16)
                zv = sbt(st, "zv", [128, 32, 256], BF16)
                xb = [sbt(st, f"xb{i}", [128, D], F32) for i in range(2)]
                xs = sbt(st, "xs", [128, D], F32)
                ss = sbt(st, "ss", [128, 4], F32)
                xnT = sbt(st, "xnT", [128, 8, 128], BF16)
                hT = sbt(st, "hT", [128, 2, 256], BF16)
                g1 = sbt(st, "g1", [128, 256], F32)
                g2 = sbt(st, "g2", [128, 256], F32)
                kc32 = sbt(st, "kc32", [128, 64], F32)
                kcn = sbt(st, "kcn", [128, 64], BF16)
                tmp = dict(sq=sbt(st, "t_sq", [128, 8, 64], F32), y=sbt(st, "t_y", [128, 8, 64], F32),
                           st=sbt(st, "t_st", [128, 32], F32), r1=sbt(st, "t_r1", [128, 8, 8], F32),
                           r2=sbt(st, "t_r2", [128, 8, 8], F32))
                tp = pst(st, "tp", [128, 1024], F32)
                cp = [pst(st, f"cp{i}", [128, 512], F32) for i in range(2)]
                hp = [pst(st, f"hp{i}", [128, 512], F32) for i in range(2)]
                kp = pst(st, "kp", [128, 512], F32)
                tb = pst(st, "tb", [128, 1024], BF16)

                load_w(wcv, wcv_d, 8, 256, 'wcv')
                load_w(pk1, pk1_d, 32, 256, 'pk1')
                load_w(pv1, pv1_d, 32, 256, 'pv1')
                load_w(pk2, pk2_d, 2, 64, 'pk2')
                load_w(pv2, pv2_d, 2, 64, 'pv2')
                for t in range(NT):
                    b = xb[t % 2]
                    bn = f"xb{t % 2}"
                    S.dma('sp', lambda e, b=b, t=t: e.dma_start(out=b[:], in_=x_d[t * 128:(t + 1) * 128, :]), writes=[bn])
                    norm_tile(b, bn, 'nw1', xs, ss, tp, xnT)
                    pb = cp[t % 2]
                    pn = f"cp{t % 2}"
                    for c in range(2):
                        for k in range(8):
                            S.op('pe', lambda e, c=c, k=k, pb=pb: e.matmul(
                                out=pb[:, c * 128:(c + 1) * 128], lhsT=wcv[:, k, c * 128:(c + 1) * 128], rhs=xnT[:, k, :],
                                start=(k == 0), stop=(k == 7)), reads=['wcv', 'xnT'], writes=[pn])
                    S.op('act', lambda e, pb=pb, t=t: e.activation(out=rawk[:, t * 128:(t + 1) * 128], in_=pb[:, 0:128], func=AF.Copy),
                         reads=[pn], writes=['rawk'])
                    S.op('act', lambda e, pb=pb, t=t: e.activation(out=rawv[:, t * 128:(t + 1) * 128], in_=pb[:, 128:256], func=AF.Copy),
                         reads=[pn], writes=['rawv'])
                if DBG['stop'] <= 1:
                    S.barrier()
                    return
                posk = C('posk')
                posv = C('posv')
                for l in range(32):
                    S.op('dve', lambda e, l=l: e.tensor_scalar(
                        out=zk[:, l, 0:NCMP], in0=rawk[:, l:l + 16 * (NCMP - 1) + 1:16], scalar1=posk[:, l:l + 1], scalar2=None,
                        op0=ALU.add), reads=['rawk', 'cst'], writes=['zk'])
                    S.op('pool', lambda e, l=l: e.tensor_scalar(
                        out=zv[:, l, 0:NCMP], in0=rawv[:, l:l + 16 * (NCMP - 1) + 1:16], scalar1=posv[:, l:l + 1], scalar2=None,
                        op0=ALU.add), reads=['rawv', 'cst'], writes=['zv'])
                if DBG['stop'] <= 2:
                    S.barrier()
                    return
                nch = [(0, min(128, NCMP))]
                if NCMP > 128:
                    nch.append((128, NCMP - 128))
                ovl = C('ovl')
                for kv in range(2):
                    z = (zk, zv)[kv]
                    zn = ('zk', 'zv')[kv]
                    w1 = (pk1, pv1)[kv]
                    w1n = ('pk1', 'pv1')[kv]
                    w2 = (pk2, pv2)[kv]
                    w2n = ('pk2', 'pv2')[kv]
                    for g in range(2):
                        gs_ = slice(g * 64, (g + 1) * 64)
                        for hc in range(2):
                            hb_ = hp[hc]
                            hn_ = f"hp{hc}"
                            for l in range(32):
                                S.op('pe', lambda e, l=l, hc=hc, hb_=hb_, z=z, w1=w1, gs_=gs_: e.matmul(
                                    out=hb_[:, 0:NCMP], lhsT=w1[gs_, l, hc * 128:(hc + 1) * 128], rhs=z[gs_, l, 0:NCMP],
                                    start=(l == 0), stop=(l == 31)), reads=[w1n, zn], writes=[hn_])
                            gelu(128, NCMP, hb_[:, 0:NCMP], hn_, hT[:, hc, 0:NCMP], f'hT{hc}',
                                 g1[:, 0:NCMP], 'g1', g2[:, 0:NCMP], 'g2')
                        if DBG['stop'] <= 3:
                            continue
                        for ci, (n0, sz) in enumerate(nch):
                            for hc in range(2):
                                S.op('pe', lambda e, hc=hc, n0=n0, sz=sz, w2=w2: e.matmul(
                                    out=kp[0:sz, 0:64], lhsT=hT[:, hc, n0:n0 + sz], rhs=w2[:, hc, :],
                                    start=(hc == 0), stop=(hc == 1)), reads=[f'hT{hc}', w2n], writes=['kp'])
                            if DBG['stop'] <= 4:
                                continue
                            if kv == 0:
                                S.op('act', lambda e, sz=sz: e.activation(out=kc32[0:sz, :], in_=kp[0:sz, 0:64], func=AF.Copy),
                                     reads=['kp'], writes=['kc32'])
                                cc = C('cosc')[:, ci * 8:(ci + 1) * 8]
                                sc = C('sinc')[:, ci * 8:(ci + 1) * 8]
                                rmsrope(sz, 1, kc32[0:sz, :].rearrange("p (h d) -> p h d", h=1), 'kc32', C('wk0')[0:sz, :],
                                        cc[0:sz, :], sc[0:sz, :], kcn[0:sz, :].rearrange("p (h d) -> p h d", h=1), 'kcn', tmp, 'p0')
                                S.op('pe', lambda e, sz=sz: e.transpose(out=tb[0:64, 0:sz], in_=kcn[0:sz, :], identity=identb[0:sz, 0:sz]),
                                     reads=['kcn', 'identb'], writes=['tb'])
                                S.op('dve', lambda e, sz=sz, n0=n0, g=g: e.tensor_copy(out=kcT[:, g, n0:n0 + sz], in_=tb[0:64, 0:sz]),
                                     reads=['tb'], writes=['kcT'])
                            else:
                                S.op('act', lambda e, sz=sz, g=g, ci=ci: e.activation(out=vca[0:sz, g, ci, 0:64], in_=kp[0:sz, 0:64], func=AF.Copy),
                                     reads=['kp'], writes=['vca'])
                                S.op('dve', lambda e, sz=sz, g=g, ci=ci: e.tensor_copy(out=vca[0:sz, g, ci, 64:129], in_=ovl[0:sz, ci * 65:(ci + 1) * 65]),
                                     reads=['cst'], writes=['vca'])
                S.barrier()

        def pass1():
            with ExitStack() as st:
                wtm = sbt(st, "wtm", [128, 8, 1048], BF16)
                wxy = sbt(st, "wxy", [128, 8, 2048], BF16)
                wa = sbt(st, "wa", [128, 8, 128], BF16)
                wx = sbt(st, "wx", [128, 8, 128], BF16)
                ksT = [sbt(st, f"ksT{g}", [128, S_], BF16) for g in range(2)]
                kwT = sbt(st, "kwT", [64, 2, S_], BF16)
                vsa = sbt(st, "vsa", [128, NT, 2, 65], BF16)
                vwa = sbt(st, "vwa", [128, NT, 2, 65], BF16)
                mk = sbt(st, "mk", [128, 1024], BF16)
                jw = sbt(st, "jw", [8, 792], BF16)
                ones2 = sbt(st, "ones2", [128, 2], BF16)
                xb = [sbt(st, f"xb{i}", [128, D], F32) for i in range(2)]
                xs = sbt(st, "xs", [128, D], F32)
                ss = sbt(st, "ss", [128, 4], F32)
                xnT = sbt(st, "xnT", [128, 8, 128], BF16)
                qraw = sbt(st, "qraw", [128, 512], F32)
                kvraw = sbt(st, "kvraw", [128, 512], F32)
                gsgs = [sbt(st, f"gsg{i}", [128, 24], F32) for i in range(2)]
                qaug = sbt(st, "qaug", [128, 8, 128], BF16)
                kn = sbt(st, "kn", [128, 4, 64], BF16)
                qT = sbt(st, "qT", [64, 8, 128], BF16)
                qaTs = [[sbt(st, f"qaT{p}{g}", [128, 512], BF16) for g in range(2)] for p in range(2)]
                NB = 3
                pT = [sbt(st, f"pT{i}", [128, 512], BF16) for i in range(NB)]
                pTc = [sbt(st, f"pTc{i}", [128, 512], BF16) for i in range(2)]
                oaccs = [sbt(st, f"oacc{i}", [128, 8, 64], F32) for i in range(2)]
                otmpF = sbt(st, "otmpF", [128, 4, 64], F32)
                otmpA = sbt(st, "otmpA", [128, 4, 64], F32)
                obf = sbt(st, "obf", [128, 512], BF16)
                onT = sbt(st, "onT", [128, 4, 128], BF16)
                imp = sbt(st, "imp", [128, 64], F32)
                sc1 = sbt(st, "sc1", [128, 64], F32)
                sc2 = sbt(st, "sc2", [128, 64], F32)
                m8 = sbt(st, "m8", [128, 16], F32)
                sm = sbt(st, "sm", [128, 16], F32)
                wq8 = sbt(st, "wq8", [128, 64], F32)
                tmp = dict(sq=sbt(st, "t_sq", [128, 8, 64], F32), y=sbt(st, "t_y", [128, 8, 64], F32),
                           st=sbt(st, "t_st", [128, 32], F32), r1=sbt(st, "t_r1", [128, 8, 8], F32),
                           r2=sbt(st, "t_r2", [128, 8, 8], F32))
                xrxs = [sbt(st, f"xrx{i}", [128, 8, 132], F32) for i in range(2)]
                yrbs = [sbt(st, f"yrb{i}", [128, 8, 128], F32) for i in range(2)]
                xc = sbt(st, "xc", [128, 8, 128], F32)
                xcb = sbt(st, "xcb", [128, 8, 128], BF16)
                rg = sbt(st, "rg", [128, 8, 128], F32)
                ig = sbt(st, "ig", [128, 8, 128], F32)
                av = sbt(st, "av", [128, 8, 128], F32)
                bt = sbt(st, "bt", [128, 8, 128], F32)
                hs = sbt(st, "hs", [128, 8, 128], F32)
                hst = sbt(st, "hst", [128, 8], F32)
                cl = sbt(st, "cl", [128, 8], F32)
                olT = sbt(st, "olT", [128, 8, 128], BF16)
                gt1 = sbt(st, "gt1", [128, 1024], F32)
                stp = [pst(st, f"st{i}", [128, 512], F32) for i in range(NB)]
                pvb = [pst(st, f"pv{i}", [128, 512], F32) for i in range(2)]
                mz = pst(st, "mz", [128, 1024], BF16)
                fab = [pst(st, "fa", [128, 512], F32), pst(st, "fb", [128, 512], F32)]
                pj = fab
                pjn = ['fa', 'fb']

                load_w(wtm, wtm_d, 8, 1048, 'wtm')
                load_w(wxy, wxy_d, 8, 2048, 'wxy')
                load_w(wa, wa_d, 8, 128, 'wa')
                load_w(wx, wx_d, 8, 128, 'wx')
                S.dma('pool', lambda e: e.dma_start(out=mk[:], in_=mk_d[:, :]), writes=['mk'])
                S.dma('pool', lambda e: e.dma_start(out=jw[:], in_=jw_d[:, :]), writes=['jw'])
                for g in range(2):
                    for c0 in range(0, S_, 2048):
                        c1 = min(S_, c0 + 2048)
                        S.dma('pool', lambda e, g=g, c0=c0, c1=c1: e.dma_start(out=ksT[g][64:128, c0:c1], in_=E_d[:, c0:c1]),
                              writes=[f'ksE{g}'])
                S.op('dve', lambda e: e.tensor_scalar(out=wq8[:], in0=C('wq'), scalar1=0.125, scalar2=None, op0=ALU.mult),
                     reads=['cst'], writes=['wq8'])
                S.op('pool', lambda e: e.memset(vsa[:, :, :, 64:65], 1.0), writes=['vsa1'])
                S.op('pool', lambda e: e.memset(vwa[:, :, :, 64:65], 1.0), writes=['vwa1'])
                S.op('pool', lambda e: e.memset(ones2[:], 1.0), writes=['ones2'])
                S.op('pool', lambda e: e.memset(xrxs[0][:, :, 0:3], 0.0), writes=['xrxh0'])
                S.op('pool', lambda e: e.memset(hst[:], 0.0), writes=['hst'])
                S.op('pool', lambda e: e.memset(qaug[:], 0.0), writes=['qaugq', 'qaugm0', 'qaugm1'])
                S.op('act', lambda e: e.activation(out=cl[:], in_=C('lam'), func=AF.Exp, scale=-1.0), reads=['cst'], writes=['cl'])
                S.op('dve', lambda e: e.tensor_scalar(out=cl[:], in0=cl[:], scalar1=1.0, scalar2=None, op0=ALU.add),
                     reads=['cl'], writes=['cl'])
                S.op('act', lambda e: e.activation(out=cl[:], in_=cl[:], func=AF.Ln), reads=['cl'], writes=['cl'])
                S.op('dve', lambda e: e.tensor_scalar(out=cl[:], in0=cl[:], scalar1=-4.0, scalar2=None, op0=ALU.mult),
                     reads=['cl'], writes=['cl'])
                clh = cl
                tkA = C('tkA')
                tkB = C('tkB')
                cw = C('cw')
                cbv = C('cb')
                bav = C('ba')
                bxv = C('bx')
                Mc4 = mk[:, 0:512]
                Ml4 = mk[:, 512:1024]
                W8 = jw[:, 280:792]
                ctr = [0]

                def front(i):
                    p = i % 2
                    gsg = gsgs[p]
                    qaT = qaTs[p]
                    oacc = oaccs[p]
                    xrx = xrxs[p]
                    yrb = yrbs[p]
                    xrxn = f'xrx{p}'
                    xrxhn = f'xrxh{p}'
                    yrbn = f'yrb{p}'
                    gsgn = f'gsg{p}'
                    oaccn = f'oacc{p}'
                    T0 = i * 128
                    b = xb[i % 2]
                    bn = f"xb{i % 2}"
                    S.dma('sp', lambda e, b=b, i=i: e.dma_start(out=b[:], in_=x_d[i * 128:(i + 1) * 128, :]), writes=[bn])
                    norm_tile(b, bn, 'nw1', xs, ss, fab, xnT, tpn=('fa', 'fb'))
                    yield
                    for (c0, c1, pb, pn) in ((0, 512, pj[0], pjn[0]), (512, 1024, pj[1], pjn[1])):
                        for k in range(8):
                            S.op('pe', lambda e, k=k, c0=c0, c1=c1, pb=pb: e.matmul(
                                out=pb[:, 0:512], lhsT=xnT[:, k, :], rhs=wtm[:, k, c0:c1], start=(k == 0), stop=(k == 7)),
                                reads=['xnT', 'wtm'], writes=[pn])
                    S.op('act', lambda e: e.activation(out=qraw[:], in_=pj[0][:, 0:512], func=AF.Copy), reads=[pjn[0]], writes=['qraw'])
                    S.op('act', lambda e: e.activation(out=kvraw[:], in_=pj[1][:, 0:512], func=AF.Copy), reads=[pjn[1]], writes=['kvraw'])
                    for k in range(8):
                        S.op('pe', lambda e, k=k: e.matmul(out=fab[0][:, 0:24], lhsT=xnT[:, k, :], rhs=wtm[:, k, 1024:1048],
                                                           start=(k == 0), stop=(k == 7)), reads=['xnT', 'wtm'], writes=['fa'])
                    S.op('act', lambda e: e.activation(out=gsg[:], in_=fab[0][:, 0:24], func=AF.Tanh, scale=0.5), reads=['fa'], writes=[gsgn])
                    S.op('dve', lambda e: e.tensor_scalar(out=gsg[:], in0=gsg[:], scalar1=0.5, scalar2=0.5, op0=ALU.mult, op1=ALU.add),
                         reads=[gsgn], writes=[gsgn])
                    for c4 in range(4):
                        pb = fab[(c4 + 1) % 2]
                        pn = pjn[(c4 + 1) % 2]
                        yield
                        for cc in range(4):
                            c = c4 * 4 + cc
                            for k in range(8):
                                S.op('pe', lambda e, c=c, cc=cc, k=k, pb=pb: e.matmul(
                                    out=pb[:, cc * 128:(cc + 1) * 128], lhsT=wxy[:, k, c * 128:(c + 1) * 128], rhs=xnT[:, k, :],
                                    start=(k == 0), stop=(k == 7)), reads=['wxy', 'xnT'], writes=[pn])
                        src = pb[:, 0:512].rearrange("p (a b) -> p a b", a=4)
                        if c4 < 2:
                            S.op('act', lambda e, c4=c4, src=src: e.activation(out=xrx[:, c4 * 4:(c4 + 1) * 4, 3:131], in_=src, func=AF.Copy),
                                 reads=[pn], writes=[xrxn])
                        else:
                            S.op('act', lambda e, c4=c4, src=src: e.activation(out=yrb[:, (c4 - 2) * 4:(c4 - 1) * 4, :], in_=src, func=AF.Copy),
                                 reads=[pn], writes=[yrbn])
                    yield
                    cosp = C('cos')[:, i * 8:(i + 1) * 8]
                    sinp = C('sin')[:, i * 8:(i + 1) * 8]
                    rmsrope(128, 8, qraw[:].rearrange("p (h d) -> p h d", h=8), 'qraw', wq8[:], cosp, sinp,
                            qaug[:, :, 0:64], 'qaugq', tmp, 'p1')
                    yield
                    rmsrope(128, 2, kvraw[:, 0:128].rearrange("p (h d) -> p h d", h=2), 'kvraw', C('wk1'), cosp, sinp,
                            kn[:, 0:2, :], 'kn', tmp, 'p1')
                    yield
                    rmsrope(128, 2, kvraw[:, 128:256].rearrange("p (h d) -> p h d", h=2), 'kvraw', C('wk2'), cosp, sinp,
                            kn[:, 2:4, :], 'kn', tmp, 'p1')
                    yield
                    for j in range(4):
                        S.op('pe', lambda e, j=j: e.transpose(out=mz[0:64, j * 128:(j + 1) * 128], in_=kn[:, j, :], identity=identb[:]),
                             reads=['kn', 'identb'], writes=['mz'])
                    for g in range(2):
                        S.op('dve', lambda e, g=g, T0=T0: e.tensor_copy(out=ksT[g][0:64, T0:T0 + 128], in_=mz[0:64, g * 128:(g + 1) * 128]),
                             reads=['mz'], writes=[f'ksT{g}_{i}'])
                    S.op('dve', lambda e, T0=T0: e.tensor_copy(out=kwT[:, :, T0:T0 + 128],
                                                               in_=mz[0:64, 256:512].rearrange("p (a b) -> p a b", a=2)),
                         reads=['mz'], writes=[f'kwT_{i}'])
                    S.op('act', lambda e, i=i: e.activation(out=vsa[:, i, :, 0:64], in_=kvraw[:, 256:384].rearrange("p (g d) -> p g d", g=2),
                                                            func=AF.Copy), reads=['kvraw'], writes=[f'vsa_{i}'])
                    S.op('act', lambda e, i=i: e.activation(out=vwa[:, i, :, 0:64], in_=kvraw[:, 384:512].rearrange("p (g d) -> p g d", g=2),
                                                            func=AF.Copy), reads=['kvraw'], writes=[f'vwa_{i}'])
                    yield
                    for h in range(8):
                        S.op('pe', lambda e, h=h: e.transpose(out=mz[0:64, h * 128:(h + 1) * 128], in_=qaug[:, h, 0:64], identity=identb[:]),
                             reads=['qaugq', 'identb'], writes=['mz'])
                    S.op('dve', lambda e: e.tensor_copy(out=qT[:].rearrange("p a b -> p (a b)"), in_=mz[0:64, :]), reads=['mz'], writes=['qT'])

                    yield
                    n_hi = min(NCMP, 8 * i + 7)
                    chunks = [(0, 0, min(128, n_hi))]
                    if n_hi > 128:
                        chunks.append((1, 128, n_hi - 128))
                    nchk = len(chunks)
                    for g in range(2):
                        bufs = []
                        for (ci, n0, Kc) in chunks:
                            bufs.append(ci)
                            sp_ = fab[ci]
                            sn_ = pjn[ci]
                            pt_ = pTc[ci]
                            ptn = f"pTc{ci}"
                            s0 = 265 + n0 - 8 * i
                            S.op('pe', lambda e, g=g, n0=n0, Kc=Kc, sp_=sp_: e.matmul(
                                out=sp_[0:Kc, :], lhsT=kcT[:, g, n0:n0 + Kc], rhs=qT[:, 4 * g:4 * g + 4, :].rearrange("p a b -> p (a b)"),
                                start=True, stop=False), reads=['kcT', 'qT'], writes=[sn_])
                            S.op('pe', lambda e, s0=s0, Kc=Kc, sp_=sp_: e.matmul(
                                out=sp_[0:Kc, :], lhsT=jw[:, s0:s0 + Kc], rhs=W8, start=False, stop=True), reads=['jw'], writes=[sn_])
                            S.op('act', lambda e, Kc=Kc, sp_=sp_, pt_=pt_: e.activation(out=pt_[0:Kc, :], in_=sp_[0:Kc, :], func=AF.Exp),
                                 reads=[sn_], writes=[ptn])
                        for h in range(4):
                            for idx, (ci, n0, Kc) in enumerate(chunks):
                                pt_ = pTc[bufs[idx]]
                                ptn = f"pTc{bufs[idx]}"
                                S.op('pe', lambda e, h=h, ci=ci, Kc=Kc, g=g, pt_=pt_, idx=idx, lastc=(idx == nchk - 1): e.matmul(
                                    out=fab[0][:, h * 128:(h + 1) * 128], lhsT=pt_[0:Kc, h * 128:(h + 1) * 128], rhs=vca[0:Kc, g, ci, 0:128],
                                    start=(idx == 0), stop=lastc), reads=[ptn, 'vca'], writes=['fa'])
                        for h in range(4):
                            for idx, (ci, n0, Kc) in enumerate(chunks):
                                pt_ = pTc[bufs[idx]]
                                ptn = f"pTc{bufs[idx]}"
                                S.op('pe', lambda e, h=h, Kc=Kc, pt_=pt_, idx=idx, lastc=(idx == nchk - 1): e.matmul(
                                    out=fab[1][:, 2 * h:2 * h + 2], lhsT=pt_[0:Kc, h * 128:(h + 1) * 128], rhs=ones2[0:Kc, :],
                                    start=(idx == 0), stop=lastc), reads=[ptn, 'ones2'], writes=['fb'])
                        S.op('dve', lambda e: e.tensor_scalar(out=sm[:, 0:4], in0=fab[1][:, 0:8:2], scalar1=1e-30, scalar2=None, op0=ALU.max),
                             reads=['fb'], writes=['sm0'])
                        S.op('dve', lambda e: e.reciprocal(out=sm[:, 4:8], in_=sm[:, 0:4]), reads=['sm0'], writes=['sm1'])
                        S.op('dve', lambda e, g=g: e.tensor_tensor(out=sm[:, 8:12], in0=sm[:, 4:8], in1=gsg[:, 4 * g:4 * g + 4], op=ALU.mult),
                             reads=['sm1', gsgn], writes=['sm2'])
                        pv3 = fab[0][:, 0:512].rearrange("p (h c) -> p h c", h=4)
                        S.op('dve', lambda e, g=g, pv3=pv3: e.tensor_tensor(
                            out=oacc[:, 4 * g:4 * g + 4, :], in0=pv3[:, :, 0:64], in1=sm[:, 8:12].unsqueeze(2).broadcast_to([128, 4, 64]),
                            op=ALU.mult), reads=['fa', 'sm2'], writes=[oaccn])
                        S.op('dve', lambda e, pv3=pv3: e.tensor_tensor(
                            out=otmpF[:], in0=pv3[:, :, 64:128], in1=sm[:, 4:8].unsqueeze(2).broadcast_to([128, 4, 64]),
                            op=ALU.mult), reads=['fa', 'sm1'], writes=['otmpF'])
                        S.op('dve', lambda e: e.tensor_reduce(out=imp[:], in_=otmpF[:].rearrange("p h j -> p j h"), axis=AX.X, op=ALU.add),
                             reads=['otmpF'], writes=['imp'])
                        yield
                        a0 = 63 - 2 * i
                        Asl = tkA[:, a0:a0 + NSLC]
                        Bsl = tkB[:, a0:a0 + NSLC]
                        S.op('dve', lambda e, Bsl=Bsl: e.tensor_tensor(out=sc1[:, 0:NSLC], in0=imp[:, 0:NSLC], in1=Bsl, op=ALU.mult),
                             reads=['imp', 'cst'], writes=['sc1'])
                        S.op('dve', lambda e, Asl=Asl: e.tensor_tensor(out=sc1[:, 0:NSLC], in0=sc1[:, 0:NSLC], in1=Asl, op=ALU.add),
                             reads=['sc1', 'cst'], writes=['sc1'])
                        S.op('dve', lambda e: e.memset(sc1[:, 0:1], 1e4), writes=['sc1'])
                        S.op('dve', lambda e: e.max(out=m8[:, 0:8], in_=sc1[:, 0:NSLC]), reads=['sc1'], writes=['m8a'])
                        S.op('dve', lambda e: e.match_replace(out=sc2[:, 0:NSLC], in_to_replace=m8[:, 0:8], in_values=sc1[:, 0:NSLC],
                                                              imm_value=-3.0e38), reads=['sc1', 'm8a'], writes=['sc2'])
                        S.op('dve', lambda e: e.max(out=m8[:, 8:16], in_=sc2[:, 0:NSLC]), reads=['sc2'], writes=['m8b'])
                        S.op('dve', lambda e, g=g: e.tensor_scalar(
                            out=qaug[:, 4 * g:4 * g + 4, 64:64 + NSLC], in0=sc1[:, 0:NSLC].unsqueeze(1).broadcast_to([128, 4, NSLC]),
                            scalar1=m8[:, 15:16], scalar2=NEGM, op0=ALU.is_lt, op1=ALU.mult), reads=['sc1', 'm8b'], writes=[f'qaugm{g}'])

                    yield
                    for g in range(2):
                        for h in range(4):
                            S.op('pe', lambda e, g=g, h=h: e.transpose(out=mz[:, h * 128:(h + 1) * 128], in_=qaug[:, 4 * g + h, :], identity=identb[:]),
                                 reads=['qaugq', f'qaugm{g}', 'identb'], writes=['mz'])
                        S.op('dve', lambda e, g=g: e.tensor_copy(out=qaT[g][:], in_=mz[:, 0:512]), reads=['mz'], writes=[f'qaT{p}{g}'])

                def attn(i):
                    p = i % 2
                    gsg = gsgs[p]
                    qaT = qaTs[p]
                    oacc = oaccs[p]
                    xrx = xrxs[p]
                    yrb = yrbs[p]
                    xrxn = f'xrx{p}'
                    xrxhn = f'xrxh{p}'
                    yrbn = f'yrb{p}'
                    gsgn = f'gsg{p}'
                    oaccn = f'oacc{p}'
                    items = []
                    gi = 0
                    for br in (1, 2):
                        for g in range(2):
                            kts = list(range(0, i + 1)) if br == 1 else list(range(max(0, i - 4), i + 1))
                            for idx, kt in enumerate(kts):
                                items.append(dict(br=br, g=g, kt=kt, first=(idx == 0), last=(idx == len(kts) - 1), grp=gi))
                            gi += 1

                    def stage_S(it):
                        bi = ctr[0] % NB
                        ctr[0] += 1
                        it['bi'] = bi
                        sp_ = stp[bi]
                        sn_ = f"st{bi}"
                        pt_ = pT[bi]
                        ptn = f"pT{bi}"
                        g, kt, br = it['g'], it['kt'], it['br']
                        mask = None
                        if kt == i:
                            mask = Mc4
                        elif br == 2 and kt == i - 4:
                            mask = Ml4
                        if br == 1:
                            S.op('pe', lambda e: e.matmul(out=sp_[:, :], lhsT=ksT[g][:, kt * 128:(kt + 1) * 128], rhs=qaT[g][:, :],
                                                          start=True, stop=(mask is None)),
                                 reads=[f'ksT{g}_{kt}', f'ksE{g}', f'qaT{p}{g}'], writes=[sn_])
                        else:
                            S.op('pe', lambda e: e.matmul(out=sp_[:, :], lhsT=kwT[:, g, kt * 128:(kt + 1) * 128], rhs=qaT[g][0:64, :],
                                                          start=True, stop=(mask is None)),
                                 reads=[f'kwT_{kt}', f'qaT{p}{g}'], writes=[sn_])
                        if mask is not None:
                            S.op('pe', lambda e: e.matmul(out=sp_[:, :], lhsT=identb[:], rhs=mask, start=False, stop=True),
                                 reads=['identb', 'mk'], writes=[sn_])
                        S.op('act', lambda e: e.activation(out=pt_[:], in_=sp_[:, :], func=AF.Exp), reads=[sn_], writes=[ptn])

                    def stage_P(it):
                        bi = it['bi']
                        pt_ = pT[bi]
                        ptn = f"pT{bi}"
                        g, kt, br = it['g'], it['kt'], it['br']
                        pvt = pvb[it['grp'] % 2]
                        pvn = f"pv{it['grp'] % 2}"
                        va = vsa if br == 1 else vwa
                        van = (f'vsa_{kt}', 'vsa1') if br == 1 else (f'vwa_{kt}', 'vwa1')
                        for h in range(4):
                            S.op('pe', lambda e, h=h: e.matmul(
                                out=pvt[:, h * 65:(h + 1) * 65], lhsT=pt_[:, h * 128:(h + 1) * 128], rhs=va[:, kt, g, :],
                                start=(it['first'] and h == 0), stop=it['last'], skip_group_check=True),
                                reads=[ptn, van[0], van[1]], writes=[pvn])
                        if it['last']:
                            pv3 = pvt[:, 0:260].rearrange("p (h c) -> p h c", h=4)
                            S.op('dve', lambda e: e.reciprocal(out=sm[:, 12:16], in_=pvt[:, 64:260:65]), reads=[pvn], writes=['sm4'])
                            S.op('dve', lambda e: e.tensor_tensor(out=sm[:, 12:16], in0=sm[:, 12:16],
                                                                  in1=gsg[:, br * 8 + 4 * g:br * 8 + 4 * g + 4], op=ALU.mult),
                                 reads=['sm4', gsgn], writes=['sm4'])
                            S.op('dve', lambda e: e.tensor_tensor(out=otmpA[:], in0=pv3[:, :, 0:64],
                                                                  in1=sm[:, 12:16].unsqueeze(2).broadcast_to([128, 4, 64]), op=ALU.mult),
                                 reads=[pvn, 'sm4'], writes=['otmpA'])
                            S.op('dve', lambda e: e.tensor_tensor(out=oacc[:, 4 * g:4 * g + 4, :], in0=oacc[:, 4 * g:4 * g + 4, :], in1=otmpA[:],
                                                                  op=ALU.add), reads=['otmpA', oaccn], writes=[oaccn])
                    LOOK = 2
                    for n_ in range(len(items) + LOOK):
                        if n_ < len(items):
                            stage_S(items[n_])
                        if n_ - LOOK >= 0:
                            stage_P(items[n_ - LOOK])
                        yield

                    S.op('act', lambda e: e.activation(out=obf[:], in_=oacc[:].rearrange("p a b -> p (a b)"), func=AF.Copy),
                         reads=[oaccn], writes=['obf'])
                    for c in range(4):
                        S.op('pe', lambda e, c=c: e.transpose(out=mz[:, c * 128:(c + 1) * 128], in_=obf[:, c * 128:(c + 1) * 128], identity=identb[:]),
                             reads=['obf', 'identb'], writes=['mz'])
                    S.op('dve', lambda e: e.tensor_copy(out=onT[:].rearrange("p a b -> p (a b)"), in_=mz[:, 0:512]), reads=['mz'], writes=['onT'])
                    S.dma('sp', lambda e, i=i: e.dma_start(out=son_d[i * 128:(i + 1) * 128, :], in_=onT[:].rearrange("p a b -> p (a b)")),
                          reads=['onT'], writes=[f'son{i}'])

                def lru(i):
                    p = i % 2
                    gsg = gsgs[p]
                    qaT = qaTs[p]
                    oacc = oaccs[p]
                    xrx = xrxs[p]
                    yrb = yrbs[p]
                    xrxn = f'xrx{p}'
                    xrxhn = f'xrxh{p}'
                    yrbn = f'yrb{p}'
                    gsgn = f'gsg{p}'
                    oaccn = f'oacc{p}'
                    for c in range(8):
                        S.op('dve', lambda e, c=c: e.tensor_scalar(out=xc[:, c, :], in0=xrx[:, c, 3:131], scalar1=cw[:, 24 + c:25 + c],
                                                                   scalar2=cbv[:, c:c + 1], op0=ALU.mult, op1=ALU.add),
                             reads=[xrxn, xrxhn, 'cst'], writes=[f'xc{c}'])
                        yield
                        for j in (2, 1, 0):
                            S.op('dve', lambda e, c=c, j=j: e.scalar_tensor_tensor(
                                out=xc[:, c, :], in0=xrx[:, c, j:j + 128], scalar=cw[:, j * 8 + c:j * 8 + c + 1], in1=xc[:, c, :],
                                op0=ALU.mult, op1=ALU.add), reads=[xrxn, xrxhn, 'cst', f'xc{c}'], writes=[f'xc{c}'])
                    xcall = [f'xc{c}' for c in range(8)]
                    S.op('act', lambda e: e.activation(out=xcb[:], in_=xc[:], func=AF.Copy), reads=xcall, writes=['xcb'])
                    S.op('dve', lambda e: e.tensor_copy(out=xrxs[1 - p][:, :, 0:3], in_=xrx[:, :, 128:131]), reads=[xrxn], writes=[f'xrxh{1 - p}'])
                    for (wmat, wn, bias, dst, dn) in ((wa, 'wa', bav, rg, 'rg'), (wx, 'wx', bxv, ig, 'ig')):
                        for c4 in range(2):
                            pb = fab[c4]
                            pn = pjn[c4]
                            yield
                            for cc in range(4):
                                c = c4 * 4 + cc
                                S.op('pe', lambda e, c=c, cc=cc, pb=pb, wmat=wmat: e.matmul(
                                    out=pb[:, cc * 128:(cc + 1) * 128], lhsT=wmat[:, c, :], rhs=xcb[:, c, :], start=True, stop=True),
                                    reads=[wn, 'xcb'], writes=[pn])
                            S.op('dve', lambda e, c4=c4, pb=pb, bias=bias, dst=dst: e.tensor_tensor(
                                out=dst[:, 4 * c4:4 * c4 + 4, :], in0=pb[:, 0:512].rearrange("p (a b) -> p a b", a=4),
                                in1=bias[:, 4 * c4:4 * c4 + 4].unsqueeze(2).broadcast_to([128, 4, 128]), op=ALU.add),
                                reads=[pn, 'cst'], writes=[dn])
                        S.op('act', lambda e, dst=dst: e.activation(out=dst[:], in_=dst[:], func=AF.Tanh, scale=0.5), reads=[dn], writes=[dn])
                    yield
                    for c in range(8):
                        S.op('act', lambda e, c=c: e.activation(out=av[:, c, :], in_=rg[:, c, :], func=AF.Exp, scale=clh[:, c:c + 1],
                                                                bias=clh[:, c:c + 1]), reads=['rg', 'cl'], writes=['av'])
                    yield
                    S.op('act', lambda e: e.activation(out=bt[:], in_=av[:], func=AF.Square), reads=['av'], writes=['bt'])
                    S.op('act', lambda e: e.activation(out=bt[:], in_=bt[:], func=AF.Sqrt, scale=-1.0, bias=1.0), reads=['bt'], writes=['bt'])
                    S.op('dve', lambda e: e.scalar_tensor_tensor(out=ig[:], in0=ig[:], scalar=1.0, in1=xc[:], op0=ALU.add, op1=ALU.mult),
                         reads=['ig'] + xcall, writes=['ig'])
                    S.op('dve', lambda e: e.scalar_tensor_tensor(out=bt[:], in0=bt[:], scalar=0.5, in1=ig[:], op0=ALU.mult, op1=ALU.mult),
                         reads=['bt', 'ig'], writes=['bt'])
                    yield
                    for c in range(8):
                        S.op('dve', lambda e, c=c: e.tensor_tensor_scan(out=hs[:, c, :], data0=av[:, c, :], data1=bt[:, c, :],
                                                                        initial=hst[:, c:c + 1], op0=ALU.mult, op1=ALU.add),
                             reads=['av', 'bt', 'hst'], writes=['hs'])
                    S.op('dve', lambda e: e.tensor_copy(out=hst[:], in_=hs[:, :, 127]), reads=['hs'], writes=['hst'])
                    yield
                    yr2 = yrb[:].rearrange("p a b -> p (a b)")
                    gelu(128, 1024, yr2, yrbn, rg[:].rearrange("p a b -> p (a b)"), 'rg',
                         gt1[:], 'gt1', ig[:].rearrange("p a b -> p (a b)"), 'ig', twice=True)
                    S.op('dve', lambda e: e.scalar_tensor_tensor(out=olT[:], in0=hs[:], scalar=0.5, in1=rg[:], op0=ALU.mult, op1=ALU.mult),
                         reads=['hs', 'rg'], writes=['olT'])
                    S.dma('sp', lambda e, i=i: e.dma_start(out=sol_d[i * 128:(i + 1) * 128, :], in_=olT[:].rearrange("p a b -> p (a b)")),
                          reads=['olT'], writes=[f'sol{i}'])
                for _ in front(0):
                    pass
                for i in range(NT):
                    n_items = 2 * (i + 1) + 2 * (min(i, 4) + 1) + 3
                    gl = [(attn(i), float(n_items)), (lru(i), 24.0)]
                    if i + 1 < NT:
                        gl.append((front(i + 1), 22.0))
                    drive(gl)
                S.barrier()

        def drive(gens_w):
            gens = [[g_, w_, 0.0] for (g_, w_) in gens_w]
            if DBG.get('seq'):
                for ent in gens:
                    for _ in ent[0]:
                        pass
                return
            wmax = max(w_ for (_, w_) in gens_w)
            while gens:
                for ent in list(gens):
                    ent[2] += ent[1] / wmax
                    while ent[2] >= 1.0:
                        ent[2] -= 1.0
                        try:
                            next(ent[0])
                        except StopIteration:
                            gens.remove(ent)
                            break

        def pass2():
            with ExitStack() as st:
                wgm = sbt(st, "wgm", [128, 8, 2048], BF16)
                wnu = sbt(st, "wnu", [128, 4, 1024], BF16)
                wlu = sbt(st, "wlu", [128, 8, 1024], BF16)
                wo = sbt(st, "wo", [128, 8, 1024], BF16)
                xb = [sbt(st, f"xb{i}", [128, D], F32) for i in range(2)]
                xs = sbt(st, "xs", [128, D], F32)
                ss = sbt(st, "ss", [128, 4], F32)
                xnT = sbt(st, "xnT", [128, 8, 128], BF16)
                gms = [sbt(st, f"gm{i}", [128, 16, 128], F32) for i in range(2)]
                onT = [sbt(st, f"onT{i}", [128, 4, 128], BF16) for i in range(2)]
                olT = [sbt(st, f"olT{i}", [128, 8, 128], BF16) for i in range(2)]
                t1 = sbt(st, "t1", [128, 4, 128], F32)
                t2 = sbt(st, "t2", [128, 4, 128], F32)
                mT = sbt(st, "mT", [128, 8, 128], BF16)
                ob = sbt(st, "ob", [128, D], F32)
                tp = pst(st, "tp", [128, 1024], F32)
                pj = [pst(st, f"pj{i}", [128, 512], F32) for i in range(2)]
                pa = pst(st, "pa", [128, 512], F32)
                pbk = pst(st, "pbk", [128, 512], F32)
                yp = pst(st, "yp", [128, 1024], F32)
                load_w(wgm, wgm_d, 8, 2048, 'wgm')
                load_w(wnu, wnu_d, 4, 1024, 'wnu')
                load_w(wlu, wlu_d, 8, 1024, 'wlu')
                load_w(wo, wo_d, 8, 1024, 'wo')

                def front(i):
                    p = i % 2
                    b = xb[p]
                    bn = f"xb{p}"
                    gm = gms[p]
                    on_ = onT[p]
                    ol_ = olT[p]
                    S.dma('sp', lambda e: e.dma_start(out=b[:], in_=x_d[i * 128:(i + 1) * 128, :]), writes=[bn])
                    S.dma('sp', lambda e: e.dma_start(out=on_[:].rearrange("p a b -> p (a b)"), in_=son_d[i * 128:(i + 1) * 128, :]),
                          reads=[f'son{i}'], writes=[f"onT{p}"])
                    S.dma('sp', lambda e: e.dma_start(out=ol_[:].rearrange("p a b -> p (a b)"), in_=sol_d[i * 128:(i + 1) * 128, :]),
                          reads=[f'sol{i}'], writes=[f"olT{p}"])
                    yield
                    norm_tile(b, bn, 'nw1', xs, ss, tp, xnT)
                    yield
                    for c4 in range(4):
                        pb = pj[c4 % 2]
                        pn = f"pj{c4 % 2}"
                        for cc in range(4):
                            c = c4 * 4 + cc
                            for k in range(8):
                                S.op('pe', lambda e, c=c, cc=cc, k=k, pb=pb: e.matmul(
                                    out=pb[:, cc * 128:(cc + 1) * 128], lhsT=wgm[:, k, c * 128:(c + 1) * 128], rhs=xnT[:, k, :],
                                    start=(k == 0), stop=(k == 7)), reads=['wgm', 'xnT'], writes=[pn])
                        S.op('act', lambda e, c4=c4, pb=pb: e.activation(out=gm[:, c4 * 4:(c4 + 1) * 4, :],
                                                                         in_=pb[:, 0:512].rearrange("p (a b) -> p a b", a=4), func=AF.Tanh, scale=0.5),
                             reads=[pn], writes=[f'gm{p}_{c4}'])
                        yield

                def back(i):
                    p = i % 2
                    b = xb[p]
                    bn = f"xb{p}"
                    gm = gms[p]
                    on_ = onT[p]
                    ol_ = olT[p]
                    onn = f"onT{p}"
                    oln = f"olT{p}"
                    for c4 in range(2):
                        for cc in range(4):
                            c = c4 * 4 + cc
                            for k in range(4):
                                S.op('pe', lambda e, c=c, cc=cc, k=k: e.matmul(
                                    out=pa[:, cc * 128:(cc + 1) * 128], lhsT=wnu[:, k, c * 128:(c + 1) * 128], rhs=on_[:, k, :],
                                    start=(k == 0), stop=(k == 3)), reads=['wnu', onn], writes=['pa'])
                        for cc in range(4):
                            c = c4 * 4 + cc
                            for k in range(8):
                                S.op('pe', lambda e, c=c, cc=cc, k=k: e.matmul(
                                    out=pbk[:, cc * 128:(cc + 1) * 128], lhsT=wlu[:, k, c * 128:(c + 1) * 128], rhs=ol_[:, k, :],
                                    start=(k == 0), stop=(k == 7)), reads=['wlu', oln], writes=['pbk'])
                        S.op('dve', lambda e, c4=c4: e.scalar_tensor_tensor(
                            out=t1[:], in0=gm[:, c4 * 4:(c4 + 1) * 4, :], scalar=1.0, in1=pa[:, 0:512].rearrange("p (a b) -> p a b", a=4),
                            op0=ALU.add, op1=ALU.mult), reads=['pa', f'gm{p}_{c4}'], writes=['t1'])
                        S.op('dve', lambda e, c4=c4: e.scalar_tensor_tensor(
                            out=t2[:], in0=gm[:, 8 + c4 * 4:8 + (c4 + 1) * 4, :], scalar=1.0, in1=pbk[:, 0:512].rearrange("p (a b) -> p a b", a=4),
                            op0=ALU.add, op1=ALU.mult), reads=['pbk', f'gm{p}_{c4 + 2}'], writes=['t2'])
                        S.op('dve', lambda e, c4=c4: e.tensor_tensor(out=mT[:, c4 * 4:(c4 + 1) * 4, :], in0=t1[:], in1=t2[:], op=ALU.add),
                             reads=['t1', 't2'], writes=[f'mT{c4}'])
                        yield
                    for n in range(2):
                        for k in range(8):
                            S.op('pe', lambda e, n=n, k=k: e.matmul(out=yp[:, n * 512:(n + 1) * 512], lhsT=mT[:, k, :],
                                                                    rhs=wo[:, k, n * 512:(n + 1) * 512], start=(k == 0), stop=(k == 7)),
                                 reads=[f'mT{k // 4}', 'wo'], writes=[f'yp{n}'])
                        yield
                    S.op('dve', lambda e: e.scalar_tensor_tensor(out=ob[:], in0=yp[:], scalar=0.5, in1=b[:], op0=ALU.mult, op1=ALU.add),
                         reads=['yp0', 'yp1', bn], writes=['ob'])
                    S.dma('sp', lambda e: e.dma_start(out=out_d[i * 128:(i + 1) * 128, :], in_=ob[:]), reads=['ob'], writes=[f'outd{i}'])

                for _ in front(0):
                    pass
                for i in range(NT):
                    gl = [(back(i), 4.0)]
                    if i + 1 < NT:
                        gl.append((front(i + 1), 6.0))
                    drive(gl)
                S.barrier()

        def pass3():
            with ExitStack() as st:
                w1 = sbt(st, "w1s", [128, 8, DFF], BF16)
                w2 = sbt(st, "w2s", [128, 32, D], BF16)
                hb = [sbt(st, f"hb{i}", [128, D], F32) for i in range(2)]
                xs = sbt(st, "xs", [128, D], F32)
                ss = sbt(st, "ss", [128, 4], F32)
                hnTs = [sbt(st, f"hnT{i}", [128, 8, 128], BF16) for i in range(2)]
                rl = [sbt(st, f"rl{i}", [128, 512], F32) for i in range(2)]
                hid = sbt(st, "hid", [128, 32, 128], BF16)
                ob = sbt(st, "ob", [128, D], F32)
                tp = pst(st, "tp", [128, 1024], F32)
                hp = [pst(st, f"hp{i}", [128, 512], F32) for i in range(2)]
                yp = pst(st, "yp", [128, 1024], F32)
                load_w(w1, wf1_d, 8, DFF, 'w1')
                load_w(w2, wf2_d, 32, D, 'w2')
                src_d = out_d if 2 in passes else x_d

                def front(t):
                    p = t % 2
                    b = hb[p]
                    bn = f"hb{p}"
                    S.dma('sp', lambda e: e.dma_start(out=b[:], in_=src_d[t * 128:(t + 1) * 128, :]),
                          reads=[f'outd{t}'], writes=[bn])
                    yield
                    norm_tile(b, bn, 'nw2', xs, ss, tp, hnTs[p], outn=f'hnT{p}')
                    yield

                def back(t):
                    p = t % 2
                    b = hb[p]
                    bn = f"hb{p}"
                    hnT = hnTs[p]
                    for c4 in range(8):
                        pb = hp[c4 % 2]
                        pn = f"hp{c4 % 2}"
                        for cc in range(4):
                            c = c4 * 4 + cc
                            for k in range(8):
                                S.op('pe', lambda e, c=c, cc=cc, k=k, pb=pb: e.matmul(
                                    out=pb[:, cc * 128:(cc + 1) * 128], lhsT=w1[:, k, c * 128:(c + 1) * 128], rhs=hnT[:, k, :],
                                    start=(k == 0), stop=(k == 7)), reads=['w1', f'hnT{p}'], writes=[pn])
                        r = rl[c4 % 2]
                        rn = f"rl{c4 % 2}"
                        S.op('act', lambda e, r=r, pb=pb: e.activation(out=r[:], in_=pb[:], func=AF.Relu), reads=[pn], writes=[rn])
                        S.op('dve', lambda e, r=r, c4=c4: e.tensor_tensor(
                            out=hid[:, c4 * 4:(c4 + 1) * 4, :], in0=r[:].rearrange("p (a b) -> p a b", a=4),
                            in1=r[:].rearrange("p (a b) -> p a b", a=4), op=ALU.mult), reads=[rn], writes=[f'hid{c4}'])
                        yield
                    for n in range(2):
                        for k in range(32):
                            S.op('pe', lambda e, n=n, k=k: e.matmul(out=yp[:, n * 512:(n + 1) * 512], lhsT=hid[:, k, :],
                                                                    rhs=w2[:, k, n * 512:(n + 1) * 512], start=(k == 0), stop=(k == 31)),
                                 reads=[f'hid{k // 4}', 'w2'], writes=[f'yp{n}'])
                            if k % 8 == 7:
                                yield
                    S.op('dve', lambda e: e.tensor_tensor(out=ob[:], in0=yp[:], in1=b[:], op=ALU.add),
                         reads=['yp0', 'yp1', bn], writes=['ob'])
                    S.dma('sp', lambda e: e.dma_start(out=out_d[t * 128:(t + 1) * 128, :], in_=ob[:]), reads=['ob'], writes=[f'outd{t}'])

                for _ in front(0):
                    pass
                for t in range(NT):
                    gl = [(back(t), 16.0)]
                    if t + 1 < NT:
                        gl.append((front(t + 1), 2.0))
                    drive(gl)
        for n_, f_ in enumerate((pass0, pass1, pass2, pass3)):
            if n_ in passes:
                f_()
        S.finish()
        with nc.Block() as block:
            S.emit(block)
    return nc


def _kmaj(w, nk):
    n = w.shape[1]
    return np.ascontiguousarray(w.reshape(nk, 128, n).transpose(1, 0, 2).reshape(128, nk * n))


def _colmaj(v):
    return np.ascontiguousarray(v.reshape(-1, 128).T)


def build_consts(NT, inp):
    S_ = NT * 128
    NCMP = (S_ - 32) // 16 + 1
    CO, NCST = cst_layout(NT)
    cst = np.zeros((128, NCST), np.float32)

    def put(name, arr):
        a, b = CO[name]
        cst[:, a:b] = arr
    put('ident', np.eye(128, dtype=np.float32))
    put('nw1', _colmaj(inp['norm1_w'][0]))
    put('nw2', _colmaj(inp['norm2_w'][0]))
    put('eps', np.full((128, 1), EPS, np.float32))
    put('wq', np.tile(inp['q_norm_w'][0][None, :], (128, 1)))
    for j in range(3):
        put(f'wk{j}', np.tile(inp['k_norm_w'][0, j][None, :], (128, 1)))
    put('posk', np.tile(inp['phi_k_pos'][0].T, (2, 1)))
    put('posv', np.tile(inp['phi_v_pos'][0].T, (2, 1)))
    cw = inp['conv_w'][0]
    put('cw', np.concatenate([_colmaj(cw[j]) for j in range(4)], axis=1))
    put('cb', _colmaj(inp['conv_b'][0]))
    put('ba', _colmaj(inp['lru_ba'][0].reshape(-1)))
    put('bx', _colmaj(inp['lru_bx'][0].reshape(-1)))
    put('lam', _colmaj(inp['lru_lambda'][0]))
    half = 8
    inv = (np.float32(500000.0) ** (-np.arange(half, dtype=np.float32) / np.float32(half))).astype(np.float32)
    pos = np.arange(S_, dtype=np.float32)
    ang = (pos[:, None] * inv[None, :]).astype(np.float32)
    cos = np.cos(ang).astype(np.float32).reshape(NT, 128, 8).transpose(1, 0, 2).reshape(128, NT * 8)
    sin = np.sin(ang).astype(np.float32).reshape(NT, 128, 8).transpose(1, 0, 2).reshape(128, NT * 8)
    put('cos', cos)
    put('sin', sin)
    cend = (np.arange(256) * 16 + 31).astype(np.float32)
    angc = (cend[:, None] * inv[None, :]).astype(np.float32)
    put('cosc', np.cos(angc).astype(np.float32).reshape(2, 128, 8).transpose(1, 0, 2).reshape(128, 16))
    put('sinc', np.sin(angc).astype(np.float32).reshape(2, 128, 8).transpose(1, 0, 2).reshape(128, 16))
    A = np.zeros((128, 127), np.float32)
    B = np.ones((128, 127), np.float32)
    for q in range(128):
        cur = 1 if q >= 64 else 0
        for r in range(127):
            rel = r - 63
            if rel > cur:
                A[q, r] = -1e30
                B[q, r] = 0.0
            elif rel == cur or rel == cur - 1:
                A[q, r] = 1e4
                B[q, r] = 0.0
    put('tkA', A)
    put('tkB', B)
    ov = np.zeros((256, 65), np.float32)
    for n in range(NCMP):
        for j in range(min(64, S_ // 64)):
            if 16 * n <= 64 * j + 63 and 16 * n + 31 >= 64 * j:
                ov[n, j] = 1.0
        ov[n, 64] = 1.0
    put('ovl', ov.reshape(2, 128, 65).transpose(1, 0, 2).reshape(128, 130))
    put('mhalf', np.full((128, 8), -0.5, np.float32))
    E = np.zeros((64, S_), np.float32)
    for j in range(S_ // 64):
        E[j, j * 64:(j + 1) * 64] = 1.0
    return cst, E


def build_masks():
    r = np.arange(128)[:, None]
    q = np.arange(128)[None, :]
    mc = np.where(r > q, NEGM, 0.0).astype(np.float32)
    ml = np.where(r <= q, NEGM, 0.0).astype(np.float32)
    mk = np.concatenate([np.tile(mc, (1, 4)), np.tile(ml, (1, 4))], axis=1)
    jw = np.zeros((8, 792), np.float32)
    for rr in range(8):
        jw[rr, rr + 264] = 1.0
        w = np.where(np.arange(128) < 16 * rr + 15, NEGM, 0.0).astype(np.float32)
        jw[rr, 280:792] = np.tile(w, 4)
    return np.ascontiguousarray(mk), jw


def host_weights(inp):
    w_in = inp['w_in'][0]
    cols = lambda a, b: w_in[:, a:b]
    wtm = np.concatenate([cols(0, 512), cols(768, 896), cols(1024, 1152), cols(896, 1024), cols(1152, 1280), cols(1280, 1304)], axis=1)
    wcv = cols(512, 768)
    wxy = cols(1304, 3352)
    wgm = cols(3352, 5400)

    def phi1(w):
        a = w.reshape(32, 64, 256).transpose(1, 0, 2).reshape(64, 32 * 256)
        return np.ascontiguousarray(np.concatenate([a, a], axis=0))
    m = {
        'wtm': _kmaj(wtm, 8), 'wcv': _kmaj(wcv, 8), 'wxy': _kmaj(wxy, 8), 'wgm': _kmaj(wgm, 8),
        'pk1': phi1(inp['phi_k_w1'][0]), 'pv1': phi1(inp['phi_v_w1'][0]),
        'pk2': _kmaj(inp['phi_k_w2'][0], 2), 'pv2': _kmaj(inp['phi_v_w2'][0], 2),
        'wa': np.ascontiguousarray(inp['lru_wa'][0].transpose(1, 0, 2).reshape(128, 1024)),
        'wx': np.ascontiguousarray(inp['lru_wx'][0].transpose(1, 0, 2).reshape(128, 1024)),
        'wnu': _kmaj(inp['w_nsa_up'][0], 4), 'wlu': _kmaj(inp['w_lru_up'][0], 8), 'wo': _kmaj(inp['w_o'][0], 8),
        'wf1': _kmaj(inp['w_ff1'][0], 8), 'wf2': _kmaj(inp['w_ff2'][0], 32),
    }
    return m


def kernel(**inputs):
    inp = {k: np.asarray(v, dtype=np.float32) for k, v in inputs.items()}
    x = inp['x']
    B, S_, _ = x.shape
    NT = S_ // 128
    cst, E = build_consts(NT, inp)
    wm = host_weights(inp)
    mkm, jwm = build_masks()
    nc = build_program(NT)
    in_maps = []
    for b in range(B):
        m = dict(wm)
        m['x'] = np.ascontiguousarray(x[b])
        m['cst'] = cst
        m['emat'] = E
        m['mk'], m['jw'] = mkm, jwm
        in_maps.append(m)
    res = run_bass_kernel_spmd(nc, in_maps, core_ids=list(range(B)))
    return np.stack([np.asarray(r['out'], dtype=np.float32) for r in res.results], axis=0)
```

```python
import math
from contextlib import ExitStack
import numpy as np
import concourse.bass as bass
import concourse.mybir as mybir
from concourse.bass_utils import run_bass_kernel_spmd

F32 = mybir.dt.float32
BF16 = mybir.dt.bfloat16
AF = mybir.ActivationFunctionType
ALU = mybir.AluOpType
AX = mybir.AxisListType

D = 1024
DFF = 4096
EPS = 1e-6
NEGM = -30000.0
GC1 = 0.044715
GC2 = 2.0 * math.sqrt(2.0 / math.pi)


class Sched:
    CE = ('pe', 'dve', 'act', 'pool')

    def __init__(self, nc, stack, ndma=8, limit=16000, strict_same=True):
        self.nc = nc
        self.stack = stack
        self.limit = limit
        self.strict_same = strict_same
        self.prog = {e: [] for e in ('pe', 'dve', 'act', 'pool', 'sp')}
        self.nsem = 0
        self.cur_sem = {}
        self.cnt = {}
        for e in self.CE:
            self.cur_sem[e] = self._newsem(e)
            self.cnt[e] = 0
        self.dma_sems = {q: [self._newsem('d' + q) for _ in range(ndma)] for q in ('sp', 'pool')}
        self.dma_n = {q: 0 for q in ('sp', 'pool')}
        self.waited = {e: {} for e in self.prog}
        self.lastw = {}
        self.readers = {}
        self.last_tok = {}

    def _newsem(self, tag):
        self.nsem += 1
        s = self.stack.enter_context(self.nc.semaphore(f"s{self.nsem}_{tag}"))
        return (self.nsem, s)

    PSUM = frozenset(['tp0', 'tp1', 'cp0', 'cp1', 'hp0', 'hp1', 'kp', 'tb', 'pj0', 'pj1', 'st0', 'st1', 'st2', 'st3',
                      'pv', 'pv0', 'pv1', 'mz', 'pa', 'pbk', 'yp0', 'yp1', 'fa', 'fb'])

    def _deps(self, reads, writes, eng=None):
        deps = []
        for b in reads:
            if b in self.lastw:
                deps.append(self.lastw[b])
            if b in self.PSUM:
                deps.extend(t for t in self.readers.get(b, ()) if t[2] != eng)
        for b in writes:
            if b in self.lastw:
                deps.append(self.lastw[b])
            deps.extend(self.readers.get(b, ()))
        return deps

    def _waits(self, eng, deps):
        waits = []
        w = self.waited[eng]
        for (sem, val, src) in deps:
            if src == eng and (eng == 'pe' or not self.strict_same):
                continue
            if w.get(sem[0], 0) >= val:
                continue
            w[sem[0]] = val
            waits.append((sem[1], val))
        return waits

    def _commit(self, tok, reads, writes):
        for b in reads:
            if b not in writes:
                self.readers.setdefault(b, []).append(tok)
        for b in writes:
            self.lastw[b] = tok
            self.readers[b] = []

    def op(self, eng, fn, reads=(), writes=()):
        deps = self._deps(reads, writes, eng)
        waits = self._waits(eng, deps)
        if self.cnt[eng] >= self.limit:
            self.cur_sem[eng] = self._newsem(eng)
            self.cnt[eng] = 0
        self.cnt[eng] += 1
        sem = self.cur_sem[eng]
        tok = (sem, self.cnt[eng], eng)
        self.last_tok[eng] = tok
        self.prog[eng].append((waits, fn, (sem[1], 1)))
        self._commit(tok, reads, writes)
        return tok

    def dma(self, q, fn, reads=(), writes=()):
        deps = self._deps(reads, writes)
        j = self.dma_n[q]
        self.dma_n[q] += 1
        K = len(self.dma_sems[q])
        sem = self.dma_sems[q][j % K]
        if j >= K:
            deps.append((sem, 16 * (j // K), 'dma'))
        waits = self._waits(q, deps)
        tok = (sem, 16 * (j // K + 1), 'dma')
        self.prog[q].append((waits, fn, (sem[1], 16)))
        self._commit(tok, reads, writes)
        return tok

    def _dma_final(self):
        deps = []
        for q in self.dma_sems:
            K = len(self.dma_sems[q])
            n = self.dma_n[q]
            for i, sem in enumerate(self.dma_sems[q]):
                cnt = (n - i + K - 1) // K if n > i else 0
                if cnt > 0:
                    deps.append((sem, 16 * cnt, 'dma'))
        return deps

    def barrier(self):
        deps = self._dma_final() + list(self.last_tok.values())
        for e in self.prog:
            saved = self.strict_same
            self.strict_same = True
            waits = []
            w = self.waited[e]
            for (sem, val, src) in deps:
                if w.get(sem[0], 0) >= val:
                    continue
                w[sem[0]] = val
                waits.append((sem[1], val))
            self.strict_same = saved
            self.prog[e].append((waits, None, None))
        self.lastw = {}
        self.readers = {}

    def finish(self):
        self.barrier()

    def emit(self, block):
        def run(engobj, items):
            for waits, fn, inc in items:
                for (s, v) in waits:
                    engobj.wait_ge(s, v)
                if fn is not None:
                    ins = fn(engobj)
                    ins.then_inc(inc[0], inc[1])

        P = self.prog

        @block.sync
        def _(e):
            run(e, P['sp'])

        @block.tensor
        def _(e):
            run(e, P['pe'])

        @block.vector
        def _(e):
            run(e, P['dve'])

        @block.scalar
        def _(e):
            run(e, P['act'])

        @block.gpsimd
        def _(e):
            run(e, P['pool'])


def cst_layout(NT):
    off = {}
    o = 0

    def add(name, n):
        nonlocal o
        off[name] = (o, o + n)
        o += n
    add('ident', 128)
    add('nw1', 8)
    add('nw2', 8)
    add('eps', 1)
    add('wq', 64)
    add('wk0', 64)
    add('wk1', 64)
    add('wk2', 64)
    add('posk', 32)
    add('posv', 32)
    add('cw', 32)
    add('cb', 8)
    add('ba', 8)
    add('bx', 8)
    add('lam', 8)
    add('cos', NT * 8)
    add('sin', NT * 8)
    add('cosc', 16)
    add('sinc', 16)
    add('tkA', 127)
    add('tkB', 127)
    add('ovl', 130)
    add('mhalf', 8)
    return off, o


DBG = {'stop': 99}


def build_program(NT=32, passes=(0, 1, 2, 3), dbg=False):
    S_ = NT * 128
    NCMP = (S_ - 32) // 16 + 1
    NSLC = S_ // 64
    CO, NCST = cst_layout(NT)
    nc = bass.Bass("TRN2", target_bir_lowering=False)

    def din(name, shape, dt=F32):
        return nc.dram_tensor(name, shape, dt, kind="ExternalInput").ap()

    x_d = din("x", [S_, D])
    cst_d = din("cst", [128, NCST])
    wtm_d = din("wtm", [128, 8 * 1048])
    wcv_d = din("wcv", [128, 8 * 256])
    wxy_d = din("wxy", [128, 8 * 2048])
    wgm_d = din("wgm", [128, 8 * 2048])
    pk1_d = din("pk1", [128, 32 * 256])
    pv1_d = din("pv1", [128, 32 * 256])
    pk2_d = din("pk2", [128, 2 * 64])
    pv2_d = din("pv2", [128, 2 * 64])
    wa_d = din("wa", [128, 8 * 128])
    wx_d = din("wx", [128, 8 * 128])
    wnu_d = din("wnu", [128, 4 * 1024])
    wlu_d = din("wlu", [128, 8 * 1024])
    wo_d = din("wo", [128, 8 * 1024])
    wf1_d = din("wf1", [128, 8 * DFF])
    wf2_d = din("wf2", [128, 32 * D])
    E_d = din("emat", [64, S_])
    mk_d = din("mk", [128, 1024])
    jw_d = din("jw", [8, 792])
    out_d = nc.dram_tensor("out", [S_, D], F32, kind="ExternalOutput").ap()
    son_d = nc.dram_tensor("sc_on", [NT * 128, 512], BF16, kind="Internal").ap()
    sol_d = nc.dram_tensor("sc_ol", [NT * 128, 1024], BF16, kind="Internal").ap()

    with ExitStack() as top:
        S = Sched(nc, top)

        uniq = [0]

        def sbt(st, name, shape, dt):
            uniq[0] += 1
            return st.enter_context(nc.sbuf_tensor(f"sb{uniq[0]}_{name}", shape, dt))

        def pst(st, name, shape, dt):
            uniq[0] += 1
            return st.enter_context(nc.psum_tensor(f"ps{uniq[0]}_{name}", shape, dt))

        cst = sbt(top, "cst", [128, NCST], F32)
        identb = sbt(top, "identb", [128, 128], BF16)
        kcT = sbt(top, "kcT", [64, 2, 256], BF16)
        vca = sbt(top, "vca", [128, 2, 2, 129], BF16)

        def C(name):
            a, b = CO[name]
            return cst[:, a:b]
        ident = C('ident')
        epsc = C('eps')

        S.dma('sp', lambda e: e.dma_start(out=cst[:], in_=cst_d[:, :]), writes=['cst'])
        S.op('dve', lambda e: e.tensor_copy(out=identb[:], in_=ident), reads=['cst'], writes=['identb'])
        mhalfw = sbt(top, "mhalfw", [128, 16], F32)
        S.op('pool', lambda e: e.memset(mhalfw[:], -0.5), writes=['cst_mh'])

        def load_w(dst3, src_d, nk, ncol, name):
            step = max(1, 2048 // ncol)
            if ncol > 2048:
                for k in range(nk):
                    for c0 in range(0, ncol, 2048):
                        c1 = min(ncol, c0 + 2048)
                        S.dma('pool', lambda e, k=k, c0=c0, c1=c1: e.dma_start(
                            out=dst3[:, k, c0:c1], in_=src_d[:, k * ncol + c0:k * ncol + c1]), writes=[name])
            else:
                for k0 in range(0, nk, step):
                    k1 = min(nk, k0 + step)
                    S.dma('pool', lambda e, k0=k0, k1=k1: e.dma_start(
                        out=dst3[:, k0:k1, :],
                        in_=src_d[:, k0 * ncol:k1 * ncol].rearrange("p (a b) -> p a b", a=k1 - k0)), writes=[name])

        def norm_tile(xb, xbn, nwname, xs, ss, tp, xnT, tpn=('tp0', 'tp1'), outn='xnT'):
            if isinstance(tp, (list, tuple)):
                banks = tp
            else:
                banks = (tp[:, 0:512], tp[:, 512:1024])
            S.op('act', lambda e: e.activation(out=xs[:], in_=xb[:], func=AF.Square, accum_out=ss[:, 0:1]),
                 reads=[xbn], writes=['xs', 'ss0'])
            S.op('pool', lambda e: e.tensor_scalar(out=ss[:, 1:2], in0=ss[:, 0:1], scalar1=1.0 / D, scalar2=EPS, op0=ALU.mult, op1=ALU.add),
                 reads=['ss0'], writes=['ss1'])
            S.op('pool', lambda e: e.tensor_tensor(out=ss[:, 2:3], in0=ss[:, 1:2], in1=C('mhalf')[:, 0:1], op=ALU.pow),
                 reads=['ss1', 'cst'], writes=['ss2'])
            S.op('act', lambda e: e.activation(out=xs[:], in_=xb[:], func=AF.Copy, scale=ss[:, 2:3]),
                 reads=[xbn, 'ss2'], writes=['xs'])
            for k in range(8):
                S.op('pe', lambda e, k=k: e.transpose(out=banks[k // 4][:, (k % 4) * 128:(k % 4 + 1) * 128],
                                                      in_=xs[:, k * 128:(k + 1) * 128],
                                                      identity=ident), reads=['xs', 'cst'], writes=[tpn[k // 4]])
            nw = C(nwname)
            for a in range(2):
                S.op('dve', lambda e, a=a: e.tensor_tensor(
                    out=xnT[:, 4 * a:4 * a + 4, :], in0=banks[a].rearrange("p (a b) -> p a b", a=4),
                    in1=nw[:, 4 * a:4 * a + 4].unsqueeze(2).broadcast_to([128, 4, 128]), op=ALU.mult),
                     reads=[tpn[a], 'cst'], writes=[outn])

        def gelu(P_, shape_free, src, srcn, dst, dstn, t1, t1n, t2, t2n, twice=False):
            S.op('act', lambda e: e.activation(out=t1, in_=src, func=AF.Square), reads=[srcn], writes=[t1n])
            S.op('dve', lambda e: e.tensor_scalar(out=t1, in0=t1, scalar1=GC1, scalar2=1.0, op0=ALU.mult, op1=ALU.add),
                 reads=[t1n], writes=[t1n])
            S.op('dve', lambda e: e.tensor_tensor(out=t1, in0=t1, in1=src, op=ALU.mult), reads=[t1n, srcn], writes=[t1n])
            S.op('act', lambda e: e.activation(out=t2, in_=t1, func=AF.Tanh, scale=GC2 * 0.5), reads=[t1n], writes=[t2n])
            if twice:
                S.op('dve', lambda e: e.scalar_tensor_tensor(out=dst, in0=t2, scalar=1.0, in1=src, op0=ALU.add, op1=ALU.mult),
                     reads=[t2n, srcn], writes=[dstn])
            else:
                S.op('dve', lambda e: e.scalar_tensor_tensor(out=t1, in0=t2, scalar=1.0, in1=src, op0=ALU.add, op1=ALU.mult),
                     reads=[t2n, srcn], writes=[t1n])
                S.op('act', lambda e: e.activation(out=dst, in_=t1, func=AF.Copy, scale=0.5), reads=[t1n], writes=[dstn])

        def rmsrope(P_, H, src3, srcn, wrep, cosp, sinp, out3, outn, tmp, pre):
            sq = tmp['sq'][0:P_, 0:H, :]
            y = tmp['y'][0:P_, 0:H, :]
            st = tmp['st']
            r1 = tmp['r1'][0:P_, 0:H, :]
            r2 = tmp['r2'][0:P_, 0:H, :]
            S.op('act', lambda e: e.activation(out=sq, in_=src3, func=AF.Square), reads=[srcn], writes=[pre + 'sq'])
            S.op('dve', lambda e: e.tensor_reduce(out=st[0:P_, 0:H], in_=sq, axis=AX.X, op=ALU.add),
                 reads=[pre + 'sq'], writes=[pre + 'st0'])
            S.op('pool', lambda e: e.tensor_scalar(out=st[0:P_, 16:16 + H], in0=st[0:P_, 0:H], scalar1=1.0 / 64, scalar2=EPS,
                                                   op0=ALU.mult, op1=ALU.add), reads=[pre + 'st0'], writes=[pre + 'st1'])
            S.op('pool', lambda e: e.tensor_tensor(out=st[0:P_, 32:32 + H], in0=st[0:P_, 16:16 + H], in1=mhalfw[0:P_, 0:H], op=ALU.pow),
                 reads=[pre + 'st1', 'cst_mh'], writes=[pre + 'st2'])
            S.op('dve', lambda e: e.tensor_tensor(out=y, in0=src3,
                                                  in1=st[0:P_, 32:32 + H].unsqueeze(2).broadcast_to([P_, H, 64]), op=ALU.mult),
                 reads=[srcn, pre + 'st2'], writes=[pre + 'y'])
            wfull = wrep if len(wrep.shape) == 3 else wrep.unsqueeze(1).broadcast_to([P_, H, 64])
            S.op('dve', lambda e: e.tensor_tensor(out=y, in0=y, in1=wfull, op=ALU.mult),
                 reads=[pre + 'y', 'cst', 'wq8', 'wqk'], writes=[pre + 'y'])
            cb_ = cosp.unsqueeze(1).broadcast_to([P_, H, 8])
            sb_ = sinp.unsqueeze(1).broadcast_to([P_, H, 8])
            y1 = y[:, :, 0:8]
            y2 = y[:, :, 8:16]
            S.op('dve', lambda e: e.tensor_tensor(out=r1, in0=y1, in1=cb_, op=ALU.mult), reads=[pre + 'y', 'cst'], writes=[pre + 'r1'])
            S.op('dve', lambda e: e.tensor_tensor(out=r2, in0=y2, in1=sb_, op=ALU.mult), reads=[pre + 'y', 'cst'], writes=[pre + 'r2'])
            S.op('dve', lambda e: e.tensor_tensor(out=out3[:, :, 0:8], in0=r1, in1=r2, op=ALU.subtract),
                 reads=[pre + 'r1', pre + 'r2'], writes=[outn])
            S.op('dve', lambda e: e.tensor_tensor(out=r1, in0=y2, in1=cb_, op=ALU.mult), reads=[pre + 'y', 'cst'], writes=[pre + 'r1'])
            S.op('dve', lambda e: e.tensor_tensor(out=r2, in0=y1, in1=sb_, op=ALU.mult), reads=[pre + 'y', 'cst'], writes=[pre + 'r2'])
            S.op('dve', lambda e: e.tensor_tensor(out=out3[:, :, 8:16], in0=r1, in1=r2, op=ALU.add),
                 reads=[pre + 'r1', pre + 'r2'], writes=[outn])
            S.op('act', lambda e: e.activation(out=out3[:, :, 16:64], in_=y[:, :, 16:64], func=AF.Copy),
                 reads=[pre + 'y'], writes=[outn])

        def pass0():
            with ExitStack() as st:
                wcv = sbt(st, "wcv", [128, 8, 256], BF16)
                pk1 = sbt(st, "pk1", [128, 32, 256], BF16)
                pv1 = sbt(st, "pv1", [128, 32, 256], BF16)
                pk2 = sbt(st, "pk2", [128, 2, 64], BF16)
                pv2 = sbt(st, "pv2", [128, 2, 64], BF16)
                rawk = sbt(st, "rawk", [128, S_ + 16], F32)
                rawv = sbt(st, "rawv", [128, S_ + 16], F32)
                zk = sbt(st, "zk", [128, 32, 256], BF16)
                zv = sbt(st, "zv", [128, 32, 256], BF16)
                xb = [sbt(st, f"xb{i}", [128, D], F32) for i in range(2)]
                xs = sbt(st, "xs", [128, D], F32)
                ss = sbt(st, "ss", [128, 4], F32)
                xnT = sbt(st, "xnT", [128, 8, 128], BF16)
                hT = sbt(st, "hT", [128, 2, 256], BF16)
                g1 = sbt(st, "g1", [128, 256], F32)
                g2 = sbt(st, "g2", [128, 256], F32)
                kc32 = sbt(st, "kc32", [128, 64], F32)
                kcn = sbt(st, "kcn", [128, 64], BF16)
                tmp = dict(sq=sbt(st, "t_sq", [128, 12, 64], F32), y=sbt(st, "t_y", [128, 12, 64], F32),
                           st=sbt(st, "t_st", [128, 48], F32), r1=sbt(st, "t_r1", [128, 12, 8], F32),
                           r2=sbt(st, "t_r2", [128, 12, 8], F32))
                tp = pst(st, "tp", [128, 1024], F32)
                cp = [pst(st, f"cp{i}", [128, 512], F32) for i in range(2)]
                hp = [pst(st, f"hp{i}", [128, 512], F32) for i in range(2)]
                kp = pst(st, "kp", [128, 512], F32)
                tb = pst(st, "tb", [128, 1024], BF16)

                load_w(wcv, wcv_d, 8, 256, 'wcv')
                load_w(pk1, pk1_d, 32, 256, 'pk1')
                load_w(pv1, pv1_d, 32, 256, 'pv1')
                load_w(pk2, pk2_d, 2, 64, 'pk2')
                load_w(pv2, pv2_d, 2, 64, 'pv2')
                for t in range(NT):
                    b = xb[t % 2]
                    bn = f"xb{t % 2}"
                    S.dma('sp', lambda e, b=b, t=t: e.dma_start(out=b[:], in_=x_d[t * 128:(t + 1) * 128, :]), writes=[bn])
                    norm_tile(b, bn, 'nw1', xs, ss, tp, xnT)
                    pb = cp[t % 2]
                    pn = f"cp{t % 2}"
                    for c in range(2):
                        for k in range(8):
                            S.op('pe', lambda e, c=c, k=k, pb=pb: e.matmul(
                                out=pb[:, c * 128:(c + 1) * 128], lhsT=wcv[:, k, c * 128:(c + 1) * 128], rhs=xnT[:, k, :],
                                start=(k == 0), stop=(k == 7)), reads=['wcv', 'xnT'], writes=[pn])
                    S.op('act', lambda e, pb=pb, t=t: e.activation(out=rawk[:, t * 128:(t + 1) * 128], in_=pb[:, 0:128], func=AF.Copy),
                         reads=[pn], writes=['rawk'])
                    S.op('act', lambda e, pb=pb, t=t: e.activation(out=rawv[:, t * 128:(t + 1) * 128], in_=pb[:, 128:256], func=AF.Copy),
                         reads=[pn], writes=['rawv'])
                if DBG['stop'] <= 1:
                    S.barrier()
                    return
                posk = C('posk')
                posv = C('posv')
                for l in range(32):
                    S.op('dve', lambda e, l=l: e.tensor_scalar(
                        out=zk[:, l, 0:NCMP], in0=rawk[:, l:l + 16 * (NCMP - 1) + 1:16], scalar1=posk[:, l:l + 1], scalar2=None,
                        op0=ALU.add), reads=['rawk', 'cst'], writes=['zk'])
                    S.op('pool', lambda e, l=l: e.tensor_scalar(
                        out=zv[:, l, 0:NCMP], in0=rawv[:, l:l + 16 * (NCMP - 1) + 1:16], scalar1=posv[:, l:l + 1], scalar2=None,
                        op0=ALU.add), reads=['rawv', 'cst'], writes=['zv'])
                if DBG['stop'] <= 2:
                    S.barrier()
                    return
                nch = [(0, min(128, NCMP))]
                if NCMP > 128:
                    nch.append((128, NCMP - 128))
                ovl = C('ovl')
                for kv in range(2):
                    z = (zk, zv)[kv]
                    zn = ('zk', 'zv')[kv]
                    w1 = (pk1, pv1)[kv]
                    w1n = ('pk1', 'pv1')[kv]
                    w2 = (pk2, pv2)[kv]
                    w2n = ('pk2', 'pv2')[kv]
                    for g in range(2):
                        gs_ = slice(g * 64, (g + 1) * 64)
                        for hc in range(2):
                            hb_ = hp[hc]
                            hn_ = f"hp{hc}"
                            for l in range(32):
                                S.op('pe', lambda e, l=l, hc=hc, hb_=hb_, z=z, w1=w1, gs_=gs_: e.matmul(
                                    out=hb_[:, 0:NCMP], lhsT=w1[gs_, l, hc * 128:(hc + 1) * 128], rhs=z[gs_, l, 0:NCMP],
                                    start=(l == 0), stop=(l == 31)), reads=[w1n, zn], writes=[hn_])
                            gelu(128, NCMP, hb_[:, 0:NCMP], hn_, hT[:, hc, 0:NCMP], f'hT{hc}',
                                 g1[:, 0:NCMP], 'g1', g2[:, 0:NCMP], 'g2')
                        if DBG['stop'] <= 3:
                            continue
                        for ci, (n0, sz) in enumerate(nch):
                            for hc in range(2):
                                S.op('pe', lambda e, hc=hc, n0=n0, sz=sz, w2=w2: e.matmul(
                                    out=kp[0:sz, 0:64], lhsT=hT[:, hc, n0:n0 + sz], rhs=w2[:, hc, :],
                                    start=(hc == 0), stop=(hc == 1)), reads=[f'hT{hc}', w2n], writes=['kp'])
                            if DBG['stop'] <= 4:
                                continue
                            if kv == 0:
                                S.op('act', lambda e, sz=sz: e.activation(out=kc32[0:sz, :], in_=kp[0:sz, 0:64], func=AF.Copy),
                                     reads=['kp'], writes=['kc32'])
                                cc = C('cosc')[:, ci * 8:(ci + 1) * 8]
                                sc = C('sinc')[:, ci * 8:(ci + 1) * 8]
                                rmsrope(sz, 1, kc32[0:sz, :].rearrange("p (h d) -> p h d", h=1), 'kc32', C('wk0')[0:sz, :],
                                        cc[0:sz, :], sc[0:sz, :], kcn[0:sz, :].rearrange("p (h d) -> p h d", h=1), 'kcn', tmp, 'p0')
                                S.op('pe', lambda e, sz=sz: e.transpose(out=tb[0:64, 0:sz], in_=kcn[0:sz, :], identity=identb[0:sz, 0:sz]),
                                     reads=['kcn', 'identb'], writes=['tb'])
                                S.op('dve', lambda e, sz=sz, n0=n0, g=g: e.tensor_copy(out=kcT[:, g, n0:n0 + sz], in_=tb[0:64, 0:sz]),
                                     reads=['tb'], writes=['kcT'])
                            else:
                                S.op('act', lambda e, sz=sz, g=g, ci=ci: e.activation(out=vca[0:sz, g, ci, 0:64], in_=kp[0:sz, 0:64], func=AF.Copy),
                                     reads=['kp'], writes=['vca'])
                                S.op('dve', lambda e, sz=sz, g=g, ci=ci: e.tensor_copy(out=vca[0:sz, g, ci, 64:129], in_=ovl[0:sz, ci * 65:(ci + 1) * 65]),
                                     reads=['cst'], writes=['vca'])
                S.barrier()

        def pass1():
            with ExitStack() as st:
                wtm = sbt(st, "wtm", [128, 8, 1048], BF16)
                wxy = sbt(st, "wxy", [128, 8, 2048], BF16)
                wa = sbt(st, "wa", [128, 8, 128], BF16)
                wx = sbt(st, "wx", [128, 8, 128], BF16)
                ksT = [sbt(st, f"ksT{g}", [128, S_], BF16) for g in range(2)]
                kwT = sbt(st, "kwT", [64, 2, S_], BF16)
                vsa = sbt(st, "vsa", [128, NT, 2, 65], BF16)
                vwa = sbt(st, "vwa", [128, 8, 2, 65], BF16)
                mk = sbt(st, "mk", [128, 1024], BF16)
                jw = sbt(st, "jw", [8, 792], BF16)
                ones2 = sbt(st, "ones2", [128, 2], BF16)
                xb = [sbt(st, f"xb{i}", [128, D], F32) for i in range(2)]
                xs = sbt(st, "xs", [128, D], F32)
                ss = sbt(st, "ss", [128, 4], F32)
                xnT = sbt(st, "xnT", [128, 8, 128], BF16)
                qkv = sbt(st, "qkv", [128, 1024], F32)
                qraw = qkv[:, 0:512]
                kvraw = qkv[:, 512:1024]
                qkn = sbt(st, "qkn", [128, 12, 64], BF16)
                wqk = sbt(st, "wqk", [128, 12, 64], F32)
                tmp8 = sbt(st, "tmp8", [128, 8], F32)
                bah = sbt(st, "bah", [128, 16], F32)
                gsgs = [sbt(st, f"gsg{i}", [128, 24], F32) for i in range(2)]
                qaug = sbt(st, "qaug", [128, 8, 128], BF16)
                qT = sbt(st, "qT", [64, 8, 128], BF16)
                qaTs = [[sbt(st, f"qaT{p}{g}", [128, 512], BF16) for g in range(2)] for p in range(2)]
                NB = 3
                pT = [sbt(st, f"pT{i}", [128, 512], BF16) for i in range(NB)]
                pTc = [sbt(st, f"pTc{i}", [128, 512], BF16) for i in range(2)]
                oaccs = [sbt(st, f"oacc{i}", [128, 8, 64], F32) for i in range(2)]
                otmpF = sbt(st, "otmpF", [128, 4, 64], F32)
                otmpA = sbt(st, "otmpA", [128, 4, 64], F32)
                obf = sbt(st, "obf", [128, 512], BF16)
                onT = sbt(st, "onT", [128, 4, 128], BF16)
                imp = sbt(st, "imp", [128, 64], F32)
                sc1 = sbt(st, "sc1", [128, 64], F32)
                sc2 = sbt(st, "sc2", [128, 64], F32)
                m8 = sbt(st, "m8", [128, 16], F32)
                sm = sbt(st, "sm", [128, 16], F32)
                wq8 = sbt(st, "wq8", [128, 64], F32)
                tmp = dict(sq=sbt(st, "t_sq", [128, 12, 64], F32), y=sbt(st, "t_y", [128, 12, 64], F32),
                           st=sbt(st, "t_st", [128, 48], F32), r1=sbt(st, "t_r1", [128, 12, 8], F32),
                           r2=sbt(st, "t_r2", [128, 12, 8], F32))
                xrxs = [sbt(st, f"xrx{i}", [128, 8, 132], F32) for i in range(2)]
                yrbs = [sbt(st, f"yrb{i}", [128, 8, 128], F32) for i in range(2)]
                xc = sbt(st, "xc", [128, 8, 128], F32)
                xcb = sbt(st, "xcb", [128, 8, 128], BF16)
                rg = sbt(st, "rg", [128, 8, 128], F32)
                ig = sbt(st, "ig", [128, 8, 128], F32)
                av = sbt(st, "av", [128, 8, 128], F32)
                bt = sbt(st, "bt", [128, 8, 128], F32)
                hs = sbt(st, "hs", [128, 8, 128], F32)
                hst = sbt(st, "hst", [128, 8], F32)
                cl = sbt(st, "cl", [128, 8], F32)
                olT = sbt(st, "olT", [128, 8, 128], BF16)
                stp = [pst(st, f"st{i}", [128, 512], F32) for i in range(NB)]
                pvb = [pst(st, f"pv{i}", [128, 512], F32) for i in range(2)]
                mz = pst(st, "mz", [128, 1024], BF16)
                fab = [pst(st, "fa", [128, 512], F32), pst(st, "fb", [128, 512], F32)]
                pj = fab
                pjn = ['fa', 'fb']

                load_w(wtm, wtm_d, 8, 1048, 'wtm')
                load_w(wxy, wxy_d, 8, 2048, 'wxy')
                load_w(wa, wa_d, 8, 128, 'wa')
                load_w(wx, wx_d, 8, 128, 'wx')
                S.dma('pool', lambda e: e.dma_start(out=mk[:], in_=mk_d[:, :]), writes=['mk'])
                S.dma('pool', lambda e: e.dma_start(out=jw[:], in_=jw_d[:, :]), writes=['jw'])
                for g in range(2):
                    for c0 in range(0, S_, 2048):
                        c1 = min(S_, c0 + 2048)
                        S.dma('pool', lambda e, g=g, c0=c0, c1=c1: e.dma_start(out=ksT[g][64:128, c0:c1], in_=E_d[:, c0:c1]),
                              writes=[f'ksE{g}'])
                S.op('dve', lambda e: e.tensor_scalar(out=wq8[:], in0=C('wq'), scalar1=0.125, scalar2=None, op0=ALU.mult),
                     reads=['cst'], writes=['wq8'])
                S.op('dve', lambda e: e.tensor_copy(out=wqk[:, 0:8, :], in_=wq8[:].unsqueeze(1).broadcast_to([128, 8, 64])), reads=['wq8'], writes=['wqk'])
                S.op('dve', lambda e: e.tensor_copy(out=wqk[:, 8:10, :], in_=C('wk1').unsqueeze(1).broadcast_to([128, 2, 64])), reads=['cst'], writes=['wqk'])
                S.op('dve', lambda e: e.tensor_copy(out=wqk[:, 10:12, :], in_=C('wk2').unsqueeze(1).broadcast_to([128, 2, 64])), reads=['cst'], writes=['wqk'])
                S.op('dve', lambda e: e.tensor_scalar(out=bah[:, 0:8], in0=C('ba'), scalar1=0.5, scalar2=None, op0=ALU.mult), reads=['cst'], writes=['bah'])
                S.op('dve', lambda e: e.tensor_scalar(out=bah[:, 8:16], in0=C('bx'), scalar1=0.5, scalar2=None, op0=ALU.mult), reads=['cst'], writes=['bah'])
                S.op('pool', lambda e: e.memset(vsa[:, :, :, 64:65], 1.0), writes=['vsa1'])
                S.op('pool', lambda e: e.memset(vwa[:, :, :, 64:65], 1.0), writes=['vwa1'])
                S.op('pool', lambda e: e.memset(ones2[:], 1.0), writes=['ones2'])
                S.op('pool', lambda e: e.memset(xrxs[0][:, :, 0:3], 0.0), writes=['xrxh0'])
                S.op('pool', lambda e: e.memset(hst[:], 0.0), writes=['hst'])
                S.op('pool', lambda e: e.memset(qaug[:], 0.0), writes=['qaugq', 'qaugm0', 'qaugm1'])
                S.op('act', lambda e: e.activation(out=cl[:], in_=C('lam'), func=AF.Exp, scale=-1.0), reads=['cst'], writes=['cl'])
                S.op('dve', lambda e: e.tensor_scalar(out=cl[:], in0=cl[:], scalar1=1.0, scalar2=None, op0=ALU.add),
                     reads=['cl'], writes=['cl'])
                S.op('act', lambda e: e.activation(out=cl[:], in_=cl[:], func=AF.Ln), reads=['cl'], writes=['cl'])
                S.op('dve', lambda e: e.tensor_scalar(out=cl[:], in0=cl[:], scalar1=-4.0, scalar2=None, op0=ALU.mult),
                     reads=['cl'], writes=['cl'])
                clh = cl
                tkA = C('tkA')
                tkB = C('tkB')
                cw = C('cw')
                cbv = C('cb')
                bav = C('ba')
                bxv = C('bx')
                Mc4 = mk[:, 0:512]
                Ml4 = mk[:, 512:1024]
                W8 = jw[:, 280:792]
                ctr = [0]

                def front(i):
                    p = i % 2
                    gsg = gsgs[p]
                    qaT = qaTs[p]
                    oacc = oaccs[p]
                    xrx = xrxs[p]
                    yrb = yrbs[p]
                    xrxn = f'xrx{p}'
                    xrxhn = f'xrxh{p}'
                    yrbn = f'yrb{p}'
                    gsgn = f'gsg{p}'
                    oaccn = f'oacc{p}'
                    T0 = i * 128
                    b = xb[i % 2]
                    bn = f"xb{i % 2}"
                    S.dma('sp', lambda e, b=b, i=i: e.dma_start(out=b[:], in_=x_d[i * 128:(i + 1) * 128, :]), writes=[bn])
                    norm_tile(b, bn, 'nw1', xs, ss, fab, xnT, tpn=('fa', 'fb'))
                    yield
                    for (c0, c1, pb, pn) in ((0, 512, pj[0], pjn[0]), (512, 1024, pj[1], pjn[1])):
                        for k in range(8):
                            S.op('pe', lambda e, k=k, c0=c0, c1=c1, pb=pb: e.matmul(
                                out=pb[:, 0:512], lhsT=xnT[:, k, :], rhs=wtm[:, k, c0:c1], start=(k == 0), stop=(k == 7)),
                                reads=['xnT', 'wtm'], writes=[pn])
                    S.op('act', lambda e: e.activation(out=qraw, in_=pj[0][:, 0:512], func=AF.Copy), reads=[pjn[0]], writes=['qraw', 'qkraw'])
                    S.op('act', lambda e: e.activation(out=kvraw, in_=pj[1][:, 0:512], func=AF.Copy), reads=[pjn[1]], writes=['kvraw', 'qkraw'])
                    for k in range(8):
                        S.op('pe', lambda e, k=k: e.matmul(out=fab[0][:, 0:24], lhsT=xnT[:, k, :], rhs=wtm[:, k, 1024:1048],
                                                           start=(k == 0), stop=(k == 7)), reads=['xnT', 'wtm'], writes=['fa'])
                    S.op('act', lambda e: e.activation(out=gsg[:], in_=fab[0][:, 0:24], func=AF.Tanh, scale=0.5), reads=['fa'], writes=[gsgn])
                    S.op('dve', lambda e: e.tensor_scalar(out=gsg[:], in0=gsg[:], scalar1=0.5, scalar2=0.5, op0=ALU.mult, op1=ALU.add),
                         reads=[gsgn], writes=[gsgn])
                    for c4 in range(4):
                        pb = fab[(c4 + 1) % 2]
                        pn = pjn[(c4 + 1) % 2]
                        yield
                        for cc in range(4):
                            c = c4 * 4 + cc
                            for k in range(8):
                                S.op('pe', lambda e, c=c, cc=cc, k=k, pb=pb: e.matmul(
                                    out=pb[:, cc * 128:(cc + 1) * 128], lhsT=wxy[:, k, c * 128:(c + 1) * 128], rhs=xnT[:, k, :],
                                    start=(k == 0), stop=(k == 7)), reads=['wxy', 'xnT'], writes=[pn])
                        src = pb[:, 0:512].rearrange("p (a b) -> p a b", a=4)
                        if c4 < 2:
                            S.op('act', lambda e, c4=c4, src=src: e.activation(out=xrx[:, c4 * 4:(c4 + 1) * 4, 3:131], in_=src, func=AF.Copy),
                                 reads=[pn], writes=[xrxn])
                        else:
                            S.op('act', lambda e, c4=c4, src=src: e.activation(out=yrb[:, (c4 - 2) * 4:(c4 - 1) * 4, :], in_=src, func=AF.Copy),
                                 reads=[pn], writes=[yrbn])
                    yield
                    cosp = C('cos')[:, i * 8:(i + 1) * 8]
                    sinp = C('sin')[:, i * 8:(i + 1) * 8]
                    rmsrope(128, 12, qkv[:, 0:768].rearrange("p (h d) -> p h d", h=12), 'qkraw', wqk[:], cosp, sinp,
                            qkn[:], 'qkn', tmp, 'p1')
                    S.op('act', lambda e: e.activation(out=qaug[:, :, 0:64], in_=qkn[:, 0:8, :], func=AF.Copy), reads=['qkn'], writes=['qaugq'])
                    yield
                    for j in range(4):
                        S.op('pe', lambda e, j=j: e.transpose(out=mz[0:64, j * 128:(j + 1) * 128], in_=qkn[:, 8 + j, :], identity=identb[:]),
                             reads=['qkn', 'identb'], writes=['mz'])
                    for g in range(2):
                        S.op('act', lambda e, g=g, T0=T0: e.activation(out=ksT[g][0:64, T0:T0 + 128], in_=mz[0:64, g * 128:(g + 1) * 128], func=AF.Copy),
                             reads=['mz'], writes=[f'ksT{g}_{i}'])
                    S.op('act', lambda e, T0=T0: e.activation(out=kwT[:, :, T0:T0 + 128],
                                                               in_=mz[0:64, 256:512].rearrange("p (a b) -> p a b", a=2), func=AF.Copy),
                         reads=['mz'], writes=[f'kwT_{i}'])
                    S.op('act', lambda e, i=i: e.activation(out=vsa[:, i, :, 0:64], in_=kvraw[:, 256:384].rearrange("p (g d) -> p g d", g=2),
                                                            func=AF.Copy), reads=['kvraw'], writes=[f'vsa_{i}'])
                    S.op('act', lambda e, i=i: e.activation(out=vwa[:, i % 8, :, 0:64], in_=kvraw[:, 384:512].rearrange("p (g d) -> p g d", g=2),
                                                            func=AF.Copy), reads=['kvraw'], writes=[f'vwa_{i % 8}'])
                    yield
                    for h in range(8):
                        S.op('pe', lambda e, h=h: e.transpose(out=mz[0:64, h * 128:(h + 1) * 128], in_=qkn[:, h, :], identity=identb[:]),
                             reads=['qkn', 'identb'], writes=['mz'])
                    S.op('act', lambda e: e.activation(out=qT[:].rearrange("p a b -> p (a b)"), in_=mz[0:64, :], func=AF.Copy), reads=['mz'], writes=['qT'])

                    yield
                    n_hi = min(NCMP, 8 * i + 7)
                    chunks = [(0, 0, min(128, n_hi))]
                    if n_hi > 128:
                        chunks.append((1, 128, n_hi - 128))
                    nchk = len(chunks)
                    for g in range(2):
                        bufs = []
                        for (ci, n0, Kc) in chunks:
                            bufs.append(ci)
                            sp_ = fab[ci]
                            sn_ = pjn[ci]
                            pt_ = pTc[ci]
                            ptn = f"pTc{ci}"
                            s0 = 265 + n0 - 8 * i
                            S.op('pe', lambda e, g=g, n0=n0, Kc=Kc, sp_=sp_: e.matmul(
                                out=sp_[0:Kc, :], lhsT=kcT[:, g, n0:n0 + Kc], rhs=qT[:, 4 * g:4 * g + 4, :].rearrange("p a b -> p (a b)"),
                                start=True, stop=False), reads=['kcT', 'qT'], writes=[sn_])
                            S.op('pe', lambda e, s0=s0, Kc=Kc, sp_=sp_: e.matmul(
                                out=sp_[0:Kc, :], lhsT=jw[:, s0:s0 + Kc], rhs=W8, start=False, stop=True), reads=['jw'], writes=[sn_])
                            S.op('act', lambda e, Kc=Kc, sp_=sp_, pt_=pt_: e.activation(out=pt_[0:Kc, :], in_=sp_[0:Kc, :], func=AF.Exp),
                                 reads=[sn_], writes=[ptn])
                        for h in range(4):
                            for idx, (ci, n0, Kc) in enumerate(chunks):
                                pt_ = pTc[bufs[idx]]
                                ptn = f"pTc{bufs[idx]}"
                                S.op('pe', lambda e, h=h, ci=ci, Kc=Kc, g=g, pt_=pt_, idx=idx, lastc=(idx == nchk - 1): e.matmul(
                                    out=fab[0][:, h * 128:(h + 1) * 128], lhsT=pt_[0:Kc, h * 128:(h + 1) * 128], rhs=vca[0:Kc, g, ci, 0:128],
                                    start=(idx == 0), stop=lastc), reads=[ptn, 'vca'], writes=['fa'])
                        for h in range(4):
                            for idx, (ci, n0, Kc) in enumerate(chunks):
                                pt_ = pTc[bufs[idx]]
                                ptn = f"pTc{bufs[idx]}"
                                S.op('pe', lambda e, h=h, Kc=Kc, pt_=pt_, idx=idx, lastc=(idx == nchk - 1): e.matmul(
                                    out=fab[1][:, 2 * h:2 * h + 2], lhsT=pt_[0:Kc, h * 128:(h + 1) * 128], rhs=ones2[0:Kc, :],
                                    start=(idx == 0), stop=lastc), reads=[ptn, 'ones2'], writes=['fb'])
                        S.op('dve', lambda e: e.tensor_scalar(out=sm[:, 0:4], in0=fab[1][:, 0:8:2], scalar1=1e-30, scalar2=None, op0=ALU.max),
                             reads=['fb'], writes=['sm0'])
                        S.op('dve', lambda e: e.reciprocal(out=sm[:, 4:8], in_=sm[:, 0:4]), reads=['sm0'], writes=['sm1'])
                        S.op('dve', lambda e, g=g: e.tensor_tensor(out=sm[:, 8:12], in0=sm[:, 4:8], in1=gsg[:, 4 * g:4 * g + 4], op=ALU.mult),
                             reads=['sm1', gsgn], writes=['sm2'])
                        pv3 = fab[0][:, 0:512].rearrange("p (h c) -> p h c", h=4)
                        S.op('dve', lambda e, g=g, pv3=pv3: e.tensor_tensor(
                            out=oacc[:, 4 * g:4 * g + 4, :], in0=pv3[:, :, 0:64], in1=sm[:, 8:12].unsqueeze(2).broadcast_to([128, 4, 64]),
                            op=ALU.mult), reads=['fa', 'sm2'], writes=[oaccn])
                        S.op('dve', lambda e, pv3=pv3: e.tensor_tensor(
                            out=otmpF[:], in0=pv3[:, :, 64:128], in1=sm[:, 4:8].unsqueeze(2).broadcast_to([128, 4, 64]),
                            op=ALU.mult), reads=['fa', 'sm1'], writes=['otmpF'])
                        S.op('dve', lambda e: e.tensor_reduce(out=imp[:], in_=otmpF[:].rearrange("p h j -> p j h"), axis=AX.X, op=ALU.add),
                             reads=['otmpF'], writes=['imp'])
                        yield
                        a0 = 63 - 2 * i
                        Asl = tkA[:, a0:a0 + NSLC]
                        Bsl = tkB[:, a0:a0 + NSLC]
                        S.op('dve', lambda e, Bsl=Bsl: e.tensor_tensor(out=sc1[:, 0:NSLC], in0=imp[:, 0:NSLC], in1=Bsl, op=ALU.mult),
                             reads=['imp', 'cst'], writes=['sc1'])
                        S.op('dve', lambda e, Asl=Asl: e.tensor_tensor(out=sc1[:, 0:NSLC], in0=sc1[:, 0:NSLC], in1=Asl, op=ALU.add),
                             reads=['sc1', 'cst'], writes=['sc1'])
                        S.op('dve', lambda e: e.memset(sc1[:, 0:1], 1e4), writes=['sc1'])
                        S.op('dve', lambda e: e.max(out=m8[:, 0:8], in_=sc1[:, 0:NSLC]), reads=['sc1'], writes=['m8a'])
                        S.op('dve', lambda e: e.match_replace(out=sc2[:, 0:NSLC], in_to_replace=m8[:, 0:8], in_values=sc1[:, 0:NSLC],
                                                              imm_value=-3.0e38), reads=['sc1', 'm8a'], writes=['sc2'])
                        S.op('dve', lambda e: e.max(out=m8[:, 8:16], in_=sc2[:, 0:NSLC]), reads=['sc2'], writes=['m8b'])
                        S.op('dve', lambda e, g=g: e.tensor_scalar(
                            out=qaug[:, 4 * g:4 * g + 4, 64:64 + NSLC], in0=sc1[:, 0:NSLC].unsqueeze(1).broadcast_to([128, 4, NSLC]),
                            scalar1=m8[:, 15:16], scalar2=NEGM, op0=ALU.is_lt, op1=ALU.mult), reads=['sc1', 'm8b'], writes=[f'qaugm{g}'])

                    yield
                    for g in range(2):
                        for h in range(4):
                            S.op('pe', lambda e, g=g, h=h: e.transpose(out=mz[:, h * 128:(h + 1) * 128], in_=qaug[:, 4 * g + h, :], identity=identb[:]),
                                 reads=['qaugq', f'qaugm{g}', 'identb'], writes=['mz'])
                        S.op('act', lambda e, g=g: e.activation(out=qaT[g][:], in_=mz[:, 0:512], func=AF.Copy), reads=['mz'], writes=[f'qaT{p}{g}'])

                def attn(i):
                    p = i % 2
                    gsg = gsgs[p]
                    qaT = qaTs[p]
                    oacc = oaccs[p]
                    xrx = xrxs[p]
                    yrb = yrbs[p]
                    xrxn = f'xrx{p}'
                    xrxhn = f'xrxh{p}'
                    yrbn = f'yrb{p}'
                    gsgn = f'gsg{p}'
                    oaccn = f'oacc{p}'
                    items = []
                    gi = 0
                    for br in (1, 2):
                        for g in range(2):
                            kts = list(range(0, i + 1)) if br == 1 else list(range(max(0, i - 4), i + 1))
                            for idx, kt in enumerate(kts):
                                items.append(dict(br=br, g=g, kt=kt, first=(idx == 0), last=(idx == len(kts) - 1), grp=gi))
                            gi += 1

                    def stage_S(it):
                        bi = ctr[0] % NB
                        ctr[0] += 1
                        it['bi'] = bi
                        sp_ = stp[bi]
                        sn_ = f"st{bi}"
                        pt_ = pT[bi]
                        ptn = f"pT{bi}"
                        g, kt, br = it['g'], it['kt'], it['br']
                        mask = None
                        if kt == i:
                            mask = Mc4
                        elif br == 2 and kt == i - 4:
                            mask = Ml4
                        if br == 1:
                            S.op('pe', lambda e: e.matmul(out=sp_[:, :], lhsT=ksT[g][:, kt * 128:(kt + 1) * 128], rhs=qaT[g][:, :],
                                                          start=True, stop=(mask is None)),
                                 reads=[f'ksT{g}_{kt}', f'ksE{g}', f'qaT{p}{g}'], writes=[sn_])
                        else:
                            S.op('pe', lambda e: e.matmul(out=sp_[:, :], lhsT=kwT[:, g, kt * 128:(kt + 1) * 128], rhs=qaT[g][0:64, :],
                                                          start=True, stop=(mask is None)),
                                 reads=[f'kwT_{kt}', f'qaT{p}{g}'], writes=[sn_])
                        if mask is not None:
                            S.op('pe', lambda e: e.matmul(out=sp_[:, :], lhsT=identb[:], rhs=mask, start=False, stop=True),
                                 reads=['identb', 'mk'], writes=[sn_])
                        S.op('act', lambda e: e.activation(out=pt_[:], in_=sp_[:, :], func=AF.Exp), reads=[sn_], writes=[ptn])

                    def stage_P(it):
                        bi = it['bi']
                        pt_ = pT[bi]
                        ptn = f"pT{bi}"
                        g, kt, br = it['g'], it['kt'], it['br']
                        pvt = pvb[it['grp'] % 2]
                        pvn = f"pv{it['grp'] % 2}"
                        va = vsa if br == 1 else vwa
                        van = (f'vsa_{kt}', 'vsa1') if br == 1 else (f'vwa_{kt % 8}', 'vwa1')
                        kslot = kt if br == 1 else kt % 8
                        for h in range(4):
                            S.op('pe', lambda e, h=h: e.matmul(
                                out=pvt[:, h * 65:(h + 1) * 65], lhsT=pt_[:, h * 128:(h + 1) * 128], rhs=va[:, kslot, g, :],
                                start=(it['first'] and h == 0), stop=it['last'], skip_group_check=True),
                                reads=[ptn, van[0], van[1]], writes=[pvn])
                        if it['last']:
                            pv3 = pvt[:, 0:260].rearrange("p (h c) -> p h c", h=4)
                            S.op('dve', lambda e: e.reciprocal(out=sm[:, 12:16], in_=pvt[:, 64:260:65]), reads=[pvn], writes=['sm4'])
                            S.op('dve', lambda e: e.tensor_tensor(out=sm[:, 12:16], in0=sm[:, 12:16],
                                                                  in1=gsg[:, br * 8 + 4 * g:br * 8 + 4 * g + 4], op=ALU.mult),
                                 reads=['sm4', gsgn], writes=['sm4'])
                            S.op('dve', lambda e: e.tensor_tensor(out=otmpA[:], in0=pv3[:, :, 0:64],
                                                                  in1=sm[:, 12:16].unsqueeze(2).broadcast_to([128, 4, 64]), op=ALU.mult),
                                 reads=[pvn, 'sm4'], writes=['otmpA'])
                            S.op('dve', lambda e: e.tensor_tensor(out=oacc[:, 4 * g:4 * g + 4, :], in0=oacc[:, 4 * g:4 * g + 4, :], in1=otmpA[:],
                                                                  op=ALU.add), reads=['otmpA', oaccn], writes=[oaccn])
                    LOOK = 2
                    for n_ in range(len(items) + LOOK):
                        if n_ < len(items):
                            stage_S(items[n_])
                        if n_ - LOOK >= 0:
                            stage_P(items[n_ - LOOK])
                        yield

                    S.op('act', lambda e: e.activation(out=obf[:], in_=oacc[:].rearrange("p a b -> p (a b)"), func=AF.Copy),
                         reads=[oaccn], writes=['obf'])
                    for c in range(4):
                        S.op('pe', lambda e, c=c: e.transpose(out=mz[:, c * 128:(c + 1) * 128], in_=obf[:, c * 128:(c + 1) * 128], identity=identb[:]),
                             reads=['obf', 'identb'], writes=['mz'])
                    S.op('act', lambda e: e.activation(out=onT[:].rearrange("p a b -> p (a b)"), in_=mz[:, 0:512], func=AF.Copy), reads=['mz'], writes=['onT'])
                    S.dma('sp', lambda e, i=i: e.dma_start(out=son_d[i * 128:(i + 1) * 128, :], in_=onT[:].rearrange("p a b -> p (a b)")),
                          reads=['onT'], writes=[f'son{i}'])

                def lru(i):
                    p = i % 2
                    gsg = gsgs[p]
                    qaT = qaTs[p]
                    oacc = oaccs[p]
                    xrx = xrxs[p]
                    yrb = yrbs[p]
                    xrxn = f'xrx{p}'
                    xrxhn = f'xrxh{p}'
                    yrbn = f'yrb{p}'
                    gsgn = f'gsg{p}'
                    oaccn = f'oacc{p}'
                    for c in range(8):
                        S.op('act', lambda e, c=c: e.activation(out=xc[:, c, :], in_=xrx[:, c, 3:131], func=AF.Identity,
                                                                scale=cw[:, 24 + c:25 + c], bias=cbv[:, c:c + 1]),
                             reads=[xrxn, xrxhn, 'cst'], writes=['xc'])
                    yield
                    for j in (2, 1, 0):
                        S.op('dve', lambda e, j=j: e.tensor_tensor(
                            out=hs[:], in0=xrx[:, :, j:j + 128],
                            in1=cw[:, j * 8:j * 8 + 8].unsqueeze(2).broadcast_to([128, 8, 128]), op=ALU.mult),
                            reads=[xrxn, xrxhn, 'cst'], writes=['hs'])
                        S.op('dve', lambda e: e.tensor_tensor(out=xc[:], in0=xc[:], in1=hs[:], op=ALU.add),
                             reads=['xc', 'hs'], writes=['xc'])
                        yield
                    xcall = ['xc']
                    S.op('act', lambda e: e.activation(out=xcb[:], in_=xc[:], func=AF.Copy), reads=xcall, writes=['xcb'])
                    S.op('act', lambda e: e.activation(out=xrxs[1 - p][:, :, 0:3], in_=xrx[:, :, 128:131], func=AF.Copy),
                         reads=[xrxn], writes=[f'xrxh{1 - p}'])
                    for gi_, (wmat, wn, dst, dn) in enumerate(((wa, 'wa', rg, 'rg'), (wx, 'wx', ig, 'ig'))):
                        for c4 in range(2):
                            pb = fab[c4]
                            pn = pjn[c4]
                            yield
                            for cc in range(4):
                                c = c4 * 4 + cc
                                S.op('pe', lambda e, c=c, cc=cc, pb=pb, wmat=wmat: e.matmul(
                                    out=pb[:, cc * 128:(cc + 1) * 128], lhsT=wmat[:, c, :], rhs=xcb[:, c, :], start=True, stop=True),
                                    reads=[wn, 'xcb'], writes=[pn])
                            for cc in range(4):
                                c = c4 * 4 + cc
                                S.op('act', lambda e, c=c, cc=cc, pb=pb, dst=dst, gi_=gi_: e.activation(
                                    out=dst[:, c, :], in_=pb[:, cc * 128:(cc + 1) * 128], func=AF.Tanh, scale=0.5,
                                    bias=bah[:, gi_ * 8 + c:gi_ * 8 + c + 1]), reads=[pn, 'bah'], writes=[dn])
                    yield
                    for c in range(8):
                        S.op('act', lambda e, c=c: e.activation(out=av[:, c, :], in_=rg[:, c, :], func=AF.Exp, scale=clh[:, c:c + 1],
                                                                bias=clh[:, c:c + 1]), reads=['rg', 'cl'], writes=['av'])
                    yield
                    S.op('act', lambda e: e.activation(out=bt[:], in_=av[:], func=AF.Square), reads=['av'], writes=['bt'])
                    S.op('act', lambda e: e.activation(out=bt[:], in_=bt[:], func=AF.Sqrt, scale=-1.0, bias=1.0), reads=['bt'], writes=['bt'])
                    S.op('dve', lambda e: e.scalar_tensor_tensor(out=ig[:], in0=ig[:], scalar=1.0, in1=xc[:], op0=ALU.add, op1=ALU.mult),
                         reads=['ig'] + xcall, writes=['ig'])
                    S.op('dve', lambda e: e.scalar_tensor_tensor(out=bt[:], in0=bt[:], scalar=0.5, in1=ig[:], op0=ALU.mult, op1=ALU.mult),
                         reads=['bt', 'ig'], writes=['bt'])
                    yield
                    S.op('dve', lambda e: e.tensor_tensor(out=tmp8[:], in0=av[:, :, 0], in1=hst[:], op=ALU.mult), reads=['av', 'hst'], writes=['tmp8'])
                    S.op('dve', lambda e: e.tensor_tensor(out=bt[:, :, 0], in0=bt[:, :, 0], in1=tmp8[:], op=ALU.add), reads=['bt', 'tmp8'], writes=['bt'])
                    S.op('dve', lambda e: e.memset(av[:, :, 0:1], 0.0), reads=['tmp8'], writes=['av'])
                    S.op('dve', lambda e: e.tensor_tensor_scan(out=hs[:].rearrange("p a b -> p (a b)"), data0=av[:].rearrange("p a b -> p (a b)"),
                                                               data1=bt[:].rearrange("p a b -> p (a b)"), initial=0.0, op0=ALU.mult, op1=ALU.add),
                         reads=['av', 'bt'], writes=['hs'])
                    S.op('dve', lambda e: e.tensor_copy(out=hst[:], in_=hs[:, :, 127]), reads=['hs'], writes=['hst'])
                    yield
                    yr2 = yrb[:].rearrange("p a b -> p (a b)")
                    gelu(128, 1024, yr2, yrbn, rg[:].rearrange("p a b -> p (a b)"), 'rg',
                         xc[:].rearrange("p a b -> p (a b)"), 'xc', ig[:].rearrange("p a b -> p (a b)"), 'ig', twice=True)
                    S.op('dve', lambda e: e.scalar_tensor_tensor(out=olT[:], in0=hs[:], scalar=0.5, in1=rg[:], op0=ALU.mult, op1=ALU.mult),
                         reads=['hs', 'rg'], writes=['olT'])
                    S.dma('sp', lambda e, i=i: e.dma_start(out=sol_d[i * 128:(i + 1) * 128, :], in_=olT[:].rearrange("p a b -> p (a b)")),
                          reads=['olT'], writes=[f'sol{i}'])
                for _ in front(0):
                    pass
                for i in range(NT):
                    n_items = 2 * (i + 1) + 2 * (min(i, 4) + 1) + 3
                    gl = [(attn(i), float(n_items)), (lru(i), 24.0)]
                    if i + 1 < NT:
                        gl.append((front(i + 1), 22.0))
                    drive(gl)
                S.barrier()

        def drive(gens_w):
            gens = [[g_, w_, 0.0] for (g_, w_) in gens_w]
            if DBG.get('seq'):
                for ent in gens:
                    for _ in ent[0]:
                        pass
                return
            wmax = max(w_ for (_, w_) in gens_w)
            while gens:
                for ent in list(gens):
                    ent[2] += ent[1] / wmax
                    while ent[2] >= 1.0:
                        ent[2] -= 1.0
                        try:
                            next(ent[0])
                        except StopIteration:
                            gens.remove(ent)
                            break

        def pass2():
            with ExitStack() as st:
                wgm = sbt(st, "wgm", [128, 8, 2048], BF16)
                wnu = sbt(st, "wnu", [128, 4, 1024], BF16)
                wlu = sbt(st, "wlu", [128, 8, 1024], BF16)
                wo = sbt(st, "wo", [128, 8, 1024], BF16)
                xb = [sbt(st, f"xb{i}", [128, D], F32) for i in range(2)]
                xs = sbt(st, "xs", [128, D], F32)
                ss = sbt(st, "ss", [128, 4], F32)
                xnT = sbt(st, "xnT", [128, 8, 128], BF16)
                gms = [sbt(st, f"gm{i}", [128, 16, 128], F32) for i in range(2)]
                onT = [sbt(st, f"onT{i}", [128, 4, 128], BF16) for i in range(2)]
                olT = [sbt(st, f"olT{i}", [128, 8, 128], BF16) for i in range(2)]
                t1 = sbt(st, "t1", [128, 4, 128], F32)
                t2 = sbt(st, "t2", [128, 4, 128], F32)
                mT = sbt(st, "mT", [128, 8, 128], BF16)
                ob = sbt(st, "ob", [128, D], F32)
                tp = pst(st, "tp", [128, 1024], F32)
                pj = [pst(st, f"pj{i}", [128, 512], F32) for i in range(2)]
                pa = pst(st, "pa", [128, 512], F32)
                pbk = pst(st, "pbk", [128, 512], F32)
                yp = pst(st, "yp", [128, 1024], F32)
                load_w(wgm, wgm_d, 8, 2048, 'wgm')
                load_w(wnu, wnu_d, 4, 1024, 'wnu')
                load_w(wlu, wlu_d, 8, 1024, 'wlu')
                load_w(wo, wo_d, 8, 1024, 'wo')

                def front(i):
                    p = i % 2
                    b = xb[p]
                    bn = f"xb{p}"
                    gm = gms[p]
                    on_ = onT[p]
                    ol_ = olT[p]
                    S.dma('sp', lambda e: e.dma_start(out=b[:], in_=x_d[i * 128:(i + 1) * 128, :]), writes=[bn])
                    S.dma('sp', lambda e: e.dma_start(out=on_[:].rearrange("p a b -> p (a b)"), in_=son_d[i * 128:(i + 1) * 128, :]),
                          reads=[f'son{i}'], writes=[f"onT{p}"])
                    S.dma('sp', lambda e: e.dma_start(out=ol_[:].rearrange("p a b -> p (a b)"), in_=sol_d[i * 128:(i + 1) * 128, :]),
                          reads=[f'sol{i}'], writes=[f"olT{p}"])
                    yield
                    norm_tile(b, bn, 'nw1', xs, ss, tp, xnT)
                    yield
                    for c4 in range(4):
                        pb = pj[c4 % 2]
                        pn = f"pj{c4 % 2}"
                        for cc in range(4):
                            c = c4 * 4 + cc
                            for k in range(8):
                                S.op('pe', lambda e, c=c, cc=cc, k=k, pb=pb: e.matmul(
                                    out=pb[:, cc * 128:(cc + 1) * 128], lhsT=wgm[:, k, c * 128:(c + 1) * 128], rhs=xnT[:, k, :],
                                    start=(k == 0), stop=(k == 7)), reads=['wgm', 'xnT'], writes=[pn])
                        S.op('act', lambda e, c4=c4, pb=pb: e.activation(out=gm[:, c4 * 4:(c4 + 1) * 4, :],
                                                                         in_=pb[:, 0:512].rearrange("p (a b) -> p a b", a=4), func=AF.Tanh, scale=0.5),
                             reads=[pn], writes=[f'gm{p}_{c4}'])
                        yield

                def back(i):
                    p = i % 2
                    b = xb[p]
                    bn = f"xb{p}"
                    gm = gms[p]
                    on_ = onT[p]
                    ol_ = olT[p]
                    onn = f"onT{p}"
                    oln = f"olT{p}"
                    for c4 in range(2):
                        for cc in range(4):
                            c = c4 * 4 + cc
                            for k in range(4):
                                S.op('pe', lambda e, c=c, cc=cc, k=k: e.matmul(
                                    out=pa[:, cc * 128:(cc + 1) * 128], lhsT=wnu[:, k, c * 128:(c + 1) * 128], rhs=on_[:, k, :],
                                    start=(k == 0), stop=(k == 3)), reads=['wnu', onn], writes=['pa'])
                        for cc in range(4):
                            c = c4 * 4 + cc
                            for k in range(8):
                                S.op('pe', lambda e, c=c, cc=cc, k=k: e.matmul(
                                    out=pbk[:, cc * 128:(cc + 1) * 128], lhsT=wlu[:, k, c * 128:(c + 1) * 128], rhs=ol_[:, k, :],
                                    start=(k == 0), stop=(k == 7)), reads=['wlu', oln], writes=['pbk'])
                        S.op('dve', lambda e, c4=c4: e.scalar_tensor_tensor(
                            out=t1[:], in0=gm[:, c4 * 4:(c4 + 1) * 4, :], scalar=1.0, in1=pa[:, 0:512].rearrange("p (a b) -> p a b", a=4),
                            op0=ALU.add, op1=ALU.mult), reads=['pa', f'gm{p}_{c4}'], writes=['t1'])
                        S.op('dve', lambda e, c4=c4: e.scalar_tensor_tensor(
                            out=t2[:], in0=gm[:, 8 + c4 * 4:8 + (c4 + 1) * 4, :], scalar=1.0, in1=pbk[:, 0:512].rearrange("p (a b) -> p a b", a=4),
                            op0=ALU.add, op1=ALU.mult), reads=['pbk', f'gm{p}_{c4 + 2}'], writes=['t2'])
                        S.op('dve', lambda e, c4=c4: e.tensor_tensor(out=mT[:, c4 * 4:(c4 + 1) * 4, :], in0=t1[:], in1=t2[:], op=ALU.add),
                             reads=['t1', 't2'], writes=[f'mT{c4}'])
                        yield
                    for n in range(2):
                        for k in range(8):
                            S.op('pe', lambda e, n=n, k=k: e.matmul(out=yp[:, n * 512:(n + 1) * 512], lhsT=mT[:, k, :],
                                                                    rhs=wo[:, k, n * 512:(n + 1) * 512], start=(k == 0), stop=(k == 7)),
                                 reads=[f'mT{k // 4}', 'wo'], writes=[f'yp{n}'])
                        yield
                    S.op('dve', lambda e: e.scalar_tensor_tensor(out=ob[:], in0=yp[:], scalar=0.5, in1=b[:], op0=ALU.mult, op1=ALU.add),
                         reads=['yp0', 'yp1', bn], writes=['ob'])
                    S.dma('sp', lambda e: e.dma_start(out=out_d[i * 128:(i + 1) * 128, :], in_=ob[:]), reads=['ob'], writes=[f'outd{i}'])

                for _ in front(0):
                    pass
                for i in range(NT):
                    gl = [(back(i), 4.0)]
                    if i + 1 < NT:
                        gl.append((front(i + 1), 6.0))
                    drive(gl)
                S.barrier()

        def pass3():
            with ExitStack() as st:
                w1 = sbt(st, "w1s", [128, 8, DFF], BF16)
                w2 = sbt(st, "w2s", [128, 32, D], BF16)
                hb = [sbt(st, f"hb{i}", [128, D], F32) for i in range(2)]
                xs = sbt(st, "xs", [128, D], F32)
                ss = sbt(st, "ss", [128, 4], F32)
                hnTs = [sbt(st, f"hnT{i}", [128, 8, 128], BF16) for i in range(2)]
                rl = [sbt(st, f"rl{i}", [128, 512], F32) for i in range(2)]
                hid = sbt(st, "hid", [128, 32, 128], BF16)
                ob = sbt(st, "ob", [128, D], F32)
                tp = pst(st, "tp", [128, 1024], F32)
                hp = [pst(st, f"hp{i}", [128, 512], F32) for i in range(2)]
                yp = pst(st, "yp", [128, 1024], F32)
                load_w(w1, wf1_d, 8, DFF, 'w1')
                load_w(w2, wf2_d, 32, D, 'w2')
                src_d = out_d if 2 in passes else x_d

                def front(t):
                    p = t % 2
                    b = hb[p]
                    bn = f"hb{p}"
                    S.dma('sp', lambda e: e.dma_start(out=b[:], in_=src_d[t * 128:(t + 1) * 128, :]),
                          reads=[f'outd{t}'], writes=[bn])
                    yield
                    norm_tile(b, bn, 'nw2', xs, ss, tp, hnTs[p], outn=f'hnT{p}')
                    yield

                def back(t):
                    p = t % 2
                    b = hb[p]
                    bn = f"hb{p}"
                    hnT = hnTs[p]
                    for c4 in range(8):
                        pb = hp[c4 % 2]
                        pn = f"hp{c4 % 2}"
                        for cc in range(4):
                            c = c4 * 4 + cc
                            for k in range(8):
                                S.op('pe', lambda e, c=c, cc=cc, k=k, pb=pb: e.matmul(
                                    out=pb[:, cc * 128:(cc + 1) * 128], lhsT=w1[:, k, c * 128:(c + 1) * 128], rhs=hnT[:, k, :],
                                    start=(k == 0), stop=(k == 7)), reads=['w1', f'hnT{p}'], writes=[pn])
                        r = rl[c4 % 2]
                        rn = f"rl{c4 % 2}"
                        S.op('act', lambda e, r=r, pb=pb: e.activation(out=r[:], in_=pb[:], func=AF.Relu), reads=[pn], writes=[rn])
                        S.op('dve', lambda e, r=r, c4=c4: e.tensor_tensor(
                            out=hid[:, c4 * 4:(c4 + 1) * 4, :], in0=r[:].rearrange("p (a b) -> p a b", a=4),
                            in1=r[:].rearrange("p (a b) -> p a b", a=4), op=ALU.mult), reads=[rn], writes=[f'hid{c4}'])
                        yield
                    for n in range(2):
                        for k in range(32):
                            S.op('pe', lambda e, n=n, k=k: e.matmul(out=yp[:, n * 512:(n + 1) * 512], lhsT=hid[:, k, :],
                                                                    rhs=w2[:, k, n * 512:(n + 1) * 512], start=(k == 0), stop=(k == 31)),
                                 reads=[f'hid{k // 4}', 'w2'], writes=[f'yp{n}'])
                            if k % 8 == 7:
                                yield
                    S.op('dve', lambda e: e.tensor_tensor(out=ob[:], in0=yp[:], in1=b[:], op=ALU.add),
                         reads=['yp0', 'yp1', bn], writes=['ob'])
                    S.dma('sp', lambda e: e.dma_start(out=out_d[t * 128:(t + 1) * 128, :], in_=ob[:]), reads=['ob'], writes=[f'outd{t}'])

                for _ in front(0):
                    pass
                for t in range(NT):
                    gl = [(back(t), 16.0)]
                    if t + 1 < NT:
                        gl.append((front(t + 1), 2.0))
                    drive(gl)
        for n_, f_ in enumerate((pass0, pass1, pass2, pass3)):
            if n_ in passes:
                f_()
        S.finish()
        with nc.Block() as block:
            S.emit(block)
    return nc


def _kmaj(w, nk):
    n = w.shape[1]
    return np.ascontiguousarray(w.reshape(nk, 128, n).transpose(1, 0, 2).reshape(128, nk * n))


def _colmaj(v):
    return np.ascontiguousarray(v.reshape(-1, 128).T)


def build_consts(NT, inp):
    S_ = NT * 128
    NCMP = (S_ - 32) // 16 + 1
    CO, NCST = cst_layout(NT)
    cst = np.zeros((128, NCST), np.float32)

    def put(name, arr):
        a, b = CO[name]
        cst[:, a:b] = arr
    put('ident', np.eye(128, dtype=np.float32))
    put('nw1', _colmaj(inp['norm1_w'][0]))
    put('nw2', _colmaj(inp['norm2_w'][0]))
    put('eps', np.full((128, 1), EPS, np.float32))
    put('wq', np.tile(inp['q_norm_w'][0][None, :], (128, 1)))
    for j in range(3):
        put(f'wk{j}', np.tile(inp['k_norm_w'][0, j][None, :], (128, 1)))
    put('posk', np.tile(inp['phi_k_pos'][0].T, (2, 1)))
    put('posv', np.tile(inp['phi_v_pos'][0].T, (2, 1)))
    cw = inp['conv_w'][0]
    put('cw', np.concatenate([_colmaj(cw[j]) for j in range(4)], axis=1))
    put('cb', _colmaj(inp['conv_b'][0]))
    put('ba', _colmaj(inp['lru_ba'][0].reshape(-1)))
    put('bx', _colmaj(inp['lru_bx'][0].reshape(-1)))
    put('lam', _colmaj(inp['lru_lambda'][0]))
    half = 8
    inv = (np.float32(500000.0) ** (-np.arange(half, dtype=np.float32) / np.float32(half))).astype(np.float32)
    pos = np.arange(S_, dtype=np.float32)
    ang = (pos[:, None] * inv[None, :]).astype(np.float32)
    cos = np.cos(ang).astype(np.float32).reshape(NT, 128, 8).transpose(1, 0, 2).reshape(128, NT * 8)
    sin = np.sin(ang).astype(np.float32).reshape(NT, 128, 8).transpose(1, 0, 2).reshape(128, NT * 8)
    put('cos', cos)
    put('sin', sin)
    cend = (np.arange(256) * 16 + 31).astype(np.float32)
    angc = (cend[:, None] * inv[None, :]).astype(np.float32)
    put('cosc', np.cos(angc).astype(np.float32).reshape(2, 128, 8).transpose(1, 0, 2).reshape(128, 16))
    put('sinc', np.sin(angc).astype(np.float32).reshape(2, 128, 8).transpose(1, 0, 2).reshape(128, 16))
    A = np.zeros((128, 127), np.float32)
    B = np.ones((128, 127), np.float32)
    for q in range(128):
        cur = 1 if q >= 64 else 0
        for r in range(127):
            rel = r - 63
            if rel > cur:
                A[q, r] = -1e30
                B[q, r] = 0.0
            elif rel == cur or rel == cur - 1:
                A[q, r] = 1e4
                B[q, r] = 0.0
    put('tkA', A)
    put('tkB', B)
    ov = np.zeros((256, 65), np.float32)
    for n in range(NCMP):
        for j in range(min(64, S_ // 64)):
            if 16 * n <= 64 * j + 63 and 16 * n + 31 >= 64 * j:
                ov[n, j] = 1.0
        ov[n, 64] = 1.0
    put('ovl', ov.reshape(2, 128, 65).transpose(1, 0, 2).reshape(128, 130))
    put('mhalf', np.full((128, 8), -0.5, np.float32))
    E = np.zeros((64, S_), np.float32)
    for j in range(S_ // 64):
        E[j, j * 64:(j + 1) * 64] = 1.0
    return cst, E


def build_masks():
    r = np.arange(128)[:, None]
    q = np.arange(128)[None, :]
    mc = np.where(r > q, NEGM, 0.0).astype(np.float32)
    ml = np.where(r <= q, NEGM, 0.0).astype(np.float32)
    mk = np.concatenate([np.tile(mc, (1, 4)), np.tile(ml, (1, 4))], axis=1)
    jw = np.zeros((8, 792), np.float32)
    for rr in range(8):
        jw[rr, rr + 264] = 1.0
        w = np.where(np.arange(128) < 16 * rr + 15, NEGM, 0.0).astype(np.float32)
        jw[rr, 280:792] = np.tile(w, 4)
    return np.ascontiguousarray(mk), jw


def host_weights(inp):
    w_in = inp['w_in'][0]
    cols = lambda a, b: w_in[:, a:b]
    wtm = np.concatenate([cols(0, 512), cols(768, 896), cols(1024, 1152), cols(896, 1024), cols(1152, 1280), cols(1280, 1304)], axis=1)
    wcv = cols(512, 768)
    wxy = cols(1304, 3352)
    wgm = cols(3352, 5400)

    def phi1(w):
        a = w.reshape(32, 64, 256).transpose(1, 0, 2).reshape(64, 32 * 256)
        return np.ascontiguousarray(np.concatenate([a, a], axis=0))
    m = {
        'wtm': _kmaj(wtm, 8), 'wcv': _kmaj(wcv, 8), 'wxy': _kmaj(wxy, 8), 'wgm': _kmaj(wgm, 8),
        'pk1': phi1(inp['phi_k_w1'][0]), 'pv1': phi1(inp['phi_v_w1'][0]),
        'pk2': _kmaj(inp['phi_k_w2'][0], 2), 'pv2': _kmaj(inp['phi_v_w2'][0], 2),
        'wa': np.ascontiguousarray(inp['lru_wa'][0].transpose(1, 0, 2).reshape(128, 1024)),
        'wx': np.ascontiguousarray(inp['lru_wx'][0].transpose(1, 0, 2).reshape(128, 1024)),
        'wnu': _kmaj(inp['w_nsa_up'][0], 4), 'wlu': _kmaj(inp['w_lru_up'][0], 8), 'wo': _kmaj(inp['w_o'][0], 8),
        'wf1': _kmaj(inp['w_ff1'][0], 8), 'wf2': _kmaj(inp['w_ff2'][0], 32),
    }
    return m


def kernel(**inputs):
    inp = {k: np.asarray(v, dtype=np.float32) for k, v in inputs.items()}
    x = inp['x']
    B, S_, _ = x.shape
    NT = S_ // 128
    cst, E = build_consts(NT, inp)
    wm = host_weights(inp)
    mkm, jwm = build_masks()
    nc = build_program(NT)
    in_maps = []
    for b in range(B):
        m = dict(wm)
        m['x'] = np.ascontiguousarray(x[b])
        m['cst'] = cst
        m['emat'] = E
        m['mk'], m['jw'] = mkm, jwm
        in_maps.append(m)
    res = run_bass_kernel_spmd(nc, in_maps, core_ids=list(range(B)))
    return np.stack([np.asarray(r['out'], dtype=np.float32) for r in res.results], axis=0)
```

```python
import math
from contextlib import ExitStack
import numpy as np
import concourse.bass as bass
import concourse.mybir as mybir
from concourse.bass_utils import run_bass_kernel_spmd

F32 = mybir.dt.float32
BF16 = mybir.dt.bfloat16
AF = mybir.ActivationFunctionType
ALU = mybir.AluOpType
AX = mybir.AxisListType

D = 1024
DFF = 4096
EPS = 1e-6
NEGM = -30000.0
GC1 = 0.044715
GC2 = 2.0 * math.sqrt(2.0 / math.pi)


class Sched:
    CE = ('pe', 'dve', 'act', 'pool')

    def __init__(self, nc, stack, ndma=8, limit=16000, strict_same=True):
        self.nc = nc
        self.stack = stack
        self.limit = limit
        self.strict_same = strict_same
        self.prog = {e: [] for e in ('pe', 'dve', 'act', 'pool', 'sp')}
        self.nsem = 0
        self.cur_sem = {}
        self.cnt = {}
        for e in self.CE:
            self.cur_sem[e] = self._newsem(e)
            self.cnt[e] = 0
        self.dma_sems = {q: [self._newsem('d' + q) for _ in range(ndma)] for q in ('sp', 'pool')}
        self.dma_n = {q: 0 for q in ('sp', 'pool')}
        self.waited = {e: {} for e in self.prog}
        self.lastw = {}
        self.readers = {}
        self.last_tok = {}

    def _newsem(self, tag):
        self.nsem += 1
        s = self.stack.enter_context(self.nc.semaphore(f"s{self.nsem}_{tag}"))
        return (self.nsem, s)

    PSUM = frozenset(['tp0', 'tp1', 'cp0', 'cp1', 'hp0', 'hp1', 'kp', 'tb', 'pj0', 'pj1', 'st0', 'st1', 'st2', 'st3',
                      'pv', 'pv0', 'pv1', 'mz', 'pa', 'pbk', 'yp0', 'yp1', 'fa', 'fb', 'ypA0', 'ypA1', 'ypB0', 'ypB1'])

    def _deps(self, reads, writes, eng=None):
        deps = []
        for b in reads:
            if b in self.lastw:
                deps.append(self.lastw[b])
            if b in self.PSUM:
                deps.extend(t for t in self.readers.get(b, ()) if t[2] != eng)
        for b in writes:
            if b in self.lastw:
                deps.append(self.lastw[b])
            deps.extend(self.readers.get(b, ()))
        return deps

    def _waits(self, eng, deps):
        waits = []
        w = self.waited[eng]
        for (sem, val, src) in deps:
            if src == eng and (eng == 'pe' or not self.strict_same):
                continue
            if w.get(sem[0], 0) >= val:
                continue
            w[sem[0]] = val
            waits.append((sem[1], val))
        return waits

    def _commit(self, tok, reads, writes):
        for b in reads:
            if b not in writes:
                self.readers.setdefault(b, []).append(tok)
        for b in writes:
            self.lastw[b] = tok
            self.readers[b] = []

    def op(self, eng, fn, reads=(), writes=()):
        deps = self._deps(reads, writes, eng)
        waits = self._waits(eng, deps)
        if self.cnt[eng] >= self.limit:
            self.cur_sem[eng] = self._newsem(eng)
            self.cnt[eng] = 0
        self.cnt[eng] += 1
        sem = self.cur_sem[eng]
        tok = (sem, self.cnt[eng], eng)
        self.last_tok[eng] = tok
        self.prog[eng].append((waits, fn, (sem[1], 1)))
        self._commit(tok, reads, writes)
        return tok

    def dma(self, q, fn, reads=(), writes=()):
        deps = self._deps(reads, writes)
        j = self.dma_n[q]
        self.dma_n[q] += 1
        K = len(self.dma_sems[q])
        sem = self.dma_sems[q][j % K]
        if j >= K:
            deps.append((sem, 16 * (j // K), 'dma'))
        waits = self._waits(q, deps)
        tok = (sem, 16 * (j // K + 1), 'dma')
        self.prog[q].append((waits, fn, (sem[1], 16)))
        self._commit(tok, reads, writes)
        return tok

    def _dma_final(self):
        deps = []
        for q in self.dma_sems:
            K = len(self.dma_sems[q])
            n = self.dma_n[q]
            for i, sem in enumerate(self.dma_sems[q]):
                cnt = (n - i + K - 1) // K if n > i else 0
                if cnt > 0:
                    deps.append((sem, 16 * cnt, 'dma'))
        return deps

    def barrier(self):
        deps = self._dma_final() + list(self.last_tok.values())
        for e in self.prog:
            saved = self.strict_same
            self.strict_same = True
            waits = []
            w = self.waited[e]
            for (sem, val, src) in deps:
                if w.get(sem[0], 0) >= val:
                    continue
                w[sem[0]] = val
                waits.append((sem[1], val))
            self.strict_same = saved
            self.prog[e].append((waits, None, None))
        self.lastw = {}
        self.readers = {}

    def finish(self):
        self.barrier()

    def emit(self, block):
        def run(engobj, items):
            for waits, fn, inc in items:
                for (s, v) in waits:
                    engobj.wait_ge(s, v)
                if fn is not None:
                    ins = fn(engobj)
                    ins.then_inc(inc[0], inc[1])

        P = self.prog

        @block.sync
        def _(e):
            run(e, P['sp'])

        @block.tensor
        def _(e):
            run(e, P['pe'])

        @block.vector
        def _(e):
            run(e, P['dve'])

        @block.scalar
        def _(e):
            run(e, P['act'])

        @block.gpsimd
        def _(e):
            run(e, P['pool'])


def cst_layout(NT):
    off = {}
    o = 0

    def add(name, n):
        nonlocal o
        off[name] = (o, o + n)
        o += n
    add('ident', 128)
    add('nw1', 8)
    add('nw2', 8)
    add('eps', 1)
    add('wq', 64)
    add('wk0', 64)
    add('wk1', 64)
    add('wk2', 64)
    add('posk', 32)
    add('posv', 32)
    add('cw', 32)
    add('cb', 8)
    add('ba', 8)
    add('bx', 8)
    add('lam', 8)
    add('cos', NT * 8)
    add('sin', NT * 8)
    add('cosc', 16)
    add('sinc', 16)
    add('tkA', 127)
    add('tkB', 127)
    add('ovl', 130)
    add('mhalf', 8)
    return off, o


DBG = {'stop': 99}


def build_program(NT=32, passes=(0, 1, 2, 3), dbg=False):
    S_ = NT * 128
    NCMP = (S_ - 32) // 16 + 1
    NSLC = S_ // 64
    CO, NCST = cst_layout(NT)
    nc = bass.Bass("TRN2", target_bir_lowering=False)

    def din(name, shape, dt=F32):
        return nc.dram_tensor(name, shape, dt, kind="ExternalInput").ap()

    x_d = din("x", [S_, D])
    cst_d = din("cst", [128, NCST])
    wtm_d = din("wtm", [128, 8 * 1048])
    wcv_d = din("wcv", [128, 8 * 256])
    wxy_d = din("wxy", [128, 8 * 2048])
    wgm_d = din("wgm", [128, 8 * 2048])
    pk1_d = din("pk1", [128, 32 * 256])
    pv1_d = din("pv1", [128, 32 * 256])
    pk2_d = din("pk2", [128, 2 * 64])
    pv2_d = din("pv2", [128, 2 * 64])
    wa_d = din("wa", [128, 8 * 128])
    wx_d = din("wx", [128, 8 * 128])
    wnu_d = din("wnu", [128, 4 * 1024])
    wlu_d = din("wlu", [128, 8 * 1024])
    wo_d = din("wo", [128, 8 * 1024])
    wf1_d = din("wf1", [128, 8 * DFF])
    wf2_d = din("wf2", [128, 32 * D])
    E_d = din("emat", [64, S_])
    mk_d = din("mk", [128, 1024])
    jw_d = din("jw", [8, 792])
    out_d = nc.dram_tensor("out", [S_, D], F32, kind="ExternalOutput").ap()
    son_d = nc.dram_tensor("sc_on", [NT * 128, 512], BF16, kind="Internal").ap()
    sol_d = nc.dram_tensor("sc_ol", [NT * 128, 1024], BF16, kind="Internal").ap()

    with ExitStack() as top:
        S = Sched(nc, top)

        uniq = [0]

        def sbt(st, name, shape, dt):
            uniq[0] += 1
            return st.enter_context(nc.sbuf_tensor(f"sb{uniq[0]}_{name}", shape, dt))

        def pst(st, name, shape, dt):
            uniq[0] += 1
            return st.enter_context(nc.psum_tensor(f"ps{uniq[0]}_{name}", shape, dt))

        cst = sbt(top, "cst", [128, NCST], F32)
        identb = sbt(top, "identb", [128, 128], BF16)
        kcT = sbt(top, "kcT", [64, 2, 256], BF16)
        vca = sbt(top, "vca", [128, 2, 2, 129], BF16)

        def C(name):
            a, b = CO[name]
            return cst[:, a:b]
        ident = C('ident')
        epsc = C('eps')

        S.dma('sp', lambda e: e.dma_start(out=cst[:], in_=cst_d[:, :]), writes=['cst'])
        S.op('dve', lambda e: e.tensor_copy(out=identb[:], in_=ident), reads=['cst'], writes=['identb'])
        mhalfw = sbt(top, "mhalfw", [128, 16], F32)
        S.op('pool', lambda e: e.memset(mhalfw[:], -0.5), writes=['cst_mh'])

        def load_w(dst3, src_d, nk, ncol, name):
            step = max(1, 2048 // ncol)
            if ncol > 2048:
                for k in range(nk):
                    for c0 in range(0, ncol, 2048):
                        c1 = min(ncol, c0 + 2048)
                        S.dma('pool', lambda e, k=k, c0=c0, c1=c1: e.dma_start(
                            out=dst3[:, k, c0:c1], in_=src_d[:, k * ncol + c0:k * ncol + c1]), writes=[name])
            else:
                for k0 in range(0, nk, step):
                    k1 = min(nk, k0 + step)
                    S.dma('pool', lambda e, k0=k0, k1=k1: e.dma_start(
                        out=dst3[:, k0:k1, :],
                        in_=src_d[:, k0 * ncol:k1 * ncol].rearrange("p (a b) -> p a b", a=k1 - k0)), writes=[name])

        def norm_tile(xb, xbn, nwname, xs, ss, tp, xnT, tpn=('tp0', 'tp1'), outn='xnT'):
            if isinstance(tp, (list, tuple)):
                banks = tp
            else:
                banks = (tp[:, 0:512], tp[:, 512:1024])
            S.op('act', lambda e: e.activation(out=xs[:], in_=xb[:], func=AF.Square, accum_out=ss[:, 0:1]),
                 reads=[xbn], writes=['xs', 'ss0'])
            S.op('pool', lambda e: e.tensor_scalar(out=ss[:, 1:2], in0=ss[:, 0:1], scalar1=1.0 / D, scalar2=EPS, op0=ALU.mult, op1=ALU.add),
                 reads=['ss0'], writes=['ss1'])
            S.op('pool', lambda e: e.tensor_tensor(out=ss[:, 2:3], in0=ss[:, 1:2], in1=C('mhalf')[:, 0:1], op=ALU.pow),
                 reads=['ss1', 'cst'], writes=['ss2'])
            S.op('act', lambda e: e.activation(out=xs[:], in_=xb[:], func=AF.Copy, scale=ss[:, 2:3]),
                 reads=[xbn, 'ss2'], writes=['xs'])
            for k in range(8):
                S.op('pe', lambda e, k=k: e.transpose(out=banks[k // 4][:, (k % 4) * 128:(k % 4 + 1) * 128],
                                                      in_=xs[:, k * 128:(k + 1) * 128],
                                                      identity=ident), reads=['xs', 'cst'], writes=[tpn[k // 4]])
            nw = C(nwname)
            for a in range(2):
                S.op('dve', lambda e, a=a: e.tensor_tensor(
                    out=xnT[:, 4 * a:4 * a + 4, :], in0=banks[a].rearrange("p (a b) -> p a b", a=4),
                    in1=nw[:, 4 * a:4 * a + 4].unsqueeze(2).broadcast_to([128, 4, 128]), op=ALU.mult),
                     reads=[tpn[a], 'cst'], writes=[outn])

        def gelu(P_, shape_free, src, srcn, dst, dstn, t1, t1n, t2, t2n, twice=False):
            S.op('act', lambda e: e.activation(out=t1, in_=src, func=AF.Square), reads=[srcn], writes=[t1n])
            S.op('dve', lambda e: e.tensor_scalar(out=t1, in0=t1, scalar1=GC1, scalar2=1.0, op0=ALU.mult, op1=ALU.add),
                 reads=[t1n], writes=[t1n])
            S.op('dve', lambda e: e.tensor_tensor(out=t1, in0=t1, in1=src, op=ALU.mult), reads=[t1n, srcn], writes=[t1n])
            S.op('act', lambda e: e.activation(out=t2, in_=t1, func=AF.Tanh, scale=GC2 * 0.5), reads=[t1n], writes=[t2n])
            if twice:
                S.op('dve', lambda e: e.scalar_tensor_tensor(out=dst, in0=t2, scalar=1.0, in1=src, op0=ALU.add, op1=ALU.mult),
                     reads=[t2n, srcn], writes=[dstn])
            else:
                S.op('dve', lambda e: e.scalar_tensor_tensor(out=t1, in0=t2, scalar=1.0, in1=src, op0=ALU.add, op1=ALU.mult),
                     reads=[t2n, srcn], writes=[t1n])
                S.op('act', lambda e: e.activation(out=dst, in_=t1, func=AF.Copy, scale=0.5), reads=[t1n], writes=[dstn])

        def rmsrope(P_, H, src3, srcn, wrep, cosp, sinp, out3, outn, tmp, pre):
            sq = tmp['sq'][0:P_, 0:H, :]
            y = tmp['y'][0:P_, 0:H, :]
            st = tmp['st']
            r1 = tmp['r1'][0:P_, 0:H, :]
            r2 = tmp['r2'][0:P_, 0:H, :]
            S.op('act', lambda e: e.activation(out=sq, in_=src3, func=AF.Square), reads=[srcn], writes=[pre + 'sq'])
            S.op('dve', lambda e: e.tensor_reduce(out=st[0:P_, 0:H], in_=sq, axis=AX.X, op=ALU.add),
                 reads=[pre + 'sq'], writes=[pre + 'st0'])
            S.op('pool', lambda e: e.tensor_scalar(out=st[0:P_, 16:16 + H], in0=st[0:P_, 0:H], scalar1=1.0 / 64, scalar2=EPS,
                                                   op0=ALU.mult, op1=ALU.add), reads=[pre + 'st0'], writes=[pre + 'st1'])
            S.op('pool', lambda e: e.tensor_tensor(out=st[0:P_, 32:32 + H], in0=st[0:P_, 16:16 + H], in1=mhalfw[0:P_, 0:H], op=ALU.pow),
                 reads=[pre + 'st1', 'cst_mh'], writes=[pre + 'st2'])
            S.op('dve', lambda e: e.tensor_tensor(out=y, in0=src3,
                                                  in1=st[0:P_, 32:32 + H].unsqueeze(2).broadcast_to([P_, H, 64]), op=ALU.mult),
                 reads=[srcn, pre + 'st2'], writes=[pre + 'y'])
            wfull = wrep if len(wrep.shape) == 3 else wrep.unsqueeze(1).broadcast_to([P_, H, 64])
            S.op('dve', lambda e: e.tensor_tensor(out=y, in0=y, in1=wfull, op=ALU.mult),
                 reads=[pre + 'y', 'cst', 'wq8', 'wqk'], writes=[pre + 'y'])
            cb_ = cosp.unsqueeze(1).broadcast_to([P_, H, 8])
            sb_ = sinp.unsqueeze(1).broadcast_to([P_, H, 8])
            y1 = y[:, :, 0:8]
            y2 = y[:, :, 8:16]
            S.op('dve', lambda e: e.tensor_tensor(out=r1, in0=y1, in1=cb_, op=ALU.mult), reads=[pre + 'y', 'cst'], writes=[pre + 'r1'])
            S.op('dve', lambda e: e.tensor_tensor(out=r2, in0=y2, in1=sb_, op=ALU.mult), reads=[pre + 'y', 'cst'], writes=[pre + 'r2'])
            S.op('dve', lambda e: e.tensor_tensor(out=out3[:, :, 0:8], in0=r1, in1=r2, op=ALU.subtract),
                 reads=[pre + 'r1', pre + 'r2'], writes=[outn])
            S.op('dve', lambda e: e.tensor_tensor(out=r1, in0=y2, in1=cb_, op=ALU.mult), reads=[pre + 'y', 'cst'], writes=[pre + 'r1'])
            S.op('dve', lambda e: e.tensor_tensor(out=r2, in0=y1, in1=sb_, op=ALU.mult), reads=[pre + 'y', 'cst'], writes=[pre + 'r2'])
            S.op('dve', lambda e: e.tensor_tensor(out=out3[:, :, 8:16], in0=r1, in1=r2, op=ALU.add),
                 reads=[pre + 'r1', pre + 'r2'], writes=[outn])
            S.op('act', lambda e: e.activation(out=out3[:, :, 16:64], in_=y[:, :, 16:64], func=AF.Copy),
                 reads=[pre + 'y'], writes=[outn])

        def pass0():
            with ExitStack() as st:
                wcv = sbt(st, "wcv", [128, 8, 256], BF16)
                pk1 = sbt(st, "pk1", [128, 32, 256], BF16)
                pv1 = sbt(st, "pv1", [128, 32, 256], BF16)
                pk2 = sbt(st, "pk2", [128, 2, 64], BF16)
                pv2 = sbt(st, "pv2", [128, 2, 64], BF16)
                rawk = sbt(st, "rawk", [128, S_ + 16], F32)
                rawv = sbt(st, "rawv", [128, S_ + 16], F32)
                zk = sbt(st, "zk", [128, 32, 256], BF16)
                zv = sbt(st, "zv", [128, 32, 256], BF16)
                xb = [sbt(st, f"xb{i}", [128, D], F32) for i in range(2)]
                xs = sbt(st, "xs", [128, D], F32)
                ss = sbt(st, "ss", [128, 4], F32)
                xnT = sbt(st, "xnT", [128, 8, 128], BF16)
                hT = sbt(st, "hT", [128, 2, 256], BF16)
                g1 = sbt(st, "g1", [128, 256], F32)
                g2 = sbt(st, "g2", [128, 256], F32)
                kc32 = sbt(st, "kc32", [128, 64], F32)
                kcn = sbt(st, "kcn", [128, 64], BF16)
                tmp = dict(sq=sbt(st, "t_sq", [128, 12, 64], F32), y=sbt(st, "t_y", [128, 12, 64], F32),
                           st=sbt(st, "t_st", [128, 48], F32), r1=sbt(st, "t_r1", [128, 12, 8], F32),
                           r2=sbt(st, "t_r2", [128, 12, 8], F32))
                tp = pst(st, "tp", [128, 1024], F32)
                cp = [pst(st, f"cp{i}", [128, 512], F32) for i in range(2)]
                hp = [pst(st, f"hp{i}", [128, 512], F32) for i in range(2)]
                kp = pst(st, "kp", [128, 512], F32)
                tb = pst(st, "tb", [128, 1024], BF16)

                load_w(wcv, wcv_d, 8, 256, 'wcv')
                load_w(pk1, pk1_d, 32, 256, 'pk1')
                load_w(pv1, pv1_d, 32, 256, 'pv1')
                load_w(pk2, pk2_d, 2, 64, 'pk2')
                load_w(pv2, pv2_d, 2, 64, 'pv2')
                for t in range(NT):
                    b = xb[t % 2]
                    bn = f"xb{t % 2}"
                    S.dma('sp', lambda e, b=b, t=t: e.dma_start(out=b[:], in_=x_d[t * 128:(t + 1) * 128, :]), writes=[bn])
                    norm_tile(b, bn, 'nw1', xs, ss, tp, xnT)
                    pb = cp[t % 2]
                    pn = f"cp{t % 2}"
                    for c in range(2):
                        for k in range(8):
                            S.op('pe', lambda e, c=c, k=k, pb=pb: e.matmul(
                                out=pb[:, c * 128:(c + 1) * 128], lhsT=wcv[:, k, c * 128:(c + 1) * 128], rhs=xnT[:, k, :],
                                start=(k == 0), stop=(k == 7)), reads=['wcv', 'xnT'], writes=[pn])
                    S.op('act', lambda e, pb=pb, t=t: e.activation(out=rawk[:, t * 128:(t + 1) * 128], in_=pb[:, 0:128], func=AF.Copy),
                         reads=[pn], writes=['rawk'])
                    S.op('act', lambda e, pb=pb, t=t: e.activation(out=rawv[:, t * 128:(t + 1) * 128], in_=pb[:, 128:256], func=AF.Copy),
                         reads=[pn], writes=['rawv'])
                if DBG['stop'] <= 1:
                    S.barrier()
                    return
                posk = C('posk')
                posv = C('posv')
                for l in range(32):
                    S.op('dve', lambda e, l=l: e.tensor_scalar(
                        out=zk[:, l, 0:NCMP], in0=rawk[:, l:l + 16 * (NCMP - 1) + 1:16], scalar1=posk[:, l:l + 1], scalar2=None,
                        op0=ALU.add), reads=['rawk', 'cst'], writes=['zk'])
                    S.op('pool', lambda e, l=l: e.tensor_scalar(
                        out=zv[:, l, 0:NCMP], in0=rawv[:, l:l + 16 * (NCMP - 1) + 1:16], scalar1=posv[:, l:l + 1], scalar2=None,
                        op0=ALU.add), reads=['rawv', 'cst'], writes=['zv'])
                if DBG['stop'] <= 2:
                    S.barrier()
                    return
                nch = [(0, min(128, NCMP))]
                if NCMP > 128:
                    nch.append((128, NCMP - 128))
                ovl = C('ovl')
                for kv in range(2):
                    z = (zk, zv)[kv]
                    zn = ('zk', 'zv')[kv]
                    w1 = (pk1, pv1)[kv]
                    w1n = ('pk1', 'pv1')[kv]
                    w2 = (pk2, pv2)[kv]
                    w2n = ('pk2', 'pv2')[kv]
                    for g in range(2):
                        gs_ = slice(g * 64, (g + 1) * 64)
                        for hc in range(2):
                            hb_ = hp[hc]
                            hn_ = f"hp{hc}"
                            for l in range(32):
                                S.op('pe', lambda e, l=l, hc=hc, hb_=hb_, z=z, w1=w1, gs_=gs_: e.matmul(
                                    out=hb_[:, 0:NCMP], lhsT=w1[gs_, l, hc * 128:(hc + 1) * 128], rhs=z[gs_, l, 0:NCMP],
                                    start=(l == 0), stop=(l == 31)), reads=[w1n, zn], writes=[hn_])
                            gelu(128, NCMP, hb_[:, 0:NCMP], hn_, hT[:, hc, 0:NCMP], f'hT{hc}',
                                 g1[:, 0:NCMP], 'g1', g2[:, 0:NCMP], 'g2')
                        if DBG['stop'] <= 3:
                            continue
                        for ci, (n0, sz) in enumerate(nch):
                            for hc in range(2):
                                S.op('pe', lambda e, hc=hc, n0=n0, sz=sz, w2=w2: e.matmul(
                                    out=kp[0:sz, 0:64], lhsT=hT[:, hc, n0:n0 + sz], rhs=w2[:, hc, :],
                                    start=(hc == 0), stop=(hc == 1)), reads=[f'hT{hc}', w2n], writes=['kp'])
                            if DBG['stop'] <= 4:
                                continue
                            if kv == 0:
                                S.op('act', lambda e, sz=sz: e.activation(out=kc32[0:sz, :], in_=kp[0:sz, 0:64], func=AF.Copy),
                                     reads=['kp'], writes=['kc32'])
                                cc = C('cosc')[:, ci * 8:(ci + 1) * 8]
                                sc = C('sinc')[:, ci * 8:(ci + 1) * 8]
                                rmsrope(sz, 1, kc32[0:sz, :].rearrange("p (h d) -> p h d", h=1), 'kc32', C('wk0')[0:sz, :],
                                        cc[0:sz, :], sc[0:sz, :], kcn[0:sz, :].rearrange("p (h d) -> p h d", h=1), 'kcn', tmp, 'p0')
                                S.op('pe', lambda e, sz=sz: e.transpose(out=tb[0:64, 0:sz], in_=kcn[0:sz, :], identity=identb[0:sz, 0:sz]),
                                     reads=['kcn', 'identb'], writes=['tb'])
                                S.op('dve', lambda e, sz=sz, n0=n0, g=g: e.tensor_copy(out=kcT[:, g, n0:n0 + sz], in_=tb[0:64, 0:sz]),
                                     reads=['tb'], writes=['kcT'])
                            else:
                                S.op('act', lambda e, sz=sz, g=g, ci=ci: e.activation(out=vca[0:sz, g, ci, 0:64], in_=kp[0:sz, 0:64], func=AF.Copy),
                                     reads=['kp'], writes=['vca'])
                                S.op('dve', lambda e, sz=sz, g=g, ci=ci: e.tensor_copy(out=vca[0:sz, g, ci, 64:129], in_=ovl[0:sz, ci * 65:(ci + 1) * 65]),
                                     reads=['cst'], writes=['vca'])
                S.barrier()

        def pass1():
            with ExitStack() as st:
                wtm = sbt(st, "wtm", [128, 8, 1048], BF16)
                wxy = sbt(st, "wxy", [128, 8, 2048], BF16)
                wa = sbt(st, "wa", [128, 8, 128], BF16)
                wx = sbt(st, "wx", [128, 8, 128], BF16)
                ksT = [sbt(st, f"ksT{g}", [128, S_], BF16) for g in range(2)]
                kwT = sbt(st, "kwT", [64, 2, S_], BF16)
                vsa = sbt(st, "vsa", [128, NT, 2, 65], BF16)
                vwa = sbt(st, "vwa", [128, 8, 2, 65], BF16)
                mk = sbt(st, "mk", [128, 1024], BF16)
                jw = sbt(st, "jw", [8, 792], BF16)
                ones2 = sbt(st, "ones2", [128, 2], BF16)
                xb = [sbt(st, f"xb{i}", [128, D], F32) for i in range(2)]
                xs = sbt(st, "xs", [128, D], F32)
                ss = sbt(st, "ss", [128, 4], F32)
                xnT = sbt(st, "xnT", [128, 8, 128], BF16)
                qkv = sbt(st, "qkv", [128, 1024], F32)
                qraw = qkv[:, 0:512]
                kvraw = qkv[:, 512:1024]
                qkn = sbt(st, "qkn", [128, 12, 64], BF16)
                wqk = sbt(st, "wqk", [128, 12, 64], F32)
                tmp8 = sbt(st, "tmp8", [128, 8], F32)
                bah = sbt(st, "bah", [128, 16], F32)
                gsgs = [sbt(st, f"gsg{i}", [128, 24], F32) for i in range(2)]
                qaug = sbt(st, "qaug", [128, 8, 128], BF16)
                qT = sbt(st, "qT", [64, 8, 128], BF16)
                qaTs = [[sbt(st, f"qaT{p}{g}", [128, 512], BF16) for g in range(2)] for p in range(2)]
                NB = 3
                pT = [sbt(st, f"pT{i}", [128, 512], BF16) for i in range(NB)]
                pTc = [sbt(st, f"pTc{i}", [128, 512], BF16) for i in range(2)]
                oaccs = [sbt(st, f"oacc{i}", [128, 8, 64], F32) for i in range(2)]
                otmpF = sbt(st, "otmpF", [128, 4, 64], F32)
                otmpA = sbt(st, "otmpA", [128, 4, 64], F32)
                obf = sbt(st, "obf", [128, 512], BF16)
                onT = sbt(st, "onT", [128, 4, 128], BF16)
                imp = sbt(st, "imp", [128, 64], F32)
                sc1 = sbt(st, "sc1", [128, 64], F32)
                sc2 = sbt(st, "sc2", [128, 64], F32)
                m8 = sbt(st, "m8", [128, 16], F32)
                sm = sbt(st, "sm", [128, 16], F32)
                wq8 = sbt(st, "wq8", [128, 64], F32)
                tmp = dict(sq=sbt(st, "t_sq", [128, 12, 64], F32), y=sbt(st, "t_y", [128, 12, 64], F32),
                           st=sbt(st, "t_st", [128, 48], F32), r1=sbt(st, "t_r1", [128, 12, 8], F32),
                           r2=sbt(st, "t_r2", [128, 12, 8], F32))
                xrxs = [sbt(st, f"xrx{i}", [128, 8, 132], F32) for i in range(2)]
                yrbs = [sbt(st, f"yrb{i}", [128, 8, 128], F32) for i in range(2)]
                xc = sbt(st, "xc", [128, 8, 128], F32)
                xcb = sbt(st, "xcb", [128, 8, 128], BF16)
                rg = sbt(st, "rg", [128, 8, 128], F32)
                ig = sbt(st, "ig", [128, 8, 128], F32)
                av = sbt(st, "av", [128, 8, 128], F32)
                bt = sbt(st, "bt", [128, 8, 128], F32)
                hs = sbt(st, "hs", [128, 8, 128], F32)
                hst = sbt(st, "hst", [128, 8], F32)
                cl = sbt(st, "cl", [128, 8], F32)
                olT = sbt(st, "olT", [128, 8, 128], BF16)
                stp = [pst(st, f"st{i}", [128, 512], F32) for i in range(NB)]
                pvb = [pst(st, f"pv{i}", [128, 512], F32) for i in range(2)]
                mz = pst(st, "mz", [128, 1024], BF16)
                fab = [pst(st, "fa", [128, 512], F32), pst(st, "fb", [128, 512], F32)]
                pj = fab
                pjn = ['fa', 'fb']

                load_w(wtm, wtm_d, 8, 1048, 'wtm')
                load_w(wxy, wxy_d, 8, 2048, 'wxy')
                load_w(wa, wa_d, 8, 128, 'wa')
                load_w(wx, wx_d, 8, 128, 'wx')
                S.dma('pool', lambda e: e.dma_start(out=mk[:], in_=mk_d[:, :]), writes=['mk'])
                S.dma('pool', lambda e: e.dma_start(out=jw[:], in_=jw_d[:, :]), writes=['jw'])
                for g in range(2):
                    for c0 in range(0, S_, 2048):
                        c1 = min(S_, c0 + 2048)
                        S.dma('pool', lambda e, g=g, c0=c0, c1=c1: e.dma_start(out=ksT[g][64:128, c0:c1], in_=E_d[:, c0:c1]),
                              writes=[f'ksE{g}'])
                S.op('dve', lambda e: e.tensor_scalar(out=wq8[:], in0=C('wq'), scalar1=0.125, scalar2=None, op0=ALU.mult),
                     reads=['cst'], writes=['wq8'])
                S.op('dve', lambda e: e.tensor_copy(out=wqk[:, 0:8, :], in_=wq8[:].unsqueeze(1).broadcast_to([128, 8, 64])), reads=['wq8'], writes=['wqk'])
                S.op('dve', lambda e: e.tensor_copy(out=wqk[:, 8:10, :], in_=C('wk1').unsqueeze(1).broadcast_to([128, 2, 64])), reads=['cst'], writes=['wqk'])
                S.op('dve', lambda e: e.tensor_copy(out=wqk[:, 10:12, :], in_=C('wk2').unsqueeze(1).broadcast_to([128, 2, 64])), reads=['cst'], writes=['wqk'])
                S.op('dve', lambda e: e.tensor_scalar(out=bah[:, 0:8], in0=C('ba'), scalar1=0.5, scalar2=None, op0=ALU.mult), reads=['cst'], writes=['bah'])
                S.op('dve', lambda e: e.tensor_scalar(out=bah[:, 8:16], in0=C('bx'), scalar1=0.5, scalar2=None, op0=ALU.mult), reads=['cst'], writes=['bah'])
                S.op('pool', lambda e: e.memset(vsa[:, :, :, 64:65], 1.0), writes=['vsa1'])
                S.op('pool', lambda e: e.memset(vwa[:, :, :, 64:65], 1.0), writes=['vwa1'])
                S.op('pool', lambda e: e.memset(ones2[:], 1.0), writes=['ones2'])
                S.op('pool', lambda e: e.memset(xrxs[0][:, :, 0:3], 0.0), writes=['xrxh0'])
                S.op('pool', lambda e: e.memset(hst[:], 0.0), writes=['hst'])
                S.op('pool', lambda e: e.memset(qaug[:], 0.0), writes=['qaugq', 'qaugm0', 'qaugm1'])
                S.op('act', lambda e: e.activation(out=cl[:], in_=C('lam'), func=AF.Exp, scale=-1.0), reads=['cst'], writes=['cl'])
                S.op('dve', lambda e: e.tensor_scalar(out=cl[:], in0=cl[:], scalar1=1.0, scalar2=None, op0=ALU.add),
                     reads=['cl'], writes=['cl'])
                S.op('act', lambda e: e.activation(out=cl[:], in_=cl[:], func=AF.Ln), reads=['cl'], writes=['cl'])
                S.op('dve', lambda e: e.tensor_scalar(out=cl[:], in0=cl[:], scalar1=-4.0, scalar2=None, op0=ALU.mult),
                     reads=['cl'], writes=['cl'])
                clh = cl
                tkA = C('tkA')
                tkB = C('tkB')
                cw = C('cw')
                cbv = C('cb')
                bav = C('ba')
                bxv = C('bx')
                Mc4 = mk[:, 0:512]
                Ml4 = mk[:, 512:1024]
                W8 = jw[:, 280:792]
                ctr = [0]

                def front(i):
                    p = i % 2
                    gsg = gsgs[p]
                    qaT = qaTs[p]
                    oacc = oaccs[p]
                    xrx = xrxs[p]
                    yrb = yrbs[p]
                    xrxn = f'xrx{p}'
                    xrxhn = f'xrxh{p}'
                    yrbn = f'yrb{p}'
                    gsgn = f'gsg{p}'
                    oaccn = f'oacc{p}'
                    T0 = i * 128
                    b = xb[i % 2]
                    bn = f"xb{i % 2}"
                    S.dma('sp', lambda e, b=b, i=i: e.dma_start(out=b[:], in_=x_d[i * 128:(i + 1) * 128, :]), writes=[bn])
                    norm_tile(b, bn, 'nw1', xs, ss, fab, xnT, tpn=('fa', 'fb'))
                    yield
                    for (c0, c1, pb, pn) in ((0, 512, pj[0], pjn[0]), (512, 1024, pj[1], pjn[1])):
                        for k in range(8):
                            S.op('pe', lambda e, k=k, c0=c0, c1=c1, pb=pb: e.matmul(
                                out=pb[:, 0:512], lhsT=xnT[:, k, :], rhs=wtm[:, k, c0:c1], start=(k == 0), stop=(k == 7)),
                                reads=['xnT', 'wtm'], writes=[pn])
                    S.op('act', lambda e: e.activation(out=qraw, in_=pj[0][:, 0:512], func=AF.Copy), reads=[pjn[0]], writes=['qraw', 'qkraw'])
                    S.op('act', lambda e: e.activation(out=kvraw, in_=pj[1][:, 0:512], func=AF.Copy), reads=[pjn[1]], writes=['kvraw', 'qkraw'])
                    for k in range(8):
                        S.op('pe', lambda e, k=k: e.matmul(out=fab[0][:, 0:24], lhsT=xnT[:, k, :], rhs=wtm[:, k, 1024:1048],
                                                           start=(k == 0), stop=(k == 7)), reads=['xnT', 'wtm'], writes=['fa'])
                    S.op('act', lambda e: e.activation(out=gsg[:], in_=fab[0][:, 0:24], func=AF.Tanh, scale=0.5), reads=['fa'], writes=[gsgn])
                    S.op('dve', lambda e: e.tensor_scalar(out=gsg[:], in0=gsg[:], scalar1=0.5, scalar2=0.5, op0=ALU.mult, op1=ALU.add),
                         reads=[gsgn], writes=[gsgn])
                    for c4 in range(4):
                        pb = fab[(c4 + 1) % 2]
                        pn = pjn[(c4 + 1) % 2]
                        yield
                        for cc in range(4):
                            c = c4 * 4 + cc
                            for k in range(8):
                                S.op('pe', lambda e, c=c, cc=cc, k=k, pb=pb: e.matmul(
                                    out=pb[:, cc * 128:(cc + 1) * 128], lhsT=wxy[:, k, c * 128:(c + 1) * 128], rhs=xnT[:, k, :],
                                    start=(k == 0), stop=(k == 7)), reads=['wxy', 'xnT'], writes=[pn])
                        src = pb[:, 0:512].rearrange("p (a b) -> p a b", a=4)
                        if c4 < 2:
                            S.op('act', lambda e, c4=c4, src=src: e.activation(out=xrx[:, c4 * 4:(c4 + 1) * 4, 3:131], in_=src, func=AF.Copy),
                                 reads=[pn], writes=[xrxn])
                        else:
                            S.op('act', lambda e, c4=c4, src=src: e.activation(out=yrb[:, (c4 - 2) * 4:(c4 - 1) * 4, :], in_=src, func=AF.Copy),
                                 reads=[pn], writes=[yrbn])
                    yield
                    cosp = C('cos')[:, i * 8:(i + 1) * 8]
                    sinp = C('sin')[:, i * 8:(i + 1) * 8]
                    rmsrope(128, 12, qkv[:, 0:768].rearrange("p (h d) -> p h d", h=12), 'qkraw', wqk[:], cosp, sinp,
                            qkn[:], 'qkn', tmp, 'p1')
                    S.op('act', lambda e: e.activation(out=qaug[:, :, 0:64], in_=qkn[:, 0:8, :], func=AF.Copy), reads=['qkn'], writes=['qaugq'])
                    yield
                    for j in range(4):
                        S.op('pe', lambda e, j=j: e.transpose(out=mz[0:64, j * 128:(j + 1) * 128], in_=qkn[:, 8 + j, :], identity=identb[:]),
                             reads=['qkn', 'identb'], writes=['mz'])
                    for g in range(2):
                        S.op('act', lambda e, g=g, T0=T0: e.activation(out=ksT[g][0:64, T0:T0 + 128], in_=mz[0:64, g * 128:(g + 1) * 128], func=AF.Copy),
                             reads=['mz'], writes=[f'ksT{g}_{i}'])
                    S.op('act', lambda e, T0=T0: e.activation(out=kwT[:, :, T0:T0 + 128],
                                                               in_=mz[0:64, 256:512].rearrange("p (a b) -> p a b", a=2), func=AF.Copy),
                         reads=['mz'], writes=[f'kwT_{i}'])
                    S.op('act', lambda e, i=i: e.activation(out=vsa[:, i, :, 0:64], in_=kvraw[:, 256:384].rearrange("p (g d) -> p g d", g=2),
                                                            func=AF.Copy), reads=['kvraw'], writes=[f'vsa_{i}'])
                    S.op('act', lambda e, i=i: e.activation(out=vwa[:, i % 8, :, 0:64], in_=kvraw[:, 384:512].rearrange("p (g d) -> p g d", g=2),
                                                            func=AF.Copy), reads=['kvraw'], writes=[f'vwa_{i % 8}'])
                    yield
                    for h in range(8):
                        S.op('pe', lambda e, h=h: e.transpose(out=mz[0:64, h * 128:(h + 1) * 128], in_=qkn[:, h, :], identity=identb[:]),
                             reads=['qkn', 'identb'], writes=['mz'])
                    S.op('act', lambda e: e.activation(out=qT[:].rearrange("p a b -> p (a b)"), in_=mz[0:64, :], func=AF.Copy), reads=['mz'], writes=['qT'])

                    yield
                    n_hi = min(NCMP, 8 * i + 7)
                    chunks = [(0, 0, min(128, n_hi))]
                    if n_hi > 128:
                        chunks.append((1, 128, n_hi - 128))
                    nchk = len(chunks)
                    for g in range(2):
                        bufs = []
                        for (ci, n0, Kc) in chunks:
                            bufs.append(ci)
                            sp_ = fab[ci]
                            sn_ = pjn[ci]
                            pt_ = pTc[ci]
                            ptn = f"pTc{ci}"
                            s0 = 265 + n0 - 8 * i
                            S.op('pe', lambda e, g=g, n0=n0, Kc=Kc, sp_=sp_: e.matmul(
                                out=sp_[0:Kc, :], lhsT=kcT[:, g, n0:n0 + Kc], rhs=qT[:, 4 * g:4 * g + 4, :].rearrange("p a b -> p (a b)"),
                                start=True, stop=False), reads=['kcT', 'qT'], writes=[sn_])
                            S.op('pe', lambda e, s0=s0, Kc=Kc, sp_=sp_: e.matmul(
                                out=sp_[0:Kc, :], lhsT=jw[:, s0:s0 + Kc], rhs=W8, start=False, stop=True), reads=['jw'], writes=[sn_])
                            S.op('act', lambda e, Kc=Kc, sp_=sp_, pt_=pt_: e.activation(out=pt_[0:Kc, :], in_=sp_[0:Kc, :], func=AF.Exp),
                                 reads=[sn_], writes=[ptn])
                        for h in range(4):
                            for idx, (ci, n0, Kc) in enumerate(chunks):
                                pt_ = pTc[bufs[idx]]
                                ptn = f"pTc{bufs[idx]}"
                                S.op('pe', lambda e, h=h, ci=ci, Kc=Kc, g=g, pt_=pt_, idx=idx, lastc=(idx == nchk - 1): e.matmul(
                                    out=fab[0][:, h * 128:(h + 1) * 128], lhsT=pt_[0:Kc, h * 128:(h + 1) * 128], rhs=vca[0:Kc, g, ci, 0:128],
                                    start=(idx == 0), stop=lastc), reads=[ptn, 'vca'], writes=['fa'])
                        for h in range(4):
                            for idx, (ci, n0, Kc) in enumerate(chunks):
                                pt_ = pTc[bufs[idx]]
                                ptn = f"pTc{bufs[idx]}"
                                S.op('pe', lambda e, h=h, Kc=Kc, pt_=pt_, idx=idx, lastc=(idx == nchk - 1): e.matmul(
                                    out=fab[1][:, 2 * h:2 * h + 2], lhsT=pt_[0:Kc, h * 128:(h + 1) * 128], rhs=ones2[0:Kc, :],
                                    start=(idx == 0), stop=lastc), reads=[ptn, 'ones2'], writes=['fb'])
                        S.op('dve', lambda e: e.tensor_scalar(out=sm[:, 0:4], in0=fab[1][:, 0:8:2], scalar1=1e-30, scalar2=None, op0=ALU.max),
                             reads=['fb'], writes=['sm0'])
                        S.op('dve', lambda e: e.reciprocal(out=sm[:, 4:8], in_=sm[:, 0:4]), reads=['sm0'], writes=['sm1'])
                        S.op('dve', lambda e, g=g: e.tensor_tensor(out=sm[:, 8:12], in0=sm[:, 4:8], in1=gsg[:, 4 * g:4 * g + 4], op=ALU.mult),
                             reads=['sm1', gsgn], writes=['sm2'])
                        pv3 = fab[0][:, 0:512].rearrange("p (h c) -> p h c", h=4)
                        S.op('dve', lambda e, g=g, pv3=pv3: e.tensor_tensor(
                            out=oacc[:, 4 * g:4 * g + 4, :], in0=pv3[:, :, 0:64], in1=sm[:, 8:12].unsqueeze(2).broadcast_to([128, 4, 64]),
                            op=ALU.mult), reads=['fa', 'sm2'], writes=[oaccn])
                        S.op('dve', lambda e, pv3=pv3: e.tensor_tensor(
                            out=otmpF[:], in0=pv3[:, :, 64:128], in1=sm[:, 4:8].unsqueeze(2).broadcast_to([128, 4, 64]),
                            op=ALU.mult), reads=['fa', 'sm1'], writes=['otmpF'])
                        S.op('dve', lambda e: e.tensor_reduce(out=imp[:], in_=otmpF[:].rearrange("p h j -> p j h"), axis=AX.X, op=ALU.add),
                             reads=['otmpF'], writes=['imp'])
                        yield
                        a0 = 63 - 2 * i
                        Asl = tkA[:, a0:a0 + NSLC]
                        Bsl = tkB[:, a0:a0 + NSLC]
                        S.op('dve', lambda e, Bsl=Bsl: e.tensor_tensor(out=sc1[:, 0:NSLC], in0=imp[:, 0:NSLC], in1=Bsl, op=ALU.mult),
                             reads=['imp', 'cst'], writes=['sc1'])
                        S.op('dve', lambda e, Asl=Asl: e.tensor_tensor(out=sc1[:, 0:NSLC], in0=sc1[:, 0:NSLC], in1=Asl, op=ALU.add),
                             reads=['sc1', 'cst'], writes=['sc1'])
                        S.op('dve', lambda e: e.memset(sc1[:, 0:1], 1e4), writes=['sc1'])
                        S.op('dve', lambda e: e.max(out=m8[:, 0:8], in_=sc1[:, 0:NSLC]), reads=['sc1'], writes=['m8a'])
                        S.op('dve', lambda e: e.match_replace(out=sc2[:, 0:NSLC], in_to_replace=m8[:, 0:8], in_values=sc1[:, 0:NSLC],
                                                              imm_value=-3.0e38), reads=['sc1', 'm8a'], writes=['sc2'])
                        S.op('dve', lambda e: e.max(out=m8[:, 8:16], in_=sc2[:, 0:NSLC]), reads=['sc2'], writes=['m8b'])
                        S.op('dve', lambda e, g=g: e.tensor_scalar(
                            out=qaug[:, 4 * g:4 * g + 4, 64:64 + NSLC], in0=sc1[:, 0:NSLC].unsqueeze(1).broadcast_to([128, 4, NSLC]),
                            scalar1=m8[:, 15:16], scalar2=NEGM, op0=ALU.is_lt, op1=ALU.mult), reads=['sc1', 'm8b'], writes=[f'qaugm{g}'])

                    yield
                    for g in range(2):
                        for h in range(4):
                            S.op('pe', lambda e, g=g, h=h: e.transpose(out=mz[:, h * 128:(h + 1) * 128], in_=qaug[:, 4 * g + h, :], identity=identb[:]),
                                 reads=['qaugq', f'qaugm{g}', 'identb'], writes=['mz'])
                        S.op('act', lambda e, g=g: e.activation(out=qaT[g][:], in_=mz[:, 0:512], func=AF.Copy), reads=['mz'], writes=[f'qaT{p}{g}'])

                def attn(i):
                    p = i % 2
                    gsg = gsgs[p]
                    qaT = qaTs[p]
                    oacc = oaccs[p]
                    xrx = xrxs[p]
                    yrb = yrbs[p]
                    xrxn = f'xrx{p}'
                    xrxhn = f'xrxh{p}'
                    yrbn = f'yrb{p}'
                    gsgn = f'gsg{p}'
                    oaccn = f'oacc{p}'
                    items = []
                    gi = 0
                    for br in (1, 2):
                        for g in range(2):
                            kts = list(range(0, i + 1)) if br == 1 else list(range(max(0, i - 4), i + 1))
                            for idx, kt in enumerate(kts):
                                items.append(dict(br=br, g=g, kt=kt, first=(idx == 0), last=(idx == len(kts) - 1), grp=gi))
                            gi += 1

                    def stage_S(it):
                        bi = ctr[0] % NB
                        ctr[0] += 1
                        it['bi'] = bi
                        sp_ = stp[bi]
                        sn_ = f"st{bi}"
                        pt_ = pT[bi]
                        ptn = f"pT{bi}"
                        g, kt, br = it['g'], it['kt'], it['br']
                        mask = None
                        if kt == i:
                            mask = Mc4
                        elif br == 2 and kt == i - 4:
                            mask = Ml4
                        if br == 1:
                            S.op('pe', lambda e: e.matmul(out=sp_[:, :], lhsT=ksT[g][:, kt * 128:(kt + 1) * 128], rhs=qaT[g][:, :],
                                                          start=True, stop=(mask is None)),
                                 reads=[f'ksT{g}_{kt}', f'ksE{g}', f'qaT{p}{g}'], writes=[sn_])
                        else:
                            S.op('pe', lambda e: e.matmul(out=sp_[:, :], lhsT=kwT[:, g, kt * 128:(kt + 1) * 128], rhs=qaT[g][0:64, :],
                                                          start=True, stop=(mask is None)),
                                 reads=[f'kwT_{kt}', f'qaT{p}{g}'], writes=[sn_])
                        if mask is not None:
                            S.op('pe', lambda e: e.matmul(out=sp_[:, :], lhsT=identb[:], rhs=mask, start=False, stop=True),
                                 reads=['identb', 'mk'], writes=[sn_])
                        S.op('act', lambda e: e.activation(out=pt_[:], in_=sp_[:, :], func=AF.Exp), reads=[sn_], writes=[ptn])

                    def stage_P(it):
                        bi = it['bi']
                        pt_ = pT[bi]
                        ptn = f"pT{bi}"
                        g, kt, br = it['g'], it['kt'], it['br']
                        pvt = pvb[it['grp'] % 2]
                        pvn = f"pv{it['grp'] % 2}"
                        va = vsa if br == 1 else vwa
                        van = (f'vsa_{kt}', 'vsa1') if br == 1 else (f'vwa_{kt % 8}', 'vwa1')
                        kslot = kt if br == 1 else kt % 8
                        for h in range(4):
                            S.op('pe', lambda e, h=h: e.matmul(
                                out=pvt[:, h * 65:(h + 1) * 65], lhsT=pt_[:, h * 128:(h + 1) * 128], rhs=va[:, kslot, g, :],
                                start=(it['first'] and h == 0), stop=it['last'], skip_group_check=True),
                                reads=[ptn, van[0], van[1]], writes=[pvn])
                        if it['last']:
                            pv3 = pvt[:, 0:260].rearrange("p (h c) -> p h c", h=4)
                            S.op('dve', lambda e: e.reciprocal(out=sm[:, 12:16], in_=pvt[:, 64:260:65]), reads=[pvn], writes=['sm4'])
                            S.op('dve', lambda e: e.tensor_tensor(out=sm[:, 12:16], in0=sm[:, 12:16],
                                                                  in1=gsg[:, br * 8 + 4 * g:br * 8 + 4 * g + 4], op=ALU.mult),
                                 reads=['sm4', gsgn], writes=['sm4'])
                            S.op('dve', lambda e: e.tensor_tensor(out=otmpA[:], in0=pv3[:, :, 0:64],
                                                                  in1=sm[:, 12:16].unsqueeze(2).broadcast_to([128, 4, 64]), op=ALU.mult),
                                 reads=[pvn, 'sm4'], writes=['otmpA'])
                            S.op('dve', lambda e: e.tensor_tensor(out=oacc[:, 4 * g:4 * g + 4, :], in0=oacc[:, 4 * g:4 * g + 4, :], in1=otmpA[:],
                                                                  op=ALU.add), reads=['otmpA', oaccn], writes=[oaccn])
                    LOOK = 2
                    for n_ in range(len(items) + LOOK):
                        if n_ < len(items):
                            stage_S(items[n_])
                        if n_ - LOOK >= 0:
                            stage_P(items[n_ - LOOK])
                        yield

                    S.op('act', lambda e: e.activation(out=obf[:], in_=oacc[:].rearrange("p a b -> p (a b)"), func=AF.Copy),
                         reads=[oaccn], writes=['obf'])
                    for c in range(4):
                        S.op('pe', lambda e, c=c: e.transpose(out=mz[:, c * 128:(c + 1) * 128], in_=obf[:, c * 128:(c + 1) * 128], identity=identb[:]),
                             reads=['obf', 'identb'], writes=['mz'])
                    S.op('act', lambda e: e.activation(out=onT[:].rearrange("p a b -> p (a b)"), in_=mz[:, 0:512], func=AF.Copy), reads=['mz'], writes=['onT'])
                    S.dma('sp', lambda e, i=i: e.dma_start(out=son_d[i * 128:(i + 1) * 128, :], in_=onT[:].rearrange("p a b -> p (a b)")),
                          reads=['onT'], writes=[f'son{i}'])

                def lru(i):
                    p = i % 2
                    gsg = gsgs[p]
                    qaT = qaTs[p]
                    oacc = oaccs[p]
                    xrx = xrxs[p]
                    yrb = yrbs[p]
                    xrxn = f'xrx{p}'
                    xrxhn = f'xrxh{p}'
                    yrbn = f'yrb{p}'
                    gsgn = f'gsg{p}'
                    oaccn = f'oacc{p}'
                    for c in range(8):
                        S.op('act', lambda e, c=c: e.activation(out=xc[:, c, :], in_=xrx[:, c, 3:131], func=AF.Identity,
                                                                scale=cw[:, 24 + c:25 + c], bias=cbv[:, c:c + 1]),
                             reads=[xrxn, xrxhn, 'cst'], writes=['xc'])
                    yield
                    for j in (2, 1, 0):
                        S.op('dve', lambda e, j=j: e.tensor_tensor(
                            out=hs[:], in0=xrx[:, :, j:j + 128],
                            in1=cw[:, j * 8:j * 8 + 8].unsqueeze(2).broadcast_to([128, 8, 128]), op=ALU.mult),
                            reads=[xrxn, xrxhn, 'cst'], writes=['hs'])
                        S.op('dve', lambda e: e.tensor_tensor(out=xc[:], in0=xc[:], in1=hs[:], op=ALU.add),
                             reads=['xc', 'hs'], writes=['xc'])
                        yield
                    xcall = ['xc']
                    S.op('act', lambda e: e.activation(out=xcb[:], in_=xc[:], func=AF.Copy), reads=xcall, writes=['xcb'])
                    S.op('act', lambda e: e.activation(out=xrxs[1 - p][:, :, 0:3], in_=xrx[:, :, 128:131], func=AF.Copy),
                         reads=[xrxn], writes=[f'xrxh{1 - p}'])
                    for gi_, (wmat, wn, dst, dn) in enumerate(((wa, 'wa', rg, 'rg'), (wx, 'wx', ig, 'ig'))):
                        for c4 in range(2):
                            pb = fab[c4]
                            pn = pjn[c4]
                            yield
                            for cc in range(4):
                                c = c4 * 4 + cc
                                S.op('pe', lambda e, c=c, cc=cc, pb=pb, wmat=wmat: e.matmul(
                                    out=pb[:, cc * 128:(cc + 1) * 128], lhsT=wmat[:, c, :], rhs=xcb[:, c, :], start=True, stop=True),
                                    reads=[wn, 'xcb'], writes=[pn])
                            for cc in range(4):
                                c = c4 * 4 + cc
                                S.op('act', lambda e, c=c, cc=cc, pb=pb, dst=dst, gi_=gi_: e.activation(
                                    out=dst[:, c, :], in_=pb[:, cc * 128:(cc + 1) * 128], func=AF.Tanh, scale=0.5,
                                    bias=bah[:, gi_ * 8 + c:gi_ * 8 + c + 1]), reads=[pn, 'bah'], writes=[dn])
                    yield
                    for c in range(8):
                        S.op('act', lambda e, c=c: e.activation(out=av[:, c, :], in_=rg[:, c, :], func=AF.Exp, scale=clh[:, c:c + 1],
                                                                bias=clh[:, c:c + 1]), reads=['rg', 'cl'], writes=['av'])
                    yield
                    S.op('act', lambda e: e.activation(out=bt[:], in_=av[:], func=AF.Square), reads=['av'], writes=['bt'])
                    S.op('act', lambda e: e.activation(out=bt[:], in_=bt[:], func=AF.Sqrt, scale=-1.0, bias=1.0), reads=['bt'], writes=['bt'])
                    S.op('dve', lambda e: e.scalar_tensor_tensor(out=ig[:], in0=ig[:], scalar=1.0, in1=xc[:], op0=ALU.add, op1=ALU.mult),
                         reads=['ig'] + xcall, writes=['ig'])
                    S.op('dve', lambda e: e.scalar_tensor_tensor(out=bt[:], in0=bt[:], scalar=0.5, in1=ig[:], op0=ALU.mult, op1=ALU.mult),
                         reads=['bt', 'ig'], writes=['bt'])
                    yield
                    S.op('dve', lambda e: e.tensor_tensor(out=tmp8[:], in0=av[:, :, 0], in1=hst[:], op=ALU.mult), reads=['av', 'hst'], writes=['tmp8'])
                    S.op('dve', lambda e: e.tensor_tensor(out=bt[:, :, 0], in0=bt[:, :, 0], in1=tmp8[:], op=ALU.add), reads=['bt', 'tmp8'], writes=['bt'])
                    S.op('dve', lambda e: e.memset(av[:, :, 0:1], 0.0), reads=['tmp8'], writes=['av'])
                    S.op('dve', lambda e: e.tensor_tensor_scan(out=hs[:].rearrange("p a b -> p (a b)"), data0=av[:].rearrange("p a b -> p (a b)"),
                                                               data1=bt[:].rearrange("p a b -> p (a b)"), initial=0.0, op0=ALU.mult, op1=ALU.add),
                         reads=['av', 'bt'], writes=['hs'])
                    S.op('dve', lambda e: e.tensor_copy(out=hst[:], in_=hs[:, :, 127]), reads=['hs'], writes=['hst'])
                    yield
                    yr2 = yrb[:].rearrange("p a b -> p (a b)")
                    gelu(128, 1024, yr2, yrbn, rg[:].rearrange("p a b -> p (a b)"), 'rg',
                         xc[:].rearrange("p a b -> p (a b)"), 'xc', ig[:].rearrange("p a b -> p (a b)"), 'ig', twice=True)
                    S.op('dve', lambda e: e.scalar_tensor_tensor(out=olT[:], in0=hs[:], scalar=0.5, in1=rg[:], op0=ALU.mult, op1=ALU.mult),
                         reads=['hs', 'rg'], writes=['olT'])
                    S.dma('sp', lambda e, i=i: e.dma_start(out=sol_d[i * 128:(i + 1) * 128, :], in_=olT[:].rearrange("p a b -> p (a b)")),
                          reads=['olT'], writes=[f'sol{i}'])
                for _ in front(0):
                    pass
                for i in range(NT):
                    n_items = 2 * (i + 1) + 2 * (min(i, 4) + 1) + 3
                    gl = [(attn(i), float(n_items)), (lru(i), 24.0)]
                    if i + 1 < NT:
                        gl.append((front(i + 1), 22.0))
                    drive(gl)
                S.barrier()

        def load_w1_half(dst3, half, name):
            for k in range(8):
                S.dma('pool', lambda e, k=k: e.dma_start(out=dst3[:, k, :],
                                                         in_=wf1_d[:, k * DFF + half * 2048:k * DFF + (half + 1) * 2048]), writes=[name])

        def drive(gens_w):
            gens = [[g_, w_, 0.0] for (g_, w_) in gens_w]
            if DBG.get('seq'):
                for ent in gens:
                    for _ in ent[0]:
                        pass
                return
            wmax = max(w_ for (_, w_) in gens_w)
            while gens:
                for ent in list(gens):
                    ent[2] += ent[1] / wmax
                    while ent[2] >= 1.0:
                        ent[2] -= 1.0
                        try:
                            next(ent[0])
                        except StopIteration:
                            gens.remove(ent)
                            break

        def pass2(shared):
            SUP = 2
            NU = NT // SUP
            TW = SUP * 128
            with ExitStack() as st:
                wgm = sbt(st, "wgm", [128, 8, 2048], BF16)
                wnu = sbt(st, "wnu", [128, 4, 1024], BF16)
                wlu = sbt(st, "wlu", [128, 8, 1024], BF16)
                wo = sbt(st, "wo", [128, 8, 1024], BF16)
                xb = [[sbt(st, f"xb{p}{j}", [128, D], F32) for j in range(SUP)] for p in range(2)]
                xs = sbt(st, "xs", [128, D], F32)
                ss = sbt(st, "ss", [128, 4], F32)
                xnT = sbt(st, "xnT", [128, 8, TW], BF16)
                gms = [sbt(st, f"gm{i}", [128, 16, TW], F32) for i in range(2)]
                onT = [sbt(st, f"onT{i}", [128, 4, TW], BF16) for i in range(2)]
                olT = [sbt(st, f"olT{i}", [128, 8, TW], BF16) for i in range(2)]
                t1 = sbt(st, "t1", [128, 2, TW], F32)
                t2 = sbt(st, "t2", [128, 2, TW], F32)
                mT = sbt(st, "mT", [128, 8, TW], BF16)
                ob = sbt(st, "ob", [128, D], F32)
                tp = pst(st, "tp", [128, 1024], F32)
                pj = [pst(st, f"pj{i}", [128, 512], F32) for i in range(2)]
                pa = pst(st, "pa", [128, 512], F32)
                pbk = pst(st, "pbk", [128, 512], F32)
                yp = pst(st, "yp", [128, 1024], F32)
                load_w(wgm, wgm_d, 8, 2048, 'wgm')
                load_w(wnu, wnu_d, 4, 1024, 'wnu')
                load_w(wlu, wlu_d, 8, 1024, 'wlu')
                load_w(wo, wo_d, 8, 1024, 'wo')
                if 'w1a' in shared:
                    load_w1_half(shared['w1a'], 0, 'w1a')
                    shared['w1a_loaded'] = True

                def front(u):
                    p = u % 2
                    gm = gms[p]
                    on_ = onT[p]
                    ol_ = olT[p]
                    for j in range(SUP):
                        i = u * SUP + j
                        b = xb[p][j]
                        S.dma('sp', lambda e, b=b, i=i: e.dma_start(out=b[:], in_=x_d[i * 128:(i + 1) * 128, :]), writes=[f"xb{p}{j}"])
                        S.dma('sp', lambda e, i=i, j=j: e.dma_start(out=on_[:, :, j * 128:(j + 1) * 128],
                                                                    in_=son_d[i * 128:(i + 1) * 128, :].rearrange("p (a b) -> p a b", a=4)),
                              reads=[f'son{i}'], writes=[f"onT{p}"])
                        S.dma('sp', lambda e, i=i, j=j: e.dma_start(out=ol_[:, :, j * 128:(j + 1) * 128],
                                                                    in_=sol_d[i * 128:(i + 1) * 128, :].rearrange("p (a b) -> p a b", a=8)),
                              reads=[f'sol{i}'], writes=[f"olT{p}"])
                    yield
                    for j in range(SUP):
                        norm_tile(xb[p][j], f"xb{p}{j}", 'nw1', xs, ss, tp, xnT[:, :, j * 128:(j + 1) * 128])
                        yield
                    for c2 in range(8):
                        pb = pj[c2 % 2]
                        pn = f"pj{c2 % 2}"
                        for cc in range(2):
                            c = c2 * 2 + cc
                            for k in range(8):
                                S.op('pe', lambda e, c=c, cc=cc, k=k, pb=pb: e.matmul(
                                    out=pb[:, cc * TW:(cc + 1) * TW], lhsT=wgm[:, k, c * 128:(c + 1) * 128], rhs=xnT[:, k, :],
                                    start=(k == 0), stop=(k == 7)), reads=['wgm', 'xnT'], writes=[pn])
                        S.op('act', lambda e, c2=c2, pb=pb: e.activation(out=gm[:, c2 * 2:(c2 + 1) * 2, :],
                                                                         in_=pb[:, 0:512].rearrange("p (a b) -> p a b", a=2), func=AF.Tanh, scale=0.5),
                             reads=[pn], writes=[f'gm{p}_{c2}'])
                        yield

                def back(u):
                    p = u % 2
                    gm = gms[p]
                    on_ = onT[p]
                    ol_ = olT[p]
                    onn = f"onT{p}"
                    oln = f"olT{p}"
                    for c2 in range(4):
                        for cc in range(2):
                            c = c2 * 2 + cc
                            for k in range(4):
                                S.op('pe', lambda e, c=c, cc=cc, k=k: e.matmul(
                                    out=pa[:, cc * TW:(cc + 1) * TW], lhsT=wnu[:, k, c * 128:(c + 1) * 128], rhs=on_[:, k, :],
                                    start=(k == 0), stop=(k == 3)), reads=['wnu', onn], writes=['pa'])
                        for cc in range(2):
                            c = c2 * 2 + cc
                            for k in range(8):
                                S.op('pe', lambda e, c=c, cc=cc, k=k: e.matmul(
                                    out=pbk[:, cc * TW:(cc + 1) * TW], lhsT=wlu[:, k, c * 128:(c + 1) * 128], rhs=ol_[:, k, :],
                                    start=(k == 0), stop=(k == 7)), reads=['wlu', oln], writes=['pbk'])
                        S.op('dve', lambda e, c2=c2: e.scalar_tensor_tensor(
                            out=t1[:], in0=gm[:, c2 * 2:(c2 + 1) * 2, :], scalar=1.0, in1=pa[:, 0:512].rearrange("p (a b) -> p a b", a=2),
                            op0=ALU.add, op1=ALU.mult), reads=['pa', f'gm{p}_{c2}'], writes=['t1'])
                        S.op('dve', lambda e, c2=c2: e.scalar_tensor_tensor(
                            out=t2[:], in0=gm[:, 8 + c2 * 2:8 + (c2 + 1) * 2, :], scalar=1.0, in1=pbk[:, 0:512].rearrange("p (a b) -> p a b", a=2),
                            op0=ALU.add, op1=ALU.mult), reads=['pbk', f'gm{p}_{c2 + 4}'], writes=['t2'])
                        S.op('dve', lambda e, c2=c2: e.tensor_tensor(out=mT[:, c2 * 2:(c2 + 1) * 2, :], in0=t1[:], in1=t2[:], op=ALU.add),
                             reads=['t1', 't2'], writes=[f'mT{c2}'])
                        yield
                    for j in range(SUP):
                        i = u * SUP + j
                        b = xb[p][j]
                        for n in range(2):
                            for k in range(8):
                                S.op('pe', lambda e, n=n, k=k, j=j: e.matmul(out=yp[:, n * 512:(n + 1) * 512], lhsT=mT[:, k, j * 128:(j + 1) * 128],
                                                                             rhs=wo[:, k, n * 512:(n + 1) * 512], start=(k == 0), stop=(k == 7)),
                                     reads=[f'mT{k // 2}', 'wo'], writes=[f'yp{n}'])
                            yield
                        S.op('dve', lambda e, b=b: e.scalar_tensor_tensor(out=ob[:], in0=yp[:], scalar=0.5, in1=b[:], op0=ALU.mult, op1=ALU.add),
                             reads=['yp0', 'yp1', f"xb{p}{j}"], writes=['ob'])
                        S.dma('sp', lambda e, i=i: e.dma_start(out=out_d[i * 128:(i + 1) * 128, :], in_=ob[:]), reads=['ob'], writes=[f'outd{i}'])

                for _ in front(0):
                    pass
                for u in range(NU):
                    gl = [(back(u), 8.0)]
                    if u + 1 < NU:
                        gl.append((front(u + 1), 11.0))
                    drive(gl)
                S.barrier()

        def pass3(shared):
            SUP = 2
            NU = NT // SUP
            TW = SUP * 128
            with ExitStack() as st:
                w1a = shared['w1a']
                w1b = sbt(st, "w1b", [128, 8, 2048], BF16)
                w2 = sbt(st, "w2s", [128, 32, D], BF16)
                hb = [[sbt(st, f"hb{p}{j}", [128, D], F32) for j in range(SUP)] for p in range(2)]
                xs = sbt(st, "xs", [128, D], F32)
                ss = sbt(st, "ss", [128, 4], F32)
                hnTs = [sbt(st, f"hnT{i}", [128, 8, TW], BF16) for i in range(2)]
                rl = [sbt(st, f"rl{i}", [128, 512], F32) for i in range(2)]
                hid = sbt(st, "hid", [128, 32, TW], BF16)
                ob = sbt(st, "ob", [128, D], F32)
                tp = pst(st, "tp", [128, 1024], F32)
                hp = [pst(st, f"hp{i}", [128, 512], F32) for i in range(2)]
                yps = [pst(st, f"ypA", [128, 1024], F32), pst(st, f"ypB", [128, 1024], F32)]
                if not shared.get('w1a_loaded'):
                    load_w1_half(w1a, 0, 'w1a')
                load_w1_half(w1b, 1, 'w1b')
                load_w(w2, wf2_d, 32, D, 'w2')
                src_d = out_d if 2 in passes else x_d

                def front(u):
                    p = u % 2
                    for j in range(SUP):
                        t = u * SUP + j
                        b = hb[p][j]
                        S.dma('sp', lambda e, b=b, t=t: e.dma_start(out=b[:], in_=src_d[t * 128:(t + 1) * 128, :]),
                              reads=[f'outd{t}'], writes=[f"hb{p}{j}"])
                    yield
                    for j in range(SUP):
                        norm_tile(hb[p][j], f"hb{p}{j}", 'nw2', xs, ss, tp, hnTs[p][:, :, j * 128:(j + 1) * 128], outn=f'hnT{p}')
                        yield

                def back(u):
                    p = u % 2
                    hnT = hnTs[p]
                    for c2 in range(16):
                        pb = hp[c2 % 2]
                        pn = f"hp{c2 % 2}"
                        for cc in range(2):
                            c = c2 * 2 + cc
                            for k in range(8):
                                wh = w1a if c < 16 else w1b
                                whn = 'w1a' if c < 16 else 'w1b'
                                S.op('pe', lambda e, c=c, cc=cc, k=k, pb=pb, wh=wh: e.matmul(
                                    out=pb[:, cc * TW:(cc + 1) * TW], lhsT=wh[:, k, (c % 16) * 128:(c % 16 + 1) * 128], rhs=hnT[:, k, :],
                                    start=(k == 0), stop=(k == 7)), reads=[whn, f'hnT{p}'], writes=[pn])
                        r = rl[c2 % 2]
                        rn = f"rl{c2 % 2}"
                        S.op('act', lambda e, r=r, pb=pb: e.activation(out=r[:], in_=pb[:], func=AF.Relu), reads=[pn], writes=[rn])
                        S.op('dve', lambda e, r=r, c2=c2: e.tensor_tensor(
                            out=hid[:, c2 * 2:(c2 + 1) * 2, :], in0=r[:].rearrange("p (a b) -> p a b", a=2),
                            in1=r[:].rearrange("p (a b) -> p a b", a=2), op=ALU.mult), reads=[rn], writes=[f'hid{c2}'])
                        if c2 % 2 == 1:
                            yield
                    for j in range(SUP):
                        t = u * SUP + j
                        b = hb[p][j]
                        yp = yps[j % 2]
                        ypn = ('ypA', 'ypB')[j % 2]
                        for n in range(2):
                            for k in range(32):
                                S.op('pe', lambda e, n=n, k=k, j=j, yp=yp: e.matmul(out=yp[:, n * 512:(n + 1) * 512], lhsT=hid[:, k, j * 128:(j + 1) * 128],
                                                                                    rhs=w2[:, k, n * 512:(n + 1) * 512], start=(k == 0), stop=(k == 31)),
                                     reads=[f'hid{k // 2}', 'w2'], writes=[f'{ypn}{n}'])
                                if k % 8 == 7:
                                    yield
                        S.op('dve', lambda e, b=b, yp=yp: e.tensor_tensor(out=ob[:], in0=yp[:], in1=b[:], op=ALU.add),
                             reads=[f'{ypn}0', f'{ypn}1', f"hb{p}{j}"], writes=['ob'])
                        S.dma('sp', lambda e, t=t: e.dma_start(out=out_d[t * 128:(t + 1) * 128, :], in_=ob[:]), reads=['ob'], writes=[f'outd{t}'])

                for _ in front(0):
                    pass
                for u in range(NU):
                    gl = [(back(u), 26.0)]
                    if u + 1 < NU:
                        gl.append((front(u + 1), 3.0))
                    drive(gl)
        for n_, f_ in enumerate((pass0, pass1)):
            if n_ in passes:
                f_()
        with ExitStack() as st23:
            shared = {}
            if 3 in passes:
                shared['w1a'] = sbt(st23, "w1a", [128, 8, 2048], BF16)
            if 2 in passes:
                pass2(shared)
            if 3 in passes:
                pass3(shared)
        S.finish()
        with nc.Block() as block:
            S.emit(block)
    return nc


def _kmaj(w, nk):
    n = w.shape[1]
    return np.ascontiguousarray(w.reshape(nk, 128, n).transpose(1, 0, 2).reshape(128, nk * n))


def _colmaj(v):
    return np.ascontiguousarray(v.reshape(-1, 128).T)


def build_consts(NT, inp):
    S_ = NT * 128
    NCMP = (S_ - 32) // 16 + 1
    CO, NCST = cst_layout(NT)
    cst = np.zeros((128, NCST), np.float32)

    def put(name, arr):
        a, b = CO[name]
        cst[:, a:b] = arr
    put('ident', np.eye(128, dtype=np.float32))
    put('nw1', _colmaj(inp['norm1_w'][0]))
    put('nw2', _colmaj(inp['norm2_w'][0]))
    put('eps', np.full((128, 1), EPS, np.float32))
    put('wq', np.tile(inp['q_norm_w'][0][None, :], (128, 1)))
    for j in range(3):
        put(f'wk{j}', np.tile(inp['k_norm_w'][0, j][None, :], (128, 1)))
    put('posk', np.tile(inp['phi_k_pos'][0].T, (2, 1)))
    put('posv', np.tile(inp['phi_v_pos'][0].T, (2, 1)))
    cw = inp['conv_w'][0]
    put('cw', np.concatenate([_colmaj(cw[j]) for j in range(4)], axis=1))
    put('cb', _colmaj(inp['conv_b'][0]))
    put('ba', _colmaj(inp['lru_ba'][0].reshape(-1)))
    put('bx', _colmaj(inp['lru_bx'][0].reshape(-1)))
    put('lam', _colmaj(inp['lru_lambda'][0]))
    half = 8
    inv = (np.float32(500000.0) ** (-np.arange(half, dtype=np.float32) / np.float32(half))).astype(np.float32)
    pos = np.arange(S_, dtype=np.float32)
    ang = (pos[:, None] * inv[None, :]).astype(np.float32)
    cos = np.cos(ang).astype(np.float32).reshape(NT, 128, 8).transpose(1, 0, 2).reshape(128, NT * 8)
    sin = np.sin(ang).astype(np.float32).reshape(NT, 128, 8).transpose(1, 0, 2).reshape(128, NT * 8)
    put('cos', cos)
    put('sin', sin)
    cend = (np.arange(256) * 16 + 31).astype(np.float32)
    angc = (cend[:, None] * inv[None, :]).astype(np.float32)
    put('cosc', np.cos(angc).astype(np.float32).reshape(2, 128, 8).transpose(1, 0, 2).reshape(128, 16))
    put('sinc', np.sin(angc).astype(np.float32).reshape(2, 128, 8).transpose(1, 0, 2).reshape(128, 16))
    A = np.zeros((128, 127), np.float32)
    B = np.ones((128, 127), np.float32)
    for q in range(128):
        cur = 1 if q >= 64 else 0
        for r in range(127):
            rel = r - 63
            if rel > cur:
                A[q, r] = -1e30
                B[q, r] = 0.0
            elif rel == cur or rel == cur - 1:
                A[q, r] = 1e4
                B[q, r] = 0.0
    put('tkA', A)
    put('tkB', B)
    ov = np.zeros((256, 65), np.float32)
    for n in range(NCMP):
        for j in range(min(64, S_ // 64)):
            if 16 * n <= 64 * j + 63 and 16 * n + 31 >= 64 * j:
                ov[n, j] = 1.0
        ov[n, 64] = 1.0
    put('ovl', ov.reshape(2, 128, 65).transpose(1, 0, 2).reshape(128, 130))
    put('mhalf', np.full((128, 8), -0.5, np.float32))
    E = np.zeros((64, S_), np.float32)
    for j in range(S_ // 64):
        E[j, j * 64:(j + 1) * 64] = 1.0
    return cst, E


def build_masks():
    r = np.arange(128)[:, None]
    q = np.arange(128)[None, :]
    mc = np.where(r > q, NEGM, 0.0).astype(np.float32)
    ml = np.where(r <= q, NEGM, 0.0).astype(np.float32)
    mk = np.concatenate([np.tile(mc, (1, 4)), np.tile(ml, (1, 4))], axis=1)
    jw = np.zeros((8, 792), np.float32)
    for rr in range(8):
        jw[rr, rr + 264] = 1.0
        w = np.where(np.arange(128) < 16 * rr + 15, NEGM, 0.0).astype(np.float32)
        jw[rr, 280:792] = np.tile(w, 4)
    return np.ascontiguousarray(mk), jw


def host_weights(inp):
    w_in = inp['w_in'][0]
    cols = lambda a, b: w_in[:, a:b]
    wtm = np.concatenate([cols(0, 512), cols(768, 896), cols(1024, 1152), cols(896, 1024), cols(1152, 1280), cols(1280, 1304)], axis=1)
    wcv = cols(512, 768)
    wxy = cols(1304, 3352)
    wgm = cols(3352, 5400)

    def phi1(w):
        a = w.reshape(32, 64, 256).transpose(1, 0, 2).reshape(64, 32 * 256)
        return np.ascontiguousarray(np.concatenate([a, a], axis=0))
    m = {
        'wtm': _kmaj(wtm, 8), 'wcv': _kmaj(wcv, 8), 'wxy': _kmaj(wxy, 8), 'wgm': _kmaj(wgm, 8),
        'pk1': phi1(inp['phi_k_w1'][0]), 'pv1': phi1(inp['phi_v_w1'][0]),
        'pk2': _kmaj(inp['phi_k_w2'][0], 2), 'pv2': _kmaj(inp['phi_v_w2'][0], 2),
        'wa': np.ascontiguousarray(inp['lru_wa'][0].transpose(1, 0, 2).reshape(128, 1024)),
        'wx': np.ascontiguousarray(inp['lru_wx'][0].transpose(1, 0, 2).reshape(128, 1024)),
        'wnu': _kmaj(inp['w_nsa_up'][0], 4), 'wlu': _kmaj(inp['w_lru_up'][0], 8), 'wo': _kmaj(inp['w_o'][0], 8),
        'wf1': _kmaj(inp['w_ff1'][0], 8), 'wf2': _kmaj(inp['w_ff2'][0], 32),
    }
    return m


def kernel(**inputs):
    inp = {k: np.asarray(v, dtype=np.float32) for k, v in inputs.items()}
    x = inp['x']
    B, S_, _ = x.shape
    NT = S_ // 128
    cst, E = build_consts(NT, inp)
    wm = host_weights(inp)
    mkm, jwm = build_masks()
    nc = build_program(NT)
    in_maps = []
    for b in range(B):
        m = dict(wm)
        m['x'] = np.ascontiguousarray(x[b])
        m['cst'] = cst
        m['emat'] = E
        m['mk'], m['jw'] = mkm, jwm
        in_maps.append(m)
    res = run_bass_kernel_spmd(nc, in_maps, core_ids=list(range(B)))
    return np.stack([np.asarray(r['out'], dtype=np.float32) for r in res.results], axis=0)
```
